# Optimizing a Trainium2 kernel written in Bass

```python
import math
import jax, jax.numpy as jnp
from jax import lax
import numpy as np


D_MODEL = 4096
BATCH = 4
SEQ = 4096
DEPTH = 1

N_META = 16
ATT_HEADS = D_MODEL // 256
HEAD_DIM = 128
ATT_WIDTH = ATT_HEADS * HEAD_DIM
Q_BLOCK = 128
SSM_GROUP_CH = 16
SSM_GROUPS = D_MODEL // 32
SSM_WIDTH = SSM_GROUPS * SSM_GROUP_CH
SSM_STATE = 64
PEER_HEADS = 8
PEER_KEYS = 128
PEER_N = PEER_KEYS * PEER_KEYS
PEER_TOPK = 16
PEER_KEY_DIM = 256
PEER_HALF = PEER_KEY_DIM // 2
PEER_CHUNK = 16
EPS = 1e-6

COL_Q = 0
COL_K = COL_Q + ATT_WIDTH
COL_V = COL_K + ATT_WIDTH
COL_F = COL_V + ATT_WIDTH
COL_U = COL_F + ATT_HEADS
COL_GA = COL_U + SSM_WIDTH
COL_GB = COL_GA + D_MODEL
N_COLS = COL_GB + D_MODEL

kernel_name = 'hybrid_fox_s5_peer_block'


def rmsnorm(x, g):
    xf = x.astype(jnp.float32)
    y = xf * lax.rsqrt(jnp.mean(xf * xf, axis=-1, keepdims=True) + EPS) * g.astype(jnp.float32)
    return y.astype(x.dtype)


def forgetting_attention(q, k, v, logf):
    L = q.shape[1]
    scale = 1.0 / math.sqrt(HEAD_DIM)
    c = jnp.swapaxes(jnp.cumsum(logf, axis=1), 1, 2)
    n_blocks = (L - N_META) // Q_BLOCK
    bounds = [(0, N_META)] + [(N_META + i * Q_BLOCK, N_META + (i + 1) * Q_BLOCK) for i in range(n_blocks)]
    outs = []
    for s0, s1 in bounds:
        qb = q[:, s0:s1]
        kb = k[:, :s1]
        vb = v[:, :s1]
        logits = jnp.einsum('bqhd,bkhd->bhqk', qb, kb, preferred_element_type=jnp.float32) * scale
        logits = logits + (c[:, :, s0:s1, None] - c[:, :, None, :s1])
        mask = (s0 + jnp.arange(s1 - s0))[:, None] >= jnp.arange(s1)[None, :]
        logits = jnp.where(mask, logits, -jnp.inf)
        p = jax.nn.softmax(logits, axis=-1)
        outs.append(jnp.einsum('bhqk,bkhd->bqhd', p.astype(vb.dtype), vb))
    return jnp.concatenate(outs, axis=1)


def s5_branch(u, lam_re, lam_im, log_dt, b_re, b_im, c_re, c_im, d_skip, w_glu):
    Bb, L, _ = u.shape
    f32 = jnp.float32
    uf = u.astype(f32).reshape(Bb, L, SSM_GROUPS, SSM_GROUP_CH)
    dt = jnp.exp(log_dt.astype(f32))[:, None]
    lr = lam_re.astype(f32)
    li = lam_im.astype(f32)
    mag = jnp.exp(lr * dt)
    ar = mag * jnp.cos(li * dt)
    ai = mag * jnp.sin(li * dt)
    nr = ar - 1.0
    den = lr * lr + li * li
    fr = (nr * lr + ai * li) / den
    fi = (ai * lr - nr * li) / den
    br = b_re.astype(f32)
    bi = b_im.astype(f32)
    bbr = fr[..., None] * br - fi[..., None] * bi
    bbi = fr[..., None] * bi + fi[..., None] * br
    xr = jnp.einsum('blgc,gpc->blgp', uf, bbr)
    xi = jnp.einsum('blgc,gpc->blgp', uf, bbi)
    a_r = jnp.broadcast_to(ar, (1, L, SSM_GROUPS, SSM_STATE))
    a_i = jnp.broadcast_to(ai, (1, L, SSM_GROUPS, SSM_STATE))

    def combine(e1, e2):
        a1r, a1i, b1r, b1i = e1
        a2r, a2i, b2r, b2i = e2
        return (a1r * a2r - a1i * a2i,
                a1r * a2i + a1i * a2r,
                a2r * b1r - a2i * b1i + b2r,
                a2r * b1i + a2i * b1r + b2i)

    _, _, sr, si = lax.associative_scan(combine, (a_r, a_i, xr, xi), axis=1)
    y = (jnp.einsum('blgp,gcp->blgc', sr, c_re.astype(f32))
         - jnp.einsum('blgp,gcp->blgc', si, c_im.astype(f32))
         + d_skip.astype(f32) * uf)
    z = jax.nn.gelu(y.reshape(Bb, L, SSM_WIDTH))
    out = z * jax.nn.sigmoid(z @ w_glu.astype(f32))
    return out.astype(u.dtype)


def token_mixing(hn, w_in, b_forget, q_norm_g, k_norm_g, lam_re, lam_im, log_dt,
                 b_re, b_im, c_re, c_im, d_skip, w_glu, w_branch_attn, w_branch_ssm, w_out):
    Bb, L, _ = hn.shape
    proj = hn @ w_in
    q = rmsnorm(proj[..., COL_Q:COL_K].reshape(Bb, L, ATT_HEADS, HEAD_DIM), q_norm_g)
    k = rmsnorm(proj[..., COL_K:COL_V].reshape(Bb, L, ATT_HEADS, HEAD_DIM), k_norm_g)
    v = proj[..., COL_V:COL_F].reshape(Bb, L, ATT_HEADS, HEAD_DIM)
    logf = jax.nn.log_sigmoid((proj[..., COL_F:COL_U] + b_forget).astype(jnp.float32))
    attn = forgetting_attention(q, k, v, logf).reshape(Bb, L, ATT_WIDTH)
    ssm = s5_branch(proj[..., COL_U:COL_GA], lam_re, lam_im, log_dt, b_re, b_im,
                    c_re, c_im, d_skip, w_glu)
    g_a = jax.nn.sigmoid(proj[..., COL_GA:COL_GB])
    g_b = jax.nn.sigmoid(proj[..., COL_GB:N_COLS])
    mix = g_a * (attn @ w_branch_attn) + g_b * (ssm @ w_branch_ssm)
    return mix @ w_out


def peer_ffn(hn, w_query, sub_keys, expert_u, expert_v):
    Bb, L, D = hn.shape
    T = Bb * L
    xt = hn.reshape(T, D)
    q = (xt @ w_query).reshape(T, PEER_HEADS, 2, PEER_HALF)
    s = jnp.einsum('thcd,cnd->thcn', q, sub_keys, preferred_element_type=jnp.float32)
    sv, si = lax.top_k(s, PEER_TOPK)
    cand = (sv[:, :, 0, :, None] + sv[:, :, 1, None, :]).reshape(T, PEER_HEADS, PEER_TOPK * PEER_TOPK)
    best, pos = lax.top_k(cand, PEER_TOPK)
    i1 = jnp.take_along_axis(si[:, :, 0], pos // PEER_TOPK, axis=-1)
    i2 = jnp.take_along_axis(si[:, :, 1], pos % PEER_TOPK, axis=-1)
    experts = i1 * PEER_KEYS + i2
    gates = jax.nn.softmax(best, axis=-1)
    n_chunks = T // PEER_CHUNK

    def chunk(args):
        xc, ec, gc = args
        u = jnp.take(expert_u, ec, axis=0)
        a = jnp.einsum('cd,chkd->chk', xc, u, preferred_element_type=jnp.float32)
        w = (gc * jax.nn.gelu(a)).astype(xc.dtype)
        vv = jnp.take(expert_v, ec, axis=0)
        return jnp.einsum('chk,chkd->cd', w, vv)

    y = lax.map(chunk, (xt.reshape(n_chunks, PEER_CHUNK, D),
                        experts.reshape(n_chunks, PEER_CHUNK, PEER_HEADS, PEER_TOPK),
                        gates.reshape(n_chunks, PEER_CHUNK, PEER_HEADS, PEER_TOPK)))
    return y.reshape(Bb, L, D)


def setup_inputs(seed: int = 0) -> dict:
    key = jax.random.key(seed)
    ks = jax.random.split(key, 24)
    f32 = jnp.float32
    nrm = lambda k, shp, s: jax.random.normal(k, shp, f32) * s
    lam_im_base = math.pi * jnp.arange(SSM_STATE, dtype=f32)
    return {
        'x': nrm(ks[0], (BATCH, SEQ, D_MODEL), 1.0),
        'meta_tokens': nrm(ks[1], (N_META, D_MODEL), 1.0),
        'norm1_g': 1.0 + nrm(ks[2], (DEPTH, D_MODEL), 0.02),
        'w_in': nrm(ks[3], (DEPTH, D_MODEL, N_COLS), D_MODEL ** -0.5),
        'b_forget': 3.0 + nrm(ks[4], (DEPTH, ATT_HEADS), 0.5),
        'q_norm_g': 1.0 + nrm(ks[5], (DEPTH, HEAD_DIM), 0.02),
        'k_norm_g': 1.0 + nrm(ks[6], (DEPTH, HEAD_DIM), 0.02),
        'lam_re': -0.5 + nrm(ks[7], (DEPTH, SSM_GROUPS, SSM_STATE), 0.01),
        'lam_im': lam_im_base + nrm(ks[8], (DEPTH, SSM_GROUPS, SSM_STATE), 0.01),
        'log_dt': jax.random.uniform(ks[9], (DEPTH, SSM_GROUPS), f32, math.log(1e-3), math.log(1e-1)),
        'b_re': nrm(ks[10], (DEPTH, SSM_GROUPS, SSM_STATE, SSM_GROUP_CH), (2 * SSM_GROUP_CH) ** -0.5),
        'b_im': nrm(ks[11], (DEPTH, SSM_GROUPS, SSM_STATE, SSM_GROUP_CH), (2 * SSM_GROUP_CH) ** -0.5),
        'c_re': nrm(ks[12], (DEPTH, SSM_GROUPS, SSM_GROUP_CH, SSM_STATE), SSM_STATE ** -0.5),
        'c_im': nrm(ks[13], (DEPTH, SSM_GROUPS, SSM_GROUP_CH, SSM_STATE), SSM_STATE ** -0.5),
        'd_skip': 1.0 + nrm(ks[14], (DEPTH, SSM_GROUPS, SSM_GROUP_CH), 0.1),
        'w_glu': nrm(ks[15], (DEPTH, SSM_WIDTH, SSM_WIDTH), SSM_WIDTH ** -0.5),
        'w_branch_attn': nrm(ks[16], (DEPTH, ATT_WIDTH, D_MODEL), ATT_WIDTH ** -0.5),
        'w_branch_ssm': nrm(ks[17], (DEPTH, SSM_WIDTH, D_MODEL), SSM_WIDTH ** -0.5),
        'w_out': nrm(ks[18], (DEPTH, D_MODEL, D_MODEL), D_MODEL ** -0.5),
        'norm2_g': 1.0 + nrm(ks[19], (DEPTH, D_MODEL), 0.02),
        'w_query': nrm(ks[20], (DEPTH, D_MODEL, PEER_HEADS * PEER_KEY_DIM), D_MODEL ** -0.5),
        'sub_keys': nrm(ks[21], (DEPTH, 2, PEER_KEYS, PEER_HALF), PEER_HALF ** -0.5),
        'expert_u': nrm(ks[22], (DEPTH, PEER_N, D_MODEL), D_MODEL ** -0.5),
        'expert_v': nrm(ks[23], (DEPTH, PEER_N, D_MODEL), (PEER_HEADS * PEER_TOPK) ** -0.5),
    }


def reference(x, meta_tokens, norm1_g, w_in, b_forget, q_norm_g, k_norm_g, lam_re, lam_im,
              log_dt, b_re, b_im, c_re, c_im, d_skip, w_glu, w_branch_attn, w_branch_ssm,
              w_out, norm2_g, w_query, sub_keys, expert_u, expert_v):
    Bb = x.shape[0]
    meta = jnp.broadcast_to(meta_tokens[None].astype(x.dtype), (Bb, N_META, x.shape[-1]))
    h = jnp.concatenate([meta, x], axis=1)
    for layer in range(DEPTH):
        h = h + token_mixing(rmsnorm(h, norm1_g[layer]), w_in[layer], b_forget[layer],
                             q_norm_g[layer], k_norm_g[layer], lam_re[layer], lam_im[layer],
                             log_dt[layer], b_re[layer], b_im[layer], c_re[layer], c_im[layer],
                             d_skip[layer], w_glu[layer], w_branch_attn[layer],
                             w_branch_ssm[layer], w_out[layer])
        h = h + peer_ffn(rmsnorm(h, norm2_g[layer]), w_query[layer], sub_keys[layer],
                         expert_u[layer], expert_v[layer])
    return h[:, N_META:]
```

```python
import math
from contextlib import ExitStack
import numpy as np
import ml_dtypes
import concourse.bass as bass
import concourse.mybir as mybir
from concourse.bass_utils import run_bass_kernel_spmd

F32 = mybir.dt.float32
BF16 = mybir.dt.bfloat16
I32 = mybir.dt.int32
AF = mybir.ActivationFunctionType
ALU = mybir.AluOpType
AX = mybir.AxisListType

EPOCH = 16000
NDSEM = 40
TWO_PI = 2.0 * math.pi
STAB = 30.0


class Buf:
    __slots__ = ("w", "r")

    def __init__(self):
        self.w = {}
        self.r = {}


class TT:
    def __init__(self, t):
        self.t = t
        self.b = Buf()


def _b(x):
    return x.b if isinstance(x, TT) else x


class Sched:
    def __init__(self, nc, stack):
        self.nc = nc
        self.engs = {"pe": nc.tensor, "dve": nc.vector, "act": nc.scalar,
                     "pool": nc.gpsimd, "sp": nc.sync}
        self.stack = stack
        self.esem = {}
        self.cnt = {e: 0 for e in self.engs}
        self.seen = {e: {} for e in self.engs}
        self.dsem = [stack.enter_context(nc.semaphore(f"d{i}")) for i in range(NDSEM)]
        self.dval = [0] * NDSEM
        self.dnext = 0
        self.ninst = 0

    def _esem(self, eng, epoch):
        k = (eng, epoch)
        if k not in self.esem:
            self.esem[k] = self.stack.enter_context(self.nc.semaphore(f"e_{eng}_{epoch}"))
        return self.esem[k]

    def _sem_of(self, key):
        if key[0] == "d":
            return self.dsem[key[1]]
        return self._esem(key[0], key[1])

    def _wait(self, eng, deps):
        s = self.seen[eng]
        for k, v in deps.items():
            if eng == "pe" and k[0] == "pe":
                continue
            if s.get(k, 0) < v:
                self.engs[eng].wait_ge(self._sem_of(k), v)
                s[k] = v

    @staticmethod
    def _acc(deps, d):
        for k, v in d.items():
            if deps.get(k, 0) < v:
                deps[k] = v

    def _commit(self, tok, reads, writes, part=False):
        k, v = tok
        for b in writes:
            if not part:
                b.w = {}
            b.w[k] = max(b.w.get(k, 0), v)
            b.r = {}
        for b in reads:
            if b.r.get(k, 0) < v:
                b.r[k] = v

    def op(self, eng, fn, reads=(), writes=()):
        reads = [_b(x) for x in reads]
        writes = [_b(x) for x in writes]
        deps = {}
        for b in reads:
            self._acc(deps, b.w)
        for b in writes:
            self._acc(deps, b.w)
            self._acc(deps, b.r)
        self._wait(eng, deps)
        ins = fn(self.engs[eng])
        c = self.cnt[eng]
        epoch, val = divmod(c, EPOCH)
        ins.then_inc(self._esem(eng, epoch), 1)
        self.cnt[eng] = c + 1
        self._commit(((eng, epoch), val + 1), reads, writes)
        self.ninst += 1
        return ins

    def dma(self, q, out, in_, reads=(), writes=(), part=False):
        reads = [_b(x) for x in reads]
        writes = [_b(x) for x in writes]
        deps = {}
        for b in reads:
            self._acc(deps, b.w)
        for b in writes:
            if not part:
                self._acc(deps, b.w)
            self._acc(deps, b.r)
        i = self.dnext
        self.dnext = (i + 1) % NDSEM
        if self.dval[i]:
            deps[("d", i)] = max(deps.get(("d", i), 0), self.dval[i])
        self._wait(q, deps)
        ins = self.engs[q].dma_start(out=out, in_=in_)
        ins.then_inc(self.dsem[i], 16)
        self.dval[i] += 16
        assert self.dval[i] < 60000
        self._commit((("d", i), self.dval[i]), reads, writes, part=part)
        self.ninst += 1
        return ins


def _barrier(S):
    deps = {}
    for e, cnt in S.cnt.items():
        if cnt:
            epoch, val = divmod(cnt - 1, EPOCH)
            deps[(e, epoch)] = val + 1
    for i in range(NDSEM):
        if S.dval[i]:
            deps[("d", i)] = S.dval[i]
    for e in S.engs:
        s = S.seen[e]
        for k, v in deps.items():
            if k[0] == e:
                continue
            if s.get(k, 0) < v:
                S.engs[e].wait_ge(S._sem_of(k), v)
                s[k] = v


class Cfg:
    def __init__(s, D=4096, SEQ=4096, B=4, NPAN=1024, PPAN=512, SEGB=2, WB6=256):
        s.D, s.SEQ, s.B = D, SEQ, B
        s.NM = 16
        s.H = D // 256
        s.AW = s.H * 128
        s.G = D // 32
        s.SW = s.G * 16
        s.NJ = s.G // 2
        s.NCT = s.SW // 128
        s.PH, s.PK, s.TOPK = 8, 128, 16
        s.PN = s.PK * s.PK
        s.QW = s.PH * 256
        s.L = SEQ + s.NM
        s.NO = SEQ // 2
        s.NB = s.NO // 128
        s.KT = D // 128
        s.NPAN = min(NPAN, s.NO)
        s.PPAN = min(PPAN, s.NO)
        s.SEGB = min(SEGB, s.NB)
        s.WB6 = WB6
        s.COL_Q = 0
        s.COL_K = s.AW
        s.COL_V = 2 * s.AW
        s.COL_F = 3 * s.AW
        s.COL_U = s.COL_F + s.H
        s.COL_GA = s.COL_U + s.SW
        s.COL_GB = s.COL_GA + D
        s.NCOLS = s.COL_GB + D


def build(cfg):
    c = cfg
    D, L, NO, KT, H, NB = c.D, c.L, c.NO, c.KT, c.H, c.NB
    nc = bass.Bass("TRN2", target_bir_lowering=False)

    def din(name, shape, dt=F32):
        return nc.dram_tensor(name, list(shape), dt, kind="ExternalInput").ap()

    def dscr(name, shape, dt):
        return TT(nc.dram_tensor(name, list(shape), dt, kind="ExternalOutput" if getattr(c, "debug", False) else "Internal").ap())

    x_ctx = din("x_ctx", [L, D]); x_own = din("x_own", [NO, D])
    g1rep = din("g1rep", [128, D]); g2rep = din("g2rep", [128, D])
    w_in = din("w_in", [D, c.NCOLS])
    bfg = din("bfg", [H, 1]); qg = din("qg", [128, 1]); kg = din("kg", [128, 1])
    lrA = din("lrA", [128, c.NJ]); liA = din("liA", [128, c.NJ]); ldA = din("ldA", [128, c.NJ])
    lrB = din("lrB", [128, c.NJ * 128]); liB = din("liB", [128, c.NJ * 128]); ldB = din("ldB", [128, c.NJ * 128])
    brB = din("brB", [128, c.NJ * 128]); biB = din("biB", [128, c.NJ * 128])
    crB = din("crB", [128, c.NJ * 128]); ciB = din("ciB", [128, c.NJ * 128])
    dsk = din("dsk", [128, c.NCT])
    w_glu = din("w_glu", [c.SW, c.SW]); w_a = din("w_a", [c.AW, D]); w_b = din("w_b", [c.SW, D])
    w_out = din("w_out", [D, D]); w_query = din("w_query", [D, c.QW])
    skT = din("skT", [128, 2 * 128])
    euT = din("euT", [D, c.PN]); ev = din("ev", [c.PN, D])
    maskA = din("maskA", [128, 128]); maskB = din("maskB", [128, 128])
    w01 = din("w01", [128, 2])
    t_loc = din("t_loc", [128, 256 * c.SEGB + 16]); t_own = din("t_own", [128, NO])
    ident = din("ident", [128, 128])
    out_own = nc.dram_tensor("out_own", [NO, D], F32, kind="ExternalOutput").ap()

    KTd = dscr("KTd", [H, 128, L], BF16)
    Vd = dscr("Vd", [L, c.AW], BF16)
    UTd = dscr("UTd", [c.SW, L], BF16)
    QTd = dscr("QTd", [H, 128, NO], BF16)
    GAd = dscr("GAd", [D, NO], F32)
    GBd = dscr("GBd", [D, NO], F32)
    UOd = dscr("UOd", [c.SW, NO], F32)
    CQd = dscr("CQd", [H, NO], F32)
    CQ3d = dscr("CQ3d", [3, H, NO], BF16)
    BBRd = dscr("BBRd", [128, c.NJ * 128], BF16)
    BBId = dscr("BBId", [128, c.NJ * 128], BF16)
    ZFd = dscr("ZFd", [c.SW, NO], F32)
    ZTd = dscr("ZTd", [c.SW, NO], BF16)
    SSTd = dscr("SSTd", [c.SW, NO], BF16)
    ATd = dscr("ATd", [c.AW, NO], BF16)
    H1d = dscr("H1d", [NO, D], F32)
    WGTd = dscr("WGTd", [c.PN, NO], BF16)
    OUTb = Buf()

    with ExitStack() as gst:
        S = Sched(nc, gst)

        uid = {"n": 0}

        def sb(st, name, shape, dt):
            uid["n"] += 1
            return TT(st.enter_context(nc.sbuf_tensor(f"{name}_{uid['n']}", list(shape), dt)))

        ps = [TT(gst.enter_context(nc.psum_tensor(f"ps{i}", [128, 512], F32))) for i in range(8)]
        psb = []
        for i in range(2):
            v = TT(ps[6 + i].t.bitcast(BF16))
            v.b = ps[6 + i].b
            psb.append(v)
        id_bf = sb(gst, "id_bf", [128, 128], BF16)
        id_f = sb(gst, "id_f", [128, 128], F32)
        ones_bf = sb(gst, "ones_bf", [128, 128], BF16)
        ones_f = sb(gst, "ones_f", [1, 128], F32)
        cst = sb(gst, "cst", [128, 8], F32)
        w01t = sb(gst, "w01t", [128, 2], F32)
        qgt = sb(gst, "qgt", [128, 1], F32); kgt = sb(gst, "kgt", [128, 1], F32)
        cacc = sb(gst, "cacc", [16, L], F32)
        rhoA = sb(gst, "rhoA", [128, c.NJ], F32)
        phiA = sb(gst, "phiA", [128, c.NJ], F32)
        S.dma("pool", id_bf.t[:], ident, writes=[id_bf])
        S.dma("sp", id_f.t[:], ident, writes=[id_f])
        S.dma("sp", w01t.t[:], w01, writes=[w01t])
        S.dma("sp", qgt.t[:], qg, writes=[qgt])
        S.dma("sp", kgt.t[:], kg, writes=[kgt])
        S.op("dve", lambda e: e.memset(ones_bf.t[:], 1.0), writes=[ones_bf])
        S.op("dve", lambda e: e.memset(ones_f.t[:], 1.0), writes=[ones_f])
        S.op("dve", lambda e: e.memset(cst.t[:, 0:1], 1e-6), writes=[cst])
        S.op("dve", lambda e: e.memset(cst.t[:, 1:2], 1.0), writes=[cst])
        S.op("dve", lambda e: e.memset(cst.t[:, 2:3], 0.0), writes=[cst])
        EPS = cst.t[:, 0:1]
        ONE = cst.t[:, 1:2]

        rr = {"e": 0}

        def alt():
            rr["e"] ^= 1
            return "act" if rr["e"] else "dve"

        def copy(eng, out, in_, reads, writes):
            if eng == "act":
                S.op("act", lambda e: e.activation(out, in_, AF.Copy), reads=reads, writes=writes)
            else:
                S.op(eng, lambda e: e.tensor_copy(out, in_), reads=reads, writes=writes)

        def make_nt(st, tagp):
          xt = sb(st, tagp + "xt", [128, D], F32)
          xn = sb(st, tagp + "xn", [128, D], BF16)
          ss = sb(st, tagp + "ss", [128, 2], F32)

          def norm_transpose(src, n_rows, grep, panel, srcbuf=None):
            for t0 in range(0, n_rows, 128):
                r = min(128, n_rows - t0)
                S.dma("sp", xt.t[:r, :], src[t0:t0 + r, :], reads=[srcbuf] if srcbuf else [], writes=[xt])
                S.op("dve", lambda e: e.memset(ss.t[:r, 0:1], 0.0), writes=[ss])
                S.op("act", lambda e: e.activation(xn.t[:r, :], xt.t[:r, :], AF.Square, accum_out=ss.t[:r, 0:1]),
                     reads=[xt], writes=[xn, ss])
                S.op("dve", lambda e: e.tensor_scalar(ss.t[:r, 1:2], ss.t[:r, 0:1], 1.0 / D, 1e-6, ALU.mult, ALU.add),
                     reads=[ss], writes=[ss])
                S.op("act", lambda e: e.activation(ss.t[:r, 1:2], ss.t[:r, 1:2], AF.Sqrt), reads=[ss], writes=[ss])
                S.op("dve", lambda e: e.reciprocal(ss.t[:r, 1:2], ss.t[:r, 1:2]), reads=[ss], writes=[ss])
                S.op("dve", lambda e: e.scalar_tensor_tensor(xn.t[:r, :], xt.t[:r, :], ss.t[:r, 1:2], grep.t[:r, :],
                                                             ALU.mult, ALU.mult),
                     reads=[xt, ss, grep], writes=[xn])
                for k0 in range(0, KT, 8):
                    k1 = min(KT, k0 + 8)
                    pb = psb[(k0 // 8) % 2]
                    for kt in range(k0, k1):
                        S.op("pe", lambda e: e.transpose(pb.t[:, (kt - k0) * 128:(kt - k0) * 128 + r],
                                                         xn.t[:r, kt * 128:(kt + 1) * 128], id_bf.t[:r, :r]),
                             reads=[xn, id_bf], writes=[pb])
                    src_ap = pb.t[:, 0:(k1 - k0) * 128].rearrange("p (k r) -> p k r", r=128)[:, :, :r]
                    copy(alt(), panel.t[:, k0:k1, t0:t0 + r], src_ap, [pb], [panel])
          return norm_transpose

        wstate = {"i": 0}

        def gemm(mode, wbufs, wsrc, K, c0, ncols, act, actbufs, N, epi, kgroups=None, WB=None, banks=(0, 1, 2, 3),
                 wq="pool", wsrcbuf=None):
            KTl = K // 128
            WB = WB or wbufs[0].t.shape[2]
            kgroups = kgroups or [(0, KTl)]
            bi = 0
            for sb0 in range(0, ncols, WB):
                cw = min(WB, ncols - sb0)
                wt = wbufs[wstate["i"] % len(wbufs)]
                wstate["i"] += 1
                srcs = wsrc if isinstance(wsrc, list) else [(wsrc, K)]
                kbase = 0
                for (wap, Ks) in srcs:
                    for k0 in range(0, Ks // 128, 8):
                        k1 = min(Ks // 128, k0 + 8)
                        srcap = wap[k0 * 128:k1 * 128, c0 + sb0:c0 + sb0 + cw].rearrange("(kt p) c -> p kt c", p=128)
                        S.dma(wq, wt.t[:, kbase + k0:kbase + k1, :cw], srcap, reads=[wsrcbuf] if wsrcbuf else [],
                              writes=[wt], part=True)
                    kbase += Ks // 128
                if mode == "fm":
                    for j0 in range(0, cw, 128):
                        m = min(128, cw - j0)
                        for n0 in range(0, N, 512):
                            n1 = min(N, n0 + 512)
                            pss = []
                            for (ka, kb) in kgroups:
                                pb = ps[banks[bi % len(banks)]]
                                bi += 1
                                for kt in range(ka, kb):
                                    S.op("pe", lambda e: e.matmul(pb.t[:m, :n1 - n0], wt.t[:, kt, j0:j0 + m],
                                                                  act(kt, n0, n1), start=(kt == ka), stop=(kt == kb - 1)),
                                         reads=[wt] + actbufs, writes=[pb])
                                pss.append(pb)
                            epi(c0 + sb0 + j0, m, n0, n1, pss)
                else:
                    for t0 in range(0, N, 128):
                        t1 = min(N, t0 + 128)
                        pss = []
                        for (ka, kb) in kgroups:
                            pb = ps[banks[bi % len(banks)]]
                            bi += 1
                            for kt in range(ka, kb):
                                S.op("pe", lambda e: e.matmul(pb.t[:t1 - t0, :cw], act(kt, t0, t1), wt.t[:, kt, :cw],
                                                              start=(kt == ka), stop=(kt == kb - 1)),
                                     reads=[wt] + actbufs, writes=[pb])
                            pss.append(pb)
                        epi(t0, t1, c0 + sb0, cw, pss)

        def coeffs(st, tag, lr_ap, li_ap, ld_ap, F):
            lr = sb(st, tag + "lr", [128, F], F32); li = sb(st, tag + "li", [128, F], F32)
            ld = sb(st, tag + "ld", [128, F], F32)
            t1 = sb(st, tag + "t1", [128, F], F32); t2 = sb(st, tag + "t2", [128, F], F32)
            ti = sb(st, tag + "ti", [128, F], I32)
            rho = sb(st, tag + "rho", [128, F], F32); phi = sb(st, tag + "phi", [128, F], F32)
            sn = sb(st, tag + "sn", [128, F], F32); cs = sb(st, tag + "cs", [128, F], F32)
            fr = sb(st, tag + "fr", [128, F], F32); fi = sb(st, tag + "fi", [128, F], F32)

            def run(lr_src, li_src, ld_src):
                S.dma("sp", lr.t[:], lr_src, writes=[lr]); S.dma("sp", li.t[:], li_src, writes=[li])
                S.dma("sp", ld.t[:], ld_src, writes=[ld])
                S.op("act", lambda e: e.activation(ld.t[:], ld.t[:], AF.Exp), reads=[ld], writes=[ld])
                S.op("dve", lambda e: e.tensor_tensor(t1.t[:], lr.t[:], ld.t[:], ALU.mult), reads=[lr, ld], writes=[t1])
                S.op("act", lambda e: e.activation(rho.t[:], t1.t[:], AF.Exp), reads=[t1], writes=[rho])
                S.op("dve", lambda e: e.scalar_tensor_tensor(t1.t[:], li.t[:], 1.0 / TWO_PI, ld.t[:], ALU.mult, ALU.mult),
                     reads=[li, ld], writes=[t1])
                S.op("dve", lambda e: e.tensor_copy(ti.t[:], t1.t[:]), reads=[t1], writes=[ti])
                S.op("dve", lambda e: e.tensor_copy(t2.t[:], ti.t[:]), reads=[ti], writes=[t2])
                S.op("dve", lambda e: e.tensor_sub(phi.t[:], t1.t[:], t2.t[:]), reads=[t1, t2], writes=[phi])
                S.op("act", lambda e: e.activation(sn.t[:], phi.t[:], AF.Sin, scale=TWO_PI), reads=[phi], writes=[sn])
                S.op("dve", lambda e: e.tensor_scalar(t1.t[:], phi.t[:], 0.25, None, ALU.add), reads=[phi], writes=[t1])
                S.op("dve", lambda e: e.tensor_copy(ti.t[:], t1.t[:]), reads=[t1], writes=[ti])
                S.op("dve", lambda e: e.tensor_copy(t2.t[:], ti.t[:]), reads=[ti], writes=[t2])
                S.op("dve", lambda e: e.tensor_sub(t1.t[:], t1.t[:], t2.t[:]), reads=[t1, t2], writes=[t1])
                S.op("act", lambda e: e.activation(cs.t[:], t1.t[:], AF.Sin, scale=TWO_PI), reads=[t1], writes=[cs])
                S.op("dve", lambda e: e.tensor_tensor(cs.t[:], cs.t[:], rho.t[:], ALU.mult), reads=[cs, rho], writes=[cs])
                S.op("dve", lambda e: e.tensor_tensor(sn.t[:], sn.t[:], rho.t[:], ALU.mult), reads=[sn, rho], writes=[sn])
                S.op("dve", lambda e: e.tensor_scalar(t1.t[:], cs.t[:], -1.0, None, ALU.add), reads=[cs], writes=[t1])
                S.op("dve", lambda e: e.tensor_tensor(t2.t[:], lr.t[:], lr.t[:], ALU.mult), reads=[lr], writes=[t2])
                S.op("dve", lambda e: e.tensor_tensor(fr.t[:], li.t[:], li.t[:], ALU.mult), reads=[li], writes=[fr])
                S.op("dve", lambda e: e.tensor_add(t2.t[:], t2.t[:], fr.t[:]), reads=[t2, fr], writes=[t2])
                S.op("dve", lambda e: e.reciprocal(t2.t[:], t2.t[:]), reads=[t2], writes=[t2])
                S.op("dve", lambda e: e.tensor_tensor(fr.t[:], t1.t[:], lr.t[:], ALU.mult), reads=[t1, lr], writes=[fr])
                S.op("dve", lambda e: e.tensor_tensor(fi.t[:], sn.t[:], li.t[:], ALU.mult), reads=[sn, li], writes=[fi])
                S.op("dve", lambda e: e.tensor_add(fr.t[:], fr.t[:], fi.t[:]), reads=[fr, fi], writes=[fr])
                S.op("dve", lambda e: e.tensor_tensor(fr.t[:], fr.t[:], t2.t[:], ALU.mult), reads=[fr, t2], writes=[fr])
                S.op("dve", lambda e: e.tensor_tensor(fi.t[:], sn.t[:], lr.t[:], ALU.mult), reads=[sn, lr], writes=[fi])
                S.op("dve", lambda e: e.tensor_tensor(t1.t[:], t1.t[:], li.t[:], ALU.mult), reads=[t1, li], writes=[t1])
                S.op("dve", lambda e: e.tensor_sub(fi.t[:], fi.t[:], t1.t[:]), reads=[fi, t1], writes=[fi])
                S.op("dve", lambda e: e.tensor_tensor(fi.t[:], fi.t[:], t2.t[:], ALU.mult), reads=[fi, t2], writes=[fi])
            return run, rho, phi, fr, fi

        with ExitStack() as st:
            run, rho, phi, fr, fi = coeffs(st, "ca", None, None, None, c.NJ)
            run(lrA, liA, ldA)
            S.op("dve", lambda e: e.tensor_copy(rhoA.t[:], rho.t[:]), reads=[rho], writes=[rhoA])
            S.op("dve", lambda e: e.tensor_copy(phiA.t[:], phi.t[:]), reads=[phi], writes=[phiA])
        _barrier(S)
        with ExitStack() as st:
            FC = min(1024, c.NJ * 128)
            run, rho, phi, fr, fi = coeffs(st, "cb", None, None, None, FC)
            br = sb(st, "br", [128, FC], F32); bi_ = sb(st, "bi", [128, FC], F32)
            o1 = sb(st, "o1", [128, FC], F32); o2 = sb(st, "o2", [128, FC], F32)
            ob1 = sb(st, "ob1", [128, FC], BF16); ob2 = sb(st, "ob2", [128, FC], BF16)
            for f0 in range(0, c.NJ * 128, FC):
                run(lrB[:, f0:f0 + FC], liB[:, f0:f0 + FC], ldB[:, f0:f0 + FC])
                S.dma("sp", br.t[:], brB[:, f0:f0 + FC], writes=[br])
                S.dma("sp", bi_.t[:], biB[:, f0:f0 + FC], writes=[bi_])
                S.op("dve", lambda e: e.tensor_tensor(o1.t[:], fr.t[:], br.t[:], ALU.mult), reads=[fr, br], writes=[o1])
                S.op("dve", lambda e: e.tensor_tensor(o2.t[:], fi.t[:], bi_.t[:], ALU.mult), reads=[fi, bi_], writes=[o2])
                S.op("dve", lambda e: e.tensor_sub(ob1.t[:], o1.t[:], o2.t[:]), reads=[o1, o2], writes=[ob1])
                S.op("dve", lambda e: e.tensor_tensor(o1.t[:], fr.t[:], bi_.t[:], ALU.mult), reads=[fr, bi_], writes=[o1])
                S.op("dve", lambda e: e.tensor_tensor(o2.t[:], fi.t[:], br.t[:], ALU.mult), reads=[fi, br], writes=[o2])
                S.op("dve", lambda e: e.tensor_add(ob2.t[:], o1.t[:], o2.t[:]), reads=[o1, o2], writes=[ob2])
                S.dma("sp", BBRd.t[:, f0:f0 + FC], ob1.t[:], reads=[ob1], writes=[BBRd], part=True)
                S.dma("sp", BBId.t[:, f0:f0 + FC], ob2.t[:], reads=[ob2], writes=[BBId], part=True)

        def proj_phase(is_ctx):
            with ExitStack() as st:
                tag = "c" if is_ctx else "o"
                NPmax = c.NPAN + (c.NM if is_ctx else 0)
                panel = sb(st, tag + "pan", [128, KT, NPmax], BF16)
                grep = sb(st, tag + "g1", [128, D], F32)
                S.dma("sp", grep.t[:], g1rep, writes=[grep])
                wbufs = [sb(st, tag + f"w{i}", [128, KT, 256], BF16) for i in range(2)]
                stg = [sb(st, tag + f"stg{i}", [128, 512], BF16) for i in range(3)]
                stf = [sb(st, tag + f"stf{i}", [128, 512], F32) for i in range(3)]
                sq = sb(st, tag + "sq", [128, 512], BF16)
                rinv = sb(st, tag + "rinv", [128, 512], F32)
                bft = sb(st, tag + "bft", [16, 1], F32)
                lf = sb(st, tag + "lf", [16, 512], F32)
                ones_t = sb(st, tag + "ones", [16, 512], F32)
                S.dma("sp", bft.t[:H, :], bfg, writes=[bft])
                S.op("dve", lambda e: e.tensor_scalar(bft.t[:H, :], bft.t[:H, :], -1.0, None, ALU.mult), reads=[bft], writes=[bft])
                S.op("dve", lambda e: e.memset(ones_t.t[:], 1.0), writes=[ones_t])
                si = {"g": 0, "f": 0}
                src = x_ctx if is_ctx else x_own
                ntot = L if is_ctx else NO
                p0 = 0
                nt = make_nt(st, tag)
                while p0 < ntot:
                    pn = min(c.NPAN + (c.NM if (is_ctx and p0 == 0) else 0), ntot - p0)
                    nt(src[p0:p0 + pn, :], pn, grep, panel)

                    def act(kt, n0, n1):
                        return panel.t[:, kt, n0:n1]

                    def epi_qk(dst, gcol, colbase):
                        def epi(col, m, n0, n1, pss):
                            n = n1 - n0
                            h = (col - colbase) // 128
                            pb = pss[0]
                            S.op("act", lambda e: e.activation(sq.t[:, :n], pb.t[:, :n], AF.Square), reads=[pb], writes=[sq])
                            p2 = ps[4]
                            S.op("pe", lambda e: e.matmul(p2.t[:, :n], ones_bf.t[:], sq.t[:, :n], start=True, stop=True),
                                 reads=[ones_bf, sq], writes=[p2])
                            S.op("act", lambda e: e.activation(rinv.t[:, :n], p2.t[:, :n], AF.Sqrt, bias=EPS, scale=1.0 / 128),
                                 reads=[p2, cst], writes=[rinv])
                            S.op("dve", lambda e: e.reciprocal(rinv.t[:, :n], rinv.t[:, :n]), reads=[rinv], writes=[rinv])
                            sg = stg[si["g"] % 3]; si["g"] += 1
                            S.op("dve", lambda e: e.scalar_tensor_tensor(sg.t[:, :n], pb.t[:, :n], gcol.t[:, 0:1], rinv.t[:, :n],
                                                                         ALU.mult, ALU.mult),
                                 reads=[pb, gcol, rinv], writes=[sg])
                            S.dma("sp", dst.t[h, :, p0 + n0:p0 + n1], sg.t[:, :n], reads=[sg], writes=[dst], part=True)
                        return epi

                    def epi_store_bf(dst):
                        def epi(col, m, n0, n1, pss):
                            n = n1 - n0
                            sg = stg[si["g"] % 3]; si["g"] += 1
                            copy(alt(), sg.t[:m, :n], pss[0].t[:m, :n], [pss[0]], [sg])
                            S.dma("sp", dst.t[col:col + m, p0 + n0:p0 + n1], sg.t[:m, :n], reads=[sg], writes=[dst], part=True)
                        return epi

                    if is_ctx:
                        gemm("fm", wbufs, w_in, D, c.COL_K, c.AW, act, [panel], pn, epi_qk(KTd, kgt, c.COL_K))

                        def epi_v(t0, t1, col, cw, pss):
                            r = t1 - t0
                            sg = stg[si["g"] % 3]; si["g"] += 1
                            copy(alt(), sg.t[:r, :cw], pss[0].t[:r, :cw], [pss[0]], [sg])
                            S.dma("sp", Vd.t[p0 + t0:p0 + t1, col - c.COL_V:col - c.COL_V + cw], sg.t[:r, :cw],
                                  reads=[sg], writes=[Vd], part=True)
                        gemm("tm", wbufs, w_in, D, c.COL_V, c.AW, act, [panel], pn, epi_v)

                        def epi_u(col, m, n0, n1, pss):
                            n = n1 - n0
                            sg = stg[si["g"] % 3]; si["g"] += 1
                            copy(alt(), sg.t[:m, :n], pss[0].t[:m, :n], [pss[0]], [sg])
                            S.dma("sp", UTd.t[col - c.COL_U:col - c.COL_U + m, p0 + n0:p0 + n1], sg.t[:m, :n],
                                  reads=[sg], writes=[UTd], part=True)
                        gemm("fm", wbufs, w_in, D, c.COL_U, c.SW, act, [panel], pn, epi_u)

                        def epi_f(col, m, n0, n1, pss):
                            n = n1 - n0
                            pb = pss[0]
                            S.op("act", lambda e: e.activation(lf.t[:H, :n], pb.t[:H, :n], AF.Exp, bias=bft.t[:H, 0:1], scale=-1.0),
                                 reads=[pb, bft], writes=[lf])
                            S.op("act", lambda e: e.activation(lf.t[:H, :n], lf.t[:H, :n], AF.Ln, bias=ONE[:H, :], scale=1.0),
                                 reads=[lf, cst], writes=[lf])
                            S.op("dve", lambda e: e.tensor_scalar(lf.t[:H, :n], lf.t[:H, :n], -1.0, None, ALU.mult), reads=[lf], writes=[lf])
                            a0 = p0 + n0
                            init = 0.0 if a0 == 0 else cacc.t[:H, a0 - 1:a0]
                            S.op("dve", lambda e: e.tensor_tensor_scan(cacc.t[:H, a0:a0 + n], ones_t.t[:H, :n], lf.t[:H, :n], init,
                                                                       ALU.mult, ALU.add),
                                 reads=[ones_t, lf, cacc], writes=[cacc])
                        gemm("fm", wbufs, w_in, D, c.COL_F, H, act, [panel], pn, epi_f)
                    else:
                        gemm("fm", wbufs, w_in, D, c.COL_Q, c.AW, act, [panel], pn, epi_qk(QTd, qgt, c.COL_Q))

                        def epi_f32(dst, colbase, func):
                            def epi(col, m, n0, n1, pss):
                                n = n1 - n0
                                sf = stf[si["f"] % 3]; si["f"] += 1
                                S.op("act", lambda e: e.activation(sf.t[:m, :n], pss[0].t[:m, :n], func), reads=[pss[0]], writes=[sf])
                                S.dma("sp", dst.t[col - colbase:col - colbase + m, p0 + n0:p0 + n1], sf.t[:m, :n],
                                      reads=[sf], writes=[dst], part=True)
                            return epi
                        gemm("fm", wbufs, w_in, D, c.COL_U, c.SW, act, [panel], pn, epi_f32(UOd, c.COL_U, AF.Copy))
                        gemm("fm", wbufs, w_in, D, c.COL_GA, D, act, [panel], pn, epi_f32(GAd, c.COL_GA, AF.Sigmoid))
                        gemm("fm", wbufs, w_in, D, c.COL_GB, D, act, [panel], pn, epi_f32(GBd, c.COL_GB, AF.Sigmoid))
                    p0 += pn

        _barrier(S)
        proj_phase(True)
        _barrier(S)
        proj_phase(False)

        NKT = 2 * NB + 1

        def ktile(kt):
            return (0, 16) if kt == 0 else (16 + 128 * (kt - 1), 128)

        MAGIC = 12582912.0

        def ssm_gen(st):
            if True:
                LSM = 256 * c.SEGB + 16
                NSEG = NB // c.SEGB
                PC = min(2, c.SEGB)
                tl = sb(st, "tl", [128, LSM], F32); S.dma("sp", tl.t[:], t_loc, writes=[tl])
                onesL = sb(st, "onesL", [128, LSM], F32)
                S.op("dve", lambda e: e.memset(onesL.t[:], 1.0), writes=[onesL])
                hpi = sb(st, "hpi", [128, 1], F32)
                S.op("dve", lambda e: e.memset(hpi.t[:], math.pi / 2), writes=[hpi])
                dskt = sb(st, "dskt", [128, c.NCT], F32); S.dma("sp", dskt.t[:], dsk, writes=[dskt])
                uT = sb(st, "uT", [128, L], BF16)
                uo = sb(st, "uo", [128, NO], F32)
                wbr = sb(st, "wbr", [128, 512], BF16); wbi = sb(st, "wbi", [128, 512], BF16)
                wcr = sb(st, "wcr", [128, 512], BF16); wci = sb(st, "wci", [128, 512], BF16)
                zr = sb(st, "zr", [128, 4, LSM], BF16); nzi = sb(st, "nzi", [128, 4, LSM], BF16)
                z2 = sb(st, "z2", [128, 4, LSM], BF16); z4 = sb(st, "z4", [128, 4, LSM], BF16)
                nwcr = sb(st, "nwcr", [128, 512], BF16); nwci = sb(st, "nwci", [128, 512], BF16)
                rho_t = sb(st, "rho_t", [128, 4, LSM], F32)
                sets = [[sb(st, f"s{k}_{i}", [128, LSM], F32) for i in range(10)] + [sb(st, f"ab{k}", [128, 1], F32)] for k in range(2)]
                carry = sb(st, "carry", [128, 4, 2], F32)
                sel = sb(st, "sel", [128, 256], F32)
                yf = sb(st, "yf", [128, 256], F32); zf = sb(st, "zf", [128, 256], F32); zb = sb(st, "zb", [128, 256], BF16)
                for ct in range(c.NCT):
                    S.dma("sp", uT.t[:], UTd.t[ct * 128:(ct + 1) * 128, :], reads=[UTd], writes=[uT])
                    S.dma("sp", uo.t[:], UOd.t[ct * 128:(ct + 1) * 128, :], reads=[UOd], writes=[uo])
                    cols = slice(ct * 512, (ct + 1) * 512)
                    S.dma("sp", wbr.t[:], BBRd.t[:, cols], reads=[BBRd], writes=[wbr])
                    S.dma("sp", wbi.t[:], BBId.t[:, cols], reads=[BBId], writes=[wbi])
                    S.dma("pool", wcr.t[:], crB[:, cols], writes=[wcr])
                    S.dma("pool", wci.t[:], ciB[:, cols], writes=[wci])
                    S.op("pool", lambda e: e.tensor_scalar(nwcr.t[:], wcr.t[:], -1.0, None, ALU.mult), reads=[wcr], writes=[nwcr])
                    S.op("pool", lambda e: e.tensor_scalar(nwci.t[:], wci.t[:], -1.0, None, ALU.mult), reads=[wci], writes=[nwci])
                    for jj in range(4):
                        j = 4 * ct + jj
                        S.op("act", lambda e: e.activation(rho_t.t[:, jj, :], onesL.t[:], AF.Copy, scale=rhoA.t[:, j:j + 1]),
                             reads=[onesL, rhoA], writes=[rho_t])
                    units = [(s_, jj_) for s_ in range(NSEG) for jj_ in range(4)]

                    def seginfo(s):
                        a0 = 0 if s == 0 else 16 + 256 * s * c.SEGB
                        a1 = 16 + 256 * (s + 1) * c.SEGB
                        return a0, a1 - a0, (16 if s == 0 else 0)

                    def stage1(s, jj):
                        a0, Ls, off = seginfo(s)
                        V = lambda t: t.t[:, :Ls]
                        j = 4 * ct + jj
                        xr, xi, ang, rr, sn, cs, bA, bB, bC, rr2, ab = sets[jj % 2]
                        for n0 in range(0, Ls, 512):
                            n1 = min(Ls, n0 + 512)
                            pr, pi = ps[5], ps[6]
                            S.op("pe", lambda e: e.matmul(pr.t[:, :n1 - n0], wbr.t[:, jj * 128:(jj + 1) * 128], uT.t[:, a0 + n0:a0 + n1],
                                                          start=True, stop=True), reads=[wbr, uT], writes=[pr])
                            S.op("pe", lambda e: e.matmul(pi.t[:, :n1 - n0], wbi.t[:, jj * 128:(jj + 1) * 128], uT.t[:, a0 + n0:a0 + n1],
                                                          start=True, stop=True), reads=[wbi, uT], writes=[pi])
                            copy("act", xr.t[:, n0:n1], pr.t[:, :n1 - n0], [pr], [xr])
                            copy("act", xi.t[:, n0:n1], pi.t[:, :n1 - n0], [pi], [xi])
                        S.op("act", lambda e: e.activation(ab.t[:], phiA.t[:, j:j + 1], AF.Copy, scale=float(a0)), reads=[phiA], writes=[ab])
                        S.op("act", lambda e: e.activation(V(ang), V(tl), AF.Identity, bias=ab.t[:, 0:1], scale=phiA.t[:, j:j + 1]),
                             reads=[tl, ab, phiA], writes=[ang])
                        S.op("dve", lambda e: e.tensor_scalar(V(rr), V(ang), MAGIC, MAGIC, ALU.add, ALU.subtract), reads=[ang], writes=[rr])
                        S.op("dve", lambda e: e.tensor_tensor(V(sn), V(ang), V(rr), ALU.subtract), reads=[ang, rr], writes=[sn])
                        S.op("act", lambda e: e.activation(V(cs), V(sn), AF.Abs), reads=[sn], writes=[cs])
                        S.op("act", lambda e: e.activation(V(cs), V(cs), AF.Sin, bias=hpi.t[:, 0:1], scale=-TWO_PI), reads=[cs, hpi], writes=[cs])
                        S.op("act", lambda e: e.activation(V(sn), V(sn), AF.Sin, scale=TWO_PI), reads=[sn], writes=[sn])

                    def stage2(s, jj):
                        a0, Ls, off = seginfo(s)
                        V = lambda t: t.t[:, :Ls]
                        xr, xi, ang, rr, sn, cs, bA, bB, bC, rr2, ab = sets[jj % 2]
                        S.op("dve", lambda e: e.tensor_tensor(V(bA), V(cs), V(xr), ALU.mult), reads=[cs, xr], writes=[bA])
                        S.op("dve", lambda e: e.tensor_tensor(V(bB), V(sn), V(xi), ALU.mult), reads=[sn, xi], writes=[bB])
                        S.op("dve", lambda e: e.tensor_add(V(bA), V(bA), V(bB)), reads=[bA, bB], writes=[bA])
                        S.op("dve", lambda e: e.tensor_tensor(V(bB), V(cs), V(xi), ALU.mult), reads=[cs, xi], writes=[bB])
                        S.op("dve", lambda e: e.tensor_tensor(V(bC), V(sn), V(xr), ALU.mult), reads=[sn, xr], writes=[bC])
                        S.op("dve", lambda e: e.tensor_sub(V(bB), V(bB), V(bC)), reads=[bB, bC], writes=[bB])
                        ir = 0.0 if s == 0 else carry.t[:, jj, 0:1]
                        ii = 0.0 if s == 0 else carry.t[:, jj, 1:2]
                        S.op("dve", lambda e: e.tensor_tensor_scan(V(xr), rho_t.t[:, jj, :Ls], V(bA), ir, ALU.mult, ALU.add),
                             reads=[rho_t, bA, carry], writes=[xr])
                        S.op("dve", lambda e: e.tensor_tensor_scan(V(xi), rho_t.t[:, jj, :Ls], V(bB), ii, ALU.mult, ALU.add),
                             reads=[rho_t, bB, carry], writes=[xi])
                        S.op("dve", lambda e: e.tensor_copy(carry.t[:, jj, 0:1], xr.t[:, Ls - 1:Ls]), reads=[xr], writes=[carry])
                        S.op("dve", lambda e: e.tensor_copy(carry.t[:, jj, 1:2], xi.t[:, Ls - 1:Ls]), reads=[xi], writes=[carry])
                        S.op("dve", lambda e: e.tensor_tensor(zr.t[:, jj, :Ls], V(cs), V(xr), ALU.mult), reads=[cs, xr], writes=[zr])
                        S.op("dve", lambda e: e.tensor_tensor(z2.t[:, jj, :Ls], V(sn), V(xi), ALU.mult), reads=[sn, xi], writes=[z2])
                        S.op("dve", lambda e: e.tensor_tensor(nzi.t[:, jj, :Ls], V(sn), V(xr), ALU.mult), reads=[sn, xr], writes=[nzi])
                        S.op("dve", lambda e: e.tensor_tensor(z4.t[:, jj, :Ls], V(cs), V(xi), ALU.mult), reads=[cs, xi], writes=[z4])
                        if jj == 3:
                            ypart(s)

                    def ypart(s):
                        a0, Ls, off = seginfo(s)
                        for q0 in range(0, c.SEGB, PC):
                            n = 256 * PC
                            l0 = off + 256 * q0
                            pb = ps[7]
                            for jj in range(4):
                                wsl = slice(jj * 128, (jj + 1) * 128)
                                S.op("pe", lambda e: e.matmul(pb.t[:, :n], wcr.t[:, wsl], zr.t[:, jj, l0:l0 + n],
                                                              start=(jj == 0), stop=False), reads=[wcr, zr], writes=[pb])
                                S.op("pe", lambda e: e.matmul(pb.t[:, :n], nwcr.t[:, wsl], z2.t[:, jj, l0:l0 + n],
                                                              start=False, stop=False), reads=[nwcr, z2], writes=[pb])
                                S.op("pe", lambda e: e.matmul(pb.t[:, :n], nwci.t[:, wsl], nzi.t[:, jj, l0:l0 + n],
                                                              start=False, stop=False), reads=[nwci, nzi], writes=[pb])
                                S.op("pe", lambda e: e.matmul(pb.t[:, :n], nwci.t[:, wsl], z4.t[:, jj, l0:l0 + n],
                                                              start=False, stop=(jj == 3)), reads=[nwci, z4], writes=[pb])
                            no = 128 * PC
                            o0 = (s * c.SEGB + q0) * 128
                            p4 = pb.t[:, :n].rearrange("p (j two r) -> p j two r", two=2, r=128)
                            s3 = sel.t[:, :no].rearrange("p (j r) -> p j r", r=128)
                            S.op("dve", lambda e: e.tensor_scalar(s3, p4[:, :, 0, :], w01t.t[:, 0:1], None, ALU.mult), reads=[pb, w01t], writes=[sel])
                            S.op("dve", lambda e: e.scalar_tensor_tensor(s3, p4[:, :, 1, :], w01t.t[:, 1:2], s3, ALU.mult, ALU.add),
                                 reads=[pb, w01t, sel], writes=[sel])
                            S.op("dve", lambda e: e.scalar_tensor_tensor(yf.t[:, :no], uo.t[:, o0:o0 + no], dskt.t[:, ct:ct + 1], sel.t[:, :no],
                                                                         ALU.mult, ALU.add), reads=[uo, dskt, sel], writes=[yf])
                            S.op("act", lambda e: e.activation(zf.t[:, :no], yf.t[:, :no], AF.Gelu), reads=[yf], writes=[zf])
                            S.op("dve", lambda e: e.tensor_copy(zb.t[:, :no], zf.t[:, :no]), reads=[zf], writes=[zb])
                            S.dma("sp", ZFd.t[ct * 128:(ct + 1) * 128, o0:o0 + no], zf.t[:, :no], reads=[zf], writes=[ZFd], part=True)
                            S.dma("sp", ZTd.t[ct * 128:(ct + 1) * 128, o0:o0 + no], zb.t[:, :no], reads=[zb], writes=[ZTd], part=True)

                    for i in range(len(units) + 1):
                        if i < len(units):
                            stage1(*units[i])
                        if i >= 1:
                            stage2(*units[i - 1])
                        yield

        def glu_phase():
            with ExitStack() as st:
                NP = c.NPAN
                zp = sb(st, "zp", [128, c.NCT, NP], BF16)
                wbufs = [sb(st, f"gw{i}", [128, c.NCT, 512], BF16) for i in range(2)]
                sg = sb(st, "gsg", [128, 512], F32); zt = sb(st, "gzt", [128, 512], F32); ob = sb(st, "gob", [128, 512], BF16)
                for p0 in range(0, NO, NP):
                    S.dma("sp", zp.t[:], ZTd.t[:, p0:p0 + NP].rearrange("(k p) n -> p k n", p=128), reads=[ZTd], writes=[zp])

                    def epi(col, m, n0, n1, pss):
                        n = n1 - n0
                        S.op("act", lambda e: e.activation(sg.t[:m, :n], pss[0].t[:m, :n], AF.Sigmoid), reads=[pss[0]], writes=[sg])
                        S.dma("sp", zt.t[:m, :n], ZFd.t[col:col + m, p0 + n0:p0 + n1], reads=[ZFd], writes=[zt])
                        S.op("dve", lambda e: e.tensor_tensor(ob.t[:m, :n], zt.t[:m, :n], sg.t[:m, :n], ALU.mult), reads=[zt, sg], writes=[ob])
                        S.dma("sp", SSTd.t[col:col + m, p0 + n0:p0 + n1], ob.t[:m, :n], reads=[ob], writes=[SSTd], part=True)
                    gemm("fm", wbufs, w_glu, c.SW, 0, c.SW, lambda kt, n0, n1: zp.t[:, kt, n0:n1], [zp], NP, epi)

        HG = min(4, H)
        QG = min(4, NB)

        def attn_prep(stp, st):
            if True:
                cball = sb(stp, "cball", [128, NKT, H], F32)
                mA = sb(stp, "mA", [128, 128], BF16); mB = sb(stp, "mB", [128, 128], BF16)
                cq = sb(st, "cq", [16, NO], F32)
                c3 = [sb(st, f"c3_{i}", [16, NO], BF16) for i in range(3)]
                r1 = sb(st, "cr1", [16, NO], F32)
                S.dma("pool", mA.t[:], maskA, writes=[mA]); S.dma("pool", mB.t[:], maskB, writes=[mB])
                for kt in range(NKT):
                    a, nk = ktile(kt)
                    pb = ps[kt % 2]
                    S.op("pe", lambda e: e.transpose(pb.t[:nk, :H], cacc.t[:H, a:a + nk], id_f.t[:H, :H]), reads=[cacc, id_f], writes=[pb])
                    S.op("dve", lambda e: e.tensor_scalar(cball.t[:nk, kt, :], pb.t[:nk, :H], -1.0, -STAB, ALU.mult, ALU.add),
                         reads=[pb], writes=[cball])
                v4 = cacc.t[:H, 16:16 + 256 * NB].rearrange("h (j two r) -> h j two r", two=2, r=128)
                d3 = cq.t[:H, :].rearrange("h (j r) -> h j r", r=128)
                S.op("dve", lambda e: e.tensor_scalar(d3, v4[:, :, 0, :], w01t.t[:H, 0:1], None, ALU.mult), reads=[cacc, w01t], writes=[cq])
                S.op("dve", lambda e: e.scalar_tensor_tensor(d3, v4[:, :, 1, :], w01t.t[:H, 1:2], d3, ALU.mult, ALU.add),
                     reads=[cacc, w01t, cq], writes=[cq])
                S.op("dve", lambda e: e.tensor_scalar(cq.t[:H, :], cq.t[:H, :], math.sqrt(128.0), None, ALU.mult), reads=[cq], writes=[cq])
                S.op("dve", lambda e: e.tensor_copy(c3[0].t[:H, :], cq.t[:H, :]), reads=[cq], writes=[c3[0]])
                S.op("dve", lambda e: e.tensor_sub(r1.t[:H, :], cq.t[:H, :], c3[0].t[:H, :]), reads=[cq, c3[0]], writes=[r1])
                S.op("dve", lambda e: e.tensor_copy(c3[1].t[:H, :], r1.t[:H, :]), reads=[r1], writes=[c3[1]])
                S.op("dve", lambda e: e.tensor_sub(r1.t[:H, :], r1.t[:H, :], c3[1].t[:H, :]), reads=[r1, c3[1]], writes=[r1])
                S.op("dve", lambda e: e.tensor_copy(c3[2].t[:H, :], r1.t[:H, :]), reads=[r1], writes=[c3[2]])
                for i in range(3):
                    S.dma("sp", CQ3d.t[i], c3[i].t[:H, :], reads=[c3[i]], writes=[CQ3d], part=True)
                S.dma("sp", CQd.t[:, :], cq.t[:H, :], reads=[cq], writes=[CQd])
            return cball, mA, mB

        def attn_gen(st, cball, mA, mB):
            if True:
                vg = sb(st, "vg", [128, NKT, HG * 128], BF16)
                kh = [sb(st, f"kh{i}", [128, L], BF16) for i in range(2)]
                qh = [sb(st, f"qh{i}", [128, NO], BF16) for i in range(2)]
                cqh = [sb(st, f"cqh{i}", [3, NO], BF16) for i in range(2)]
                pt = [sb(st, f"pt{i}", [128, 512], BF16) for i in range(3)]
                rs = [sb(st, f"rs{i}", [128, 512], F32) for i in range(2)]
                ao = [sb(st, f"ao{i}", [128, NO], BF16) for i in range(2)]
                pti = 0
                gi = 0
                scale = 1.0 / math.sqrt(128.0)
                for hg in range(0, H, HG):
                    S.dma("sp", vg.t[:16, 0, :], Vd.t[0:16, hg * 128:(hg + HG) * 128], reads=[Vd], writes=[vg], part=True)
                    for t0 in range(0, 2 * NB, 8):
                        t1 = min(2 * NB, t0 + 8)
                        S.dma("sp", vg.t[:, 1 + t0:1 + t1, :],
                              Vd.t[16 + 128 * t0:16 + 128 * t1, hg * 128:(hg + HG) * 128].rearrange("(t p) c -> p t c", p=128),
                              reads=[Vd], writes=[vg], part=True)
                    for h in range(hg, hg + HG):
                        khb, qhb, cqb, aob = kh[h % 2], qh[h % 2], cqh[h % 2], ao[h % 2]
                        S.dma("sp", khb.t[:], KTd.t[h], reads=[KTd], writes=[khb])
                        S.dma("sp", qhb.t[:], QTd.t[h], reads=[QTd], writes=[qhb])
                        S.dma("sp", cqb.t[:], CQ3d.t[:, h, :], reads=[CQ3d], writes=[cqb])
                        for g in range(0, NB, QG):
                            OT = ps[2 + gi % 2]; SM = ps[4]
                            gi += 1
                            W = QG * 128
                            ktmax = 2 * (g + QG - 1) + 2
                            pend = None
                            for kt in range(0, ktmax + 2):
                                if kt <= ktmax:
                                    a, nk = ktile(kt)
                                    jmin = max(g, (kt - 1) // 2) if kt > 0 else g
                                    q0 = (jmin - g) * 128
                                    qa, qb = g * 128 + q0, (g + QG) * 128
                                    STb = ps[kt % 2]
                                    S.op("pe", lambda e: e.matmul(STb.t[:nk, q0:W], khb.t[:, a:a + nk], qhb.t[:, qa:qb], start=True, stop=False),
                                         reads=[khb, qhb], writes=[STb])
                                    S.op("pe", lambda e: e.matmul(STb.t[:nk, q0:W], ones_bf.t[0:3, :nk], cqb.t[0:3, qa:qb], start=False, stop=True),
                                         reads=[ones_bf, cqb], writes=[STb])
                                    p = pt[pti % 3]; pti += 1
                                    S.op("act", lambda e: e.activation(p.t[:nk, q0:W], STb.t[:nk, q0:W], AF.Exp, bias=cball.t[:nk, kt, h:h + 1], scale=scale),
                                         reads=[STb, cball], writes=[p])
                                    if kt >= 1 and kt % 2 == 1:
                                        j = (kt - 1) // 2
                                        if g <= j < g + QG:
                                            cs_ = slice((j - g) * 128, (j - g + 1) * 128)
                                            S.op("pool", lambda e: e.tensor_tensor(p.t[:, cs_], p.t[:, cs_], mA.t[:, :], ALU.mult), reads=[p, mA], writes=[p])
                                    if kt >= 2 and kt % 2 == 0:
                                        j = (kt - 2) // 2
                                        if g <= j < g + QG:
                                            cs_ = slice((j - g) * 128, (j - g + 1) * 128)
                                            S.op("pool", lambda e: e.tensor_tensor(p.t[:, cs_], p.t[:, cs_], mB.t[:, :], ALU.mult), reads=[p, mB], writes=[p])
                                    cur = (kt, nk, q0, p)
                                else:
                                    cur = None
                                if pend is not None:
                                    kt_, nk_, q0_, p_ = pend
                                    last = kt_ == ktmax
                                    S.op("pe", lambda e: e.matmul(OT.t[:, q0_:W], vg.t[:nk_, kt_, (h - hg) * 128:(h - hg + 1) * 128], p_.t[:nk_, q0_:W],
                                                                  start=(kt_ == 0), stop=last), reads=[vg, p_], writes=[OT])
                                    S.op("pe", lambda e: e.matmul(SM.t[:, q0_:W], ones_bf.t[:nk_, :], p_.t[:nk_, q0_:W], start=(kt_ == 0), stop=last),
                                         reads=[ones_bf, p_], writes=[SM])
                                pend = cur
                                if kt % 2 == 1:
                                    yield
                            r = rs[gi % 2]
                            S.op("dve", lambda e: e.reciprocal(r.t[:, :W], SM.t[:, :W]), reads=[SM], writes=[r])
                            S.op("dve", lambda e: e.tensor_tensor(aob.t[:, g * 128:g * 128 + W], OT.t[:, :W], r.t[:, :W], ALU.mult), reads=[OT, r], writes=[aob])
                        S.dma("sp", ATd.t[h * 128:(h + 1) * 128, :], aob.t[:], reads=[aob], writes=[ATd], part=True)

        _barrier(S)
        with ExitStack() as stp:
            with ExitStack() as st0:
                cball_, mA_, mB_ = attn_prep(stp, st0)
            _barrier(S)
            with ExitStack() as stj:
                g1_ = ssm_gen(stj)
                g2_ = attn_gen(stj, cball_, mA_, mB_)
                live = [g1_, g2_]
                while live:
                    for g_ in list(live):
                        try:
                            next(g_)
                        except StopIteration:
                            live.remove(g_)
        _barrier(S)
        glu_phase()

        def mix_phase():
            with ExitStack() as st:
                NP = min(1024, NO)
                KA, KS = c.AW // 128, c.SW // 128
                cat = sb(st, "cat", [128, KA + KS, NP], BF16)
                mixT = sb(st, "mixT", [128, KT, NP], BF16)
                wbufs = [sb(st, f"mw{i}", [128, max(KT, KA + KS), 256], BF16) for i in range(2)]
                ga = sb(st, "ga", [128, 512], F32); gb = sb(st, "gb", [128, 512], F32)
                m1 = sb(st, "m1", [128, 512], F32); m2 = sb(st, "m2", [128, 512], F32)
                xo = [sb(st, f"xo{i}", [128, 512], F32) for i in range(2)]
                ho = [sb(st, f"ho{i}", [128, 512], F32) for i in range(2)]
                k = {"i": 0}
                for p0 in range(0, NO, NP):
                    S.dma("sp", cat.t[:, 0:KA, :], ATd.t[:, p0:p0 + NP].rearrange("(k p) n -> p k n", p=128), reads=[ATd], writes=[cat], part=True)
                    S.dma("sp", cat.t[:, KA:KA + KS, :], SSTd.t[:, p0:p0 + NP].rearrange("(k p) n -> p k n", p=128), reads=[SSTd], writes=[cat], part=True)

                    def epi(col, m, n0, n1, pss):
                        n = n1 - n0
                        S.dma("sp", ga.t[:m, :n], GAd.t[col:col + m, p0 + n0:p0 + n1], reads=[GAd], writes=[ga])
                        S.dma("sp", gb.t[:m, :n], GBd.t[col:col + m, p0 + n0:p0 + n1], reads=[GBd], writes=[gb])
                        S.op("dve", lambda e: e.tensor_tensor(m1.t[:m, :n], ga.t[:m, :n], pss[0].t[:m, :n], ALU.mult), reads=[ga, pss[0]], writes=[m1])
                        S.op("dve", lambda e: e.tensor_tensor(m2.t[:m, :n], gb.t[:m, :n], pss[1].t[:m, :n], ALU.mult), reads=[gb, pss[1]], writes=[m2])
                        S.op("pool", lambda e: e.tensor_add(mixT.t[:m, col // 128, n0:n1], m1.t[:m, :n], m2.t[:m, :n]), reads=[m1, m2], writes=[mixT])
                    gemm("fm", wbufs, [(w_a, c.AW), (w_b, c.SW)], c.AW + c.SW, 0, D, lambda kt, n0, n1: cat.t[:, kt, n0:n1], [cat], NP, epi,
                         kgroups=[(0, KA), (KA, KA + KS)])

                    def epi2(t0, t1, col, cw, pss):
                        r = t1 - t0
                        x_ = xo[k["i"] % 2]; h_ = ho[k["i"] % 2]; k["i"] += 1
                        S.dma("sp", x_.t[:r, :cw], x_own[p0 + t0:p0 + t1, col:col + cw], writes=[x_])
                        S.op("dve", lambda e: e.tensor_tensor(h_.t[:r, :cw], x_.t[:r, :cw], pss[0].t[:r, :cw], ALU.add), reads=[x_, pss[0]], writes=[h_])
                        S.dma("sp", H1d.t[p0 + t0:p0 + t1, col:col + cw], h_.t[:r, :cw], reads=[h_], writes=[H1d], part=True)
                    gemm("tm", wbufs, w_out, D, 0, D, lambda kt, t0, t1: mixT.t[:, kt, t0:t1], [mixT], NP, epi2)

        _barrier(S)
        mix_phase()

        def peer_phase():
            with ExitStack() as st:
                NP = c.PPAN
                NTT = NP // 128
                pan = sb(st, "ppan", [128, KT, NP], BF16)
                et = [sb(st, f"et{i}", [128, 16, 128], F32) for i in range(NTT)]
                thr = sb(st, "thr", [128, NTT, 8], F32); rz = sb(st, "rz", [128, NTT, 8], F32)
                for p0 in range(0, NO, NP):
                    _barrier(S)
                    with ExitStack() as s1:
                        grep = sb(s1, "g2", [128, D], F32); S.dma("sp", grep.t[:], g2rep, writes=[grep])
                        nt = make_nt(s1, "p")
                        nt(H1d.t[p0:p0 + NP, :], NP, grep, pan, srcbuf=H1d)
                    _barrier(S)
                    with ExitStack() as s2:
                        qpT = sb(s2, "qpT", [128, 16, NP], BF16)
                        skb = sb(s2, "skb", [128, 256], BF16); S.dma("pool", skb.t[:], skT, writes=[skb])
                        wbufs = [sb(s2, f"pw{i}", [128, KT, 256], BF16) for i in range(2)]
                        sc = sb(s2, "sc", [128, 16, 128], F32)
                        wk = sb(s2, "wk", [128, 256], F32)
                        m16 = sb(s2, "m16", [128, 16, 16], F32); e16 = sb(s2, "e16", [128, 16, 16], F32)
                        nm = sb(s2, "nm", [128, 16], F32)
                        cand = sb(s2, "cand", [128, 256], F32); c16 = sb(s2, "c16", [128, 16], F32)

                        def epi_q(col, m, n0, n1, pss):
                            copy(alt(), qpT.t[:, col // 128, n0:n1], pss[0].t[:, :n1 - n0], [pss[0]], [qpT])
                        gemm("fm", wbufs, w_query, D, 0, c.QW, lambda kt, n0, n1: pan.t[:, kt, n0:n1], [pan], NP, epi_q)
                        for tt in range(NTT):
                            ts_ = slice(tt * 128, (tt + 1) * 128)
                            for hc in range(16):
                                pb = ps[hc // 4]
                                S.op("pe", lambda e: e.matmul(pb.t[:, (hc % 4) * 128:(hc % 4 + 1) * 128], qpT.t[:, hc, ts_],
                                                              skb.t[:, (hc % 2) * 128:(hc % 2 + 1) * 128], start=True, stop=True),
                                     reads=[qpT, skb], writes=[pb])
                                if hc % 4 == 3:
                                    copy(alt(), sc.t[:, hc - 3:hc + 1, :], pb.t[:, :].rearrange("p (a b) -> p a b", b=128), [pb], [sc])
                            for hc in range(16):
                                S.op("dve", lambda e: e.max(m16.t[:, hc, 0:8], sc.t[:, hc, :]), reads=[sc], writes=[m16])
                                S.op("dve", lambda e: e.match_replace(wk.t[:, :128], m16.t[:, hc, 0:8], sc.t[:, hc, :], -1e30),
                                     reads=[sc, m16], writes=[wk])
                                S.op("dve", lambda e: e.max(m16.t[:, hc, 8:16], wk.t[:, :128]), reads=[wk], writes=[m16])
                            S.op("dve", lambda e: e.tensor_scalar(nm.t[:, :], m16.t[:, :, 0], -1.0, None, ALU.mult), reads=[m16], writes=[nm])
                            for hc in range(16):
                                S.op("act", lambda e: e.activation(et[tt].t[:, hc, :], sc.t[:, hc, :], AF.Exp, bias=nm.t[:, hc:hc + 1], scale=1.0),
                                     reads=[sc, nm], writes=[et[tt]])
                                S.op("act", lambda e: e.activation(e16.t[:, hc, :], m16.t[:, hc, :], AF.Exp, bias=nm.t[:, hc:hc + 1], scale=1.0),
                                     reads=[m16, nm], writes=[e16])
                            for h in range(8):
                                c3 = cand.t[:, :].rearrange("p (a b) -> p a b", b=16)
                                S.op("dve", lambda e: e.tensor_tensor(c3, e16.t[:, 2 * h, :].unsqueeze(2).to_broadcast([128, 16, 16]),
                                                                      e16.t[:, 2 * h + 1, :].unsqueeze(1).to_broadcast([128, 16, 16]), ALU.mult),
                                     reads=[e16], writes=[cand])
                                S.op("dve", lambda e: e.max(c16.t[:, 0:8], cand.t[:, :]), reads=[cand], writes=[c16])
                                S.op("dve", lambda e: e.match_replace(wk.t[:, :], c16.t[:, 0:8], cand.t[:, :], -1.0), reads=[cand, c16], writes=[wk])
                                S.op("dve", lambda e: e.max(c16.t[:, 8:16], wk.t[:, :]), reads=[wk], writes=[c16])
                                S.op("dve", lambda e: e.tensor_scalar(thr.t[:, tt, h:h + 1], c16.t[:, 15:16], 1.0 - 1e-5, None, ALU.mult),
                                     reads=[c16], writes=[thr])
                                S.op("dve", lambda e: e.tensor_reduce(rz.t[:, tt, h:h + 1], c16.t[:, :], AX.X, ALU.add), reads=[c16], writes=[rz])
                            S.op("dve", lambda e: e.reciprocal(rz.t[:, tt, :], rz.t[:, tt, :]), reads=[rz], writes=[rz])
                            for h in range(8):
                                S.op("dve", lambda e: e.tensor_scalar(e16.t[:, 2 * h, :], e16.t[:, 2 * h, :], rz.t[:, tt, h:h + 1], None, ALU.mult),
                                     reads=[e16, rz], writes=[e16])
                                S.op("dve", lambda e: e.tensor_scalar(et[tt].t[:, 2 * h, :], et[tt].t[:, 2 * h, :], rz.t[:, tt, h:h + 1], None, ALU.mult),
                                     reads=[et[tt], rz], writes=[et[tt]])
                                c3 = cand.t[:, :].rearrange("p (a b) -> p a b", b=16)
                                S.op("dve", lambda e: e.tensor_tensor(c3, e16.t[:, 2 * h, :].unsqueeze(2).to_broadcast([128, 16, 16]),
                                                                      e16.t[:, 2 * h + 1, :].unsqueeze(1).to_broadcast([128, 16, 16]), ALU.mult),
                                     reads=[e16], writes=[cand])
                                S.op("dve", lambda e: e.max(c16.t[:, 0:8], cand.t[:, :]), reads=[cand], writes=[c16])
                                S.op("dve", lambda e: e.match_replace(wk.t[:, :], c16.t[:, 0:8], cand.t[:, :], -1.0), reads=[cand, c16], writes=[wk])
                                S.op("dve", lambda e: e.max(c16.t[:, 8:16], wk.t[:, :]), reads=[wk], writes=[c16])
                                S.op("dve", lambda e: e.tensor_scalar(thr.t[:, tt, h:h + 1], c16.t[:, 15:16], 1.0 - 1e-5, None, ALU.mult),
                                     reads=[c16], writes=[thr])
                    _barrier(S)
                    with ExitStack() as s3:
                        wbufs = [sb(s3, f"aw{i}", [128, KT, 512], BF16) for i in range(2)]
                        gT = [sb(s3, f"gT{i}", [128, 4, NP], F32) for i in range(2)]
                        Mb = [sb(s3, f"Mb{i}", [128, 512], F32) for i in range(3)]
                        Mh = [sb(s3, f"Mh{i}", [128, 512], BF16) for i in range(3)]
                        wst = [sb(s3, f"wst{i}", [128, 4, NP], BF16) for i in range(2)]
                        k = {"sb": 0, "m": 0}

                        def epi_a(col, m, n0, n1, pss):
                            j = (col // 128) % 4
                            g_ = gT[k["sb"] % 2]
                            S.op("act", lambda e: e.activation(g_.t[:, j, :], pss[0].t[:, :NP], AF.Gelu), reads=[pss[0]], writes=[g_])
                            if j != 3:
                                return
                            col0 = col - 3 * 128
                            i1a = col0 // 128
                            WT = [ps[4 + b] for b in range(4)]
                            for tt in range(NTT):
                                for h in range(8):
                                    M_ = Mb[k["m"] % 3]; Mh_ = Mh[k["m"] % 3]; k["m"] += 1
                                    P3 = M_.t[:, :].rearrange("p (a b) -> p a b", b=128)
                                    S.op("dve", lambda e: e.tensor_tensor(P3, et[tt].t[:, 2 * h, i1a:i1a + 4].unsqueeze(2).to_broadcast([128, 4, 128]),
                                                                          et[tt].t[:, 2 * h + 1, :].unsqueeze(1).to_broadcast([128, 4, 128]), ALU.mult),
                                         reads=[et[tt]], writes=[M_])
                                    S.op("dve", lambda e: e.scalar_tensor_tensor(Mh_.t[:, :], M_.t[:, :], thr.t[:, tt, h:h + 1], M_.t[:, :],
                                                                                 ALU.is_ge, ALU.mult), reads=[M_, thr], writes=[Mh_])
                                    for b in range(4):
                                        S.op("pe", lambda e: e.matmul(WT[b].t[:, tt * 128:(tt + 1) * 128], Mh_.t[:, b * 128:(b + 1) * 128],
                                                                      id_bf.t[:, :], start=(h == 0), stop=(h == 7)),
                                             reads=[Mh_, id_bf], writes=[WT[b]])
                            ws = wst[k["sb"] % 2]
                            for b in range(4):
                                S.op("dve", lambda e: e.tensor_tensor(ws.t[:, b, :], WT[b].t[:, :NP], g_.t[:, b, :], ALU.mult),
                                     reads=[WT[b], g_], writes=[ws])
                                S.dma("sp", WGTd.t[col0 + b * 128:col0 + (b + 1) * 128, p0:p0 + NP], ws.t[:, b, :], reads=[ws], writes=[WGTd], part=True)
                            k["sb"] += 1
                        gemm("fm", wbufs, euT, D, 0, c.PN, lambda kt, n0, n1: pan.t[:, kt, n0:n1], [pan], NP, epi_a, banks=(0, 1, 2, 3))

            _barrier(S)
            with ExitStack() as st:
                NY = min(1024, NO)
                NYT = NY // 128
                EC = 16
                wv = [sb(st, f"yv{i}", [128, EC, 512], BF16) for i in range(2)]
                wa = [sb(st, f"ya{i}", [128, EC, NY], BF16) for i in range(2)]
                h1t = [sb(st, f"yh{i}", [128, 512], F32) for i in range(2)]
                yo = [sb(st, f"yo{i}", [128, 512], F32) for i in range(2)]
                NEC = c.PN // (128 * EC)
                gi = 0
                ci = 0
                oi = 0
                for cb in range(0, D, 512):
                    for p0 in range(0, NO, NY):
                        bks = [ps[i] for i in range(NYT)]
                        gi += 1
                        for ec in range(NEC):
                            e0 = ec * EC * 128
                            v_, a_ = wv[ci % 2], wa[ci % 2]
                            ci += 1
                            for k0 in range(0, EC, 8):
                                S.dma("pool", v_.t[:, k0:k0 + 8, :], ev[e0 + k0 * 128:e0 + (k0 + 8) * 128, cb:cb + 512].rearrange("(k p) c -> p k c", p=128),
                                      writes=[v_], part=True)
                                S.dma("sp", a_.t[:, k0:k0 + 8, :], WGTd.t[e0 + k0 * 128:e0 + (k0 + 8) * 128, p0:p0 + NY].rearrange("(k p) c -> p k c", p=128),
                                      reads=[WGTd], writes=[a_], part=True)
                            for tt in range(NYT):
                                for kt in range(EC):
                                    S.op("pe", lambda e: e.matmul(bks[tt].t[:, :512], a_.t[:, kt, tt * 128:(tt + 1) * 128], v_.t[:, kt, :],
                                                                  start=(ec == 0 and kt == 0), stop=(ec == NEC - 1 and kt == EC - 1)),
                                         reads=[a_, v_], writes=[bks[tt]])
                        for tt in range(NYT):
                            h_, o_ = h1t[oi % 2], yo[oi % 2]
                            oi += 1
                            r0 = p0 + tt * 128
                            S.dma("sp", h_.t[:], H1d.t[r0:r0 + 128, cb:cb + 512], reads=[H1d], writes=[h_])
                            S.op("dve", lambda e: e.tensor_tensor(o_.t[:], h_.t[:], bks[tt].t[:, :512], ALU.add), reads=[h_, bks[tt]], writes=[o_])
                            S.dma("sp", out_own[r0:r0 + 128, cb:cb + 512], o_.t[:], reads=[o_], writes=[OUTb], part=True)

        _barrier(S)
        peer_phase()

        for i in range(NDSEM):
            if S.dval[i]:
                nc.sync.wait_ge(S.dsem[i], S.dval[i])
        print("instructions:", S.ninst, {k: v for k, v in S.cnt.items()})
    return nc


def _prep(c, inp, core):
    f32 = np.float32
    b, hh = core // 2, core % 2
    D, SEQ, NO, H, G, NJ, NCT = c.D, c.SEQ, c.NO, c.H, c.G, c.NJ, c.NCT
    A = lambda v: np.ascontiguousarray(np.asarray(v), dtype=f32)
    x = np.asarray(inp["x"][b], dtype=f32)
    m = {}
    m["x_ctx"] = np.concatenate([np.asarray(inp["meta_tokens"], dtype=f32), x], 0)
    m["x_own"] = A(x.reshape(SEQ // 128, 128, D)[hh::2].reshape(NO, D))
    m["g1rep"] = A(np.broadcast_to(np.asarray(inp["norm1_g"][0])[None, :], (128, D)))
    m["g2rep"] = A(np.broadcast_to(np.asarray(inp["norm2_g"][0])[None, :], (128, D)))
    m["w_in"] = A(inp["w_in"][0])
    m["bfg"] = A(np.asarray(inp["b_forget"][0]).reshape(H, 1))
    m["qg"] = A(np.asarray(inp["q_norm_g"][0]).reshape(128, 1))
    m["kg"] = A(np.asarray(inp["k_norm_g"][0]).reshape(128, 1))
    lr = np.asarray(inp["lam_re"][0], dtype=f32); li = np.asarray(inp["lam_im"][0], dtype=f32)
    ld = np.asarray(inp["log_dt"][0], dtype=f32)
    m["lrA"] = A(lr.reshape(NJ, 128).T); m["liA"] = A(li.reshape(NJ, 128).T)
    m["ldA"] = A(np.repeat(ld.reshape(NJ, 2), 64, axis=1).T)
    m["lrB"] = A(np.broadcast_to(lr.reshape(1, -1), (128, G * 64)))
    m["liB"] = A(np.broadcast_to(li.reshape(1, -1), (128, G * 64)))
    m["ldB"] = A(np.broadcast_to(np.repeat(ld, 64)[None, :], (128, G * 64)))
    bre = np.asarray(inp["b_re"][0], dtype=f32); bim = np.asarray(inp["b_im"][0], dtype=f32)
    cre = np.asarray(inp["c_re"][0], dtype=f32); cim = np.asarray(inp["c_im"][0], dtype=f32)
    brB = np.zeros((128, G * 64), f32); biB = np.zeros((128, G * 64), f32)
    crB = np.zeros((128, NJ * 128), f32); ciB = np.zeros((128, NJ * 128), f32)
    for g in range(G):
        r0 = (g % 8) * 16
        brB[r0:r0 + 16, g * 64:(g + 1) * 64] = bre[g].T
        biB[r0:r0 + 16, g * 64:(g + 1) * 64] = bim[g].T
        j, g2 = g // 2, g % 2
        crB[g2 * 64:(g2 + 1) * 64, j * 128 + r0:j * 128 + r0 + 16] = cre[g].T
        ciB[g2 * 64:(g2 + 1) * 64, j * 128 + r0:j * 128 + r0 + 16] = cim[g].T
    m["brB"], m["biB"], m["crB"], m["ciB"] = brB, biB, crB, ciB
    m["dsk"] = A(np.asarray(inp["d_skip"][0]).reshape(NCT, 128).T)
    m["w_glu"] = A(inp["w_glu"][0]); m["w_a"] = A(inp["w_branch_attn"][0]); m["w_b"] = A(inp["w_branch_ssm"][0])
    m["w_out"] = A(inp["w_out"][0]); m["w_query"] = A(inp["w_query"][0])
    m["skT"] = A(np.asarray(inp["sub_keys"][0]).transpose(2, 0, 1).reshape(128, 256))
    m["euT"] = A(np.asarray(inp["expert_u"][0]).T); m["ev"] = A(inp["expert_v"][0])
    tri = (np.arange(128)[None, :] >= np.arange(128)[:, None]).astype(f32)
    m["maskA"] = tri if hh == 0 else np.ones((128, 128), f32)
    m["maskB"] = np.zeros((128, 128), f32) if hh == 0 else tri
    m["w01"] = A(np.broadcast_to(np.array([[1.0, 0.0]] if hh == 0 else [[0.0, 1.0]], f32), (128, 2)))
    LSM = 256 * c.SEGB + 16
    m["t_loc"] = A(np.broadcast_to(np.arange(LSM, dtype=f32)[None, :], (128, LSM)))
    io = np.arange(NO)
    pos = 16 + 128 * (2 * (io // 128) + hh) + (io % 128)
    m["t_own"] = A(np.broadcast_to(pos.astype(f32)[None, :], (128, NO)))
    m["ident"] = np.eye(128, dtype=f32)
    return m


_NC_CACHE = {}


def run_cfg(c, inputs):
    key = (c.D, c.SEQ, c.B)
    if key not in _NC_CACHE:
        _NC_CACHE[key] = build(c)
    nc = _NC_CACHE[key]
    ncores = 2 * c.B
    shared = None
    in_maps = []
    for core in range(ncores):
        in_maps.append(_prep(c, inputs, core))
    res = run_bass_kernel_spmd(nc, in_maps, core_ids=list(range(ncores)))
    if getattr(c, "debug", False):
        c.dbg = res.results
    out = np.zeros((c.B, c.SEQ, c.D), np.float32)
    for core in range(ncores):
        b, hh = core // 2, core % 2
        o = np.asarray(res.results[core]["out_own"], dtype=np.float32).reshape(c.NB, 128, c.D)
        out[b].reshape(c.SEQ // 128, 128, c.D)[hh::2] = o
    return out


def kernel(**inputs):
    return run_cfg(Cfg(), inputs)
```

```python
import math
from contextlib import ExitStack
import numpy as np
import ml_dtypes
import concourse.bass as bass
import concourse.mybir as mybir
from concourse.bass_utils import run_bass_kernel_spmd

F32 = mybir.dt.float32
BF16 = mybir.dt.bfloat16
I32 = mybir.dt.int32
AF = mybir.ActivationFunctionType
ALU = mybir.AluOpType
AX = mybir.AxisListType

EPOCH = 16000
NDSEM = 40
TWO_PI = 2.0 * math.pi
STAB = 30.0


class Buf:
    __slots__ = ("w", "r")

    def __init__(self):
        self.w = {}
        self.r = {}


class TT:
    def __init__(self, t):
        self.t = t
        self.b = Buf()


def _b(x):
    return x.b if isinstance(x, TT) else x


class Sched:
    def __init__(self, nc, stack):
        self.nc = nc
        self.engs = {"pe": nc.tensor, "dve": nc.vector, "act": nc.scalar,
                     "pool": nc.gpsimd, "sp": nc.sync}
        self.stack = stack
        self.esem = {}
        self.cnt = {e: 0 for e in self.engs}
        self.seen = {e: {} for e in self.engs}
        self.dsem = [stack.enter_context(nc.semaphore(f"d{i}")) for i in range(NDSEM)]
        self.dval = [0] * NDSEM
        self.dnext = 0
        self.ninst = 0

    def _esem(self, eng, epoch):
        k = (eng, epoch)
        if k not in self.esem:
            self.esem[k] = self.stack.enter_context(self.nc.semaphore(f"e_{eng}_{epoch}"))
        return self.esem[k]

    def _sem_of(self, key):
        if key[0] == "d":
            return self.dsem[key[1]]
        return self._esem(key[0], key[1])

    def _wait(self, eng, deps):
        s = self.seen[eng]
        for k, v in deps.items():
            if eng == "pe" and k[0] == "pe":
                continue
            if s.get(k, 0) < v:
                self.engs[eng].wait_ge(self._sem_of(k), v)
                s[k] = v

    @staticmethod
    def _acc(deps, d):
        for k, v in d.items():
            if deps.get(k, 0) < v:
                deps[k] = v

    def _commit(self, tok, reads, writes, part=False):
        k, v = tok
        for b in writes:
            if not part:
                b.w = {}
            b.w[k] = max(b.w.get(k, 0), v)
            b.r = {}
        for b in reads:
            if b.r.get(k, 0) < v:
                b.r[k] = v

    def op(self, eng, fn, reads=(), writes=()):
        reads = [_b(x) for x in reads]
        writes = [_b(x) for x in writes]
        deps = {}
        for b in reads:
            self._acc(deps, b.w)
        for b in writes:
            self._acc(deps, b.w)
            self._acc(deps, b.r)
        self._wait(eng, deps)
        ins = fn(self.engs[eng])
        c = self.cnt[eng]
        epoch, val = divmod(c, EPOCH)
        ins.then_inc(self._esem(eng, epoch), 1)
        self.cnt[eng] = c + 1
        self._commit(((eng, epoch), val + 1), reads, writes)
        self.ninst += 1
        return ins

    def dma(self, q, out, in_, reads=(), writes=(), part=False):
        reads = [_b(x) for x in reads]
        writes = [_b(x) for x in writes]
        deps = {}
        for b in reads:
            self._acc(deps, b.w)
        for b in writes:
            if not part:
                self._acc(deps, b.w)
            self._acc(deps, b.r)
        i = self.dnext
        self.dnext = (i + 1) % NDSEM
        if self.dval[i]:
            deps[("d", i)] = max(deps.get(("d", i), 0), self.dval[i])
        self._wait(q, deps)
        ins = self.engs[q].dma_start(out=out, in_=in_)
        ins.then_inc(self.dsem[i], 16)
        self.dval[i] += 16
        assert self.dval[i] < 60000
        self._commit((("d", i), self.dval[i]), reads, writes, part=part)
        self.ninst += 1
        return ins


def _barrier(S):
    deps = {}
    for e, cnt in S.cnt.items():
        if cnt:
            epoch, val = divmod(cnt - 1, EPOCH)
            deps[(e, epoch)] = val + 1
    for i in range(NDSEM):
        if S.dval[i]:
            deps[("d", i)] = S.dval[i]
    for e in S.engs:
        s = S.seen[e]
        for k, v in deps.items():
            if k[0] == e:
                continue
            if s.get(k, 0) < v:
                S.engs[e].wait_ge(S._sem_of(k), v)
                s[k] = v


class Cfg:
    def __init__(s, D=4096, SEQ=4096, B=4, NPAN=1024, PPAN=512, SEGB=2, WB6=256):
        s.D, s.SEQ, s.B = D, SEQ, B
        s.NM = 16
        s.H = D // 256
        s.AW = s.H * 128
        s.G = D // 32
        s.SW = s.G * 16
        s.NJ = s.G // 2
        s.NCT = s.SW // 128
        s.PH, s.PK, s.TOPK = 8, 128, 16
        s.PN = s.PK * s.PK
        s.QW = s.PH * 256
        s.L = SEQ + s.NM
        s.NO = SEQ // 2
        s.NB = s.NO // 128
        s.KT = D // 128
        s.NPAN = min(NPAN, s.NO)
        s.PPAN = min(PPAN, s.NO)
        s.SEGB = min(SEGB, s.NB)
        s.WB6 = WB6
        s.COL_Q = 0
        s.COL_K = s.AW
        s.COL_V = 2 * s.AW
        s.COL_F = 3 * s.AW
        s.COL_U = s.COL_F + s.H
        s.COL_GA = s.COL_U + s.SW
        s.COL_GB = s.COL_GA + D
        s.NCOLS = s.COL_GB + D


def build(cfg):
    c = cfg
    D, L, NO, KT, H, NB = c.D, c.L, c.NO, c.KT, c.H, c.NB
    nc = bass.Bass("TRN2", target_bir_lowering=False)

    def din(name, shape, dt=F32):
        return nc.dram_tensor(name, list(shape), dt, kind="ExternalInput").ap()

    def dscr(name, shape, dt):
        return TT(nc.dram_tensor(name, list(shape), dt, kind="ExternalOutput" if getattr(c, "debug", False) else "Internal").ap())

    x_ctx = din("x_ctx", [L, D]); x_own = din("x_own", [NO, D])
    g1rep = din("g1rep", [128, D]); g2rep = din("g2rep", [128, D])
    w_in = din("w_in", [D, c.NCOLS])
    bfg = din("bfg", [H, 1]); qg = din("qg", [128, 1]); kg = din("kg", [128, 1])
    lrA = din("lrA", [128, c.NJ]); liA = din("liA", [128, c.NJ]); ldA = din("ldA", [128, c.NJ])
    lrB = din("lrB", [128, c.NJ * 128]); liB = din("liB", [128, c.NJ * 128]); ldB = din("ldB", [128, c.NJ * 128])
    brB = din("brB", [128, c.NJ * 128]); biB = din("biB", [128, c.NJ * 128])
    crB = din("crB", [128, c.NJ * 128]); ciB = din("ciB", [128, c.NJ * 128])
    dsk = din("dsk", [128, c.NCT])
    w_glu = din("w_glu", [c.SW, c.SW]); w_a = din("w_a", [c.AW, D]); w_b = din("w_b", [c.SW, D])
    w_out = din("w_out", [D, D]); w_query = din("w_query", [D, c.QW])
    skT = din("skT", [128, 2 * 128])
    euT = din("euT", [D, c.PN]); ev = din("ev", [c.PN, D])
    maskA = din("maskA", [128, 128]); maskB = din("maskB", [128, 128])
    w01 = din("w01", [128, 2])
    t_loc = din("t_loc", [128, 256 * c.SEGB + 16]); t_own = din("t_own", [128, NO])
    ident = din("ident", [128, 128])
    out_own = nc.dram_tensor("out_own", [NO, D], F32, kind="ExternalOutput").ap()

    KTd = dscr("KTd", [H, 128, L], BF16)
    Vd = dscr("Vd", [L, c.AW], BF16)
    UTd = dscr("UTd", [c.SW, L], BF16)
    QTd = dscr("QTd", [H, 128, NO], BF16)
    GAd = dscr("GAd", [D, NO], F32)
    GBd = dscr("GBd", [D, NO], F32)
    UOd = dscr("UOd", [c.SW, NO], F32)
    CQd = dscr("CQd", [H, NO], F32)
    CQ3d = dscr("CQ3d", [3, H, NO], BF16)
    BBRd = dscr("BBRd", [128, c.NJ * 128], BF16)
    BBId = dscr("BBId", [128, c.NJ * 128], BF16)
    ZFd = dscr("ZFd", [c.SW, NO], F32)
    ZTd = dscr("ZTd", [c.SW, NO], BF16)
    SSTd = dscr("SSTd", [c.SW, NO], BF16)
    ATd = dscr("ATd", [c.AW, NO], BF16)
    H1d = dscr("H1d", [NO, D], F32)
    WGTd = dscr("WGTd", [c.PN, NO], BF16)
    OUTb = Buf()

    with ExitStack() as gst:
        S = Sched(nc, gst)

        uid = {"n": 0}

        def sb(st, name, shape, dt):
            uid["n"] += 1
            return TT(st.enter_context(nc.sbuf_tensor(f"{name}_{uid['n']}", list(shape), dt)))

        ps = [TT(gst.enter_context(nc.psum_tensor(f"ps{i}", [128, 512], F32))) for i in range(8)]
        psb = []
        for i in range(2):
            v = TT(ps[6 + i].t.bitcast(BF16))
            v.b = ps[6 + i].b
            psb.append(v)
        id_bf = sb(gst, "id_bf", [128, 128], BF16)
        id_f = sb(gst, "id_f", [128, 128], F32)
        ones_bf = sb(gst, "ones_bf", [128, 128], BF16)
        ones_f = sb(gst, "ones_f", [1, 128], F32)
        cst = sb(gst, "cst", [128, 8], F32)
        w01t = sb(gst, "w01t", [128, 2], F32)
        qgt = sb(gst, "qgt", [128, 1], F32); kgt = sb(gst, "kgt", [128, 1], F32)
        cacc = sb(gst, "cacc", [16, L], F32)
        rhoA = sb(gst, "rhoA", [128, c.NJ], F32)
        phiA = sb(gst, "phiA", [128, c.NJ], F32)
        S.dma("pool", id_bf.t[:], ident, writes=[id_bf])
        S.dma("sp", id_f.t[:], ident, writes=[id_f])
        S.dma("sp", w01t.t[:], w01, writes=[w01t])
        S.dma("sp", qgt.t[:], qg, writes=[qgt])
        S.dma("sp", kgt.t[:], kg, writes=[kgt])
        S.op("dve", lambda e: e.memset(ones_bf.t[:], 1.0), writes=[ones_bf])
        S.op("dve", lambda e: e.memset(ones_f.t[:], 1.0), writes=[ones_f])
        S.op("dve", lambda e: e.memset(cst.t[:, 0:1], 1e-6), writes=[cst])
        S.op("dve", lambda e: e.memset(cst.t[:, 1:2], 1.0), writes=[cst])
        S.op("dve", lambda e: e.memset(cst.t[:, 2:3], 0.0), writes=[cst])
        EPS = cst.t[:, 0:1]
        ONE = cst.t[:, 1:2]

        rr = {"e": 0}

        def alt():
            rr["e"] ^= 1
            return "act" if rr["e"] else "dve"

        def copy(eng, out, in_, reads, writes):
            if eng == "act":
                S.op("act", lambda e: e.activation(out, in_, AF.Copy), reads=reads, writes=writes)
            else:
                S.op(eng, lambda e: e.tensor_copy(out, in_), reads=reads, writes=writes)

        def make_nt(st, tagp):
          xt = sb(st, tagp + "xt", [128, D], F32)
          xn = sb(st, tagp + "xn", [128, D], BF16)
          ss = sb(st, tagp + "ss", [128, 2], F32)

          def norm_transpose(src, n_rows, grep, panel, srcbuf=None):
            for t0 in range(0, n_rows, 128):
                r = min(128, n_rows - t0)
                S.dma("sp", xt.t[:r, :], src[t0:t0 + r, :], reads=[srcbuf] if srcbuf else [], writes=[xt])
                S.op("dve", lambda e: e.memset(ss.t[:r, 0:1], 0.0), writes=[ss])
                S.op("act", lambda e: e.activation(xn.t[:r, :], xt.t[:r, :], AF.Square, accum_out=ss.t[:r, 0:1]),
                     reads=[xt], writes=[xn, ss])
                S.op("dve", lambda e: e.tensor_scalar(ss.t[:r, 1:2], ss.t[:r, 0:1], 1.0 / D, 1e-6, ALU.mult, ALU.add),
                     reads=[ss], writes=[ss])
                S.op("act", lambda e: e.activation(ss.t[:r, 1:2], ss.t[:r, 1:2], AF.Sqrt), reads=[ss], writes=[ss])
                S.op("dve", lambda e: e.reciprocal(ss.t[:r, 1:2], ss.t[:r, 1:2]), reads=[ss], writes=[ss])
                S.op("dve", lambda e: e.scalar_tensor_tensor(xn.t[:r, :], xt.t[:r, :], ss.t[:r, 1:2], grep.t[:r, :],
                                                             ALU.mult, ALU.mult),
                     reads=[xt, ss, grep], writes=[xn])
                for k0 in range(0, KT, 8):
                    k1 = min(KT, k0 + 8)
                    pb = psb[(k0 // 8) % 2]
                    for kt in range(k0, k1):
                        S.op("pe", lambda e: e.transpose(pb.t[:, (kt - k0) * 128:(kt - k0) * 128 + r],
                                                         xn.t[:r, kt * 128:(kt + 1) * 128], id_bf.t[:r, :r]),
                             reads=[xn, id_bf], writes=[pb])
                    src_ap = pb.t[:, 0:(k1 - k0) * 128].rearrange("p (k r) -> p k r", r=128)[:, :, :r]
                    copy(alt(), panel.t[:, k0:k1, t0:t0 + r], src_ap, [pb], [panel])
          return norm_transpose

        wstate = {"i": 0}

        def gemm(mode, wbufs, wsrc, K, c0, ncols, act, actbufs, N, epi, kgroups=None, WB=None, banks=(0, 1, 2, 3),
                 wq="pool", wsrcbuf=None):
            KTl = K // 128
            WB = WB or wbufs[0].t.shape[2]
            kgroups = kgroups or [(0, KTl)]
            bi = 0
            for sb0 in range(0, ncols, WB):
                cw = min(WB, ncols - sb0)
                wt = wbufs[wstate["i"] % len(wbufs)]
                wstate["i"] += 1
                srcs = wsrc if isinstance(wsrc, list) else [(wsrc, K)]
                kbase = 0
                for (wap, Ks) in srcs:
                    for k0 in range(0, Ks // 128, 8):
                        k1 = min(Ks // 128, k0 + 8)
                        srcap = wap[k0 * 128:k1 * 128, c0 + sb0:c0 + sb0 + cw].rearrange("(kt p) c -> p kt c", p=128)
                        S.dma(wq, wt.t[:, kbase + k0:kbase + k1, :cw], srcap, reads=[wsrcbuf] if wsrcbuf else [],
                              writes=[wt], part=True)
                    kbase += Ks // 128
                if mode == "fm":
                    for j0 in range(0, cw, 128):
                        m = min(128, cw - j0)
                        for n0 in range(0, N, 512):
                            n1 = min(N, n0 + 512)
                            pss = []
                            for (ka, kb) in kgroups:
                                pb = ps[banks[bi % len(banks)]]
                                bi += 1
                                for kt in range(ka, kb):
                                    S.op("pe", lambda e: e.matmul(pb.t[:m, :n1 - n0], wt.t[:, kt, j0:j0 + m],
                                                                  act(kt, n0, n1), start=(kt == ka), stop=(kt == kb - 1)),
                                         reads=[wt] + actbufs, writes=[pb])
                                pss.append(pb)
                            epi(c0 + sb0 + j0, m, n0, n1, pss)
                else:
                    for t0 in range(0, N, 128):
                        t1 = min(N, t0 + 128)
                        pss = []
                        for (ka, kb) in kgroups:
                            pb = ps[banks[bi % len(banks)]]
                            bi += 1
                            for kt in range(ka, kb):
                                S.op("pe", lambda e: e.matmul(pb.t[:t1 - t0, :cw], act(kt, t0, t1), wt.t[:, kt, :cw],
                                                              start=(kt == ka), stop=(kt == kb - 1)),
                                     reads=[wt] + actbufs, writes=[pb])
                            pss.append(pb)
                        epi(t0, t1, c0 + sb0, cw, pss)

        def coeffs(st, tag, lr_ap, li_ap, ld_ap, F):
            lr = sb(st, tag + "lr", [128, F], F32); li = sb(st, tag + "li", [128, F], F32)
            ld = sb(st, tag + "ld", [128, F], F32)
            t1 = sb(st, tag + "t1", [128, F], F32); t2 = sb(st, tag + "t2", [128, F], F32)
            ti = sb(st, tag + "ti", [128, F], I32)
            rho = sb(st, tag + "rho", [128, F], F32); phi = sb(st, tag + "phi", [128, F], F32)
            sn = sb(st, tag + "sn", [128, F], F32); cs = sb(st, tag + "cs", [128, F], F32)
            fr = sb(st, tag + "fr", [128, F], F32); fi = sb(st, tag + "fi", [128, F], F32)

            def run(lr_src, li_src, ld_src):
                S.dma("sp", lr.t[:], lr_src, writes=[lr]); S.dma("sp", li.t[:], li_src, writes=[li])
                S.dma("sp", ld.t[:], ld_src, writes=[ld])
                S.op("act", lambda e: e.activation(ld.t[:], ld.t[:], AF.Exp), reads=[ld], writes=[ld])
                S.op("dve", lambda e: e.tensor_tensor(t1.t[:], lr.t[:], ld.t[:], ALU.mult), reads=[lr, ld], writes=[t1])
                S.op("act", lambda e: e.activation(rho.t[:], t1.t[:], AF.Exp), reads=[t1], writes=[rho])
                S.op("dve", lambda e: e.scalar_tensor_tensor(t1.t[:], li.t[:], 1.0 / TWO_PI, ld.t[:], ALU.mult, ALU.mult),
                     reads=[li, ld], writes=[t1])
                S.op("dve", lambda e: e.tensor_copy(ti.t[:], t1.t[:]), reads=[t1], writes=[ti])
                S.op("dve", lambda e: e.tensor_copy(t2.t[:], ti.t[:]), reads=[ti], writes=[t2])
                S.op("dve", lambda e: e.tensor_sub(phi.t[:], t1.t[:], t2.t[:]), reads=[t1, t2], writes=[phi])
                S.op("act", lambda e: e.activation(sn.t[:], phi.t[:], AF.Sin, scale=TWO_PI), reads=[phi], writes=[sn])
                S.op("dve", lambda e: e.tensor_scalar(t1.t[:], phi.t[:], 0.25, None, ALU.add), reads=[phi], writes=[t1])
                S.op("dve", lambda e: e.tensor_copy(ti.t[:], t1.t[:]), reads=[t1], writes=[ti])
                S.op("dve", lambda e: e.tensor_copy(t2.t[:], ti.t[:]), reads=[ti], writes=[t2])
                S.op("dve", lambda e: e.tensor_sub(t1.t[:], t1.t[:], t2.t[:]), reads=[t1, t2], writes=[t1])
                S.op("act", lambda e: e.activation(cs.t[:], t1.t[:], AF.Sin, scale=TWO_PI), reads=[t1], writes=[cs])
                S.op("dve", lambda e: e.tensor_tensor(cs.t[:], cs.t[:], rho.t[:], ALU.mult), reads=[cs, rho], writes=[cs])
                S.op("dve", lambda e: e.tensor_tensor(sn.t[:], sn.t[:], rho.t[:], ALU.mult), reads=[sn, rho], writes=[sn])
                S.op("dve", lambda e: e.tensor_scalar(t1.t[:], cs.t[:], -1.0, None, ALU.add), reads=[cs], writes=[t1])
                S.op("dve", lambda e: e.tensor_tensor(t2.t[:], lr.t[:], lr.t[:], ALU.mult), reads=[lr], writes=[t2])
                S.op("dve", lambda e: e.tensor_tensor(fr.t[:], li.t[:], li.t[:], ALU.mult), reads=[li], writes=[fr])
                S.op("dve", lambda e: e.tensor_add(t2.t[:], t2.t[:], fr.t[:]), reads=[t2, fr], writes=[t2])
                S.op("dve", lambda e: e.reciprocal(t2.t[:], t2.t[:]), reads=[t2], writes=[t2])
                S.op("dve", lambda e: e.tensor_tensor(fr.t[:], t1.t[:], lr.t[:], ALU.mult), reads=[t1, lr], writes=[fr])
                S.op("dve", lambda e: e.tensor_tensor(fi.t[:], sn.t[:], li.t[:], ALU.mult), reads=[sn, li], writes=[fi])
                S.op("dve", lambda e: e.tensor_add(fr.t[:], fr.t[:], fi.t[:]), reads=[fr, fi], writes=[fr])
                S.op("dve", lambda e: e.tensor_tensor(fr.t[:], fr.t[:], t2.t[:], ALU.mult), reads=[fr, t2], writes=[fr])
                S.op("dve", lambda e: e.tensor_tensor(fi.t[:], sn.t[:], lr.t[:], ALU.mult), reads=[sn, lr], writes=[fi])
                S.op("dve", lambda e: e.tensor_tensor(t1.t[:], t1.t[:], li.t[:], ALU.mult), reads=[t1, li], writes=[t1])
                S.op("dve", lambda e: e.tensor_sub(fi.t[:], fi.t[:], t1.t[:]), reads=[fi, t1], writes=[fi])
                S.op("dve", lambda e: e.tensor_tensor(fi.t[:], fi.t[:], t2.t[:], ALU.mult), reads=[fi, t2], writes=[fi])
            return run, rho, phi, fr, fi

        with ExitStack() as st:
            run, rho, phi, fr, fi = coeffs(st, "ca", None, None, None, c.NJ)
            run(lrA, liA, ldA)
            S.op("dve", lambda e: e.tensor_copy(rhoA.t[:], rho.t[:]), reads=[rho], writes=[rhoA])
            S.op("dve", lambda e: e.tensor_copy(phiA.t[:], phi.t[:]), reads=[phi], writes=[phiA])
        _barrier(S)
        with ExitStack() as st:
            FC = min(1024, c.NJ * 128)
            run, rho, phi, fr, fi = coeffs(st, "cb", None, None, None, FC)
            br = sb(st, "br", [128, FC], F32); bi_ = sb(st, "bi", [128, FC], F32)
            o1 = sb(st, "o1", [128, FC], F32); o2 = sb(st, "o2", [128, FC], F32)
            ob1 = sb(st, "ob1", [128, FC], BF16); ob2 = sb(st, "ob2", [128, FC], BF16)
            for f0 in range(0, c.NJ * 128, FC):
                run(lrB[:, f0:f0 + FC], liB[:, f0:f0 + FC], ldB[:, f0:f0 + FC])
                S.dma("sp", br.t[:], brB[:, f0:f0 + FC], writes=[br])
                S.dma("sp", bi_.t[:], biB[:, f0:f0 + FC], writes=[bi_])
                S.op("dve", lambda e: e.tensor_tensor(o1.t[:], fr.t[:], br.t[:], ALU.mult), reads=[fr, br], writes=[o1])
                S.op("dve", lambda e: e.tensor_tensor(o2.t[:], fi.t[:], bi_.t[:], ALU.mult), reads=[fi, bi_], writes=[o2])
                S.op("dve", lambda e: e.tensor_sub(ob1.t[:], o1.t[:], o2.t[:]), reads=[o1, o2], writes=[ob1])
                S.op("dve", lambda e: e.tensor_tensor(o1.t[:], fr.t[:], bi_.t[:], ALU.mult), reads=[fr, bi_], writes=[o1])
                S.op("dve", lambda e: e.tensor_tensor(o2.t[:], fi.t[:], br.t[:], ALU.mult), reads=[fi, br], writes=[o2])
                S.op("dve", lambda e: e.tensor_add(ob2.t[:], o1.t[:], o2.t[:]), reads=[o1, o2], writes=[ob2])
                S.dma("sp", BBRd.t[:, f0:f0 + FC], ob1.t[:], reads=[ob1], writes=[BBRd], part=True)
                S.dma("sp", BBId.t[:, f0:f0 + FC], ob2.t[:], reads=[ob2], writes=[BBId], part=True)

        def proj_phase(is_ctx):
            with ExitStack() as st:
                tag = "c" if is_ctx else "o"
                NPmax = c.NPAN + (c.NM if is_ctx else 0)
                panel = sb(st, tag + "pan", [128, KT, NPmax], BF16)
                grep = sb(st, tag + "g1", [128, D], F32)
                S.dma("sp", grep.t[:], g1rep, writes=[grep])
                wbufs = [sb(st, tag + f"w{i}", [128, KT, 256], BF16) for i in range(2)]
                stg = [sb(st, tag + f"stg{i}", [128, 512], BF16) for i in range(3)]
                stf = [sb(st, tag + f"stf{i}", [128, 512], F32) for i in range(3)]
                sq = sb(st, tag + "sq", [128, 512], BF16)
                rinv = sb(st, tag + "rinv", [128, 512], F32)
                bft = sb(st, tag + "bft", [16, 1], F32)
                lf = sb(st, tag + "lf", [16, 512], F32)
                ones_t = sb(st, tag + "ones", [16, 512], F32)
                S.dma("sp", bft.t[:H, :], bfg, writes=[bft])
                S.op("dve", lambda e: e.tensor_scalar(bft.t[:H, :], bft.t[:H, :], -1.0, None, ALU.mult), reads=[bft], writes=[bft])
                S.op("dve", lambda e: e.memset(ones_t.t[:], 1.0), writes=[ones_t])
                si = {"g": 0, "f": 0}
                src = x_ctx if is_ctx else x_own
                ntot = L if is_ctx else NO
                p0 = 0
                nt = make_nt(st, tag)
                while p0 < ntot:
                    pn = min(c.NPAN + (c.NM if (is_ctx and p0 == 0) else 0), ntot - p0)
                    nt(src[p0:p0 + pn, :], pn, grep, panel)

                    def act(kt, n0, n1):
                        return panel.t[:, kt, n0:n1]

                    def epi_qk(dst, gcol, colbase):
                        def epi(col, m, n0, n1, pss):
                            n = n1 - n0
                            h = (col - colbase) // 128
                            pb = pss[0]
                            S.op("act", lambda e: e.activation(sq.t[:, :n], pb.t[:, :n], AF.Square), reads=[pb], writes=[sq])
                            p2 = ps[4]
                            S.op("pe", lambda e: e.matmul(p2.t[:, :n], ones_bf.t[:], sq.t[:, :n], start=True, stop=True),
                                 reads=[ones_bf, sq], writes=[p2])
                            S.op("act", lambda e: e.activation(rinv.t[:, :n], p2.t[:, :n], AF.Sqrt, bias=EPS, scale=1.0 / 128),
                                 reads=[p2, cst], writes=[rinv])
                            S.op("dve", lambda e: e.reciprocal(rinv.t[:, :n], rinv.t[:, :n]), reads=[rinv], writes=[rinv])
                            sg = stg[si["g"] % 3]; si["g"] += 1
                            S.op("dve", lambda e: e.scalar_tensor_tensor(sg.t[:, :n], pb.t[:, :n], gcol.t[:, 0:1], rinv.t[:, :n],
                                                                         ALU.mult, ALU.mult),
                                 reads=[pb, gcol, rinv], writes=[sg])
                            S.dma("sp", dst.t[h, :, p0 + n0:p0 + n1], sg.t[:, :n], reads=[sg], writes=[dst], part=True)
                        return epi

                    def epi_store_bf(dst):
                        def epi(col, m, n0, n1, pss):
                            n = n1 - n0
                            sg = stg[si["g"] % 3]; si["g"] += 1
                            copy(alt(), sg.t[:m, :n], pss[0].t[:m, :n], [pss[0]], [sg])
                            S.dma("sp", dst.t[col:col + m, p0 + n0:p0 + n1], sg.t[:m, :n], reads=[sg], writes=[dst], part=True)
                        return epi

                    if is_ctx:
                        gemm("fm", wbufs, w_in, D, c.COL_K, c.AW, act, [panel], pn, epi_qk(KTd, kgt, c.COL_K))

                        def epi_v(t0, t1, col, cw, pss):
                            r = t1 - t0
                            sg = stg[si["g"] % 3]; si["g"] += 1
                            copy(alt(), sg.t[:r, :cw], pss[0].t[:r, :cw], [pss[0]], [sg])
                            S.dma("sp", Vd.t[p0 + t0:p0 + t1, col - c.COL_V:col - c.COL_V + cw], sg.t[:r, :cw],
                                  reads=[sg], writes=[Vd], part=True)
                        gemm("tm", wbufs, w_in, D, c.COL_V, c.AW, act, [panel], pn, epi_v)

                        def epi_u(col, m, n0, n1, pss):
                            n = n1 - n0
                            sg = stg[si["g"] % 3]; si["g"] += 1
                            copy(alt(), sg.t[:m, :n], pss[0].t[:m, :n], [pss[0]], [sg])
                            S.dma("sp", UTd.t[col - c.COL_U:col - c.COL_U + m, p0 + n0:p0 + n1], sg.t[:m, :n],
                                  reads=[sg], writes=[UTd], part=True)
                        gemm("fm", wbufs, w_in, D, c.COL_U, c.SW, act, [panel], pn, epi_u)

                        def epi_f(col, m, n0, n1, pss):
                            n = n1 - n0
                            pb = pss[0]
                            S.op("act", lambda e: e.activation(lf.t[:H, :n], pb.t[:H, :n], AF.Exp, bias=bft.t[:H, 0:1], scale=-1.0),
                                 reads=[pb, bft], writes=[lf])
                            S.op("act", lambda e: e.activation(lf.t[:H, :n], lf.t[:H, :n], AF.Ln, bias=ONE[:H, :], scale=1.0),
                                 reads=[lf, cst], writes=[lf])
                            S.op("dve", lambda e: e.tensor_scalar(lf.t[:H, :n], lf.t[:H, :n], -1.0, None, ALU.mult), reads=[lf], writes=[lf])
                            a0 = p0 + n0
                            init = 0.0 if a0 == 0 else cacc.t[:H, a0 - 1:a0]
                            S.op("dve", lambda e: e.tensor_tensor_scan(cacc.t[:H, a0:a0 + n], ones_t.t[:H, :n], lf.t[:H, :n], init,
                                                                       ALU.mult, ALU.add),
                                 reads=[ones_t, lf, cacc], writes=[cacc])
                        gemm("fm", wbufs, w_in, D, c.COL_F, H, act, [panel], pn, epi_f)
                    else:
                        gemm("fm", wbufs, w_in, D, c.COL_Q, c.AW, act, [panel], pn, epi_qk(QTd, qgt, c.COL_Q))

                        def epi_f32(dst, colbase, func):
                            def epi(col, m, n0, n1, pss):
                                n = n1 - n0
                                sf = stf[si["f"] % 3]; si["f"] += 1
                                S.op("act", lambda e: e.activation(sf.t[:m, :n], pss[0].t[:m, :n], func), reads=[pss[0]], writes=[sf])
                                S.dma("sp", dst.t[col - colbase:col - colbase + m, p0 + n0:p0 + n1], sf.t[:m, :n],
                                      reads=[sf], writes=[dst], part=True)
                            return epi
                        gemm("fm", wbufs, w_in, D, c.COL_U, c.SW, act, [panel], pn, epi_f32(UOd, c.COL_U, AF.Copy))
                        gemm("fm", wbufs, w_in, D, c.COL_GA, D, act, [panel], pn, epi_f32(GAd, c.COL_GA, AF.Sigmoid))
                        gemm("fm", wbufs, w_in, D, c.COL_GB, D, act, [panel], pn, epi_f32(GBd, c.COL_GB, AF.Sigmoid))
                    p0 += pn

        _barrier(S)
        proj_phase(True)
        _barrier(S)
        proj_phase(False)

        NKT = 2 * NB + 1

        def ktile(kt):
            return (0, 16) if kt == 0 else (16 + 128 * (kt - 1), 128)

        MAGIC = 12582912.0

        def ssm_gen(st):
            if True:
                LSM = 256 * c.SEGB + 16
                NSEG = NB // c.SEGB
                PC = min(2, c.SEGB)
                tl = sb(st, "tl", [128, LSM], F32); S.dma("sp", tl.t[:], t_loc, writes=[tl])
                onesL = sb(st, "onesL", [128, LSM], F32)
                S.op("dve", lambda e: e.memset(onesL.t[:], 1.0), writes=[onesL])
                hpi = sb(st, "hpi", [128, 1], F32)
                S.op("dve", lambda e: e.memset(hpi.t[:], math.pi / 2), writes=[hpi])
                dskt = sb(st, "dskt", [128, c.NCT], F32); S.dma("sp", dskt.t[:], dsk, writes=[dskt])
                uT = sb(st, "uT", [128, L], BF16)
                uo = sb(st, "uo", [128, NO], F32)
                wbr = sb(st, "wbr", [128, 512], BF16); wbi = sb(st, "wbi", [128, 512], BF16)
                wcr = sb(st, "wcr", [128, 512], BF16); wci = sb(st, "wci", [128, 512], BF16)
                zr = sb(st, "zr", [128, 4, LSM], BF16); nzi = sb(st, "nzi", [128, 4, LSM], BF16)
                z2 = sb(st, "z2", [128, 4, LSM], BF16); z4 = sb(st, "z4", [128, 4, LSM], BF16)
                nwcr = sb(st, "nwcr", [128, 512], BF16); nwci = sb(st, "nwci", [128, 512], BF16)
                rho_t = sb(st, "rho_t", [128, 4, LSM], F32)
                sets = [[sb(st, f"s{k}_{i}", [128, LSM], F32) for i in range(10)] + [sb(st, f"ab{k}", [128, 1], F32)] for k in range(2)]
                carry = sb(st, "carry", [128, 4, 2], F32)
                sel = sb(st, "sel", [128, 256], F32)
                yf = sb(st, "yf", [128, 256], F32); zf = sb(st, "zf", [128, 256], F32); zb = sb(st, "zb", [128, 256], BF16)
                for ct in range(c.NCT):
                    S.dma("sp", uT.t[:], UTd.t[ct * 128:(ct + 1) * 128, :], reads=[UTd], writes=[uT])
                    S.dma("sp", uo.t[:], UOd.t[ct * 128:(ct + 1) * 128, :], reads=[UOd], writes=[uo])
                    cols = slice(ct * 512, (ct + 1) * 512)
                    S.dma("sp", wbr.t[:], BBRd.t[:, cols], reads=[BBRd], writes=[wbr])
                    S.dma("sp", wbi.t[:], BBId.t[:, cols], reads=[BBId], writes=[wbi])
                    S.dma("pool", wcr.t[:], crB[:, cols], writes=[wcr])
                    S.dma("pool", wci.t[:], ciB[:, cols], writes=[wci])
                    S.op("pool", lambda e: e.tensor_scalar(nwcr.t[:], wcr.t[:], -1.0, None, ALU.mult), reads=[wcr], writes=[nwcr])
                    S.op("pool", lambda e: e.tensor_scalar(nwci.t[:], wci.t[:], -1.0, None, ALU.mult), reads=[wci], writes=[nwci])
                    for jj in range(4):
                        j = 4 * ct + jj
                        S.op("act", lambda e: e.activation(rho_t.t[:, jj, :], onesL.t[:], AF.Copy, scale=rhoA.t[:, j:j + 1]),
                             reads=[onesL, rhoA], writes=[rho_t])
                    units = [(s_, jj_) for s_ in range(NSEG) for jj_ in range(4)]

                    def seginfo(s):
                        a0 = 0 if s == 0 else 16 + 256 * s * c.SEGB
                        a1 = 16 + 256 * (s + 1) * c.SEGB
                        return a0, a1 - a0, (16 if s == 0 else 0)

                    def stage1(s, jj):
                        a0, Ls, off = seginfo(s)
                        V = lambda t: t.t[:, :Ls]
                        j = 4 * ct + jj
                        xr, xi, ang, rr, sn, cs, bA, bB, bC, rr2, ab = sets[jj % 2]
                        for n0 in range(0, Ls, 512):
                            n1 = min(Ls, n0 + 512)
                            pr, pi = ps[5], ps[6]
                            S.op("pe", lambda e: e.matmul(pr.t[:, :n1 - n0], wbr.t[:, jj * 128:(jj + 1) * 128], uT.t[:, a0 + n0:a0 + n1],
                                                          start=True, stop=True), reads=[wbr, uT], writes=[pr])
                            S.op("pe", lambda e: e.matmul(pi.t[:, :n1 - n0], wbi.t[:, jj * 128:(jj + 1) * 128], uT.t[:, a0 + n0:a0 + n1],
                                                          start=True, stop=True), reads=[wbi, uT], writes=[pi])
                            copy("act", xr.t[:, n0:n1], pr.t[:, :n1 - n0], [pr], [xr])
                            copy("act", xi.t[:, n0:n1], pi.t[:, :n1 - n0], [pi], [xi])
                        S.op("act", lambda e: e.activation(ab.t[:], phiA.t[:, j:j + 1], AF.Copy, scale=float(a0)), reads=[phiA], writes=[ab])
                        S.op("act", lambda e: e.activation(V(ang), V(tl), AF.Identity, bias=ab.t[:, 0:1], scale=phiA.t[:, j:j + 1]),
                             reads=[tl, ab, phiA], writes=[ang])
                        S.op("dve", lambda e: e.tensor_scalar(V(rr), V(ang), MAGIC, MAGIC, ALU.add, ALU.subtract), reads=[ang], writes=[rr])
                        S.op("dve", lambda e: e.tensor_tensor(V(sn), V(ang), V(rr), ALU.subtract), reads=[ang, rr], writes=[sn])
                        S.op("act", lambda e: e.activation(V(cs), V(sn), AF.Abs), reads=[sn], writes=[cs])
                        S.op("act", lambda e: e.activation(V(cs), V(cs), AF.Sin, bias=hpi.t[:, 0:1], scale=-TWO_PI), reads=[cs, hpi], writes=[cs])
                        S.op("act", lambda e: e.activation(V(sn), V(sn), AF.Sin, scale=TWO_PI), reads=[sn], writes=[sn])

                    def stage2(s, jj):
                        a0, Ls, off = seginfo(s)
                        V = lambda t: t.t[:, :Ls]
                        xr, xi, ang, rr, sn, cs, bA, bB, bC, rr2, ab = sets[jj % 2]
                        S.op("dve", lambda e: e.tensor_tensor(V(bA), V(cs), V(xr), ALU.mult), reads=[cs, xr], writes=[bA])
                        S.op("dve", lambda e: e.tensor_tensor(V(bB), V(cs), V(xi), ALU.mult), reads=[cs, xi], writes=[bB])
                        S.op("dve", lambda e: e.tensor_tensor(V(bC), V(sn), V(xi), ALU.mult), reads=[sn, xi], writes=[bC])
                        S.op("dve", lambda e: e.tensor_tensor(V(rr), V(sn), V(xr), ALU.mult), reads=[sn, xr], writes=[rr])
                        S.op("dve", lambda e: e.tensor_add(V(bA), V(bA), V(bC)), reads=[bA, bC], writes=[bA])
                        S.op("dve", lambda e: e.tensor_sub(V(bB), V(bB), V(rr)), reads=[bB, rr], writes=[bB])
                        ir = 0.0 if s == 0 else carry.t[:, jj, 0:1]
                        ii = 0.0 if s == 0 else carry.t[:, jj, 1:2]
                        S.op("dve", lambda e: e.tensor_tensor_scan(V(xr), rho_t.t[:, jj, :Ls], V(bA), ir, ALU.mult, ALU.add),
                             reads=[rho_t, bA, carry], writes=[xr])
                        S.op("dve", lambda e: e.tensor_tensor_scan(V(xi), rho_t.t[:, jj, :Ls], V(bB), ii, ALU.mult, ALU.add),
                             reads=[rho_t, bB, carry], writes=[xi])
                        S.op("dve", lambda e: e.tensor_tensor(zr.t[:, jj, :Ls], V(cs), V(xr), ALU.mult), reads=[cs, xr], writes=[zr])
                        S.op("dve", lambda e: e.tensor_tensor(z2.t[:, jj, :Ls], V(sn), V(xi), ALU.mult), reads=[sn, xi], writes=[z2])
                        S.op("dve", lambda e: e.tensor_tensor(nzi.t[:, jj, :Ls], V(sn), V(xr), ALU.mult), reads=[sn, xr], writes=[nzi])
                        S.op("dve", lambda e: e.tensor_tensor(z4.t[:, jj, :Ls], V(cs), V(xi), ALU.mult), reads=[cs, xi], writes=[z4])
                        S.op("dve", lambda e: e.tensor_copy(carry.t[:, jj, 0:1], xr.t[:, Ls - 1:Ls]), reads=[xr], writes=[carry])
                        S.op("dve", lambda e: e.tensor_copy(carry.t[:, jj, 1:2], xi.t[:, Ls - 1:Ls]), reads=[xi], writes=[carry])
                        if jj == 3:
                            ypart(s)

                    def ypart(s):
                        a0, Ls, off = seginfo(s)
                        for q0 in range(0, c.SEGB, PC):
                            n = 256 * PC
                            l0 = off + 256 * q0
                            pb = ps[7]
                            for jj in range(4):
                                wsl = slice(jj * 128, (jj + 1) * 128)
                                S.op("pe", lambda e: e.matmul(pb.t[:, :n], wcr.t[:, wsl], zr.t[:, jj, l0:l0 + n],
                                                              start=(jj == 0), stop=False), reads=[wcr, zr], writes=[pb])
                                S.op("pe", lambda e: e.matmul(pb.t[:, :n], nwcr.t[:, wsl], z2.t[:, jj, l0:l0 + n],
                                                              start=False, stop=False), reads=[nwcr, z2], writes=[pb])
                                S.op("pe", lambda e: e.matmul(pb.t[:, :n], nwci.t[:, wsl], nzi.t[:, jj, l0:l0 + n],
                                                              start=False, stop=False), reads=[nwci, nzi], writes=[pb])
                                S.op("pe", lambda e: e.matmul(pb.t[:, :n], nwci.t[:, wsl], z4.t[:, jj, l0:l0 + n],
                                                              start=False, stop=(jj == 3)), reads=[nwci, z4], writes=[pb])
                            no = 128 * PC
                            o0 = (s * c.SEGB + q0) * 128
                            p4 = pb.t[:, :n].rearrange("p (j two r) -> p j two r", two=2, r=128)
                            s3 = sel.t[:, :no].rearrange("p (j r) -> p j r", r=128)
                            S.op("dve", lambda e: e.tensor_scalar(s3, p4[:, :, 0, :], w01t.t[:, 0:1], None, ALU.mult), reads=[pb, w01t], writes=[sel])
                            S.op("dve", lambda e: e.scalar_tensor_tensor(s3, p4[:, :, 1, :], w01t.t[:, 1:2], s3, ALU.mult, ALU.add),
                                 reads=[pb, w01t, sel], writes=[sel])
                            S.op("dve", lambda e: e.scalar_tensor_tensor(yf.t[:, :no], uo.t[:, o0:o0 + no], dskt.t[:, ct:ct + 1], sel.t[:, :no],
                                                                         ALU.mult, ALU.add), reads=[uo, dskt, sel], writes=[yf])
                            S.op("act", lambda e: e.activation(zf.t[:, :no], yf.t[:, :no], AF.Gelu), reads=[yf], writes=[zf])
                            S.op("dve", lambda e: e.tensor_copy(zb.t[:, :no], zf.t[:, :no]), reads=[zf], writes=[zb])
                            S.dma("sp", ZFd.t[ct * 128:(ct + 1) * 128, o0:o0 + no], zf.t[:, :no], reads=[zf], writes=[ZFd], part=True)
                            S.dma("sp", ZTd.t[ct * 128:(ct + 1) * 128, o0:o0 + no], zb.t[:, :no], reads=[zb], writes=[ZTd], part=True)

                    for i in range(len(units) + 1):
                        if i < len(units):
                            stage1(*units[i])
                        if i >= 1:
                            stage2(*units[i - 1])
                        yield

        def glu_phase():
            with ExitStack() as st:
                NP = c.NPAN
                zp = sb(st, "zp", [128, c.NCT, NP], BF16)
                wbufs = [sb(st, f"gw{i}", [128, c.NCT, 512], BF16) for i in range(2)]
                sg = sb(st, "gsg", [128, 512], F32); zt = sb(st, "gzt", [128, 512], F32); ob = sb(st, "gob", [128, 512], BF16)
                for p0 in range(0, NO, NP):
                    S.dma("sp", zp.t[:], ZTd.t[:, p0:p0 + NP].rearrange("(k p) n -> p k n", p=128), reads=[ZTd], writes=[zp])

                    def epi(col, m, n0, n1, pss):
                        n = n1 - n0
                        S.op("act", lambda e: e.activation(sg.t[:m, :n], pss[0].t[:m, :n], AF.Sigmoid), reads=[pss[0]], writes=[sg])
                        S.dma("sp", zt.t[:m, :n], ZFd.t[col:col + m, p0 + n0:p0 + n1], reads=[ZFd], writes=[zt])
                        S.op("dve", lambda e: e.tensor_tensor(ob.t[:m, :n], zt.t[:m, :n], sg.t[:m, :n], ALU.mult), reads=[zt, sg], writes=[ob])
                        S.dma("sp", SSTd.t[col:col + m, p0 + n0:p0 + n1], ob.t[:m, :n], reads=[ob], writes=[SSTd], part=True)
                    gemm("fm", wbufs, w_glu, c.SW, 0, c.SW, lambda kt, n0, n1: zp.t[:, kt, n0:n1], [zp], NP, epi)

        HG = min(4, H)
        QG = min(4, NB)

        def attn_prep(stp, st):
            if True:
                cball = sb(stp, "cball", [128, NKT, H], F32)
                mA = sb(stp, "mA", [128, 128], BF16); mB = sb(stp, "mB", [128, 128], BF16)
                cq = sb(st, "cq", [16, NO], F32)
                c3 = [sb(st, f"c3_{i}", [16, NO], BF16) for i in range(3)]
                r1 = sb(st, "cr1", [16, NO], F32)
                S.dma("pool", mA.t[:], maskA, writes=[mA]); S.dma("pool", mB.t[:], maskB, writes=[mB])
                for kt in range(NKT):
                    a, nk = ktile(kt)
                    pb = ps[kt % 2]
                    S.op("pe", lambda e: e.transpose(pb.t[:nk, :H], cacc.t[:H, a:a + nk], id_f.t[:H, :H]), reads=[cacc, id_f], writes=[pb])
                    S.op("dve", lambda e: e.tensor_scalar(cball.t[:nk, kt, :], pb.t[:nk, :H], -1.0, -STAB, ALU.mult, ALU.add),
                         reads=[pb], writes=[cball])
                v4 = cacc.t[:H, 16:16 + 256 * NB].rearrange("h (j two r) -> h j two r", two=2, r=128)
                d3 = cq.t[:H, :].rearrange("h (j r) -> h j r", r=128)
                S.op("dve", lambda e: e.tensor_scalar(d3, v4[:, :, 0, :], w01t.t[:H, 0:1], None, ALU.mult), reads=[cacc, w01t], writes=[cq])
                S.op("dve", lambda e: e.scalar_tensor_tensor(d3, v4[:, :, 1, :], w01t.t[:H, 1:2], d3, ALU.mult, ALU.add),
                     reads=[cacc, w01t, cq], writes=[cq])
                S.op("dve", lambda e: e.tensor_scalar(cq.t[:H, :], cq.t[:H, :], math.sqrt(128.0), None, ALU.mult), reads=[cq], writes=[cq])
                S.op("dve", lambda e: e.tensor_copy(c3[0].t[:H, :], cq.t[:H, :]), reads=[cq], writes=[c3[0]])
                S.op("dve", lambda e: e.tensor_sub(r1.t[:H, :], cq.t[:H, :], c3[0].t[:H, :]), reads=[cq, c3[0]], writes=[r1])
                S.op("dve", lambda e: e.tensor_copy(c3[1].t[:H, :], r1.t[:H, :]), reads=[r1], writes=[c3[1]])
                S.op("dve", lambda e: e.tensor_sub(r1.t[:H, :], r1.t[:H, :], c3[1].t[:H, :]), reads=[r1, c3[1]], writes=[r1])
                S.op("dve", lambda e: e.tensor_copy(c3[2].t[:H, :], r1.t[:H, :]), reads=[r1], writes=[c3[2]])
                for i in range(3):
                    S.dma("sp", CQ3d.t[i], c3[i].t[:H, :], reads=[c3[i]], writes=[CQ3d], part=True)
                S.dma("sp", CQd.t[:, :], cq.t[:H, :], reads=[cq], writes=[CQd])
            return cball, mA, mB

        def attn_gen(st, cball, mA, mB):
            if True:
                vg = sb(st, "vg", [128, NKT, HG * 128], BF16)
                kh = [sb(st, f"kh{i}", [128, L], BF16) for i in range(2)]
                qh = [sb(st, f"qh{i}", [128, NO], BF16) for i in range(2)]
                cqh = [sb(st, f"cqh{i}", [3, NO], BF16) for i in range(2)]
                pt = [sb(st, f"pt{i}", [128, 512], BF16) for i in range(3)]
                rs = [sb(st, f"rs{i}", [128, 512], F32) for i in range(2)]
                ao = [sb(st, f"ao{i}", [128, NO], BF16) for i in range(2)]
                pti = 0
                gi = 0
                scale = 1.0 / math.sqrt(128.0)
                for hg in range(0, H, HG):
                    S.dma("sp", vg.t[:16, 0, :], Vd.t[0:16, hg * 128:(hg + HG) * 128], reads=[Vd], writes=[vg], part=True)
                    for t0 in range(0, 2 * NB, 8):
                        t1 = min(2 * NB, t0 + 8)
                        S.dma("sp", vg.t[:, 1 + t0:1 + t1, :],
                              Vd.t[16 + 128 * t0:16 + 128 * t1, hg * 128:(hg + HG) * 128].rearrange("(t p) c -> p t c", p=128),
                              reads=[Vd], writes=[vg], part=True)
                    for h in range(hg, hg + HG):
                        khb, qhb, cqb, aob = kh[h % 2], qh[h % 2], cqh[h % 2], ao[h % 2]
                        S.dma("sp", khb.t[:], KTd.t[h], reads=[KTd], writes=[khb])
                        S.dma("sp", qhb.t[:], QTd.t[h], reads=[QTd], writes=[qhb])
                        S.dma("sp", cqb.t[:], CQ3d.t[:, h, :], reads=[CQ3d], writes=[cqb])
                        for g in range(0, NB, QG):
                            OT = ps[2 + gi % 2]; SM = ps[4]
                            gi += 1
                            W = QG * 128
                            ktmax = 2 * (g + QG - 1) + 2
                            pend = None
                            for kt in range(0, ktmax + 2):
                                if kt <= ktmax:
                                    a, nk = ktile(kt)
                                    jmin = max(g, (kt - 1) // 2) if kt > 0 else g
                                    q0 = (jmin - g) * 128
                                    qa, qb = g * 128 + q0, (g + QG) * 128
                                    STb = ps[kt % 2]
                                    S.op("pe", lambda e: e.matmul(STb.t[:nk, q0:W], khb.t[:, a:a + nk], qhb.t[:, qa:qb], start=True, stop=False),
                                         reads=[khb, qhb], writes=[STb])
                                    S.op("pe", lambda e: e.matmul(STb.t[:nk, q0:W], ones_bf.t[0:3, :nk], cqb.t[0:3, qa:qb], start=False, stop=True),
                                         reads=[ones_bf, cqb], writes=[STb])
                                    p = pt[pti % 3]; pti += 1
                                    S.op("act", lambda e: e.activation(p.t[:nk, q0:W], STb.t[:nk, q0:W], AF.Exp, bias=cball.t[:nk, kt, h:h + 1], scale=scale),
                                         reads=[STb, cball], writes=[p])
                                    if kt >= 1 and kt % 2 == 1:
                                        j = (kt - 1) // 2
                                        if g <= j < g + QG:
                                            cs_ = slice((j - g) * 128, (j - g + 1) * 128)
                                            S.op("pool", lambda e: e.tensor_tensor(p.t[:, cs_], p.t[:, cs_], mA.t[:, :], ALU.mult), reads=[p, mA], writes=[p])
                                    if kt >= 2 and kt % 2 == 0:
                                        j = (kt - 2) // 2
                                        if g <= j < g + QG:
                                            cs_ = slice((j - g) * 128, (j - g + 1) * 128)
                                            S.op("pool", lambda e: e.tensor_tensor(p.t[:, cs_], p.t[:, cs_], mB.t[:, :], ALU.mult), reads=[p, mB], writes=[p])
                                    cur = (kt, nk, q0, p)
                                else:
                                    cur = None
                                if pend is not None:
                                    kt_, nk_, q0_, p_ = pend
                                    last = kt_ == ktmax
                                    S.op("pe", lambda e: e.matmul(OT.t[:, q0_:W], vg.t[:nk_, kt_, (h - hg) * 128:(h - hg + 1) * 128], p_.t[:nk_, q0_:W],
                                                                  start=(kt_ == 0), stop=last), reads=[vg, p_], writes=[OT])
                                    S.op("pe", lambda e: e.matmul(SM.t[:, q0_:W], ones_bf.t[:nk_, :], p_.t[:nk_, q0_:W], start=(kt_ == 0), stop=last),
                                         reads=[ones_bf, p_], writes=[SM])
                                pend = cur
                                if kt % 2 == 1:
                                    yield
                            r = rs[gi % 2]
                            S.op("dve", lambda e: e.reciprocal(r.t[:, :W], SM.t[:, :W]), reads=[SM], writes=[r])
                            S.op("dve", lambda e: e.tensor_tensor(aob.t[:, g * 128:g * 128 + W], OT.t[:, :W], r.t[:, :W], ALU.mult), reads=[OT, r], writes=[aob])
                        S.dma("sp", ATd.t[h * 128:(h + 1) * 128, :], aob.t[:], reads=[aob], writes=[ATd], part=True)

        _barrier(S)
        with ExitStack() as stp:
            with ExitStack() as st0:
                cball_, mA_, mB_ = attn_prep(stp, st0)
            _barrier(S)
            with ExitStack() as stj:
                g1_ = ssm_gen(stj)
                g2_ = attn_gen(stj, cball_, mA_, mB_)
                live = [g1_, g2_]
                while live:
                    for g_ in list(live):
                        try:
                            next(g_)
                        except StopIteration:
                            live.remove(g_)
        _barrier(S)
        glu_phase()

        def mix_phase():
            with ExitStack() as st:
                NP = min(1024, NO)
                KA, KS = c.AW // 128, c.SW // 128
                cat = sb(st, "cat", [128, KA + KS, NP], BF16)
                mixT = sb(st, "mixT", [128, KT, NP], BF16)
                wbufs = [sb(st, f"mw{i}", [128, max(KT, KA + KS), 256], BF16) for i in range(2)]
                ga = sb(st, "ga", [128, 512], F32); gb = sb(st, "gb", [128, 512], F32)
                m1 = sb(st, "m1", [128, 512], F32); m2 = sb(st, "m2", [128, 512], F32)
                xo = [sb(st, f"xo{i}", [128, 512], F32) for i in range(2)]
                ho = [sb(st, f"ho{i}", [128, 512], F32) for i in range(2)]
                k = {"i": 0}
                for p0 in range(0, NO, NP):
                    S.dma("sp", cat.t[:, 0:KA, :], ATd.t[:, p0:p0 + NP].rearrange("(k p) n -> p k n", p=128), reads=[ATd], writes=[cat], part=True)
                    S.dma("sp", cat.t[:, KA:KA + KS, :], SSTd.t[:, p0:p0 + NP].rearrange("(k p) n -> p k n", p=128), reads=[SSTd], writes=[cat], part=True)

                    def epi(col, m, n0, n1, pss):
                        n = n1 - n0
                        S.dma("sp", ga.t[:m, :n], GAd.t[col:col + m, p0 + n0:p0 + n1], reads=[GAd], writes=[ga])
                        S.dma("sp", gb.t[:m, :n], GBd.t[col:col + m, p0 + n0:p0 + n1], reads=[GBd], writes=[gb])
                        S.op("dve", lambda e: e.tensor_tensor(m1.t[:m, :n], ga.t[:m, :n], pss[0].t[:m, :n], ALU.mult), reads=[ga, pss[0]], writes=[m1])
                        S.op("dve", lambda e: e.tensor_tensor(m2.t[:m, :n], gb.t[:m, :n], pss[1].t[:m, :n], ALU.mult), reads=[gb, pss[1]], writes=[m2])
                        S.op("pool", lambda e: e.tensor_add(mixT.t[:m, col // 128, n0:n1], m1.t[:m, :n], m2.t[:m, :n]), reads=[m1, m2], writes=[mixT])
                    gemm("fm", wbufs, [(w_a, c.AW), (w_b, c.SW)], c.AW + c.SW, 0, D, lambda kt, n0, n1: cat.t[:, kt, n0:n1], [cat], NP, epi,
                         kgroups=[(0, KA), (KA, KA + KS)])

                    def epi2(t0, t1, col, cw, pss):
                        r = t1 - t0
                        x_ = xo[k["i"] % 2]; h_ = ho[k["i"] % 2]; k["i"] += 1
                        S.dma("sp", x_.t[:r, :cw], x_own[p0 + t0:p0 + t1, col:col + cw], writes=[x_])
                        S.op("dve", lambda e: e.tensor_tensor(h_.t[:r, :cw], x_.t[:r, :cw], pss[0].t[:r, :cw], ALU.add), reads=[x_, pss[0]], writes=[h_])
                        S.dma("sp", H1d.t[p0 + t0:p0 + t1, col:col + cw], h_.t[:r, :cw], reads=[h_], writes=[H1d], part=True)
                    gemm("tm", wbufs, w_out, D, 0, D, lambda kt, t0, t1: mixT.t[:, kt, t0:t1], [mixT], NP, epi2)

        _barrier(S)
        mix_phase()

        def peer_phase():
            with ExitStack() as st:
                NP = c.PPAN
                NTT = NP // 128
                pan = sb(st, "ppan", [128, KT, NP], BF16)
                et = [sb(st, f"et{i}", [128, 16, 128], F32) for i in range(NTT)]
                thr = sb(st, "thr", [128, NTT, 8], F32); rz = sb(st, "rz", [128, NTT, 8], F32)
                for p0 in range(0, NO, NP):
                    _barrier(S)
                    with ExitStack() as s1:
                        grep = sb(s1, "g2", [128, D], F32); S.dma("sp", grep.t[:], g2rep, writes=[grep])
                        nt = make_nt(s1, "p")
                        nt(H1d.t[p0:p0 + NP, :], NP, grep, pan, srcbuf=H1d)
                    _barrier(S)
                    with ExitStack() as s2:
                        qpT = sb(s2, "qpT", [128, 16, NP], BF16)
                        skb = sb(s2, "skb", [128, 256], BF16); S.dma("pool", skb.t[:], skT, writes=[skb])
                        wbufs = [sb(s2, f"pw{i}", [128, KT, 256], BF16) for i in range(2)]
                        sc = sb(s2, "sc", [128, 16, 128], F32)
                        wk = sb(s2, "wk", [128, 256], F32)
                        m16 = sb(s2, "m16", [128, 16, 16], F32); e16 = sb(s2, "e16", [128, 16, 16], F32)
                        nm = sb(s2, "nm", [128, 16], F32)
                        cand = sb(s2, "cand", [128, 256], F32); c16 = sb(s2, "c16", [128, 16], F32)

                        def epi_q(col, m, n0, n1, pss):
                            copy(alt(), qpT.t[:, col // 128, n0:n1], pss[0].t[:, :n1 - n0], [pss[0]], [qpT])
                        gemm("fm", wbufs, w_query, D, 0, c.QW, lambda kt, n0, n1: pan.t[:, kt, n0:n1], [pan], NP, epi_q)
                        for tt in range(NTT):
                            ts_ = slice(tt * 128, (tt + 1) * 128)
                            for hc in range(16):
                                pb = ps[hc // 4]
                                S.op("pe", lambda e: e.matmul(pb.t[:, (hc % 4) * 128:(hc % 4 + 1) * 128], qpT.t[:, hc, ts_],
                                                              skb.t[:, (hc % 2) * 128:(hc % 2 + 1) * 128], start=True, stop=True),
                                     reads=[qpT, skb], writes=[pb])
                                if hc % 4 == 3:
                                    copy(alt(), sc.t[:, hc - 3:hc + 1, :], pb.t[:, :].rearrange("p (a b) -> p a b", b=128), [pb], [sc])
                            for hc in range(16):
                                S.op("dve", lambda e: e.max(m16.t[:, hc, 0:8], sc.t[:, hc, :]), reads=[sc], writes=[m16])
                                S.op("dve", lambda e: e.match_replace(wk.t[:, :128], m16.t[:, hc, 0:8], sc.t[:, hc, :], -1e30),
                                     reads=[sc, m16], writes=[wk])
                                S.op("dve", lambda e: e.max(m16.t[:, hc, 8:16], wk.t[:, :128]), reads=[wk], writes=[m16])
                            S.op("dve", lambda e: e.tensor_scalar(nm.t[:, :], m16.t[:, :, 0], -1.0, None, ALU.mult), reads=[m16], writes=[nm])
                            for hc in range(16):
                                S.op("act", lambda e: e.activation(et[tt].t[:, hc, :], sc.t[:, hc, :], AF.Exp, bias=nm.t[:, hc:hc + 1], scale=1.0),
                                     reads=[sc, nm], writes=[et[tt]])
                                S.op("act", lambda e: e.activation(e16.t[:, hc, :], m16.t[:, hc, :], AF.Exp, bias=nm.t[:, hc:hc + 1], scale=1.0),
                                     reads=[m16, nm], writes=[e16])
                            for h in range(8):
                                c3 = cand.t[:, :].rearrange("p (a b) -> p a b", b=16)
                                S.op("dve", lambda e: e.tensor_tensor(c3, e16.t[:, 2 * h, :].unsqueeze(2).to_broadcast([128, 16, 16]),
                                                                      e16.t[:, 2 * h + 1, :].unsqueeze(1).to_broadcast([128, 16, 16]), ALU.mult),
                                     reads=[e16], writes=[cand])
                                S.op("dve", lambda e: e.max(c16.t[:, 0:8], cand.t[:, :]), reads=[cand], writes=[c16])
                                S.op("dve", lambda e: e.match_replace(wk.t[:, :], c16.t[:, 0:8], cand.t[:, :], -1.0), reads=[cand, c16], writes=[wk])
                                S.op("dve", lambda e: e.max(c16.t[:, 8:16], wk.t[:, :]), reads=[wk], writes=[c16])
                                S.op("dve", lambda e: e.tensor_scalar(thr.t[:, tt, h:h + 1], c16.t[:, 15:16], 1.0 - 1e-5, None, ALU.mult),
                                     reads=[c16], writes=[thr])
                                S.op("dve", lambda e: e.tensor_reduce(rz.t[:, tt, h:h + 1], c16.t[:, :], AX.X, ALU.add), reads=[c16], writes=[rz])
                            S.op("dve", lambda e: e.reciprocal(rz.t[:, tt, :], rz.t[:, tt, :]), reads=[rz], writes=[rz])
                            for h in range(8):
                                S.op("dve", lambda e: e.tensor_scalar(e16.t[:, 2 * h, :], e16.t[:, 2 * h, :], rz.t[:, tt, h:h + 1], None, ALU.mult),
                                     reads=[e16, rz], writes=[e16])
                                S.op("dve", lambda e: e.tensor_scalar(et[tt].t[:, 2 * h, :], et[tt].t[:, 2 * h, :], rz.t[:, tt, h:h + 1], None, ALU.mult),
                                     reads=[et[tt], rz], writes=[et[tt]])
                                c3 = cand.t[:, :].rearrange("p (a b) -> p a b", b=16)
                                S.op("dve", lambda e: e.tensor_tensor(c3, e16.t[:, 2 * h, :].unsqueeze(2).to_broadcast([128, 16, 16]),
                                                                      e16.t[:, 2 * h + 1, :].unsqueeze(1).to_broadcast([128, 16, 16]), ALU.mult),
                                     reads=[e16], writes=[cand])
                                S.op("dve", lambda e: e.max(c16.t[:, 0:8], cand.t[:, :]), reads=[cand], writes=[c16])
                                S.op("dve", lambda e: e.match_replace(wk.t[:, :], c16.t[:, 0:8], cand.t[:, :], -1.0), reads=[cand, c16], writes=[wk])
                                S.op("dve", lambda e: e.max(c16.t[:, 8:16], wk.t[:, :]), reads=[wk], writes=[c16])
                                S.op("dve", lambda e: e.tensor_scalar(thr.t[:, tt, h:h + 1], c16.t[:, 15:16], 1.0 - 1e-5, None, ALU.mult),
                                     reads=[c16], writes=[thr])
                    _barrier(S)
                    with ExitStack() as s3:
                        wbufs = [sb(s3, f"aw{i}", [128, KT, 512], BF16) for i in range(2)]
                        gT = [sb(s3, f"gT{i}", [128, 4, NP], F32) for i in range(2)]
                        Mb = [sb(s3, f"Mb{i}", [128, 512], F32) for i in range(3)]
                        Mh = [sb(s3, f"Mh{i}", [128, 512], BF16) for i in range(3)]
                        wst = [sb(s3, f"wst{i}", [128, 4, NP], BF16) for i in range(2)]
                        k = {"sb": 0, "m": 0}

                        def epi_a(col, m, n0, n1, pss):
                            j = (col // 128) % 4
                            g_ = gT[k["sb"] % 2]
                            S.op("act", lambda e: e.activation(g_.t[:, j, :], pss[0].t[:, :NP], AF.Gelu), reads=[pss[0]], writes=[g_])
                            if j != 3:
                                return
                            col0 = col - 3 * 128
                            i1a = col0 // 128
                            WT = [ps[4 + b] for b in range(4)]
                            pairs = [(tt, h) for tt in range(NTT) for h in range(8)]
                            held = []
                            for i in range(len(pairs) + 1):
                                if i < len(pairs):
                                    tt, h = pairs[i]
                                    M_ = Mb[k["m"] % 3]; Mh_ = Mh[k["m"] % 3]; k["m"] += 1
                                    P3 = M_.t[:, :].rearrange("p (a b) -> p a b", b=128)
                                    S.op("dve", lambda e: e.tensor_tensor(P3, et[tt].t[:, 2 * h, i1a:i1a + 4].unsqueeze(2).to_broadcast([128, 4, 128]),
                                                                          et[tt].t[:, 2 * h + 1, :].unsqueeze(1).to_broadcast([128, 4, 128]), ALU.mult),
                                         reads=[et[tt]], writes=[M_])
                                    held.append((tt, h, M_, Mh_))
                                if i >= 1:
                                    tt, h, M_, Mh_ = held.pop(0)
                                    S.op("dve", lambda e: e.scalar_tensor_tensor(Mh_.t[:, :], M_.t[:, :], thr.t[:, tt, h:h + 1], M_.t[:, :],
                                                                                 ALU.is_ge, ALU.mult), reads=[M_, thr], writes=[Mh_])
                                    for b in range(4):
                                        S.op("pe", lambda e: e.matmul(WT[b].t[:, tt * 128:(tt + 1) * 128], Mh_.t[:, b * 128:(b + 1) * 128],
                                                                      id_bf.t[:, :], start=(h == 0), stop=(h == 7)),
                                             reads=[Mh_, id_bf], writes=[WT[b]])
                            ws = wst[k["sb"] % 2]
                            for b in range(4):
                                S.op("dve", lambda e: e.tensor_tensor(ws.t[:, b, :], WT[b].t[:, :NP], g_.t[:, b, :], ALU.mult),
                                     reads=[WT[b], g_], writes=[ws])
                                S.dma("sp", WGTd.t[col0 + b * 128:col0 + (b + 1) * 128, p0:p0 + NP], ws.t[:, b, :], reads=[ws], writes=[WGTd], part=True)
                            k["sb"] += 1
                        gemm("fm", wbufs, euT, D, 0, c.PN, lambda kt, n0, n1: pan.t[:, kt, n0:n1], [pan], NP, epi_a, banks=(0, 1, 2, 3))

            _barrier(S)
            with ExitStack() as st:
                NY = min(1024, NO)
                NYT = NY // 128
                EC = 16
                wv = [sb(st, f"yv{i}", [128, EC, 512], BF16) for i in range(2)]
                wa = [sb(st, f"ya{i}", [128, EC, NY], BF16) for i in range(2)]
                h1t = [sb(st, f"yh{i}", [128, 512], F32) for i in range(2)]
                yo = [sb(st, f"yo{i}", [128, 512], F32) for i in range(2)]
                NEC = c.PN // (128 * EC)
                gi = 0
                ci = 0
                oi = 0
                for cb in range(0, D, 512):
                    for p0 in range(0, NO, NY):
                        bks = [ps[i] for i in range(NYT)]
                        gi += 1
                        for ec in range(NEC):
                            e0 = ec * EC * 128
                            v_, a_ = wv[ci % 2], wa[ci % 2]
                            ci += 1
                            for k0 in range(0, EC, 8):
                                S.dma("pool", v_.t[:, k0:k0 + 8, :], ev[e0 + k0 * 128:e0 + (k0 + 8) * 128, cb:cb + 512].rearrange("(k p) c -> p k c", p=128),
                                      writes=[v_], part=True)
                                S.dma("sp", a_.t[:, k0:k0 + 8, :], WGTd.t[e0 + k0 * 128:e0 + (k0 + 8) * 128, p0:p0 + NY].rearrange("(k p) c -> p k c", p=128),
                                      reads=[WGTd], writes=[a_], part=True)
                            for tt in range(NYT):
                                for kt in range(EC):
                                    S.op("pe", lambda e: e.matmul(bks[tt].t[:, :512], a_.t[:, kt, tt * 128:(tt + 1) * 128], v_.t[:, kt, :],
                                                                  start=(ec == 0 and kt == 0), stop=(ec == NEC - 1 and kt == EC - 1)),
                                         reads=[a_, v_], writes=[bks[tt]])
                        for tt in range(NYT):
                            h_, o_ = h1t[oi % 2], yo[oi % 2]
                            oi += 1
                            r0 = p0 + tt * 128
                            S.dma("sp", h_.t[:], H1d.t[r0:r0 + 128, cb:cb + 512], reads=[H1d], writes=[h_])
                            S.op("dve", lambda e: e.tensor_tensor(o_.t[:], h_.t[:], bks[tt].t[:, :512], ALU.add), reads=[h_, bks[tt]], writes=[o_])
                            S.dma("sp", out_own[r0:r0 + 128, cb:cb + 512], o_.t[:], reads=[o_], writes=[OUTb], part=True)

        _barrier(S)
        peer_phase()

        for i in range(NDSEM):
            if S.dval[i]:
                nc.sync.wait_ge(S.dsem[i], S.dval[i])
        print("instructions:", S.ninst, {k: v for k, v in S.cnt.items()})
    return nc


def _prep(c, inp, core):
    f32 = np.float32
    b, hh = core // 2, core % 2
    D, SEQ, NO, H, G, NJ, NCT = c.D, c.SEQ, c.NO, c.H, c.G, c.NJ, c.NCT
    A = lambda v: np.ascontiguousarray(np.asarray(v), dtype=f32)
    x = np.asarray(inp["x"][b], dtype=f32)
    m = {}
    m["x_ctx"] = np.concatenate([np.asarray(inp["meta_tokens"], dtype=f32), x], 0)
    m["x_own"] = A(x.reshape(SEQ // 128, 128, D)[hh::2].reshape(NO, D))
    m["g1rep"] = A(np.broadcast_to(np.asarray(inp["norm1_g"][0])[None, :], (128, D)))
    m["g2rep"] = A(np.broadcast_to(np.asarray(inp["norm2_g"][0])[None, :], (128, D)))
    m["w_in"] = A(inp["w_in"][0])
    m["bfg"] = A(np.asarray(inp["b_forget"][0]).reshape(H, 1))
    m["qg"] = A(np.asarray(inp["q_norm_g"][0]).reshape(128, 1))
    m["kg"] = A(np.asarray(inp["k_norm_g"][0]).reshape(128, 1))
    lr = np.asarray(inp["lam_re"][0], dtype=f32); li = np.asarray(inp["lam_im"][0], dtype=f32)
    ld = np.asarray(inp["log_dt"][0], dtype=f32)
    m["lrA"] = A(lr.reshape(NJ, 128).T); m["liA"] = A(li.reshape(NJ, 128).T)
    m["ldA"] = A(np.repeat(ld.reshape(NJ, 2), 64, axis=1).T)
    m["lrB"] = A(np.broadcast_to(lr.reshape(1, -1), (128, G * 64)))
    m["liB"] = A(np.broadcast_to(li.reshape(1, -1), (128, G * 64)))
    m["ldB"] = A(np.broadcast_to(np.repeat(ld, 64)[None, :], (128, G * 64)))
    bre = np.asarray(inp["b_re"][0], dtype=f32); bim = np.asarray(inp["b_im"][0], dtype=f32)
    cre = np.asarray(inp["c_re"][0], dtype=f32); cim = np.asarray(inp["c_im"][0], dtype=f32)
    brB = np.zeros((128, G * 64), f32); biB = np.zeros((128, G * 64), f32)
    crB = np.zeros((128, NJ * 128), f32); ciB = np.zeros((128, NJ * 128), f32)
    for g in range(G):
        r0 = (g % 8) * 16
        brB[r0:r0 + 16, g * 64:(g + 1) * 64] = bre[g].T
        biB[r0:r0 + 16, g * 64:(g + 1) * 64] = bim[g].T
        j, g2 = g // 2, g % 2
        crB[g2 * 64:(g2 + 1) * 64, j * 128 + r0:j * 128 + r0 + 16] = cre[g].T
        ciB[g2 * 64:(g2 + 1) * 64, j * 128 + r0:j * 128 + r0 + 16] = cim[g].T
    m["brB"], m["biB"], m["crB"], m["ciB"] = brB, biB, crB, ciB
    m["dsk"] = A(np.asarray(inp["d_skip"][0]).reshape(NCT, 128).T)
    m["w_glu"] = A(inp["w_glu"][0]); m["w_a"] = A(inp["w_branch_attn"][0]); m["w_b"] = A(inp["w_branch_ssm"][0])
    m["w_out"] = A(inp["w_out"][0]); m["w_query"] = A(inp["w_query"][0])
    m["skT"] = A(np.asarray(inp["sub_keys"][0]).transpose(2, 0, 1).reshape(128, 256))
    m["euT"] = A(np.asarray(inp["expert_u"][0]).T); m["ev"] = A(inp["expert_v"][0])
    tri = (np.arange(128)[None, :] >= np.arange(128)[:, None]).astype(f32)
    m["maskA"] = tri if hh == 0 else np.ones((128, 128), f32)
    m["maskB"] = np.zeros((128, 128), f32) if hh == 0 else tri
    m["w01"] = A(np.broadcast_to(np.array([[1.0, 0.0]] if hh == 0 else [[0.0, 1.0]], f32), (128, 2)))
    LSM = 256 * c.SEGB + 16
    m["t_loc"] = A(np.broadcast_to(np.arange(LSM, dtype=f32)[None, :], (128, LSM)))
    io = np.arange(NO)
    pos = 16 + 128 * (2 * (io // 128) + hh) + (io % 128)
    m["t_own"] = A(np.broadcast_to(pos.astype(f32)[None, :], (128, NO)))
    m["ident"] = np.eye(128, dtype=f32)
    return m


_NC_CACHE = {}


def run_cfg(c, inputs):
    key = (c.D, c.SEQ, c.B)
    if key not in _NC_CACHE:
        _NC_CACHE[key] = build(c)
    nc = _NC_CACHE[key]
    ncores = 2 * c.B
    shared = None
    in_maps = []
    for core in range(ncores):
        in_maps.append(_prep(c, inputs, core))
    res = run_bass_kernel_spmd(nc, in_maps, core_ids=list(range(ncores)))
    if getattr(c, "debug", False):
        c.dbg = res.results
    out = np.zeros((c.B, c.SEQ, c.D), np.float32)
    for core in range(ncores):
        b, hh = core // 2, core % 2
        o = np.asarray(res.results[core]["out_own"], dtype=np.float32).reshape(c.NB, 128, c.D)
        out[b].reshape(c.SEQ // 128, 128, c.D)[hh::2] = o
    return out


def kernel(**inputs):
    return run_cfg(Cfg(), inputs)
```

```python
import math
from contextlib import ExitStack
import numpy as np
import ml_dtypes
import concourse.bass as bass
import concourse.mybir as mybir
from concourse.bass_utils import run_bass_kernel_spmd

F32 = mybir.dt.float32
BF16 = mybir.dt.bfloat16
I32 = mybir.dt.int32
AF = mybir.ActivationFunctionType
ALU = mybir.AluOpType
AX = mybir.AxisListType

EPOCH = 16000
NDSEM = 40
TWO_PI = 2.0 * math.pi
STAB = 30.0


class Buf:
    __slots__ = ("w", "r")

    def __init__(self):
        self.w = {}
        self.r = {}


class TT:
    def __init__(self, t):
        self.t = t
        self.b = Buf()


def _b(x):
    return x.b if isinstance(x, TT) else x


class Sched:
    def __init__(self, nc, stack):
        self.nc = nc
        self.engs = {"pe": nc.tensor, "dve": nc.vector, "act": nc.scalar,
                     "pool": nc.gpsimd, "sp": nc.sync}
        self.stack = stack
        self.esem = {}
        self.cnt = {e: 0 for e in self.engs}
        self.seen = {e: {} for e in self.engs}
        self.dsem = [stack.enter_context(nc.semaphore(f"d{i}")) for i in range(NDSEM)]
        self.dval = [0] * NDSEM
        self.dnext = 0
        self.ninst = 0

    def _esem(self, eng, epoch):
        k = (eng, epoch)
        if k not in self.esem:
            self.esem[k] = self.stack.enter_context(self.nc.semaphore(f"e_{eng}_{epoch}"))
        return self.esem[k]

    def _sem_of(self, key):
        if key[0] == "d":
            return self.dsem[key[1]]
        return self._esem(key[0], key[1])

    def _wait(self, eng, deps):
        s = self.seen[eng]
        for k, v in deps.items():
            if eng == "pe" and k[0] == "pe":
                continue
            if s.get(k, 0) < v:
                self.engs[eng].wait_ge(self._sem_of(k), v)
                s[k] = v

    @staticmethod
    def _acc(deps, d):
        for k, v in d.items():
            if deps.get(k, 0) < v:
                deps[k] = v

    def _commit(self, tok, reads, writes, part=False):
        k, v = tok
        for b in writes:
            if not part:
                b.w = {}
            b.w[k] = max(b.w.get(k, 0), v)
            b.r = {}
        for b in reads:
            if b.r.get(k, 0) < v:
                b.r[k] = v

    def op(self, eng, fn, reads=(), writes=()):
        reads = [_b(x) for x in reads]
        writes = [_b(x) for x in writes]
        deps = {}
        for b in reads:
            self._acc(deps, b.w)
        for b in writes:
            self._acc(deps, b.w)
            self._acc(deps, b.r)
        self._wait(eng, deps)
        ins = fn(self.engs[eng])
        c = self.cnt[eng]
        epoch, val = divmod(c, EPOCH)
        ins.then_inc(self._esem(eng, epoch), 1)
        self.cnt[eng] = c + 1
        self._commit(((eng, epoch), val + 1), reads, writes)
        self.ninst += 1
        return ins

    def dma(self, q, out, in_, reads=(), writes=(), part=False):
        reads = [_b(x) for x in reads]
        writes = [_b(x) for x in writes]
        deps = {}
        for b in reads:
            self._acc(deps, b.w)
        for b in writes:
            if not part:
                self._acc(deps, b.w)
            self._acc(deps, b.r)
        i = self.dnext
        self.dnext = (i + 1) % NDSEM
        if self.dval[i]:
            deps[("d", i)] = max(deps.get(("d", i), 0), self.dval[i])
        self._wait(q, deps)
        ins = self.engs[q].dma_start(out=out, in_=in_)
        ins.then_inc(self.dsem[i], 16)
        self.dval[i] += 16
        assert self.dval[i] < 60000
        self._commit((("d", i), self.dval[i]), reads, writes, part=part)
        self.ninst += 1
        return ins


def _barrier(S):
    deps = {}
    for e, cnt in S.cnt.items():
        if cnt:
            epoch, val = divmod(cnt - 1, EPOCH)
            deps[(e, epoch)] = val + 1
    for i in range(NDSEM):
        if S.dval[i]:
            deps[("d", i)] = S.dval[i]
    for e in S.engs:
        s = S.seen[e]
        for k, v in deps.items():
            if k[0] == e:
                continue
            if s.get(k, 0) < v:
                S.engs[e].wait_ge(S._sem_of(k), v)
                s[k] = v


class Cfg:
    def __init__(s, D=4096, SEQ=4096, B=4, NPAN=1024, PPAN=512, SEGB=2, WB6=256):
        s.D, s.SEQ, s.B = D, SEQ, B
        s.NM = 16
        s.H = D // 256
        s.AW = s.H * 128
        s.G = D // 32
        s.SW = s.G * 16
        s.NJ = s.G // 2
        s.NCT = s.SW // 128
        s.PH, s.PK, s.TOPK = 8, 128, 16
        s.PN = s.PK * s.PK
        s.QW = s.PH * 256
        s.L = SEQ + s.NM
        s.NO = SEQ // 2
        s.NB = s.NO // 128
        s.KT = D // 128
        s.NPAN = min(NPAN, s.NO)
        s.PPAN = min(PPAN, s.NO)
        s.SEGB = min(SEGB, s.NB)
        s.WB6 = WB6
        s.COL_Q = 0
        s.COL_K = s.AW
        s.COL_V = 2 * s.AW
        s.COL_F = 3 * s.AW
        s.COL_U = s.COL_F + s.H
        s.COL_GA = s.COL_U + s.SW
        s.COL_GB = s.COL_GA + D
        s.NCOLS = s.COL_GB + D


def build(cfg):
    c = cfg
    D, L, NO, KT, H, NB = c.D, c.L, c.NO, c.KT, c.H, c.NB
    nc = bass.Bass("TRN2", target_bir_lowering=False)

    def din(name, shape, dt=F32):
        return nc.dram_tensor(name, list(shape), dt, kind="ExternalInput").ap()

    def dscr(name, shape, dt):
        return TT(nc.dram_tensor(name, list(shape), dt, kind="ExternalOutput" if getattr(c, "debug", False) else "Internal").ap())

    x_ctx = din("x_ctx", [L, D]); x_own = din("x_own", [NO, D])
    g1rep = din("g1rep", [128, D]); g2rep = din("g2rep", [128, D])
    w_in = din("w_in", [D, c.NCOLS])
    bfg = din("bfg", [H, 1]); qg = din("qg", [128, 1]); kg = din("kg", [128, 1])
    lrA = din("lrA", [128, c.NJ]); liA = din("liA", [128, c.NJ]); ldA = din("ldA", [128, c.NJ])
    lrB = din("lrB", [128, c.NJ * 128]); liB = din("liB", [128, c.NJ * 128]); ldB = din("ldB", [128, c.NJ * 128])
    brB = din("brB", [128, c.NJ * 128]); biB = din("biB", [128, c.NJ * 128])
    crB = din("crB", [128, c.NJ * 128]); ciB = din("ciB", [128, c.NJ * 128])
    dsk = din("dsk", [128, c.NCT])
    w_glu = din("w_glu", [c.SW, c.SW]); w_a = din("w_a", [c.AW, D]); w_b = din("w_b", [c.SW, D])
    w_out = din("w_out", [D, D]); w_query = din("w_query", [D, c.QW])
    skT = din("skT", [128, 2 * 128])
    euT = din("euT", [D, c.PN]); ev = din("ev", [c.PN, D])
    maskA = din("maskA", [128, 128]); maskB = din("maskB", [128, 128])
    w01 = din("w01", [128, 2])
    t_loc = din("t_loc", [128, 256 * c.SEGB + 16]); t_own = din("t_own", [128, NO])
    ident = din("ident", [128, 128])
    out_own = nc.dram_tensor("out_own", [NO, D], F32, kind="ExternalOutput").ap()

    KTd = dscr("KTd", [H, 128, L], BF16)
    Vd = dscr("Vd", [L, c.AW], BF16)
    UTd = dscr("UTd", [c.SW, L], BF16)
    QTd = dscr("QTd", [H, 128, NO], BF16)
    GAd = dscr("GAd", [D, NO], F32)
    GBd = dscr("GBd", [D, NO], F32)
    UOd = dscr("UOd", [c.SW, NO], F32)
    CQd = dscr("CQd", [H, NO], F32)
    CQ3d = dscr("CQ3d", [3, H, NO], BF16)
    BBRd = dscr("BBRd", [128, c.NJ * 128], BF16)
    BBId = dscr("BBId", [128, c.NJ * 128], BF16)
    ZFd = dscr("ZFd", [c.SW, NO], F32)
    ZTd = dscr("ZTd", [c.SW, NO], BF16)
    SSTd = dscr("SSTd", [c.SW, NO], BF16)
    ATd = dscr("ATd", [c.AW, NO], BF16)
    H1d = dscr("H1d", [NO, D], F32)
    WGTd = dscr("WGTd", [c.PN, NO], BF16)
    OUTb = Buf()

    with ExitStack() as gst:
        S = Sched(nc, gst)

        uid = {"n": 0}

        def sb(st, name, shape, dt):
            uid["n"] += 1
            return TT(st.enter_context(nc.sbuf_tensor(f"{name}_{uid['n']}", list(shape), dt)))

        ps = [TT(gst.enter_context(nc.psum_tensor(f"ps{i}", [128, 512], F32))) for i in range(8)]
        psb = []
        for i in range(2):
            v = TT(ps[6 + i].t.bitcast(BF16))
            v.b = ps[6 + i].b
            psb.append(v)
        id_bf = sb(gst, "id_bf", [128, 128], BF16)
        id_f = sb(gst, "id_f", [128, 128], F32)
        ones_bf = sb(gst, "ones_bf", [128, 128], BF16)
        ones_f = sb(gst, "ones_f", [1, 128], F32)
        cst = sb(gst, "cst", [128, 8], F32)
        w01t = sb(gst, "w01t", [128, 2], F32)
        qgt = sb(gst, "qgt", [128, 1], F32); kgt = sb(gst, "kgt", [128, 1], F32)
        cacc = sb(gst, "cacc", [16, L], F32)
        rhoA = sb(gst, "rhoA", [128, c.NJ], F32)
        phiA = sb(gst, "phiA", [128, c.NJ], F32)
        S.dma("pool", id_bf.t[:], ident, writes=[id_bf])
        S.dma("sp", id_f.t[:], ident, writes=[id_f])
        S.dma("sp", w01t.t[:], w01, writes=[w01t])
        S.dma("sp", qgt.t[:], qg, writes=[qgt])
        S.dma("sp", kgt.t[:], kg, writes=[kgt])
        S.op("dve", lambda e: e.memset(ones_bf.t[:], 1.0), writes=[ones_bf])
        S.op("dve", lambda e: e.memset(ones_f.t[:], 1.0), writes=[ones_f])
        S.op("dve", lambda e: e.memset(cst.t[:, 0:1], 1e-6), writes=[cst])
        S.op("dve", lambda e: e.memset(cst.t[:, 1:2], 1.0), writes=[cst])
        S.op("dve", lambda e: e.memset(cst.t[:, 2:3], 0.0), writes=[cst])
        EPS = cst.t[:, 0:1]
        ONE = cst.t[:, 1:2]

        rr = {"e": 0}

        def alt():
            rr["e"] ^= 1
            return "act" if rr["e"] else "dve"

        def copy(eng, out, in_, reads, writes):
            if eng == "act":
                S.op("act", lambda e: e.activation(out, in_, AF.Copy), reads=reads, writes=writes)
            else:
                S.op(eng, lambda e: e.tensor_copy(out, in_), reads=reads, writes=writes)

        def make_nt(st, tagp):
          xt = sb(st, tagp + "xt", [128, D], F32)
          xn = sb(st, tagp + "xn", [128, D], BF16)
          ss = sb(st, tagp + "ss", [128, 2], F32)

          def norm_transpose(src, n_rows, grep, panel, srcbuf=None):
            for t0 in range(0, n_rows, 128):
                r = min(128, n_rows - t0)
                S.dma("sp", xt.t[:r, :], src[t0:t0 + r, :], reads=[srcbuf] if srcbuf else [], writes=[xt])
                S.op("dve", lambda e: e.memset(ss.t[:r, 0:1], 0.0), writes=[ss])
                S.op("act", lambda e: e.activation(xn.t[:r, :], xt.t[:r, :], AF.Square, accum_out=ss.t[:r, 0:1]),
                     reads=[xt], writes=[xn, ss])
                S.op("dve", lambda e: e.tensor_scalar(ss.t[:r, 1:2], ss.t[:r, 0:1], 1.0 / D, 1e-6, ALU.mult, ALU.add),
                     reads=[ss], writes=[ss])
                S.op("act", lambda e: e.activation(ss.t[:r, 1:2], ss.t[:r, 1:2], AF.Sqrt), reads=[ss], writes=[ss])
                S.op("dve", lambda e: e.reciprocal(ss.t[:r, 1:2], ss.t[:r, 1:2]), reads=[ss], writes=[ss])
                S.op("dve", lambda e: e.scalar_tensor_tensor(xn.t[:r, :], xt.t[:r, :], ss.t[:r, 1:2], grep.t[:r, :],
                                                             ALU.mult, ALU.mult),
                     reads=[xt, ss, grep], writes=[xn])
                for k0 in range(0, KT, 8):
                    k1 = min(KT, k0 + 8)
                    pb = psb[(k0 // 8) % 2]
                    for kt in range(k0, k1):
                        S.op("pe", lambda e: e.transpose(pb.t[:, (kt - k0) * 128:(kt - k0) * 128 + r],
                                                         xn.t[:r, kt * 128:(kt + 1) * 128], id_bf.t[:r, :r]),
                             reads=[xn, id_bf], writes=[pb])
                    src_ap = pb.t[:, 0:(k1 - k0) * 128].rearrange("p (k r) -> p k r", r=128)[:, :, :r]
                    copy(alt(), panel.t[:, k0:k1, t0:t0 + r], src_ap, [pb], [panel])
          return norm_transpose

        wstate = {"i": 0}

        def gemm(mode, wbufs, wsrc, K, c0, ncols, act, actbufs, N, epi, kgroups=None, WB=None, banks=(0, 1, 2, 3),
                 wq="pool", wsrcbuf=None):
            KTl = K // 128
            WB = WB or wbufs[0].t.shape[2]
            kgroups = kgroups or [(0, KTl)]
            bi = 0
            for sb0 in range(0, ncols, WB):
                cw = min(WB, ncols - sb0)
                wt = wbufs[wstate["i"] % len(wbufs)]
                wstate["i"] += 1
                srcs = wsrc if isinstance(wsrc, list) else [(wsrc, K)]
                kbase = 0
                for (wap, Ks) in srcs:
                    for k0 in range(0, Ks // 128, 8):
                        k1 = min(Ks // 128, k0 + 8)
                        srcap = wap[k0 * 128:k1 * 128, c0 + sb0:c0 + sb0 + cw].rearrange("(kt p) c -> p kt c", p=128)
                        S.dma(wq, wt.t[:, kbase + k0:kbase + k1, :cw], srcap, reads=[wsrcbuf] if wsrcbuf else [],
                              writes=[wt], part=True)
                    kbase += Ks // 128
                if mode == "fm":
                    for j0 in range(0, cw, 128):
                        m = min(128, cw - j0)
                        for n0 in range(0, N, 512):
                            n1 = min(N, n0 + 512)
                            pss = []
                            for (ka, kb) in kgroups:
                                pb = ps[banks[bi % len(banks)]]
                                bi += 1
                                for kt in range(ka, kb):
                                    S.op("pe", lambda e: e.matmul(pb.t[:m, :n1 - n0], wt.t[:, kt, j0:j0 + m],
                                                                  act(kt, n0, n1), start=(kt == ka), stop=(kt == kb - 1)),
                                         reads=[wt] + actbufs, writes=[pb])
                                pss.append(pb)
                            epi(c0 + sb0 + j0, m, n0, n1, pss)
                else:
                    for t0 in range(0, N, 128):
                        t1 = min(N, t0 + 128)
                        pss = []
                        for (ka, kb) in kgroups:
                            pb = ps[banks[bi % len(banks)]]
                            bi += 1
                            for kt in range(ka, kb):
                                S.op("pe", lambda e: e.matmul(pb.t[:t1 - t0, :cw], act(kt, t0, t1), wt.t[:, kt, :cw],
                                                              start=(kt == ka), stop=(kt == kb - 1)),
                                     reads=[wt] + actbufs, writes=[pb])
                            pss.append(pb)
                        epi(t0, t1, c0 + sb0, cw, pss)

        def coeffs(st, tag, lr_ap, li_ap, ld_ap, F):
            lr = sb(st, tag + "lr", [128, F], F32); li = sb(st, tag + "li", [128, F], F32)
            ld = sb(st, tag + "ld", [128, F], F32)
            t1 = sb(st, tag + "t1", [128, F], F32); t2 = sb(st, tag + "t2", [128, F], F32)
            ti = sb(st, tag + "ti", [128, F], I32)
            rho = sb(st, tag + "rho", [128, F], F32); phi = sb(st, tag + "phi", [128, F], F32)
            sn = sb(st, tag + "sn", [128, F], F32); cs = sb(st, tag + "cs", [128, F], F32)
            fr = sb(st, tag + "fr", [128, F], F32); fi = sb(st, tag + "fi", [128, F], F32)

            def run(lr_src, li_src, ld_src):
                S.dma("sp", lr.t[:], lr_src, writes=[lr]); S.dma("sp", li.t[:], li_src, writes=[li])
                S.dma("sp", ld.t[:], ld_src, writes=[ld])
                S.op("act", lambda e: e.activation(ld.t[:], ld.t[:], AF.Exp), reads=[ld], writes=[ld])
                S.op("dve", lambda e: e.tensor_tensor(t1.t[:], lr.t[:], ld.t[:], ALU.mult), reads=[lr, ld], writes=[t1])
                S.op("act", lambda e: e.activation(rho.t[:], t1.t[:], AF.Exp), reads=[t1], writes=[rho])
                S.op("dve", lambda e: e.scalar_tensor_tensor(t1.t[:], li.t[:], 1.0 / TWO_PI, ld.t[:], ALU.mult, ALU.mult),
                     reads=[li, ld], writes=[t1])
                S.op("dve", lambda e: e.tensor_copy(ti.t[:], t1.t[:]), reads=[t1], writes=[ti])
                S.op("dve", lambda e: e.tensor_copy(t2.t[:], ti.t[:]), reads=[ti], writes=[t2])
                S.op("dve", lambda e: e.tensor_sub(phi.t[:], t1.t[:], t2.t[:]), reads=[t1, t2], writes=[phi])
                S.op("act", lambda e: e.activation(sn.t[:], phi.t[:], AF.Sin, scale=TWO_PI), reads=[phi], writes=[sn])
                S.op("dve", lambda e: e.tensor_scalar(t1.t[:], phi.t[:], 0.25, None, ALU.add), reads=[phi], writes=[t1])
                S.op("dve", lambda e: e.tensor_copy(ti.t[:], t1.t[:]), reads=[t1], writes=[ti])
                S.op("dve", lambda e: e.tensor_copy(t2.t[:], ti.t[:]), reads=[ti], writes=[t2])
                S.op("dve", lambda e: e.tensor_sub(t1.t[:], t1.t[:], t2.t[:]), reads=[t1, t2], writes=[t1])
                S.op("act", lambda e: e.activation(cs.t[:], t1.t[:], AF.Sin, scale=TWO_PI), reads=[t1], writes=[cs])
                S.op("dve", lambda e: e.tensor_tensor(cs.t[:], cs.t[:], rho.t[:], ALU.mult), reads=[cs, rho], writes=[cs])
                S.op("dve", lambda e: e.tensor_tensor(sn.t[:], sn.t[:], rho.t[:], ALU.mult), reads=[sn, rho], writes=[sn])
                S.op("dve", lambda e: e.tensor_scalar(t1.t[:], cs.t[:], -1.0, None, ALU.add), reads=[cs], writes=[t1])
                S.op("dve", lambda e: e.tensor_tensor(t2.t[:], lr.t[:], lr.t[:], ALU.mult), reads=[lr], writes=[t2])
                S.op("dve", lambda e: e.tensor_tensor(fr.t[:], li.t[:], li.t[:], ALU.mult), reads=[li], writes=[fr])
                S.op("dve", lambda e: e.tensor_add(t2.t[:], t2.t[:], fr.t[:]), reads=[t2, fr], writes=[t2])
                S.op("dve", lambda e: e.reciprocal(t2.t[:], t2.t[:]), reads=[t2], writes=[t2])
                S.op("dve", lambda e: e.tensor_tensor(fr.t[:], t1.t[:], lr.t[:], ALU.mult), reads=[t1, lr], writes=[fr])
                S.op("dve", lambda e: e.tensor_tensor(fi.t[:], sn.t[:], li.t[:], ALU.mult), reads=[sn, li], writes=[fi])
                S.op("dve", lambda e: e.tensor_add(fr.t[:], fr.t[:], fi.t[:]), reads=[fr, fi], writes=[fr])
                S.op("dve", lambda e: e.tensor_tensor(fr.t[:], fr.t[:], t2.t[:], ALU.mult), reads=[fr, t2], writes=[fr])
                S.op("dve", lambda e: e.tensor_tensor(fi.t[:], sn.t[:], lr.t[:], ALU.mult), reads=[sn, lr], writes=[fi])
                S.op("dve", lambda e: e.tensor_tensor(t1.t[:], t1.t[:], li.t[:], ALU.mult), reads=[t1, li], writes=[t1])
                S.op("dve", lambda e: e.tensor_sub(fi.t[:], fi.t[:], t1.t[:]), reads=[fi, t1], writes=[fi])
                S.op("dve", lambda e: e.tensor_tensor(fi.t[:], fi.t[:], t2.t[:], ALU.mult), reads=[fi, t2], writes=[fi])
            return run, rho, phi, fr, fi

        with ExitStack() as st:
            run, rho, phi, fr, fi = coeffs(st, "ca", None, None, None, c.NJ)
            run(lrA, liA, ldA)
            S.op("dve", lambda e: e.tensor_copy(rhoA.t[:], rho.t[:]), reads=[rho], writes=[rhoA])
            S.op("dve", lambda e: e.tensor_copy(phiA.t[:], phi.t[:]), reads=[phi], writes=[phiA])
        _barrier(S)
        with ExitStack() as st:
            FC = min(1024, c.NJ * 128)
            run, rho, phi, fr, fi = coeffs(st, "cb", None, None, None, FC)
            br = sb(st, "br", [128, FC], F32); bi_ = sb(st, "bi", [128, FC], F32)
            o1 = sb(st, "o1", [128, FC], F32); o2 = sb(st, "o2", [128, FC], F32)
            ob1 = sb(st, "ob1", [128, FC], BF16); ob2 = sb(st, "ob2", [128, FC], BF16)
            for f0 in range(0, c.NJ * 128, FC):
                run(lrB[:, f0:f0 + FC], liB[:, f0:f0 + FC], ldB[:, f0:f0 + FC])
                S.dma("sp", br.t[:], brB[:, f0:f0 + FC], writes=[br])
                S.dma("sp", bi_.t[:], biB[:, f0:f0 + FC], writes=[bi_])
                S.op("dve", lambda e: e.tensor_tensor(o1.t[:], fr.t[:], br.t[:], ALU.mult), reads=[fr, br], writes=[o1])
                S.op("dve", lambda e: e.tensor_tensor(o2.t[:], fi.t[:], bi_.t[:], ALU.mult), reads=[fi, bi_], writes=[o2])
                S.op("dve", lambda e: e.tensor_sub(ob1.t[:], o1.t[:], o2.t[:]), reads=[o1, o2], writes=[ob1])
                S.op("dve", lambda e: e.tensor_tensor(o1.t[:], fr.t[:], bi_.t[:], ALU.mult), reads=[fr, bi_], writes=[o1])
                S.op("dve", lambda e: e.tensor_tensor(o2.t[:], fi.t[:], br.t[:], ALU.mult), reads=[fi, br], writes=[o2])
                S.op("dve", lambda e: e.tensor_add(ob2.t[:], o1.t[:], o2.t[:]), reads=[o1, o2], writes=[ob2])
                S.dma("sp", BBRd.t[:, f0:f0 + FC], ob1.t[:], reads=[ob1], writes=[BBRd], part=True)
                S.dma("sp", BBId.t[:, f0:f0 + FC], ob2.t[:], reads=[ob2], writes=[BBId], part=True)

        def proj_phase(is_ctx):
            with ExitStack() as st:
                tag = "c" if is_ctx else "o"
                NPmax = c.NPAN + (c.NM if is_ctx else 0)
                panel = sb(st, tag + "pan", [128, KT, NPmax], BF16)
                grep = sb(st, tag + "g1", [128, D], F32)
                S.dma("sp", grep.t[:], g1rep, writes=[grep])
                wbufs = [sb(st, tag + f"w{i}", [128, KT, 256], BF16) for i in range(2)]
                stg = [sb(st, tag + f"stg{i}", [128, 512], BF16) for i in range(3)]
                stf = [sb(st, tag + f"stf{i}", [128, 512], F32) for i in range(3)]
                sq = sb(st, tag + "sq", [128, 512], BF16)
                rinv = sb(st, tag + "rinv", [128, 512], F32)
                bft = sb(st, tag + "bft", [16, 1], F32)
                lf = sb(st, tag + "lf", [16, 512], F32)
                ones_t = sb(st, tag + "ones", [16, 512], F32)
                S.dma("sp", bft.t[:H, :], bfg, writes=[bft])
                S.op("dve", lambda e: e.tensor_scalar(bft.t[:H, :], bft.t[:H, :], -1.0, None, ALU.mult), reads=[bft], writes=[bft])
                S.op("dve", lambda e: e.memset(ones_t.t[:], 1.0), writes=[ones_t])
                si = {"g": 0, "f": 0}
                src = x_ctx if is_ctx else x_own
                ntot = L if is_ctx else NO
                p0 = 0
                nt = make_nt(st, tag)
                while p0 < ntot:
                    pn = min(c.NPAN + (c.NM if (is_ctx and p0 == 0) else 0), ntot - p0)
                    nt(src[p0:p0 + pn, :], pn, grep, panel)

                    def act(kt, n0, n1):
                        return panel.t[:, kt, n0:n1]

                    def epi_qk(dst, gcol, colbase):
                        def epi(col, m, n0, n1, pss):
                            n = n1 - n0
                            h = (col - colbase) // 128
                            pb = pss[0]
                            S.op("act", lambda e: e.activation(sq.t[:, :n], pb.t[:, :n], AF.Square), reads=[pb], writes=[sq])
                            p2 = ps[4]
                            S.op("pe", lambda e: e.matmul(p2.t[:, :n], ones_bf.t[:], sq.t[:, :n], start=True, stop=True),
                                 reads=[ones_bf, sq], writes=[p2])
                            S.op("act", lambda e: e.activation(rinv.t[:, :n], p2.t[:, :n], AF.Sqrt, bias=EPS, scale=1.0 / 128),
                                 reads=[p2, cst], writes=[rinv])
                            S.op("dve", lambda e: e.reciprocal(rinv.t[:, :n], rinv.t[:, :n]), reads=[rinv], writes=[rinv])
                            sg = stg[si["g"] % 3]; si["g"] += 1
                            S.op("dve", lambda e: e.scalar_tensor_tensor(sg.t[:, :n], pb.t[:, :n], gcol.t[:, 0:1], rinv.t[:, :n],
                                                                         ALU.mult, ALU.mult),
                                 reads=[pb, gcol, rinv], writes=[sg])
                            S.dma("sp", dst.t[h, :, p0 + n0:p0 + n1], sg.t[:, :n], reads=[sg], writes=[dst], part=True)
                        return epi

                    def epi_store_bf(dst):
                        def epi(col, m, n0, n1, pss):
                            n = n1 - n0
                            sg = stg[si["g"] % 3]; si["g"] += 1
                            copy(alt(), sg.t[:m, :n], pss[0].t[:m, :n], [pss[0]], [sg])
                            S.dma("sp", dst.t[col:col + m, p0 + n0:p0 + n1], sg.t[:m, :n], reads=[sg], writes=[dst], part=True)
                        return epi

                    if is_ctx:
                        gemm("fm", wbufs, w_in, D, c.COL_K, c.AW, act, [panel], pn, epi_qk(KTd, kgt, c.COL_K))

                        def epi_v(t0, t1, col, cw, pss):
                            r = t1 - t0
                            sg = stg[si["g"] % 3]; si["g"] += 1
                            copy(alt(), sg.t[:r, :cw], pss[0].t[:r, :cw], [pss[0]], [sg])
                            S.dma("sp", Vd.t[p0 + t0:p0 + t1, col - c.COL_V:col - c.COL_V + cw], sg.t[:r, :cw],
                                  reads=[sg], writes=[Vd], part=True)
                        gemm("tm", wbufs, w_in, D, c.COL_V, c.AW, act, [panel], pn, epi_v)

                        def epi_u(col, m, n0, n1, pss):
                            n = n1 - n0
                            sg = stg[si["g"] % 3]; si["g"] += 1
                            copy(alt(), sg.t[:m, :n], pss[0].t[:m, :n], [pss[0]], [sg])
                            S.dma("sp", UTd.t[col - c.COL_U:col - c.COL_U + m, p0 + n0:p0 + n1], sg.t[:m, :n],
                                  reads=[sg], writes=[UTd], part=True)
                        gemm("fm", wbufs, w_in, D, c.COL_U, c.SW, act, [panel], pn, epi_u)

                        def epi_f(col, m, n0, n1, pss):
                            n = n1 - n0
                            pb = pss[0]
                            S.op("act", lambda e: e.activation(lf.t[:H, :n], pb.t[:H, :n], AF.Exp, bias=bft.t[:H, 0:1], scale=-1.0),
                                 reads=[pb, bft], writes=[lf])
                            S.op("act", lambda e: e.activation(lf.t[:H, :n], lf.t[:H, :n], AF.Ln, bias=ONE[:H, :], scale=1.0),
                                 reads=[lf, cst], writes=[lf])
                            S.op("dve", lambda e: e.tensor_scalar(lf.t[:H, :n], lf.t[:H, :n], -1.0, None, ALU.mult), reads=[lf], writes=[lf])
                            a0 = p0 + n0
                            init = 0.0 if a0 == 0 else cacc.t[:H, a0 - 1:a0]
                            S.op("dve", lambda e: e.tensor_tensor_scan(cacc.t[:H, a0:a0 + n], ones_t.t[:H, :n], lf.t[:H, :n], init,
                                                                       ALU.mult, ALU.add),
                                 reads=[ones_t, lf, cacc], writes=[cacc])
                        gemm("fm", wbufs, w_in, D, c.COL_F, H, act, [panel], pn, epi_f)
                    else:
                        gemm("fm", wbufs, w_in, D, c.COL_Q, c.AW, act, [panel], pn, epi_qk(QTd, qgt, c.COL_Q))

                        def epi_f32(dst, colbase, func):
                            def epi(col, m, n0, n1, pss):
                                n = n1 - n0
                                sf = stf[si["f"] % 3]; si["f"] += 1
                                S.op("act", lambda e: e.activation(sf.t[:m, :n], pss[0].t[:m, :n], func), reads=[pss[0]], writes=[sf])
                                S.dma("sp", dst.t[col - colbase:col - colbase + m, p0 + n0:p0 + n1], sf.t[:m, :n],
                                      reads=[sf], writes=[dst], part=True)
                            return epi
                        gemm("fm", wbufs, w_in, D, c.COL_U, c.SW, act, [panel], pn, epi_f32(UOd, c.COL_U, AF.Copy))
                        gemm("fm", wbufs, w_in, D, c.COL_GA, D, act, [panel], pn, epi_f32(GAd, c.COL_GA, AF.Sigmoid))
                        gemm("fm", wbufs, w_in, D, c.COL_GB, D, act, [panel], pn, epi_f32(GBd, c.COL_GB, AF.Sigmoid))
                    p0 += pn

        _barrier(S)
        proj_phase(True)
        _barrier(S)
        proj_phase(False)

        NKT = 2 * NB + 1

        def ktile(kt):
            return (0, 16) if kt == 0 else (16 + 128 * (kt - 1), 128)

        MAGIC = 12582912.0

        def ssm_gen(st):
            if True:
                LSM = 256 * c.SEGB + 16
                NSEG = NB // c.SEGB
                PC = min(2, c.SEGB)
                tl = sb(st, "tl", [128, LSM], F32); S.dma("sp", tl.t[:], t_loc, writes=[tl])
                onesL = sb(st, "onesL", [128, LSM], F32)
                S.op("dve", lambda e: e.memset(onesL.t[:], 1.0), writes=[onesL])
                hpi = sb(st, "hpi", [128, 1], F32)
                S.op("dve", lambda e: e.memset(hpi.t[:], math.pi / 2), writes=[hpi])
                dskt = sb(st, "dskt", [128, c.NCT], F32); S.dma("sp", dskt.t[:], dsk, writes=[dskt])
                uT = sb(st, "uT", [128, L], BF16)
                uo = sb(st, "uo", [128, NO], F32)
                wbr = sb(st, "wbr", [128, 512], BF16); wbi = sb(st, "wbi", [128, 512], BF16)
                wcr = sb(st, "wcr", [128, 512], BF16); wci = sb(st, "wci", [128, 512], BF16)
                zr = sb(st, "zr", [128, 4, LSM], BF16); nzi = sb(st, "nzi", [128, 4, LSM], BF16)
                z2 = sb(st, "z2", [128, 4, LSM], BF16); z4 = sb(st, "z4", [128, 4, LSM], BF16)
                nwcr = sb(st, "nwcr", [128, 512], BF16); nwci = sb(st, "nwci", [128, 512], BF16)
                rho_t = sb(st, "rho_t", [128, 4, LSM], F32)
                sets = [[sb(st, f"s{k}_{i}", [128, LSM], F32) for i in range(10)] + [sb(st, f"ab{k}", [128, 1], F32)] for k in range(2)]
                carry = sb(st, "carry", [128, 4, 2], F32)
                sel = sb(st, "sel", [128, 256], F32)
                yf = sb(st, "yf", [128, 256], F32); zf = sb(st, "zf", [128, 256], F32); zb = sb(st, "zb", [128, 256], BF16)
                for ct in range(c.NCT):
                    S.dma("sp", uT.t[:], UTd.t[ct * 128:(ct + 1) * 128, :], reads=[UTd], writes=[uT])
                    S.dma("sp", uo.t[:], UOd.t[ct * 128:(ct + 1) * 128, :], reads=[UOd], writes=[uo])
                    cols = slice(ct * 512, (ct + 1) * 512)
                    S.dma("sp", wbr.t[:], BBRd.t[:, cols], reads=[BBRd], writes=[wbr])
                    S.dma("sp", wbi.t[:], BBId.t[:, cols], reads=[BBId], writes=[wbi])
                    S.dma("pool", wcr.t[:], crB[:, cols], writes=[wcr])
                    S.dma("pool", wci.t[:], ciB[:, cols], writes=[wci])
                    S.op("pool", lambda e: e.tensor_scalar(nwcr.t[:], wcr.t[:], -1.0, None, ALU.mult), reads=[wcr], writes=[nwcr])
                    S.op("pool", lambda e: e.tensor_scalar(nwci.t[:], wci.t[:], -1.0, None, ALU.mult), reads=[wci], writes=[nwci])
                    for jj in range(4):
                        j = 4 * ct + jj
                        S.op("act", lambda e: e.activation(rho_t.t[:, jj, :], onesL.t[:], AF.Copy, scale=rhoA.t[:, j:j + 1]),
                             reads=[onesL, rhoA], writes=[rho_t])
                    units = [(s_, jj_) for s_ in range(NSEG) for jj_ in range(4)]

                    def seginfo(s):
                        a0 = 0 if s == 0 else 16 + 256 * s * c.SEGB
                        a1 = 16 + 256 * (s + 1) * c.SEGB
                        return a0, a1 - a0, (16 if s == 0 else 0)

                    def stage1(s, jj):
                        a0, Ls, off = seginfo(s)
                        V = lambda t: t.t[:, :Ls]
                        j = 4 * ct + jj
                        xr, xi, ang, rr, sn, cs, bA, bB, bC, rr2, ab = sets[jj % 2]
                        for n0 in range(0, Ls, 512):
                            n1 = min(Ls, n0 + 512)
                            pr, pi = ps[5], ps[6]
                            S.op("pe", lambda e: e.matmul(pr.t[:, :n1 - n0], wbr.t[:, jj * 128:(jj + 1) * 128], uT.t[:, a0 + n0:a0 + n1],
                                                          start=True, stop=True), reads=[wbr, uT], writes=[pr])
                            S.op("pe", lambda e: e.matmul(pi.t[:, :n1 - n0], wbi.t[:, jj * 128:(jj + 1) * 128], uT.t[:, a0 + n0:a0 + n1],
                                                          start=True, stop=True), reads=[wbi, uT], writes=[pi])
                            copy("act", xr.t[:, n0:n1], pr.t[:, :n1 - n0], [pr], [xr])
                            copy("act", xi.t[:, n0:n1], pi.t[:, :n1 - n0], [pi], [xi])
                        S.op("act", lambda e: e.activation(ab.t[:], phiA.t[:, j:j + 1], AF.Copy, scale=float(a0)), reads=[phiA], writes=[ab])
                        S.op("act", lambda e: e.activation(V(ang), V(tl), AF.Identity, bias=ab.t[:, 0:1], scale=phiA.t[:, j:j + 1]),
                             reads=[tl, ab, phiA], writes=[ang])
                        S.op("dve", lambda e: e.tensor_scalar(V(rr), V(ang), MAGIC, MAGIC, ALU.add, ALU.subtract), reads=[ang], writes=[rr])
                        S.op("dve", lambda e: e.tensor_tensor(V(sn), V(ang), V(rr), ALU.subtract), reads=[ang, rr], writes=[sn])
                        S.op("act", lambda e: e.activation(V(cs), V(sn), AF.Abs), reads=[sn], writes=[cs])
                        S.op("act", lambda e: e.activation(V(cs), V(cs), AF.Sin, bias=hpi.t[:, 0:1], scale=-TWO_PI), reads=[cs, hpi], writes=[cs])
                        S.op("act", lambda e: e.activation(V(sn), V(sn), AF.Sin, scale=TWO_PI), reads=[sn], writes=[sn])

                    def stage2(s, jj):
                        a0, Ls, off = seginfo(s)
                        V = lambda t: t.t[:, :Ls]
                        xr, xi, ang, rr, sn, cs, bA, bB, bC, rr2, ab = sets[jj % 2]
                        S.op("dve", lambda e: e.tensor_tensor(V(bA), V(cs), V(xr), ALU.mult), reads=[cs, xr], writes=[bA])
                        S.op("dve", lambda e: e.tensor_tensor(V(bB), V(cs), V(xi), ALU.mult), reads=[cs, xi], writes=[bB])
                        S.op("dve", lambda e: e.tensor_tensor(V(bC), V(sn), V(xi), ALU.mult), reads=[sn, xi], writes=[bC])
                        S.op("dve", lambda e: e.tensor_tensor(V(rr), V(sn), V(xr), ALU.mult), reads=[sn, xr], writes=[rr])
                        S.op("dve", lambda e: e.tensor_add(V(bA), V(bA), V(bC)), reads=[bA, bC], writes=[bA])
                        S.op("dve", lambda e: e.tensor_sub(V(bB), V(bB), V(rr)), reads=[bB, rr], writes=[bB])
                        ir = 0.0 if s == 0 else carry.t[:, jj, 0:1]
                        ii = 0.0 if s == 0 else carry.t[:, jj, 1:2]
                        S.op("dve", lambda e: e.tensor_tensor_scan(V(xr), rho_t.t[:, jj, :Ls], V(bA), ir, ALU.mult, ALU.add),
                             reads=[rho_t, bA, carry], writes=[xr])
                        S.op("dve", lambda e: e.tensor_tensor_scan(V(xi), rho_t.t[:, jj, :Ls], V(bB), ii, ALU.mult, ALU.add),
                             reads=[rho_t, bB, carry], writes=[xi])
                        S.op("dve", lambda e: e.tensor_tensor(zr.t[:, jj, :Ls], V(cs), V(xr), ALU.mult), reads=[cs, xr], writes=[zr])
                        S.op("dve", lambda e: e.tensor_tensor(z2.t[:, jj, :Ls], V(sn), V(xi), ALU.mult), reads=[sn, xi], writes=[z2])
                        S.op("dve", lambda e: e.tensor_tensor(nzi.t[:, jj, :Ls], V(sn), V(xr), ALU.mult), reads=[sn, xr], writes=[nzi])
                        S.op("dve", lambda e: e.tensor_tensor(z4.t[:, jj, :Ls], V(cs), V(xi), ALU.mult), reads=[cs, xi], writes=[z4])
                        S.op("dve", lambda e: e.tensor_copy(carry.t[:, jj, 0:1], xr.t[:, Ls - 1:Ls]), reads=[xr], writes=[carry])
                        S.op("dve", lambda e: e.tensor_copy(carry.t[:, jj, 1:2], xi.t[:, Ls - 1:Ls]), reads=[xi], writes=[carry])
                        if jj == 3:
                            ypart(s)

                    def ypart(s):
                        a0, Ls, off = seginfo(s)
                        for q0 in range(0, c.SEGB, PC):
                            n = 256 * PC
                            l0 = off + 256 * q0
                            pb = ps[7]
                            for jj in range(4):
                                wsl = slice(jj * 128, (jj + 1) * 128)
                                S.op("pe", lambda e: e.matmul(pb.t[:, :n], wcr.t[:, wsl], zr.t[:, jj, l0:l0 + n],
                                                              start=(jj == 0), stop=False), reads=[wcr, zr], writes=[pb])
                                S.op("pe", lambda e: e.matmul(pb.t[:, :n], nwcr.t[:, wsl], z2.t[:, jj, l0:l0 + n],
                                                              start=False, stop=False), reads=[nwcr, z2], writes=[pb])
                                S.op("pe", lambda e: e.matmul(pb.t[:, :n], nwci.t[:, wsl], nzi.t[:, jj, l0:l0 + n],
                                                              start=False, stop=False), reads=[nwci, nzi], writes=[pb])
                                S.op("pe", lambda e: e.matmul(pb.t[:, :n], nwci.t[:, wsl], z4.t[:, jj, l0:l0 + n],
                                                              start=False, stop=(jj == 3)), reads=[nwci, z4], writes=[pb])
                            no = 128 * PC
                            o0 = (s * c.SEGB + q0) * 128
                            p4 = pb.t[:, :n].rearrange("p (j two r) -> p j two r", two=2, r=128)
                            s3 = sel.t[:, :no].rearrange("p (j r) -> p j r", r=128)
                            S.op("dve", lambda e: e.tensor_scalar(s3, p4[:, :, 0, :], w01t.t[:, 0:1], None, ALU.mult), reads=[pb, w01t], writes=[sel])
                            S.op("dve", lambda e: e.scalar_tensor_tensor(s3, p4[:, :, 1, :], w01t.t[:, 1:2], s3, ALU.mult, ALU.add),
                                 reads=[pb, w01t, sel], writes=[sel])
                            S.op("dve", lambda e: e.scalar_tensor_tensor(yf.t[:, :no], uo.t[:, o0:o0 + no], dskt.t[:, ct:ct + 1], sel.t[:, :no],
                                                                         ALU.mult, ALU.add), reads=[uo, dskt, sel], writes=[yf])
                            S.op("act", lambda e: e.activation(zf.t[:, :no], yf.t[:, :no], AF.Gelu), reads=[yf], writes=[zf])
                            S.op("dve", lambda e: e.tensor_copy(zb.t[:, :no], zf.t[:, :no]), reads=[zf], writes=[zb])
                            S.dma("sp", ZFd.t[ct * 128:(ct + 1) * 128, o0:o0 + no], zf.t[:, :no], reads=[zf], writes=[ZFd], part=True)
                            S.dma("sp", ZTd.t[ct * 128:(ct + 1) * 128, o0:o0 + no], zb.t[:, :no], reads=[zb], writes=[ZTd], part=True)

                    for i in range(len(units) + 1):
                        if i < len(units):
                            stage1(*units[i])
                        if i >= 1:
                            stage2(*units[i - 1])
                        yield

        def glu_phase():
            with ExitStack() as st:
                NP = c.NPAN
                zp = sb(st, "zp", [128, c.NCT, NP], BF16)
                wbufs = [sb(st, f"gw{i}", [128, c.NCT, 512], BF16) for i in range(2)]
                sg = sb(st, "gsg", [128, 512], F32); zt = sb(st, "gzt", [128, 512], F32); ob = sb(st, "gob", [128, 512], BF16)
                for p0 in range(0, NO, NP):
                    S.dma("sp", zp.t[:], ZTd.t[:, p0:p0 + NP].rearrange("(k p) n -> p k n", p=128), reads=[ZTd], writes=[zp])

                    def epi(col, m, n0, n1, pss):
                        n = n1 - n0
                        S.op("act", lambda e: e.activation(sg.t[:m, :n], pss[0].t[:m, :n], AF.Sigmoid), reads=[pss[0]], writes=[sg])
                        S.dma("sp", zt.t[:m, :n], ZFd.t[col:col + m, p0 + n0:p0 + n1], reads=[ZFd], writes=[zt])
                        S.op("dve", lambda e: e.tensor_tensor(ob.t[:m, :n], zt.t[:m, :n], sg.t[:m, :n], ALU.mult), reads=[zt, sg], writes=[ob])
                        S.dma("sp", SSTd.t[col:col + m, p0 + n0:p0 + n1], ob.t[:m, :n], reads=[ob], writes=[SSTd], part=True)
                    gemm("fm", wbufs, w_glu, c.SW, 0, c.SW, lambda kt, n0, n1: zp.t[:, kt, n0:n1], [zp], NP, epi)

        HG = min(4, H)
        QG = min(4, NB)

        def attn_prep(stp, st):
            if True:
                cball = sb(stp, "cball", [128, NKT, H], F32)
                mA = sb(stp, "mA", [128, 128], BF16); mB = sb(stp, "mB", [128, 128], BF16)
                cq = sb(st, "cq", [16, NO], F32)
                c3 = [sb(st, f"c3_{i}", [16, NO], BF16) for i in range(3)]
                r1 = sb(st, "cr1", [16, NO], F32)
                S.dma("pool", mA.t[:], maskA, writes=[mA]); S.dma("pool", mB.t[:], maskB, writes=[mB])
                for kt in range(NKT):
                    a, nk = ktile(kt)
                    pb = ps[kt % 2]
                    S.op("pe", lambda e: e.transpose(pb.t[:nk, :H], cacc.t[:H, a:a + nk], id_f.t[:H, :H]), reads=[cacc, id_f], writes=[pb])
                    S.op("dve", lambda e: e.tensor_scalar(cball.t[:nk, kt, :], pb.t[:nk, :H], -1.0, -STAB, ALU.mult, ALU.add),
                         reads=[pb], writes=[cball])
                v4 = cacc.t[:H, 16:16 + 256 * NB].rearrange("h (j two r) -> h j two r", two=2, r=128)
                d3 = cq.t[:H, :].rearrange("h (j r) -> h j r", r=128)
                S.op("dve", lambda e: e.tensor_scalar(d3, v4[:, :, 0, :], w01t.t[:H, 0:1], None, ALU.mult), reads=[cacc, w01t], writes=[cq])
                S.op("dve", lambda e: e.scalar_tensor_tensor(d3, v4[:, :, 1, :], w01t.t[:H, 1:2], d3, ALU.mult, ALU.add),
                     reads=[cacc, w01t, cq], writes=[cq])
                S.op("dve", lambda e: e.tensor_scalar(cq.t[:H, :], cq.t[:H, :], math.sqrt(128.0), None, ALU.mult), reads=[cq], writes=[cq])
                S.op("dve", lambda e: e.tensor_copy(c3[0].t[:H, :], cq.t[:H, :]), reads=[cq], writes=[c3[0]])
                S.op("dve", lambda e: e.tensor_sub(r1.t[:H, :], cq.t[:H, :], c3[0].t[:H, :]), reads=[cq, c3[0]], writes=[r1])
                S.op("dve", lambda e: e.tensor_copy(c3[1].t[:H, :], r1.t[:H, :]), reads=[r1], writes=[c3[1]])
                S.op("dve", lambda e: e.tensor_sub(r1.t[:H, :], r1.t[:H, :], c3[1].t[:H, :]), reads=[r1, c3[1]], writes=[r1])
                S.op("dve", lambda e: e.tensor_copy(c3[2].t[:H, :], r1.t[:H, :]), reads=[r1], writes=[c3[2]])
                for i in range(3):
                    S.dma("sp", CQ3d.t[i], c3[i].t[:H, :], reads=[c3[i]], writes=[CQ3d], part=True)
                S.dma("sp", CQd.t[:, :], cq.t[:H, :], reads=[cq], writes=[CQd])
            return cball, mA, mB

        def attn_gen(st, cball, mA, mB):
            if True:
                vg = sb(st, "vg", [128, NKT, HG * 128], BF16)
                kh = [sb(st, f"kh{i}", [128, L], BF16) for i in range(2)]
                qh = [sb(st, f"qh{i}", [128, NO], BF16) for i in range(2)]
                cqh = [sb(st, f"cqh{i}", [3, NO], BF16) for i in range(2)]
                pt = [sb(st, f"pt{i}", [128, 512], BF16) for i in range(3)]
                rs = [sb(st, f"rs{i}", [128, 512], F32) for i in range(2)]
                ao = [sb(st, f"ao{i}", [128, NO], BF16) for i in range(2)]
                pti = 0
                gi = 0
                scale = 1.0 / math.sqrt(128.0)
                for hg in range(0, H, HG):
                    S.dma("sp", vg.t[:16, 0, :], Vd.t[0:16, hg * 128:(hg + HG) * 128], reads=[Vd], writes=[vg], part=True)
                    for t0 in range(0, 2 * NB, 8):
                        t1 = min(2 * NB, t0 + 8)
                        S.dma("sp", vg.t[:, 1 + t0:1 + t1, :],
                              Vd.t[16 + 128 * t0:16 + 128 * t1, hg * 128:(hg + HG) * 128].rearrange("(t p) c -> p t c", p=128),
                              reads=[Vd], writes=[vg], part=True)
                    for h in range(hg, hg + HG):
                        khb, qhb, cqb, aob = kh[h % 2], qh[h % 2], cqh[h % 2], ao[h % 2]
                        S.dma("sp", khb.t[:], KTd.t[h], reads=[KTd], writes=[khb])
                        S.dma("sp", qhb.t[:], QTd.t[h], reads=[QTd], writes=[qhb])
                        S.dma("sp", cqb.t[:], CQ3d.t[:, h, :], reads=[CQ3d], writes=[cqb])
                        for g in range(0, NB, QG):
                            OT = ps[2 + gi % 2]; SM = ps[4]
                            gi += 1
                            W = QG * 128
                            ktmax = 2 * (g + QG - 1) + 2
                            pend = None
                            for kt in range(0, ktmax + 2):
                                if kt <= ktmax:
                                    a, nk = ktile(kt)
                                    jmin = max(g, (kt - 1) // 2) if kt > 0 else g
                                    q0 = (jmin - g) * 128
                                    qa, qb = g * 128 + q0, (g + QG) * 128
                                    STb = ps[kt % 2]
                                    S.op("pe", lambda e: e.matmul(STb.t[:nk, q0:W], khb.t[:, a:a + nk], qhb.t[:, qa:qb], start=True, stop=False),
                                         reads=[khb, qhb], writes=[STb])
                                    S.op("pe", lambda e: e.matmul(STb.t[:nk, q0:W], ones_bf.t[0:3, :nk], cqb.t[0:3, qa:qb], start=False, stop=True),
                                         reads=[ones_bf, cqb], writes=[STb])
                                    p = pt[pti % 3]; pti += 1
                                    S.op("act", lambda e: e.activation(p.t[:nk, q0:W], STb.t[:nk, q0:W], AF.Exp, bias=cball.t[:nk, kt, h:h + 1], scale=scale),
                                         reads=[STb, cball], writes=[p])
                                    if kt >= 1 and kt % 2 == 1:
                                        j = (kt - 1) // 2
                                        if g <= j < g + QG:
                                            cs_ = slice((j - g) * 128, (j - g + 1) * 128)
                                            S.op("pool", lambda e: e.tensor_tensor(p.t[:, cs_], p.t[:, cs_], mA.t[:, :], ALU.mult), reads=[p, mA], writes=[p])
                                    if kt >= 2 and kt % 2 == 0:
                                        j = (kt - 2) // 2
                                        if g <= j < g + QG:
                                            cs_ = slice((j - g) * 128, (j - g + 1) * 128)
                                            S.op("pool", lambda e: e.tensor_tensor(p.t[:, cs_], p.t[:, cs_], mB.t[:, :], ALU.mult), reads=[p, mB], writes=[p])
                                    cur = (kt, nk, q0, p)
                                else:
                                    cur = None
                                if pend is not None:
                                    kt_, nk_, q0_, p_ = pend
                                    last = kt_ == ktmax
                                    S.op("pe", lambda e: e.matmul(OT.t[:, q0_:W], vg.t[:nk_, kt_, (h - hg) * 128:(h - hg + 1) * 128], p_.t[:nk_, q0_:W],
                                                                  start=(kt_ == 0), stop=last), reads=[vg, p_], writes=[OT])
                                    S.op("pe", lambda e: e.matmul(SM.t[:, q0_:W], ones_bf.t[:nk_, :], p_.t[:nk_, q0_:W], start=(kt_ == 0), stop=last),
                                         reads=[ones_bf, p_], writes=[SM])
                                pend = cur
                                if kt % 2 == 1:
                                    yield
                            r = rs[gi % 2]
                            S.op("dve", lambda e: e.reciprocal(r.t[:, :W], SM.t[:, :W]), reads=[SM], writes=[r])
                            S.op("dve", lambda e: e.tensor_tensor(aob.t[:, g * 128:g * 128 + W], OT.t[:, :W], r.t[:, :W], ALU.mult), reads=[OT, r], writes=[aob])
                        S.dma("sp", ATd.t[h * 128:(h + 1) * 128, :], aob.t[:], reads=[aob], writes=[ATd], part=True)

        _barrier(S)
        with ExitStack() as stp:
            with ExitStack() as st0:
                cball_, mA_, mB_ = attn_prep(stp, st0)
            _barrier(S)
            with ExitStack() as stj:
                g1_ = ssm_gen(stj)
                g2_ = attn_gen(stj, cball_, mA_, mB_)
                live = [g1_, g2_]
                while live:
                    for g_ in list(live):
                        try:
                            next(g_)
                        except StopIteration:
                            live.remove(g_)
        _barrier(S)
        glu_phase()

        def mix_phase():
            with ExitStack() as st:
                NP = min(1024, NO)
                KA, KS = c.AW // 128, c.SW // 128
                cat = sb(st, "cat", [128, KA + KS, NP], BF16)
                mixT = sb(st, "mixT", [128, KT, NP], BF16)
                wbufs = [sb(st, f"mw{i}", [128, max(KT, KA + KS), 256], BF16) for i in range(2)]
                ga = sb(st, "ga", [128, 512], F32); gb = sb(st, "gb", [128, 512], F32)
                m1 = sb(st, "m1", [128, 512], F32); m2 = sb(st, "m2", [128, 512], F32)
                xo = [sb(st, f"xo{i}", [128, 512], F32) for i in range(2)]
                ho = [sb(st, f"ho{i}", [128, 512], F32) for i in range(2)]
                k = {"i": 0}
                for p0 in range(0, NO, NP):
                    S.dma("sp", cat.t[:, 0:KA, :], ATd.t[:, p0:p0 + NP].rearrange("(k p) n -> p k n", p=128), reads=[ATd], writes=[cat], part=True)
                    S.dma("sp", cat.t[:, KA:KA + KS, :], SSTd.t[:, p0:p0 + NP].rearrange("(k p) n -> p k n", p=128), reads=[SSTd], writes=[cat], part=True)

                    def epi(col, m, n0, n1, pss):
                        n = n1 - n0
                        S.dma("sp", ga.t[:m, :n], GAd.t[col:col + m, p0 + n0:p0 + n1], reads=[GAd], writes=[ga])
                        S.dma("sp", gb.t[:m, :n], GBd.t[col:col + m, p0 + n0:p0 + n1], reads=[GBd], writes=[gb])
                        S.op("dve", lambda e: e.tensor_tensor(m1.t[:m, :n], ga.t[:m, :n], pss[0].t[:m, :n], ALU.mult), reads=[ga, pss[0]], writes=[m1])
                        S.op("dve", lambda e: e.tensor_tensor(m2.t[:m, :n], gb.t[:m, :n], pss[1].t[:m, :n], ALU.mult), reads=[gb, pss[1]], writes=[m2])
                        S.op("pool", lambda e: e.tensor_add(mixT.t[:m, col // 128, n0:n1], m1.t[:m, :n], m2.t[:m, :n]), reads=[m1, m2], writes=[mixT])
                    gemm("fm", wbufs, [(w_a, c.AW), (w_b, c.SW)], c.AW + c.SW, 0, D, lambda kt, n0, n1: cat.t[:, kt, n0:n1], [cat], NP, epi,
                         kgroups=[(0, KA), (KA, KA + KS)])

                    def epi2(t0, t1, col, cw, pss):
                        r = t1 - t0
                        x_ = xo[k["i"] % 2]; h_ = ho[k["i"] % 2]; k["i"] += 1
                        S.dma("sp", x_.t[:r, :cw], x_own[p0 + t0:p0 + t1, col:col + cw], writes=[x_])
                        S.op("dve", lambda e: e.tensor_tensor(h_.t[:r, :cw], x_.t[:r, :cw], pss[0].t[:r, :cw], ALU.add), reads=[x_, pss[0]], writes=[h_])
                        S.dma("sp", H1d.t[p0 + t0:p0 + t1, col:col + cw], h_.t[:r, :cw], reads=[h_], writes=[H1d], part=True)
                    gemm("tm", wbufs, w_out, D, 0, D, lambda kt, t0, t1: mixT.t[:, kt, t0:t1], [mixT], NP, epi2)

        _barrier(S)
        mix_phase()

        def peer_phase():
            with ExitStack() as st:
                NP = c.PPAN
                NTT = NP // 128
                pan = sb(st, "ppan", [128, KT, NP], BF16)
                et = [sb(st, f"et{i}", [128, 16, 128], F32) for i in range(NTT)]
                thr = sb(st, "thr", [128, NTT, 8], F32); rz = sb(st, "rz", [128, NTT, 8], F32)
                for p0 in range(0, NO, NP):
                    _barrier(S)
                    with ExitStack() as s1:
                        grep = sb(s1, "g2", [128, D], F32); S.dma("sp", grep.t[:], g2rep, writes=[grep])
                        nt = make_nt(s1, "p")
                        nt(H1d.t[p0:p0 + NP, :], NP, grep, pan, srcbuf=H1d)
                    _barrier(S)
                    with ExitStack() as s2:
                        qpT = sb(s2, "qpT", [128, 16, NP], BF16)
                        skb = sb(s2, "skb", [128, 256], BF16); S.dma("pool", skb.t[:], skT, writes=[skb])
                        wbufs = [sb(s2, f"pw{i}", [128, KT, 256], BF16) for i in range(2)]
                        sc = sb(s2, "sc", [128, 16, 128], F32)
                        wk = sb(s2, "wk", [128, 256], F32)
                        m16 = sb(s2, "m16", [128, 16, 16], F32); e16 = sb(s2, "e16", [128, 16, 16], F32)
                        nm = sb(s2, "nm", [128, 16], F32)
                        cand = sb(s2, "cand", [128, 256], F32); c16 = sb(s2, "c16", [128, 16], F32)

                        def epi_q(col, m, n0, n1, pss):
                            copy(alt(), qpT.t[:, col // 128, n0:n1], pss[0].t[:, :n1 - n0], [pss[0]], [qpT])
                        gemm("fm", wbufs, w_query, D, 0, c.QW, lambda kt, n0, n1: pan.t[:, kt, n0:n1], [pan], NP, epi_q)
                        for tt in range(NTT):
                            ts_ = slice(tt * 128, (tt + 1) * 128)
                            for hc in range(16):
                                pb = ps[hc // 4]
                                S.op("pe", lambda e: e.matmul(pb.t[:, (hc % 4) * 128:(hc % 4 + 1) * 128], qpT.t[:, hc, ts_],
                                                              skb.t[:, (hc % 2) * 128:(hc % 2 + 1) * 128], start=True, stop=True),
                                     reads=[qpT, skb], writes=[pb])
                                if hc % 4 == 3:
                                    copy(alt(), sc.t[:, hc - 3:hc + 1, :], pb.t[:, :].rearrange("p (a b) -> p a b", b=128), [pb], [sc])
                            for hc in range(16):
                                S.op("dve", lambda e: e.max(m16.t[:, hc, 0:8], sc.t[:, hc, :]), reads=[sc], writes=[m16])
                                S.op("dve", lambda e: e.match_replace(wk.t[:, :128], m16.t[:, hc, 0:8], sc.t[:, hc, :], -1e30),
                                     reads=[sc, m16], writes=[wk])
                                S.op("dve", lambda e: e.max(m16.t[:, hc, 8:16], wk.t[:, :128]), reads=[wk], writes=[m16])
                            S.op("dve", lambda e: e.tensor_scalar(nm.t[:, :], m16.t[:, :, 0], -1.0, None, ALU.mult), reads=[m16], writes=[nm])
                            for hc in range(16):
                                S.op("act", lambda e: e.activation(et[tt].t[:, hc, :], sc.t[:, hc, :], AF.Exp, bias=nm.t[:, hc:hc + 1], scale=1.0),
                                     reads=[sc, nm], writes=[et[tt]])
                                S.op("act", lambda e: e.activation(e16.t[:, hc, :], m16.t[:, hc, :], AF.Exp, bias=nm.t[:, hc:hc + 1], scale=1.0),
                                     reads=[m16, nm], writes=[e16])
                            for h in range(8):
                                c3 = cand.t[:, :].rearrange("p (a b) -> p a b", b=16)
                                S.op("dve", lambda e: e.tensor_tensor(c3, e16.t[:, 2 * h, :].unsqueeze(2).to_broadcast([128, 16, 16]),
                                                                      e16.t[:, 2 * h + 1, :].unsqueeze(1).to_broadcast([128, 16, 16]), ALU.mult),
                                     reads=[e16], writes=[cand])
                                S.op("dve", lambda e: e.max(c16.t[:, 0:8], cand.t[:, :]), reads=[cand], writes=[c16])
                                S.op("dve", lambda e: e.match_replace(wk.t[:, :], c16.t[:, 0:8], cand.t[:, :], -1.0), reads=[cand, c16], writes=[wk])
                                S.op("dve", lambda e: e.max(c16.t[:, 8:16], wk.t[:, :]), reads=[wk], writes=[c16])
                                S.op("dve", lambda e: e.tensor_scalar(thr.t[:, tt, h:h + 1], c16.t[:, 15:16], 1.0 - 1e-5, None, ALU.mult),
                                     reads=[c16], writes=[thr])
                                S.op("dve", lambda e: e.tensor_reduce(rz.t[:, tt, h:h + 1], c16.t[:, :], AX.X, ALU.add), reads=[c16], writes=[rz])
                            S.op("dve", lambda e: e.reciprocal(rz.t[:, tt, :], rz.t[:, tt, :]), reads=[rz], writes=[rz])
                            for h in range(8):
                                S.op("dve", lambda e: e.tensor_scalar(e16.t[:, 2 * h, :], e16.t[:, 2 * h, :], rz.t[:, tt, h:h + 1], None, ALU.mult),
                                     reads=[e16, rz], writes=[e16])
                                S.op("dve", lambda e: e.tensor_scalar(et[tt].t[:, 2 * h, :], et[tt].t[:, 2 * h, :], rz.t[:, tt, h:h + 1], None, ALU.mult),
                                     reads=[et[tt], rz], writes=[et[tt]])
                                c3 = cand.t[:, :].rearrange("p (a b) -> p a b", b=16)
                                S.op("dve", lambda e: e.tensor_tensor(c3, e16.t[:, 2 * h, :].unsqueeze(2).to_broadcast([128, 16, 16]),
                                                                      e16.t[:, 2 * h + 1, :].unsqueeze(1).to_broadcast([128, 16, 16]), ALU.mult),
                                     reads=[e16], writes=[cand])
                                S.op("dve", lambda e: e.max(c16.t[:, 0:8], cand.t[:, :]), reads=[cand], writes=[c16])
                                S.op("dve", lambda e: e.match_replace(wk.t[:, :], c16.t[:, 0:8], cand.t[:, :], -1.0), reads=[cand, c16], writes=[wk])
                                S.op("dve", lambda e: e.max(c16.t[:, 8:16], wk.t[:, :]), reads=[wk], writes=[c16])
                                S.op("dve", lambda e: e.tensor_scalar(thr.t[:, tt, h:h + 1], c16.t[:, 15:16], 1.0 - 1e-5, None, ALU.mult),
                                     reads=[c16], writes=[thr])
                    _barrier(S)
                    with ExitStack() as s3:
                        wbufs = [sb(s3, f"aw{i}", [128, KT, 512], BF16) for i in range(2)]
                        gT = [sb(s3, f"gT{i}", [128, 4, NP], F32) for i in range(2)]
                        Mb = [sb(s3, f"Mb{i}", [128, 512], F32) for i in range(3)]
                        Mh = [sb(s3, f"Mh{i}", [128, 512], BF16) for i in range(3)]
                        wst = [sb(s3, f"wst{i}", [128, 4, NP], BF16) for i in range(2)]
                        k = {"m": 0}
                        NSB = c.PN // 512
                        WT = [ps[4 + b] for b in range(4)]

                        def load_w(sbi):
                            wt = wbufs[sbi % 2]
                            for k0 in range(0, KT, 8):
                                k1 = min(KT, k0 + 8)
                                srcap = euT[k0 * 128:k1 * 128, sbi * 512:(sbi + 1) * 512].rearrange("(kt p) c -> p kt c", p=128)
                                S.dma("pool", wt.t[:, k0:k1, :], srcap, writes=[wt], part=True)

                        def issue_AT(sbi):
                            wt = wbufs[sbi % 2]
                            g_ = gT[sbi % 2]
                            for j in range(4):
                                pb = ps[j]
                                for kt in range(KT):
                                    S.op("pe", lambda e: e.matmul(pb.t[:, :NP], wt.t[:, kt, j * 128:(j + 1) * 128], pan.t[:, kt, 0:NP],
                                                                  start=(kt == 0), stop=(kt == KT - 1)), reads=[wt, pan], writes=[pb])
                                    if kt % 4 == 3 and kt != KT - 1:
                                        yield
                                S.op("act", lambda e: e.activation(g_.t[:, j, :], pb.t[:, :NP], AF.Gelu), reads=[pb], writes=[g_])
                                yield

                        load_w(0)
                        if NSB > 1:
                            load_w(1)
                        for _ in issue_AT(0):
                            pass
                        for sbi in range(NSB):
                            if sbi + 2 < NSB:
                                load_w(sbi + 2)
                            nxt = issue_AT(sbi + 1) if sbi + 1 < NSB else None
                            g_ = gT[sbi % 2]
                            col0 = sbi * 512
                            i1a = col0 // 128
                            pairs = [(tt, h) for tt in range(NTT) for h in range(8)]
                            held = []
                            for i in range(len(pairs) + 1):
                                if i < len(pairs):
                                    tt, h = pairs[i]
                                    M_ = Mb[k["m"] % 3]; Mh_ = Mh[k["m"] % 3]; k["m"] += 1
                                    P3 = M_.t[:, :].rearrange("p (a b) -> p a b", b=128)
                                    S.op("dve", lambda e: e.tensor_tensor(P3, et[tt].t[:, 2 * h, i1a:i1a + 4].unsqueeze(2).to_broadcast([128, 4, 128]),
                                                                          et[tt].t[:, 2 * h + 1, :].unsqueeze(1).to_broadcast([128, 4, 128]), ALU.mult),
                                         reads=[et[tt]], writes=[M_])
                                    held.append((tt, h, M_, Mh_))
                                if i >= 1:
                                    tt, h, M_, Mh_ = held.pop(0)
                                    S.op("dve", lambda e: e.scalar_tensor_tensor(Mh_.t[:, :], M_.t[:, :], thr.t[:, tt, h:h + 1], M_.t[:, :],
                                                                                 ALU.is_ge, ALU.mult), reads=[M_, thr], writes=[Mh_])
                                    for b in range(4):
                                        S.op("pe", lambda e: e.matmul(WT[b].t[:, tt * 128:(tt + 1) * 128], Mh_.t[:, b * 128:(b + 1) * 128],
                                                                      id_bf.t[:, :], start=(h == 0), stop=(h == 7)),
                                             reads=[Mh_, id_bf], writes=[WT[b]])
                                    if nxt is not None:
                                        try:
                                            next(nxt)
                                        except StopIteration:
                                            nxt = None
                            if nxt is not None:
                                for _ in nxt:
                                    pass
                            ws = wst[sbi % 2]
                            for b in range(4):
                                S.op("dve", lambda e: e.tensor_tensor(ws.t[:, b, :], WT[b].t[:, :NP], g_.t[:, b, :], ALU.mult),
                                     reads=[WT[b], g_], writes=[ws])
                                S.dma("sp", WGTd.t[col0 + b * 128:col0 + (b + 1) * 128, p0:p0 + NP], ws.t[:, b, :], reads=[ws], writes=[WGTd], part=True)

            _barrier(S)
            with ExitStack() as st:
                NY = min(1024, NO)
                NYT = NY // 128
                EC = 16
                wv = [sb(st, f"yv{i}", [128, EC, 512], BF16) for i in range(2)]
                wa = [sb(st, f"ya{i}", [128, EC, NY], BF16) for i in range(2)]
                h1t = [sb(st, f"yh{i}", [128, 512], F32) for i in range(2)]
                yo = [sb(st, f"yo{i}", [128, 512], F32) for i in range(2)]
                NEC = c.PN // (128 * EC)
                gi = 0
                ci = 0
                oi = 0
                for cb in range(0, D, 512):
                    for p0 in range(0, NO, NY):
                        bks = [ps[i] for i in range(NYT)]
                        gi += 1
                        for ec in range(NEC):
                            e0 = ec * EC * 128
                            v_, a_ = wv[ci % 2], wa[ci % 2]
                            ci += 1
                            for k0 in range(0, EC, 8):
                                S.dma("pool", v_.t[:, k0:k0 + 8, :], ev[e0 + k0 * 128:e0 + (k0 + 8) * 128, cb:cb + 512].rearrange("(k p) c -> p k c", p=128),
                                      writes=[v_], part=True)
                                S.dma("sp", a_.t[:, k0:k0 + 8, :], WGTd.t[e0 + k0 * 128:e0 + (k0 + 8) * 128, p0:p0 + NY].rearrange("(k p) c -> p k c", p=128),
                                      reads=[WGTd], writes=[a_], part=True)
                            for tt in range(NYT):
                                for kt in range(EC):
                                    S.op("pe", lambda e: e.matmul(bks[tt].t[:, :512], a_.t[:, kt, tt * 128:(tt + 1) * 128], v_.t[:, kt, :],
                                                                  start=(ec == 0 and kt == 0), stop=(ec == NEC - 1 and kt == EC - 1)),
                                         reads=[a_, v_], writes=[bks[tt]])
                        for tt in range(NYT):
                            h_, o_ = h1t[oi % 2], yo[oi % 2]
                            oi += 1
                            r0 = p0 + tt * 128
                            S.dma("sp", h_.t[:], H1d.t[r0:r0 + 128, cb:cb + 512], reads=[H1d], writes=[h_])
                            S.op("dve", lambda e: e.tensor_tensor(o_.t[:], h_.t[:], bks[tt].t[:, :512], ALU.add), reads=[h_, bks[tt]], writes=[o_])
                            S.dma("sp", out_own[r0:r0 + 128, cb:cb + 512], o_.t[:], reads=[o_], writes=[OUTb], part=True)

        _barrier(S)
        peer_phase()

        for i in range(NDSEM):
            if S.dval[i]:
                nc.sync.wait_ge(S.dsem[i], S.dval[i])
        print("instructions:", S.ninst, {k: v for k, v in S.cnt.items()})
    return nc


def _prep(c, inp, core):
    f32 = np.float32
    b, hh = core // 2, core % 2
    D, SEQ, NO, H, G, NJ, NCT = c.D, c.SEQ, c.NO, c.H, c.G, c.NJ, c.NCT
    A = lambda v: np.ascontiguousarray(np.asarray(v), dtype=f32)
    x = np.asarray(inp["x"][b], dtype=f32)
    m = {}
    m["x_ctx"] = np.concatenate([np.asarray(inp["meta_tokens"], dtype=f32), x], 0)
    m["x_own"] = A(x.reshape(SEQ // 128, 128, D)[hh::2].reshape(NO, D))
    m["g1rep"] = A(np.broadcast_to(np.asarray(inp["norm1_g"][0])[None, :], (128, D)))
    m["g2rep"] = A(np.broadcast_to(np.asarray(inp["norm2_g"][0])[None, :], (128, D)))
    m["w_in"] = A(inp["w_in"][0])
    m["bfg"] = A(np.asarray(inp["b_forget"][0]).reshape(H, 1))
    m["qg"] = A(np.asarray(inp["q_norm_g"][0]).reshape(128, 1))
    m["kg"] = A(np.asarray(inp["k_norm_g"][0]).reshape(128, 1))
    lr = np.asarray(inp["lam_re"][0], dtype=f32); li = np.asarray(inp["lam_im"][0], dtype=f32)
    ld = np.asarray(inp["log_dt"][0], dtype=f32)
    m["lrA"] = A(lr.reshape(NJ, 128).T); m["liA"] = A(li.reshape(NJ, 128).T)
    m["ldA"] = A(np.repeat(ld.reshape(NJ, 2), 64, axis=1).T)
    m["lrB"] = A(np.broadcast_to(lr.reshape(1, -1), (128, G * 64)))
    m["liB"] = A(np.broadcast_to(li.reshape(1, -1), (128, G * 64)))
    m["ldB"] = A(np.broadcast_to(np.repeat(ld, 64)[None, :], (128, G * 64)))
    bre = np.asarray(inp["b_re"][0], dtype=f32); bim = np.asarray(inp["b_im"][0], dtype=f32)
    cre = np.asarray(inp["c_re"][0], dtype=f32); cim = np.asarray(inp["c_im"][0], dtype=f32)
    brB = np.zeros((128, G * 64), f32); biB = np.zeros((128, G * 64), f32)
    crB = np.zeros((128, NJ * 128), f32); ciB = np.zeros((128, NJ * 128), f32)
    for g in range(G):
        r0 = (g % 8) * 16
        brB[r0:r0 + 16, g * 64:(g + 1) * 64] = bre[g].T
        biB[r0:r0 + 16, g * 64:(g + 1) * 64] = bim[g].T
        j, g2 = g // 2, g % 2
        crB[g2 * 64:(g2 + 1) * 64, j * 128 + r0:j * 128 + r0 + 16] = cre[g].T
        ciB[g2 * 64:(g2 + 1) * 64, j * 128 + r0:j * 128 + r0 + 16] = cim[g].T
    m["brB"], m["biB"], m["crB"], m["ciB"] = brB, biB, crB, ciB
    m["dsk"] = A(np.asarray(inp["d_skip"][0]).reshape(NCT, 128).T)
    m["w_glu"] = A(inp["w_glu"][0]); m["w_a"] = A(inp["w_branch_attn"][0]); m["w_b"] = A(inp["w_branch_ssm"][0])
    m["w_out"] = A(inp["w_out"][0]); m["w_query"] = A(inp["w_query"][0])
    m["skT"] = A(np.asarray(inp["sub_keys"][0]).transpose(2, 0, 1).reshape(128, 256))
    m["euT"] = A(np.asarray(inp["expert_u"][0]).T); m["ev"] = A(inp["expert_v"][0])
    tri = (np.arange(128)[None, :] >= np.arange(128)[:, None]).astype(f32)
    m["maskA"] = tri if hh == 0 else np.ones((128, 128), f32)
    m["maskB"] = np.zeros((128, 128), f32) if hh == 0 else tri
    m["w01"] = A(np.broadcast_to(np.array([[1.0, 0.0]] if hh == 0 else [[0.0, 1.0]], f32), (128, 2)))
    LSM = 256 * c.SEGB + 16
    m["t_loc"] = A(np.broadcast_to(np.arange(LSM, dtype=f32)[None, :], (128, LSM)))
    io = np.arange(NO)
    pos = 16 + 128 * (2 * (io // 128) + hh) + (io % 128)
    m["t_own"] = A(np.broadcast_to(pos.astype(f32)[None, :], (128, NO)))
    m["ident"] = np.eye(128, dtype=f32)
    return m


_NC_CACHE = {}


def run_cfg(c, inputs):
    key = (c.D, c.SEQ, c.B)
    if key not in _NC_CACHE:
        _NC_CACHE[key] = build(c)
    nc = _NC_CACHE[key]
    ncores = 2 * c.B
    shared = None
    in_maps = []
    for core in range(ncores):
        in_maps.append(_prep(c, inputs, core))
    res = run_bass_kernel_spmd(nc, in_maps, core_ids=list(range(ncores)))
    if getattr(c, "debug", False):
        c.dbg = res.results
    out = np.zeros((c.B, c.SEQ, c.D), np.float32)
    for core in range(ncores):
        b, hh = core // 2, core % 2
        o = np.asarray(res.results[core]["out_own"], dtype=np.float32).reshape(c.NB, 128, c.D)
        out[b].reshape(c.SEQ // 128, 128, c.D)[hh::2] = o
    return out


def kernel(**inputs):
    return run_cfg(Cfg(), inputs)
```

```python
import math
from contextlib import ExitStack
import numpy as np
import ml_dtypes
import concourse.bass as bass
import concourse.mybir as mybir
from concourse.bass_utils import run_bass_kernel_spmd

F32 = mybir.dt.float32
BF16 = mybir.dt.bfloat16
I32 = mybir.dt.int32
AF = mybir.ActivationFunctionType
ALU = mybir.AluOpType
AX = mybir.AxisListType

EPOCH = 16000
NDSEM = 40
TWO_PI = 2.0 * math.pi
STAB = 30.0


class Buf:
    __slots__ = ("w", "r")

    def __init__(self):
        self.w = {}
        self.r = {}


class TT:
    def __init__(self, t):
        self.t = t
        self.b = Buf()


def _b(x):
    return x.b if isinstance(x, TT) else x


class Sched:
    def __init__(self, nc, stack):
        self.nc = nc
        self.engs = {"pe": nc.tensor, "dve": nc.vector, "act": nc.scalar,
                     "pool": nc.gpsimd, "sp": nc.sync}
        self.stack = stack
        self.esem = {}
        self.cnt = {e: 0 for e in self.engs}
        self.seen = {e: {} for e in self.engs}
        self.dsem = [stack.enter_context(nc.semaphore(f"d{i}")) for i in range(NDSEM)]
        self.dval = [0] * NDSEM
        self.dnext = 0
        self.ninst = 0

    def _esem(self, eng, epoch):
        k = (eng, epoch)
        if k not in self.esem:
            self.esem[k] = self.stack.enter_context(self.nc.semaphore(f"e_{eng}_{epoch}"))
        return self.esem[k]

    def _sem_of(self, key):
        if key[0] == "d":
            return self.dsem[key[1]]
        return self._esem(key[0], key[1])

    def _wait(self, eng, deps):
        s = self.seen[eng]
        for k, v in deps.items():
            if eng == "pe" and k[0] == "pe":
                continue
            if s.get(k, 0) < v:
                self.engs[eng].wait_ge(self._sem_of(k), v)
                s[k] = v

    @staticmethod
    def _acc(deps, d):
        for k, v in d.items():
            if deps.get(k, 0) < v:
                deps[k] = v

    def _commit(self, tok, reads, writes, part=False):
        k, v = tok
        for b in writes:
            if not part:
                b.w = {}
            b.w[k] = max(b.w.get(k, 0), v)
            b.r = {}
        for b in reads:
            if b.r.get(k, 0) < v:
                b.r[k] = v

    def op(self, eng, fn, reads=(), writes=()):
        reads = [_b(x) for x in reads]
        writes = [_b(x) for x in writes]
        deps = {}
        for b in reads:
            self._acc(deps, b.w)
        for b in writes:
            self._acc(deps, b.w)
            self._acc(deps, b.r)
        self._wait(eng, deps)
        ins = fn(self.engs[eng])
        c = self.cnt[eng]
        epoch, val = divmod(c, EPOCH)
        ins.then_inc(self._esem(eng, epoch), 1)
        self.cnt[eng] = c + 1
        self._commit(((eng, epoch), val + 1), reads, writes)
        self.ninst += 1
        return ins

    def dma(self, q, out, in_, reads=(), writes=(), part=False):
        reads = [_b(x) for x in reads]
        writes = [_b(x) for x in writes]
        deps = {}
        for b in reads:
            self._acc(deps, b.w)
        for b in writes:
            if not part:
                self._acc(deps, b.w)
            self._acc(deps, b.r)
        i = self.dnext
        self.dnext = (i + 1) % NDSEM
        if self.dval[i]:
            deps[("d", i)] = max(deps.get(("d", i), 0), self.dval[i])
        self._wait(q, deps)
        ins = self.engs[q].dma_start(out=out, in_=in_)
        ins.then_inc(self.dsem[i], 16)
        self.dval[i] += 16
        assert self.dval[i] < 60000
        self._commit((("d", i), self.dval[i]), reads, writes, part=part)
        self.ninst += 1
        return ins


def _barrier(S):
    deps = {}
    for e, cnt in S.cnt.items():
        if cnt:
            epoch, val = divmod(cnt - 1, EPOCH)
            deps[(e, epoch)] = val + 1
    for i in range(NDSEM):
        if S.dval[i]:
            deps[("d", i)] = S.dval[i]
    for e in S.engs:
        s = S.seen[e]
        for k, v in deps.items():
            if k[0] == e:
                continue
            if s.get(k, 0) < v:
                S.engs[e].wait_ge(S._sem_of(k), v)
                s[k] = v


class Cfg:
    def __init__(s, D=4096, SEQ=4096, B=4, NPAN=1024, PPAN=512, SEGB=2, WB6=256):
        s.D, s.SEQ, s.B = D, SEQ, B
        s.NM = 16
        s.H = D // 256
        s.AW = s.H * 128
        s.G = D // 32
        s.SW = s.G * 16
        s.NJ = s.G // 2
        s.NCT = s.SW // 128
        s.PH, s.PK, s.TOPK = 8, 128, 16
        s.PN = s.PK * s.PK
        s.QW = s.PH * 256
        s.L = SEQ + s.NM
        s.NO = SEQ // 2
        s.NB = s.NO // 128
        s.KT = D // 128
        s.NPAN = min(NPAN, s.NO)
        s.PPAN = min(PPAN, s.NO)
        s.SEGB = min(SEGB, s.NB)
        s.WB6 = WB6
        s.COL_Q = 0
        s.COL_K = s.AW
        s.COL_V = 2 * s.AW
        s.COL_F = 3 * s.AW
        s.COL_U = s.COL_F + s.H
        s.COL_GA = s.COL_U + s.SW
        s.COL_GB = s.COL_GA + D
        s.NCOLS = s.COL_GB + D


def build(cfg):
    c = cfg
    D, L, NO, KT, H, NB = c.D, c.L, c.NO, c.KT, c.H, c.NB
    nc = bass.Bass("TRN2", target_bir_lowering=False)

    def din(name, shape, dt=F32):
        return nc.dram_tensor(name, list(shape), dt, kind="ExternalInput").ap()

    def dscr(name, shape, dt):
        return TT(nc.dram_tensor(name, list(shape), dt, kind="ExternalOutput" if getattr(c, "debug", False) else "Internal").ap())

    x_ctx = din("x_ctx", [L, D]); x_own = din("x_own", [NO, D])
    g1rep = din("g1rep", [128, D]); g2rep = din("g2rep", [128, D])
    w_in = din("w_in", [D, c.NCOLS])
    bfg = din("bfg", [H, 1]); qg = din("qg", [128, 1]); kg = din("kg", [128, 1])
    lrA = din("lrA", [128, c.NJ]); liA = din("liA", [128, c.NJ]); ldA = din("ldA", [128, c.NJ])
    lrB = din("lrB", [128, c.NJ * 128]); liB = din("liB", [128, c.NJ * 128]); ldB = din("ldB", [128, c.NJ * 128])
    brB = din("brB", [128, c.NJ * 128]); biB = din("biB", [128, c.NJ * 128])
    crB = din("crB", [128, c.NJ * 128]); ciB = din("ciB", [128, c.NJ * 128])
    dsk = din("dsk", [128, c.NCT])
    w_glu = din("w_glu", [c.SW, c.SW]); w_a = din("w_a", [c.AW, D]); w_b = din("w_b", [c.SW, D])
    w_out = din("w_out", [D, D]); w_query = din("w_query", [D, c.QW])
    skT = din("skT", [128, 2 * 128])
    euT = din("euT", [D, c.PN]); ev = din("ev", [c.PN, D])
    maskA = din("maskA", [128, 128]); maskB = din("maskB", [128, 128])
    w01 = din("w01", [128, 2])
    t_loc = din("t_loc", [128, 256 * c.SEGB + 16]); t_own = din("t_own", [128, NO])
    ident = din("ident", [128, 128])
    out_own = nc.dram_tensor("out_own", [NO, D], F32, kind="ExternalOutput").ap()

    KTd = dscr("KTd", [H, 128, L], BF16)
    Vd = dscr("Vd", [L, c.AW], BF16)
    UTd = dscr("UTd", [c.SW, L], BF16)
    QTd = dscr("QTd", [H, 128, NO], BF16)
    GAd = dscr("GAd", [D, NO], F32)
    GBd = dscr("GBd", [D, NO], F32)
    UOd = dscr("UOd", [c.SW, NO], F32)
    CQd = dscr("CQd", [H, NO], F32)
    CQ3d = dscr("CQ3d", [3, H, NO], BF16)
    BBRd = dscr("BBRd", [128, c.NJ * 128], BF16)
    BBId = dscr("BBId", [128, c.NJ * 128], BF16)
    ZFd = dscr("ZFd", [c.SW, NO], F32)
    ZTd = dscr("ZTd", [c.SW, NO], BF16)
    SSTd = dscr("SSTd", [c.SW, NO], BF16)
    ATd = dscr("ATd", [c.AW, NO], BF16)
    H1d = dscr("H1d", [NO, D], F32)
    WGTd = dscr("WGTd", [c.PN, NO], BF16)
    OUTb = Buf()

    with ExitStack() as gst:
        S = Sched(nc, gst)

        uid = {"n": 0}

        def sb(st, name, shape, dt):
            uid["n"] += 1
            return TT(st.enter_context(nc.sbuf_tensor(f"{name}_{uid['n']}", list(shape), dt)))

        ps = [TT(gst.enter_context(nc.psum_tensor(f"ps{i}", [128, 512], F32))) for i in range(8)]
        psb = []
        for i in range(2):
            v = TT(ps[6 + i].t.bitcast(BF16))
            v.b = ps[6 + i].b
            psb.append(v)
        id_bf = sb(gst, "id_bf", [128, 128], BF16)
        id_f = sb(gst, "id_f", [128, 128], F32)
        ones_bf = sb(gst, "ones_bf", [128, 128], BF16)
        ones_f = sb(gst, "ones_f", [1, 128], F32)
        cst = sb(gst, "cst", [128, 8], F32)
        w01t = sb(gst, "w01t", [128, 2], F32)
        qgt = sb(gst, "qgt", [128, 1], F32); kgt = sb(gst, "kgt", [128, 1], F32)
        cacc = sb(gst, "cacc", [16, L], F32)
        rhoA = sb(gst, "rhoA", [128, c.NJ], F32)
        phiA = sb(gst, "phiA", [128, c.NJ], F32)
        S.dma("pool", id_bf.t[:], ident, writes=[id_bf])
        S.dma("sp", id_f.t[:], ident, writes=[id_f])
        S.dma("sp", w01t.t[:], w01, writes=[w01t])
        S.dma("sp", qgt.t[:], qg, writes=[qgt])
        S.dma("sp", kgt.t[:], kg, writes=[kgt])
        S.op("dve", lambda e: e.memset(ones_bf.t[:], 1.0), writes=[ones_bf])
        S.op("dve", lambda e: e.memset(ones_f.t[:], 1.0), writes=[ones_f])
        S.op("dve", lambda e: e.memset(cst.t[:, 0:1], 1e-6), writes=[cst])
        S.op("dve", lambda e: e.memset(cst.t[:, 1:2], 1.0), writes=[cst])
        S.op("dve", lambda e: e.memset(cst.t[:, 2:3], 0.0), writes=[cst])
        EPS = cst.t[:, 0:1]
        ONE = cst.t[:, 1:2]

        rr = {"e": 0}

        def alt():
            rr["e"] ^= 1
            return "act" if rr["e"] else "dve"

        def copy(eng, out, in_, reads, writes):
            if eng == "act":
                S.op("act", lambda e: e.activation(out, in_, AF.Copy), reads=reads, writes=writes)
            else:
                S.op(eng, lambda e: e.tensor_copy(out, in_), reads=reads, writes=writes)

        def make_nt(st, tagp):
          xt = sb(st, tagp + "xt", [128, D], F32)
          xn = sb(st, tagp + "xn", [128, D], BF16)
          ss = sb(st, tagp + "ss", [128, 2], F32)

          def norm_transpose(src, n_rows, grep, panel, srcbuf=None):
            for t0 in range(0, n_rows, 128):
                r = min(128, n_rows - t0)
                S.dma("sp", xt.t[:r, :], src[t0:t0 + r, :], reads=[srcbuf] if srcbuf else [], writes=[xt])
                S.op("dve", lambda e: e.memset(ss.t[:r, 0:1], 0.0), writes=[ss])
                S.op("act", lambda e: e.activation(xn.t[:r, :], xt.t[:r, :], AF.Square, accum_out=ss.t[:r, 0:1]),
                     reads=[xt], writes=[xn, ss])
                S.op("dve", lambda e: e.tensor_scalar(ss.t[:r, 1:2], ss.t[:r, 0:1], 1.0 / D, 1e-6, ALU.mult, ALU.add),
                     reads=[ss], writes=[ss])
                S.op("act", lambda e: e.activation(ss.t[:r, 1:2], ss.t[:r, 1:2], AF.Sqrt), reads=[ss], writes=[ss])
                S.op("dve", lambda e: e.reciprocal(ss.t[:r, 1:2], ss.t[:r, 1:2]), reads=[ss], writes=[ss])
                S.op("dve", lambda e: e.scalar_tensor_tensor(xn.t[:r, :], xt.t[:r, :], ss.t[:r, 1:2], grep.t[:r, :],
                                                             ALU.mult, ALU.mult),
                     reads=[xt, ss, grep], writes=[xn])
                for k0 in range(0, KT, 8):
                    k1 = min(KT, k0 + 8)
                    pb = psb[(k0 // 8) % 2]
                    for kt in range(k0, k1):
                        S.op("pe", lambda e: e.transpose(pb.t[:, (kt - k0) * 128:(kt - k0) * 128 + r],
                                                         xn.t[:r, kt * 128:(kt + 1) * 128], id_bf.t[:r, :r]),
                             reads=[xn, id_bf], writes=[pb])
                    src_ap = pb.t[:, 0:(k1 - k0) * 128].rearrange("p (k r) -> p k r", r=128)[:, :, :r]
                    copy(alt(), panel.t[:, k0:k1, t0:t0 + r], src_ap, [pb], [panel])
          return norm_transpose

        wstate = {"i": 0}

        def gemm(mode, wbufs, wsrc, K, c0, ncols, act, actbufs, N, epi, kgroups=None, WB=None, banks=(0, 1, 2, 3),
                 wq="pool", wsrcbuf=None):
            KTl = K // 128
            WB = WB or wbufs[0].t.shape[2]
            kgroups = kgroups or [(0, KTl)]
            bi = 0
            for sb0 in range(0, ncols, WB):
                cw = min(WB, ncols - sb0)
                wt = wbufs[wstate["i"] % len(wbufs)]
                wstate["i"] += 1
                srcs = wsrc if isinstance(wsrc, list) else [(wsrc, K)]
                kbase = 0
                for (wap, Ks) in srcs:
                    for k0 in range(0, Ks // 128, 8):
                        k1 = min(Ks // 128, k0 + 8)
                        srcap = wap[k0 * 128:k1 * 128, c0 + sb0:c0 + sb0 + cw].rearrange("(kt p) c -> p kt c", p=128)
                        S.dma(wq, wt.t[:, kbase + k0:kbase + k1, :cw], srcap, reads=[wsrcbuf] if wsrcbuf else [],
                              writes=[wt], part=True)
                    kbase += Ks // 128
                if mode == "fm":
                    for j0 in range(0, cw, 128):
                        m = min(128, cw - j0)
                        for n0 in range(0, N, 512):
                            n1 = min(N, n0 + 512)
                            pss = []
                            for (ka, kb) in kgroups:
                                pb = ps[banks[bi % len(banks)]]
                                bi += 1
                                for kt in range(ka, kb):
                                    S.op("pe", lambda e: e.matmul(pb.t[:m, :n1 - n0], wt.t[:, kt, j0:j0 + m],
                                                                  act(kt, n0, n1), start=(kt == ka), stop=(kt == kb - 1)),
                                         reads=[wt] + actbufs, writes=[pb])
                                pss.append(pb)
                            epi(c0 + sb0 + j0, m, n0, n1, pss)
                else:
                    for t0 in range(0, N, 128):
                        t1 = min(N, t0 + 128)
                        pss = []
                        for (ka, kb) in kgroups:
                            pb = ps[banks[bi % len(banks)]]
                            bi += 1
                            for kt in range(ka, kb):
                                S.op("pe", lambda e: e.matmul(pb.t[:t1 - t0, :cw], act(kt, t0, t1), wt.t[:, kt, :cw],
                                                              start=(kt == ka), stop=(kt == kb - 1)),
                                     reads=[wt] + actbufs, writes=[pb])
                            pss.append(pb)
                        epi(t0, t1, c0 + sb0, cw, pss)

        def coeffs(st, tag, lr_ap, li_ap, ld_ap, F):
            lr = sb(st, tag + "lr", [128, F], F32); li = sb(st, tag + "li", [128, F], F32)
            ld = sb(st, tag + "ld", [128, F], F32)
            t1 = sb(st, tag + "t1", [128, F], F32); t2 = sb(st, tag + "t2", [128, F], F32)
            ti = sb(st, tag + "ti", [128, F], I32)
            rho = sb(st, tag + "rho", [128, F], F32); phi = sb(st, tag + "phi", [128, F], F32)
            sn = sb(st, tag + "sn", [128, F], F32); cs = sb(st, tag + "cs", [128, F], F32)
            fr = sb(st, tag + "fr", [128, F], F32); fi = sb(st, tag + "fi", [128, F], F32)

            def run(lr_src, li_src, ld_src):
                S.dma("sp", lr.t[:], lr_src, writes=[lr]); S.dma("sp", li.t[:], li_src, writes=[li])
                S.dma("sp", ld.t[:], ld_src, writes=[ld])
                S.op("act", lambda e: e.activation(ld.t[:], ld.t[:], AF.Exp), reads=[ld], writes=[ld])
                S.op("dve", lambda e: e.tensor_tensor(t1.t[:], lr.t[:], ld.t[:], ALU.mult), reads=[lr, ld], writes=[t1])
                S.op("act", lambda e: e.activation(rho.t[:], t1.t[:], AF.Exp), reads=[t1], writes=[rho])
                S.op("dve", lambda e: e.scalar_tensor_tensor(t1.t[:], li.t[:], 1.0 / TWO_PI, ld.t[:], ALU.mult, ALU.mult),
                     reads=[li, ld], writes=[t1])
                S.op("dve", lambda e: e.tensor_copy(ti.t[:], t1.t[:]), reads=[t1], writes=[ti])
                S.op("dve", lambda e: e.tensor_copy(t2.t[:], ti.t[:]), reads=[ti], writes=[t2])
                S.op("dve", lambda e: e.tensor_sub(phi.t[:], t1.t[:], t2.t[:]), reads=[t1, t2], writes=[phi])
                S.op("act", lambda e: e.activation(sn.t[:], phi.t[:], AF.Sin, scale=TWO_PI), reads=[phi], writes=[sn])
                S.op("dve", lambda e: e.tensor_scalar(t1.t[:], phi.t[:], 0.25, None, ALU.add), reads=[phi], writes=[t1])
                S.op("dve", lambda e: e.tensor_copy(ti.t[:], t1.t[:]), reads=[t1], writes=[ti])
                S.op("dve", lambda e: e.tensor_copy(t2.t[:], ti.t[:]), reads=[ti], writes=[t2])
                S.op("dve", lambda e: e.tensor_sub(t1.t[:], t1.t[:], t2.t[:]), reads=[t1, t2], writes=[t1])
                S.op("act", lambda e: e.activation(cs.t[:], t1.t[:], AF.Sin, scale=TWO_PI), reads=[t1], writes=[cs])
                S.op("dve", lambda e: e.tensor_tensor(cs.t[:], cs.t[:], rho.t[:], ALU.mult), reads=[cs, rho], writes=[cs])
                S.op("dve", lambda e: e.tensor_tensor(sn.t[:], sn.t[:], rho.t[:], ALU.mult), reads=[sn, rho], writes=[sn])
                S.op("dve", lambda e: e.tensor_scalar(t1.t[:], cs.t[:], -1.0, None, ALU.add), reads=[cs], writes=[t1])
                S.op("dve", lambda e: e.tensor_tensor(t2.t[:], lr.t[:], lr.t[:], ALU.mult), reads=[lr], writes=[t2])
                S.op("dve", lambda e: e.tensor_tensor(fr.t[:], li.t[:], li.t[:], ALU.mult), reads=[li], writes=[fr])
                S.op("dve", lambda e: e.tensor_add(t2.t[:], t2.t[:], fr.t[:]), reads=[t2, fr], writes=[t2])
                S.op("dve", lambda e: e.reciprocal(t2.t[:], t2.t[:]), reads=[t2], writes=[t2])
                S.op("dve", lambda e: e.tensor_tensor(fr.t[:], t1.t[:], lr.t[:], ALU.mult), reads=[t1, lr], writes=[fr])
                S.op("dve", lambda e: e.tensor_tensor(fi.t[:], sn.t[:], li.t[:], ALU.mult), reads=[sn, li], writes=[fi])
                S.op("dve", lambda e: e.tensor_add(fr.t[:], fr.t[:], fi.t[:]), reads=[fr, fi], writes=[fr])
                S.op("dve", lambda e: e.tensor_tensor(fr.t[:], fr.t[:], t2.t[:], ALU.mult), reads=[fr, t2], writes=[fr])
                S.op("dve", lambda e: e.tensor_tensor(fi.t[:], sn.t[:], lr.t[:], ALU.mult), reads=[sn, lr], writes=[fi])
                S.op("dve", lambda e: e.tensor_tensor(t1.t[:], t1.t[:], li.t[:], ALU.mult), reads=[t1, li], writes=[t1])
                S.op("dve", lambda e: e.tensor_sub(fi.t[:], fi.t[:], t1.t[:]), reads=[fi, t1], writes=[fi])
                S.op("dve", lambda e: e.tensor_tensor(fi.t[:], fi.t[:], t2.t[:], ALU.mult), reads=[fi, t2], writes=[fi])
            return run, rho, phi, fr, fi

        with ExitStack() as st:
            run, rho, phi, fr, fi = coeffs(st, "ca", None, None, None, c.NJ)
            run(lrA, liA, ldA)
            S.op("dve", lambda e: e.tensor_copy(rhoA.t[:], rho.t[:]), reads=[rho], writes=[rhoA])
            S.op("dve", lambda e: e.tensor_copy(phiA.t[:], phi.t[:]), reads=[phi], writes=[phiA])
        _barrier(S)
        with ExitStack() as st:
            FC = min(1024, c.NJ * 128)
            run, rho, phi, fr, fi = coeffs(st, "cb", None, None, None, FC)
            br = sb(st, "br", [128, FC], F32); bi_ = sb(st, "bi", [128, FC], F32)
            o1 = sb(st, "o1", [128, FC], F32); o2 = sb(st, "o2", [128, FC], F32)
            ob1 = sb(st, "ob1", [128, FC], BF16); ob2 = sb(st, "ob2", [128, FC], BF16)
            for f0 in range(0, c.NJ * 128, FC):
                run(lrB[:, f0:f0 + FC], liB[:, f0:f0 + FC], ldB[:, f0:f0 + FC])
                S.dma("sp", br.t[:], brB[:, f0:f0 + FC], writes=[br])
                S.dma("sp", bi_.t[:], biB[:, f0:f0 + FC], writes=[bi_])
                S.op("dve", lambda e: e.tensor_tensor(o1.t[:], fr.t[:], br.t[:], ALU.mult), reads=[fr, br], writes=[o1])
                S.op("dve", lambda e: e.tensor_tensor(o2.t[:], fi.t[:], bi_.t[:], ALU.mult), reads=[fi, bi_], writes=[o2])
                S.op("dve", lambda e: e.tensor_sub(ob1.t[:], o1.t[:], o2.t[:]), reads=[o1, o2], writes=[ob1])
                S.op("dve", lambda e: e.tensor_tensor(o1.t[:], fr.t[:], bi_.t[:], ALU.mult), reads=[fr, bi_], writes=[o1])
                S.op("dve", lambda e: e.tensor_tensor(o2.t[:], fi.t[:], br.t[:], ALU.mult), reads=[fi, br], writes=[o2])
                S.op("dve", lambda e: e.tensor_add(ob2.t[:], o1.t[:], o2.t[:]), reads=[o1, o2], writes=[ob2])
                S.dma("sp", BBRd.t[:, f0:f0 + FC], ob1.t[:], reads=[ob1], writes=[BBRd], part=True)
                S.dma("sp", BBId.t[:, f0:f0 + FC], ob2.t[:], reads=[ob2], writes=[BBId], part=True)

        def proj_phase(is_ctx):
            with ExitStack() as st:
                tag = "c" if is_ctx else "o"
                NPmax = c.NPAN + (c.NM if is_ctx else 0)
                panel = sb(st, tag + "pan", [128, KT, NPmax], BF16)
                grep = sb(st, tag + "g1", [128, D], F32)
                S.dma("sp", grep.t[:], g1rep, writes=[grep])
                wbufs = [sb(st, tag + f"w{i}", [128, KT, 256], BF16) for i in range(2)]
                stg = [sb(st, tag + f"stg{i}", [128, 512], BF16) for i in range(3)]
                stf = [sb(st, tag + f"stf{i}", [128, 512], F32) for i in range(3)]
                sq = sb(st, tag + "sq", [128, 512], BF16)
                rinv = sb(st, tag + "rinv", [128, 512], F32)
                bft = sb(st, tag + "bft", [16, 1], F32)
                lf = sb(st, tag + "lf", [16, 512], F32)
                ones_t = sb(st, tag + "ones", [16, 512], F32)
                S.dma("sp", bft.t[:H, :], bfg, writes=[bft])
                S.op("dve", lambda e: e.tensor_scalar(bft.t[:H, :], bft.t[:H, :], -1.0, None, ALU.mult), reads=[bft], writes=[bft])
                S.op("dve", lambda e: e.memset(ones_t.t[:], 1.0), writes=[ones_t])
                si = {"g": 0, "f": 0}
                src = x_ctx if is_ctx else x_own
                ntot = L if is_ctx else NO
                p0 = 0
                nt = make_nt(st, tag)
                while p0 < ntot:
                    pn = min(c.NPAN + (c.NM if (is_ctx and p0 == 0) else 0), ntot - p0)
                    nt(src[p0:p0 + pn, :], pn, grep, panel)

                    def act(kt, n0, n1):
                        return panel.t[:, kt, n0:n1]

                    def epi_qk(dst, gcol, colbase):
                        def epi(col, m, n0, n1, pss):
                            n = n1 - n0
                            h = (col - colbase) // 128
                            pb = pss[0]
                            S.op("act", lambda e: e.activation(sq.t[:, :n], pb.t[:, :n], AF.Square), reads=[pb], writes=[sq])
                            p2 = ps[4]
                            S.op("pe", lambda e: e.matmul(p2.t[:, :n], ones_bf.t[:], sq.t[:, :n], start=True, stop=True),
                                 reads=[ones_bf, sq], writes=[p2])
                            S.op("act", lambda e: e.activation(rinv.t[:, :n], p2.t[:, :n], AF.Sqrt, bias=EPS, scale=1.0 / 128),
                                 reads=[p2, cst], writes=[rinv])
                            S.op("dve", lambda e: e.reciprocal(rinv.t[:, :n], rinv.t[:, :n]), reads=[rinv], writes=[rinv])
                            sg = stg[si["g"] % 3]; si["g"] += 1
                            S.op("dve", lambda e: e.scalar_tensor_tensor(sg.t[:, :n], pb.t[:, :n], gcol.t[:, 0:1], rinv.t[:, :n],
                                                                         ALU.mult, ALU.mult),
                                 reads=[pb, gcol, rinv], writes=[sg])
                            S.dma("sp", dst.t[h, :, p0 + n0:p0 + n1], sg.t[:, :n], reads=[sg], writes=[dst], part=True)
                        return epi

                    def epi_store_bf(dst):
                        def epi(col, m, n0, n1, pss):
                            n = n1 - n0
                            sg = stg[si["g"] % 3]; si["g"] += 1
                            copy(alt(), sg.t[:m, :n], pss[0].t[:m, :n], [pss[0]], [sg])
                            S.dma("sp", dst.t[col:col + m, p0 + n0:p0 + n1], sg.t[:m, :n], reads=[sg], writes=[dst], part=True)
                        return epi

                    if is_ctx:
                        gemm("fm", wbufs, w_in, D, c.COL_K, c.AW, act, [panel], pn, epi_qk(KTd, kgt, c.COL_K))

                        def epi_v(t0, t1, col, cw, pss):
                            r = t1 - t0
                            sg = stg[si["g"] % 3]; si["g"] += 1
                            copy(alt(), sg.t[:r, :cw], pss[0].t[:r, :cw], [pss[0]], [sg])
                            S.dma("sp", Vd.t[p0 + t0:p0 + t1, col - c.COL_V:col - c.COL_V + cw], sg.t[:r, :cw],
                                  reads=[sg], writes=[Vd], part=True)
                        gemm("tm", wbufs, w_in, D, c.COL_V, c.AW, act, [panel], pn, epi_v)

                        def epi_u(col, m, n0, n1, pss):
                            n = n1 - n0
                            sg = stg[si["g"] % 3]; si["g"] += 1
                            copy(alt(), sg.t[:m, :n], pss[0].t[:m, :n], [pss[0]], [sg])
                            S.dma("sp", UTd.t[col - c.COL_U:col - c.COL_U + m, p0 + n0:p0 + n1], sg.t[:m, :n],
                                  reads=[sg], writes=[UTd], part=True)
                        gemm("fm", wbufs, w_in, D, c.COL_U, c.SW, act, [panel], pn, epi_u)

                        def epi_f(col, m, n0, n1, pss):
                            n = n1 - n0
                            pb = pss[0]
                            S.op("act", lambda e: e.activation(lf.t[:H, :n], pb.t[:H, :n], AF.Exp, bias=bft.t[:H, 0:1], scale=-1.0),
                                 reads=[pb, bft], writes=[lf])
                            S.op("act", lambda e: e.activation(lf.t[:H, :n], lf.t[:H, :n], AF.Ln, bias=ONE[:H, :], scale=1.0),
                                 reads=[lf, cst], writes=[lf])
                            S.op("dve", lambda e: e.tensor_scalar(lf.t[:H, :n], lf.t[:H, :n], -1.0, None, ALU.mult), reads=[lf], writes=[lf])
                            a0 = p0 + n0
                            init = 0.0 if a0 == 0 else cacc.t[:H, a0 - 1:a0]
                            S.op("dve", lambda e: e.tensor_tensor_scan(cacc.t[:H, a0:a0 + n], ones_t.t[:H, :n], lf.t[:H, :n], init,
                                                                       ALU.mult, ALU.add),
                                 reads=[ones_t, lf, cacc], writes=[cacc])
                        gemm("fm", wbufs, w_in, D, c.COL_F, H, act, [panel], pn, epi_f)
                    else:
                        gemm("fm", wbufs, w_in, D, c.COL_Q, c.AW, act, [panel], pn, epi_qk(QTd, qgt, c.COL_Q))

                        def epi_f32(dst, colbase, func):
                            def epi(col, m, n0, n1, pss):
                                n = n1 - n0
                                sf = stf[si["f"] % 3]; si["f"] += 1
                                S.op("act", lambda e: e.activation(sf.t[:m, :n], pss[0].t[:m, :n], func), reads=[pss[0]], writes=[sf])
                                S.dma("sp", dst.t[col - colbase:col - colbase + m, p0 + n0:p0 + n1], sf.t[:m, :n],
                                      reads=[sf], writes=[dst], part=True)
                            return epi
                        gemm("fm", wbufs, w_in, D, c.COL_U, c.SW, act, [panel], pn, epi_f32(UOd, c.COL_U, AF.Copy))
                        gemm("fm", wbufs, w_in, D, c.COL_GA, D, act, [panel], pn, epi_f32(GAd, c.COL_GA, AF.Sigmoid))
                        gemm("fm", wbufs, w_in, D, c.COL_GB, D, act, [panel], pn, epi_f32(GBd, c.COL_GB, AF.Sigmoid))
                    p0 += pn

        _barrier(S)
        proj_phase(True)
        _barrier(S)
        proj_phase(False)

        NKT = 2 * NB + 1

        def ktile(kt):
            return (0, 16) if kt == 0 else (16 + 128 * (kt - 1), 128)

        MAGIC = 12582912.0

        def ssm_gen(st):
            if True:
                LSM = 256 * c.SEGB + 16
                NSEG = NB // c.SEGB
                PC = min(2, c.SEGB)
                tl = sb(st, "tl", [128, LSM], F32); S.dma("sp", tl.t[:], t_loc, writes=[tl])
                onesL = sb(st, "onesL", [128, LSM], F32)
                S.op("dve", lambda e: e.memset(onesL.t[:], 1.0), writes=[onesL])
                hpi = sb(st, "hpi", [128, 1], F32)
                S.op("dve", lambda e: e.memset(hpi.t[:], math.pi / 2), writes=[hpi])
                dskt = sb(st, "dskt", [128, c.NCT], F32); S.dma("sp", dskt.t[:], dsk, writes=[dskt])
                uT = sb(st, "uT", [128, L], BF16)
                uo = sb(st, "uo", [128, NO], F32)
                wbr = sb(st, "wbr", [128, 512], BF16); wbi = sb(st, "wbi", [128, 512], BF16)
                wcr = sb(st, "wcr", [128, 512], BF16); wci = sb(st, "wci", [128, 512], BF16)
                zr = sb(st, "zr", [128, 4, LSM], BF16); nzi = sb(st, "nzi", [128, 4, LSM], BF16)
                z2 = sb(st, "z2", [128, 4, LSM], BF16); z4 = sb(st, "z4", [128, 4, LSM], BF16)
                nwcr = sb(st, "nwcr", [128, 512], BF16); nwci = sb(st, "nwci", [128, 512], BF16)
                rho_t = sb(st, "rho_t", [128, 4, LSM], F32)
                LA = 2
                sets = [[sb(st, f"s{k}_{i}", [128, LSM], F32) for i in range(9)] + [sb(st, f"ab{k}", [128, 1], F32)] for k in range(LA + 1)]
                carry = sb(st, "carry", [128, 4, 2], F32)
                sel = sb(st, "sel", [128, 256], F32)
                yf = sb(st, "yf", [128, 256], F32); zf = sb(st, "zf", [128, 256], F32); zb = sb(st, "zb", [128, 256], BF16)
                for ct in range(c.NCT):
                    S.dma("sp", uT.t[:], UTd.t[ct * 128:(ct + 1) * 128, :], reads=[UTd], writes=[uT])
                    S.dma("sp", uo.t[:], UOd.t[ct * 128:(ct + 1) * 128, :], reads=[UOd], writes=[uo])
                    cols = slice(ct * 512, (ct + 1) * 512)
                    S.dma("sp", wbr.t[:], BBRd.t[:, cols], reads=[BBRd], writes=[wbr])
                    S.dma("sp", wbi.t[:], BBId.t[:, cols], reads=[BBId], writes=[wbi])
                    S.dma("pool", wcr.t[:], crB[:, cols], writes=[wcr])
                    S.dma("pool", wci.t[:], ciB[:, cols], writes=[wci])
                    S.op("pool", lambda e: e.tensor_scalar(nwcr.t[:], wcr.t[:], -1.0, None, ALU.mult), reads=[wcr], writes=[nwcr])
                    S.op("pool", lambda e: e.tensor_scalar(nwci.t[:], wci.t[:], -1.0, None, ALU.mult), reads=[wci], writes=[nwci])
                    for jj in range(4):
                        j = 4 * ct + jj
                        S.op("act", lambda e: e.activation(rho_t.t[:, jj, :], onesL.t[:], AF.Copy, scale=rhoA.t[:, j:j + 1]),
                             reads=[onesL, rhoA], writes=[rho_t])
                    units = [(s_, jj_) for s_ in range(NSEG) for jj_ in range(4)]

                    def seginfo(s):
                        a0 = 0 if s == 0 else 16 + 256 * s * c.SEGB
                        a1 = 16 + 256 * (s + 1) * c.SEGB
                        return a0, a1 - a0, (16 if s == 0 else 0)

                    def stage1(u, s, jj):
                        a0, Ls, off = seginfo(s)
                        V = lambda t: t.t[:, :Ls]
                        j = 4 * ct + jj
                        xr, xi, ang, rr, sn, cs, bA, bB, bC, ab = sets[u % (LA + 1)]
                        for n0 in range(0, Ls, 512):
                            n1 = min(Ls, n0 + 512)
                            pr, pi = ps[5], ps[6]
                            S.op("pe", lambda e: e.matmul(pr.t[:, :n1 - n0], wbr.t[:, jj * 128:(jj + 1) * 128], uT.t[:, a0 + n0:a0 + n1],
                                                          start=True, stop=True), reads=[wbr, uT], writes=[pr])
                            S.op("pe", lambda e: e.matmul(pi.t[:, :n1 - n0], wbi.t[:, jj * 128:(jj + 1) * 128], uT.t[:, a0 + n0:a0 + n1],
                                                          start=True, stop=True), reads=[wbi, uT], writes=[pi])
                            copy("act", xr.t[:, n0:n1], pr.t[:, :n1 - n0], [pr], [xr])
                            copy("act", xi.t[:, n0:n1], pi.t[:, :n1 - n0], [pi], [xi])
                        S.op("act", lambda e: e.activation(ab.t[:], phiA.t[:, j:j + 1], AF.Copy, scale=float(a0)), reads=[phiA], writes=[ab])
                        S.op("act", lambda e: e.activation(V(ang), V(tl), AF.Identity, bias=ab.t[:, 0:1], scale=phiA.t[:, j:j + 1]),
                             reads=[tl, ab, phiA], writes=[ang])
                        S.op("dve", lambda e: e.tensor_scalar(V(rr), V(ang), MAGIC, MAGIC, ALU.add, ALU.subtract), reads=[ang], writes=[rr])
                        S.op("dve", lambda e: e.tensor_tensor(V(sn), V(ang), V(rr), ALU.subtract), reads=[ang, rr], writes=[sn])
                        S.op("act", lambda e: e.activation(V(cs), V(sn), AF.Abs), reads=[sn], writes=[cs])
                        S.op("act", lambda e: e.activation(V(cs), V(cs), AF.Sin, bias=hpi.t[:, 0:1], scale=-TWO_PI), reads=[cs, hpi], writes=[cs])
                        S.op("act", lambda e: e.activation(V(sn), V(sn), AF.Sin, scale=TWO_PI), reads=[sn], writes=[sn])

                    def stage2(u, s, jj):
                        a0, Ls, off = seginfo(s)
                        V = lambda t: t.t[:, :Ls]
                        xr, xi, ang, rr, sn, cs, bA, bB, bC, ab = sets[u % (LA + 1)]
                        S.op("dve", lambda e: e.tensor_tensor(V(bA), V(cs), V(xr), ALU.mult), reads=[cs, xr], writes=[bA])
                        S.op("dve", lambda e: e.tensor_tensor(V(bB), V(cs), V(xi), ALU.mult), reads=[cs, xi], writes=[bB])
                        S.op("dve", lambda e: e.tensor_tensor(V(bC), V(sn), V(xi), ALU.mult), reads=[sn, xi], writes=[bC])
                        S.op("dve", lambda e: e.tensor_tensor(V(rr), V(sn), V(xr), ALU.mult), reads=[sn, xr], writes=[rr])
                        S.op("dve", lambda e: e.tensor_add(V(bA), V(bA), V(bC)), reads=[bA, bC], writes=[bA])
                        S.op("dve", lambda e: e.tensor_sub(V(bB), V(bB), V(rr)), reads=[bB, rr], writes=[bB])
                        ir = 0.0 if s == 0 else carry.t[:, jj, 0:1]
                        ii = 0.0 if s == 0 else carry.t[:, jj, 1:2]
                        S.op("dve", lambda e: e.tensor_tensor_scan(V(xr), rho_t.t[:, jj, :Ls], V(bA), ir, ALU.mult, ALU.add),
                             reads=[rho_t, bA, carry], writes=[xr])
                        S.op("dve", lambda e: e.tensor_tensor_scan(V(xi), rho_t.t[:, jj, :Ls], V(bB), ii, ALU.mult, ALU.add),
                             reads=[rho_t, bB, carry], writes=[xi])
                        S.op("dve", lambda e: e.tensor_tensor(zr.t[:, jj, :Ls], V(cs), V(xr), ALU.mult), reads=[cs, xr], writes=[zr])
                        S.op("dve", lambda e: e.tensor_tensor(z2.t[:, jj, :Ls], V(sn), V(xi), ALU.mult), reads=[sn, xi], writes=[z2])
                        S.op("dve", lambda e: e.tensor_tensor(nzi.t[:, jj, :Ls], V(sn), V(xr), ALU.mult), reads=[sn, xr], writes=[nzi])
                        S.op("dve", lambda e: e.tensor_tensor(z4.t[:, jj, :Ls], V(cs), V(xi), ALU.mult), reads=[cs, xi], writes=[z4])
                        S.op("dve", lambda e: e.tensor_copy(carry.t[:, jj, 0:1], xr.t[:, Ls - 1:Ls]), reads=[xr], writes=[carry])
                        S.op("dve", lambda e: e.tensor_copy(carry.t[:, jj, 1:2], xi.t[:, Ls - 1:Ls]), reads=[xi], writes=[carry])
                        if jj == 3:
                            ypart(s)

                    def ypart(s):
                        a0, Ls, off = seginfo(s)
                        for q0 in range(0, c.SEGB, PC):
                            n = 256 * PC
                            l0 = off + 256 * q0
                            pb = ps[7]
                            for jj in range(4):
                                wsl = slice(jj * 128, (jj + 1) * 128)
                                S.op("pe", lambda e: e.matmul(pb.t[:, :n], wcr.t[:, wsl], zr.t[:, jj, l0:l0 + n],
                                                              start=(jj == 0), stop=False), reads=[wcr, zr], writes=[pb])
                                S.op("pe", lambda e: e.matmul(pb.t[:, :n], nwcr.t[:, wsl], z2.t[:, jj, l0:l0 + n],
                                                              start=False, stop=False), reads=[nwcr, z2], writes=[pb])
                                S.op("pe", lambda e: e.matmul(pb.t[:, :n], nwci.t[:, wsl], nzi.t[:, jj, l0:l0 + n],
                                                              start=False, stop=False), reads=[nwci, nzi], writes=[pb])
                                S.op("pe", lambda e: e.matmul(pb.t[:, :n], nwci.t[:, wsl], z4.t[:, jj, l0:l0 + n],
                                                              start=False, stop=(jj == 3)), reads=[nwci, z4], writes=[pb])
                            no = 128 * PC
                            o0 = (s * c.SEGB + q0) * 128
                            p4 = pb.t[:, :n].rearrange("p (j two r) -> p j two r", two=2, r=128)
                            s3 = sel.t[:, :no].rearrange("p (j r) -> p j r", r=128)
                            S.op("dve", lambda e: e.tensor_scalar(s3, p4[:, :, 0, :], w01t.t[:, 0:1], None, ALU.mult), reads=[pb, w01t], writes=[sel])
                            S.op("dve", lambda e: e.scalar_tensor_tensor(s3, p4[:, :, 1, :], w01t.t[:, 1:2], s3, ALU.mult, ALU.add),
                                 reads=[pb, w01t, sel], writes=[sel])
                            S.op("dve", lambda e: e.scalar_tensor_tensor(yf.t[:, :no], uo.t[:, o0:o0 + no], dskt.t[:, ct:ct + 1], sel.t[:, :no],
                                                                         ALU.mult, ALU.add), reads=[uo, dskt, sel], writes=[yf])
                            S.op("act", lambda e: e.activation(zf.t[:, :no], yf.t[:, :no], AF.Gelu), reads=[yf], writes=[zf])
                            S.op("dve", lambda e: e.tensor_copy(zb.t[:, :no], zf.t[:, :no]), reads=[zf], writes=[zb])
                            S.dma("sp", ZFd.t[ct * 128:(ct + 1) * 128, o0:o0 + no], zf.t[:, :no], reads=[zf], writes=[ZFd], part=True)
                            S.dma("sp", ZTd.t[ct * 128:(ct + 1) * 128, o0:o0 + no], zb.t[:, :no], reads=[zb], writes=[ZTd], part=True)

                    for i in range(len(units) + LA):
                        if i < len(units):
                            stage1(i, *units[i])
                        if i >= LA:
                            stage2(i - LA, *units[i - LA])
                        yield

        def glu_phase():
            with ExitStack() as st:
                NP = c.NPAN
                zp = sb(st, "zp", [128, c.NCT, NP], BF16)
                wbufs = [sb(st, f"gw{i}", [128, c.NCT, 512], BF16) for i in range(2)]
                sg = sb(st, "gsg", [128, 512], F32); zt = sb(st, "gzt", [128, 512], F32); ob = sb(st, "gob", [128, 512], BF16)
                for p0 in range(0, NO, NP):
                    S.dma("sp", zp.t[:], ZTd.t[:, p0:p0 + NP].rearrange("(k p) n -> p k n", p=128), reads=[ZTd], writes=[zp])

                    def epi(col, m, n0, n1, pss):
                        n = n1 - n0
                        S.op("act", lambda e: e.activation(sg.t[:m, :n], pss[0].t[:m, :n], AF.Sigmoid), reads=[pss[0]], writes=[sg])
                        S.dma("sp", zt.t[:m, :n], ZFd.t[col:col + m, p0 + n0:p0 + n1], reads=[ZFd], writes=[zt])
                        S.op("dve", lambda e: e.tensor_tensor(ob.t[:m, :n], zt.t[:m, :n], sg.t[:m, :n], ALU.mult), reads=[zt, sg], writes=[ob])
                        S.dma("sp", SSTd.t[col:col + m, p0 + n0:p0 + n1], ob.t[:m, :n], reads=[ob], writes=[SSTd], part=True)
                    gemm("fm", wbufs, w_glu, c.SW, 0, c.SW, lambda kt, n0, n1: zp.t[:, kt, n0:n1], [zp], NP, epi)

        HG = min(2, H)
        QG = min(4, NB)

        def attn_prep(stp, st):
            if True:
                cball = sb(stp, "cball", [128, NKT, H], F32)
                mA = sb(stp, "mA", [128, 128], BF16); mB = sb(stp, "mB", [128, 128], BF16)
                cq = sb(st, "cq", [16, NO], F32)
                c3 = [sb(st, f"c3_{i}", [16, NO], BF16) for i in range(3)]
                r1 = sb(st, "cr1", [16, NO], F32)
                S.dma("pool", mA.t[:], maskA, writes=[mA]); S.dma("pool", mB.t[:], maskB, writes=[mB])
                for kt in range(NKT):
                    a, nk = ktile(kt)
                    pb = ps[kt % 2]
                    S.op("pe", lambda e: e.transpose(pb.t[:nk, :H], cacc.t[:H, a:a + nk], id_f.t[:H, :H]), reads=[cacc, id_f], writes=[pb])
                    S.op("dve", lambda e: e.tensor_scalar(cball.t[:nk, kt, :], pb.t[:nk, :H], -1.0, -STAB, ALU.mult, ALU.add),
                         reads=[pb], writes=[cball])
                v4 = cacc.t[:H, 16:16 + 256 * NB].rearrange("h (j two r) -> h j two r", two=2, r=128)
                d3 = cq.t[:H, :].rearrange("h (j r) -> h j r", r=128)
                S.op("dve", lambda e: e.tensor_scalar(d3, v4[:, :, 0, :], w01t.t[:H, 0:1], None, ALU.mult), reads=[cacc, w01t], writes=[cq])
                S.op("dve", lambda e: e.scalar_tensor_tensor(d3, v4[:, :, 1, :], w01t.t[:H, 1:2], d3, ALU.mult, ALU.add),
                     reads=[cacc, w01t, cq], writes=[cq])
                S.op("dve", lambda e: e.tensor_scalar(cq.t[:H, :], cq.t[:H, :], math.sqrt(128.0), None, ALU.mult), reads=[cq], writes=[cq])
                S.op("dve", lambda e: e.tensor_copy(c3[0].t[:H, :], cq.t[:H, :]), reads=[cq], writes=[c3[0]])
                S.op("dve", lambda e: e.tensor_sub(r1.t[:H, :], cq.t[:H, :], c3[0].t[:H, :]), reads=[cq, c3[0]], writes=[r1])
                S.op("dve", lambda e: e.tensor_copy(c3[1].t[:H, :], r1.t[:H, :]), reads=[r1], writes=[c3[1]])
                S.op("dve", lambda e: e.tensor_sub(r1.t[:H, :], r1.t[:H, :], c3[1].t[:H, :]), reads=[r1, c3[1]], writes=[r1])
                S.op("dve", lambda e: e.tensor_copy(c3[2].t[:H, :], r1.t[:H, :]), reads=[r1], writes=[c3[2]])
                for i in range(3):
                    S.dma("sp", CQ3d.t[i], c3[i].t[:H, :], reads=[c3[i]], writes=[CQ3d], part=True)
                S.dma("sp", CQd.t[:, :], cq.t[:H, :], reads=[cq], writes=[CQd])
            return cball, mA, mB

        def attn_gen(st, cball, mA, mB):
            if True:
                vg = sb(st, "vg", [128, NKT, HG * 128], BF16)
                kh = [sb(st, f"kh{i}", [128, L], BF16) for i in range(2)]
                qh = [sb(st, f"qh{i}", [128, NO], BF16) for i in range(2)]
                cqh = [sb(st, f"cqh{i}", [3, NO], BF16) for i in range(2)]
                pt = [sb(st, f"pt{i}", [128, 512], BF16) for i in range(3)]
                rs = [sb(st, f"rs{i}", [128, 512], F32) for i in range(2)]
                ao = [sb(st, f"ao{i}", [128, NO], BF16) for i in range(2)]
                pti = 0
                gi = 0
                scale = 1.0 / math.sqrt(128.0)
                for hg in range(0, H, HG):
                    S.dma("sp", vg.t[:16, 0, :], Vd.t[0:16, hg * 128:(hg + HG) * 128], reads=[Vd], writes=[vg], part=True)
                    for t0 in range(0, 2 * NB, 8):
                        t1 = min(2 * NB, t0 + 8)
                        S.dma("sp", vg.t[:, 1 + t0:1 + t1, :],
                              Vd.t[16 + 128 * t0:16 + 128 * t1, hg * 128:(hg + HG) * 128].rearrange("(t p) c -> p t c", p=128),
                              reads=[Vd], writes=[vg], part=True)
                    for h in range(hg, hg + HG):
                        khb, qhb, cqb, aob = kh[h % 2], qh[h % 2], cqh[h % 2], ao[h % 2]
                        S.dma("sp", khb.t[:], KTd.t[h], reads=[KTd], writes=[khb])
                        S.dma("sp", qhb.t[:], QTd.t[h], reads=[QTd], writes=[qhb])
                        S.dma("sp", cqb.t[:], CQ3d.t[:, h, :], reads=[CQ3d], writes=[cqb])
                        for g in range(0, NB, QG):
                            OT = ps[2 + gi % 2]; SM = ps[4]
                            gi += 1
                            W = QG * 128
                            ktmax = 2 * (g + QG - 1) + 2
                            pend = None
                            for kt in range(0, ktmax + 2):
                                if kt <= ktmax:
                                    a, nk = ktile(kt)
                                    jmin = max(g, (kt - 1) // 2) if kt > 0 else g
                                    q0 = (jmin - g) * 128
                                    qa, qb = g * 128 + q0, (g + QG) * 128
                                    STb = ps[kt % 2]
                                    S.op("pe", lambda e: e.matmul(STb.t[:nk, q0:W], khb.t[:, a:a + nk], qhb.t[:, qa:qb], start=True, stop=False),
                                         reads=[khb, qhb], writes=[STb])
                                    S.op("pe", lambda e: e.matmul(STb.t[:nk, q0:W], ones_bf.t[0:3, :nk], cqb.t[0:3, qa:qb], start=False, stop=True),
                                         reads=[ones_bf, cqb], writes=[STb])
                                    p = pt[pti % 3]; pti += 1
                                    S.op("act", lambda e: e.activation(p.t[:nk, q0:W], STb.t[:nk, q0:W], AF.Exp, bias=cball.t[:nk, kt, h:h + 1], scale=scale),
                                         reads=[STb, cball], writes=[p])
                                    if kt >= 1 and kt % 2 == 1:
                                        j = (kt - 1) // 2
                                        if g <= j < g + QG:
                                            cs_ = slice((j - g) * 128, (j - g + 1) * 128)
                                            S.op("pool", lambda e: e.tensor_tensor(p.t[:, cs_], p.t[:, cs_], mA.t[:, :], ALU.mult), reads=[p, mA], writes=[p])
                                    if kt >= 2 and kt % 2 == 0:
                                        j = (kt - 2) // 2
                                        if g <= j < g + QG:
                                            cs_ = slice((j - g) * 128, (j - g + 1) * 128)
                                            S.op("pool", lambda e: e.tensor_tensor(p.t[:, cs_], p.t[:, cs_], mB.t[:, :], ALU.mult), reads=[p, mB], writes=[p])
                                    cur = (kt, nk, q0, p)
                                else:
                                    cur = None
                                if pend is not None:
                                    kt_, nk_, q0_, p_ = pend
                                    last = kt_ == ktmax
                                    S.op("pe", lambda e: e.matmul(OT.t[:, q0_:W], vg.t[:nk_, kt_, (h - hg) * 128:(h - hg + 1) * 128], p_.t[:nk_, q0_:W],
                                                                  start=(kt_ == 0), stop=last), reads=[vg, p_], writes=[OT])
                                    S.op("pe", lambda e: e.matmul(SM.t[:, q0_:W], ones_bf.t[:nk_, :], p_.t[:nk_, q0_:W], start=(kt_ == 0), stop=last),
                                         reads=[ones_bf, p_], writes=[SM])
                                pend = cur
                                if kt % 2 == 1:
                                    yield
                            r = rs[gi % 2]
                            S.op("dve", lambda e: e.reciprocal(r.t[:, :W], SM.t[:, :W]), reads=[SM], writes=[r])
                            S.op("dve", lambda e: e.tensor_tensor(aob.t[:, g * 128:g * 128 + W], OT.t[:, :W], r.t[:, :W], ALU.mult), reads=[OT, r], writes=[aob])
                        S.dma("sp", ATd.t[h * 128:(h + 1) * 128, :], aob.t[:], reads=[aob], writes=[ATd], part=True)

        _barrier(S)
        with ExitStack() as stp:
            with ExitStack() as st0:
                cball_, mA_, mB_ = attn_prep(stp, st0)
            _barrier(S)
            with ExitStack() as stj:
                g1_ = ssm_gen(stj)
                g2_ = attn_gen(stj, cball_, mA_, mB_)
                live = [g1_, g2_]
                while live:
                    for g_ in list(live):
                        try:
                            next(g_)
                        except StopIteration:
                            live.remove(g_)
        _barrier(S)
        glu_phase()

        def mix_phase():
            with ExitStack() as st:
                NP = min(1024, NO)
                KA, KS = c.AW // 128, c.SW // 128
                cat = sb(st, "cat", [128, KA + KS, NP], BF16)
                mixT = sb(st, "mixT", [128, KT, NP], BF16)
                wbufs = [sb(st, f"mw{i}", [128, max(KT, KA + KS), 256], BF16) for i in range(2)]
                ga = sb(st, "ga", [128, 512], F32); gb = sb(st, "gb", [128, 512], F32)
                m1 = sb(st, "m1", [128, 512], F32); m2 = sb(st, "m2", [128, 512], F32)
                xo = [sb(st, f"xo{i}", [128, 512], F32) for i in range(2)]
                ho = [sb(st, f"ho{i}", [128, 512], F32) for i in range(2)]
                k = {"i": 0}
                for p0 in range(0, NO, NP):
                    S.dma("sp", cat.t[:, 0:KA, :], ATd.t[:, p0:p0 + NP].rearrange("(k p) n -> p k n", p=128), reads=[ATd], writes=[cat], part=True)
                    S.dma("sp", cat.t[:, KA:KA + KS, :], SSTd.t[:, p0:p0 + NP].rearrange("(k p) n -> p k n", p=128), reads=[SSTd], writes=[cat], part=True)

                    def epi(col, m, n0, n1, pss):
                        n = n1 - n0
                        S.dma("sp", ga.t[:m, :n], GAd.t[col:col + m, p0 + n0:p0 + n1], reads=[GAd], writes=[ga])
                        S.dma("sp", gb.t[:m, :n], GBd.t[col:col + m, p0 + n0:p0 + n1], reads=[GBd], writes=[gb])
                        S.op("dve", lambda e: e.tensor_tensor(m1.t[:m, :n], ga.t[:m, :n], pss[0].t[:m, :n], ALU.mult), reads=[ga, pss[0]], writes=[m1])
                        S.op("dve", lambda e: e.tensor_tensor(m2.t[:m, :n], gb.t[:m, :n], pss[1].t[:m, :n], ALU.mult), reads=[gb, pss[1]], writes=[m2])
                        S.op("pool", lambda e: e.tensor_add(mixT.t[:m, col // 128, n0:n1], m1.t[:m, :n], m2.t[:m, :n]), reads=[m1, m2], writes=[mixT])
                    gemm("fm", wbufs, [(w_a, c.AW), (w_b, c.SW)], c.AW + c.SW, 0, D, lambda kt, n0, n1: cat.t[:, kt, n0:n1], [cat], NP, epi,
                         kgroups=[(0, KA), (KA, KA + KS)])

                    def epi2(t0, t1, col, cw, pss):
                        r = t1 - t0
                        x_ = xo[k["i"] % 2]; h_ = ho[k["i"] % 2]; k["i"] += 1
                        S.dma("sp", x_.t[:r, :cw], x_own[p0 + t0:p0 + t1, col:col + cw], writes=[x_])
                        S.op("dve", lambda e: e.tensor_tensor(h_.t[:r, :cw], x_.t[:r, :cw], pss[0].t[:r, :cw], ALU.add), reads=[x_, pss[0]], writes=[h_])
                        S.dma("act", H1d.t[p0 + t0:p0 + t1, col:col + cw], h_.t[:r, :cw], reads=[h_], writes=[H1d], part=True)
                    gemm("tm", wbufs, w_out, D, 0, D, lambda kt, t0, t1: mixT.t[:, kt, t0:t1], [mixT], NP, epi2)

        _barrier(S)
        mix_phase()

        def peer_phase():
            with ExitStack() as st:
                NP = c.PPAN
                NTT = NP // 128
                pan = sb(st, "ppan", [128, KT, NP], BF16)
                et = [sb(st, f"et{i}", [128, 16, 128], F32) for i in range(NTT)]
                thr = sb(st, "thr", [128, NTT, 8], F32); rz = sb(st, "rz", [128, NTT, 8], F32)
                for p0 in range(0, NO, NP):
                    _barrier(S)
                    with ExitStack() as s1:
                        grep = sb(s1, "g2", [128, D], F32); S.dma("sp", grep.t[:], g2rep, writes=[grep])
                        nt = make_nt(s1, "p")
                        nt(H1d.t[p0:p0 + NP, :], NP, grep, pan, srcbuf=H1d)
                    _barrier(S)
                    with ExitStack() as s2:
                        qpT = sb(s2, "qpT", [128, 16, NP], BF16)
                        skb = sb(s2, "skb", [128, 256], BF16); S.dma("pool", skb.t[:], skT, writes=[skb])
                        wbufs = [sb(s2, f"pw{i}", [128, KT, 256], BF16) for i in range(2)]
                        sc = sb(s2, "sc", [128, 16, 128], F32)
                        wk = sb(s2, "wk", [128, 256], F32)
                        m16 = sb(s2, "m16", [128, 16, 16], F32); e16 = sb(s2, "e16", [128, 16, 16], F32)
                        nm = sb(s2, "nm", [128, 16], F32)
                        cand = sb(s2, "cand", [128, 256], F32); c16 = sb(s2, "c16", [128, 16], F32)

                        def epi_q(col, m, n0, n1, pss):
                            copy(alt(), qpT.t[:, col // 128, n0:n1], pss[0].t[:, :n1 - n0], [pss[0]], [qpT])
                        gemm("fm", wbufs, w_query, D, 0, c.QW, lambda kt, n0, n1: pan.t[:, kt, n0:n1], [pan], NP, epi_q)
                        for tt in range(NTT):
                            ts_ = slice(tt * 128, (tt + 1) * 128)
                            for hc in range(16):
                                pb = ps[hc // 4]
                                S.op("pe", lambda e: e.matmul(pb.t[:, (hc % 4) * 128:(hc % 4 + 1) * 128], qpT.t[:, hc, ts_],
                                                              skb.t[:, (hc % 2) * 128:(hc % 2 + 1) * 128], start=True, stop=True),
                                     reads=[qpT, skb], writes=[pb])
                                if hc % 4 == 3:
                                    copy(alt(), sc.t[:, hc - 3:hc + 1, :], pb.t[:, :].rearrange("p (a b) -> p a b", b=128), [pb], [sc])
                            for hc in range(16):
                                S.op("dve", lambda e: e.max(m16.t[:, hc, 0:8], sc.t[:, hc, :]), reads=[sc], writes=[m16])
                                S.op("dve", lambda e: e.match_replace(wk.t[:, :128], m16.t[:, hc, 0:8], sc.t[:, hc, :], -1e30),
                                     reads=[sc, m16], writes=[wk])
                                S.op("dve", lambda e: e.max(m16.t[:, hc, 8:16], wk.t[:, :128]), reads=[wk], writes=[m16])
                            S.op("dve", lambda e: e.tensor_scalar(nm.t[:, :], m16.t[:, :, 0], -1.0, None, ALU.mult), reads=[m16], writes=[nm])
                            for hc in range(16):
                                S.op("act", lambda e: e.activation(et[tt].t[:, hc, :], sc.t[:, hc, :], AF.Exp, bias=nm.t[:, hc:hc + 1], scale=1.0),
                                     reads=[sc, nm], writes=[et[tt]])
                                S.op("act", lambda e: e.activation(e16.t[:, hc, :], m16.t[:, hc, :], AF.Exp, bias=nm.t[:, hc:hc + 1], scale=1.0),
                                     reads=[m16, nm], writes=[e16])
                            for h in range(8):
                                c3 = cand.t[:, :].rearrange("p (a b) -> p a b", b=16)
                                S.op("dve", lambda e: e.tensor_tensor(c3, e16.t[:, 2 * h, :].unsqueeze(2).to_broadcast([128, 16, 16]),
                                                                      e16.t[:, 2 * h + 1, :].unsqueeze(1).to_broadcast([128, 16, 16]), ALU.mult),
                                     reads=[e16], writes=[cand])
                                S.op("dve", lambda e: e.max(c16.t[:, 0:8], cand.t[:, :]), reads=[cand], writes=[c16])
                                S.op("dve", lambda e: e.match_replace(wk.t[:, :], c16.t[:, 0:8], cand.t[:, :], -1.0), reads=[cand, c16], writes=[wk])
                                S.op("dve", lambda e: e.max(c16.t[:, 8:16], wk.t[:, :]), reads=[wk], writes=[c16])
                                S.op("dve", lambda e: e.tensor_scalar(thr.t[:, tt, h:h + 1], c16.t[:, 15:16], 1.0 - 1e-5, None, ALU.mult),
                                     reads=[c16], writes=[thr])
                                S.op("dve", lambda e: e.tensor_reduce(rz.t[:, tt, h:h + 1], c16.t[:, :], AX.X, ALU.add), reads=[c16], writes=[rz])
                            S.op("dve", lambda e: e.reciprocal(rz.t[:, tt, :], rz.t[:, tt, :]), reads=[rz], writes=[rz])
                            for h in range(8):
                                S.op("dve", lambda e: e.tensor_scalar(e16.t[:, 2 * h, :], e16.t[:, 2 * h, :], rz.t[:, tt, h:h + 1], None, ALU.mult),
                                     reads=[e16, rz], writes=[e16])
                                S.op("dve", lambda e: e.tensor_scalar(et[tt].t[:, 2 * h, :], et[tt].t[:, 2 * h, :], rz.t[:, tt, h:h + 1], None, ALU.mult),
                                     reads=[et[tt], rz], writes=[et[tt]])
                                c3 = cand.t[:, :].rearrange("p (a b) -> p a b", b=16)
                                S.op("dve", lambda e: e.tensor_tensor(c3, e16.t[:, 2 * h, :].unsqueeze(2).to_broadcast([128, 16, 16]),
                                                                      e16.t[:, 2 * h + 1, :].unsqueeze(1).to_broadcast([128, 16, 16]), ALU.mult),
                                     reads=[e16], writes=[cand])
                                S.op("dve", lambda e: e.max(c16.t[:, 0:8], cand.t[:, :]), reads=[cand], writes=[c16])
                                S.op("dve", lambda e: e.match_replace(wk.t[:, :], c16.t[:, 0:8], cand.t[:, :], -1.0), reads=[cand, c16], writes=[wk])
                                S.op("dve", lambda e: e.max(c16.t[:, 8:16], wk.t[:, :]), reads=[wk], writes=[c16])
                                S.op("dve", lambda e: e.tensor_scalar(thr.t[:, tt, h:h + 1], c16.t[:, 15:16], 1.0 - 1e-5, None, ALU.mult),
                                     reads=[c16], writes=[thr])
                    _barrier(S)
                    with ExitStack() as s3:
                        wbufs = [sb(s3, f"aw{i}", [128, KT, 512], BF16) for i in range(2)]
                        gT = [sb(s3, f"gT{i}", [128, 4, NP], F32) for i in range(2)]
                        Mb = [sb(s3, f"Mb{i}", [128, 512], F32) for i in range(3)]
                        Mh = [sb(s3, f"Mh{i}", [128, 512], BF16) for i in range(3)]
                        wst = [sb(s3, f"wst{i}", [128, 4, NP], BF16) for i in range(2)]
                        k = {"m": 0}
                        NSB = c.PN // 512
                        WT = [ps[4 + b] for b in range(4)]

                        def load_w(sbi):
                            wt = wbufs[sbi % 2]
                            for k0 in range(0, KT, 8):
                                k1 = min(KT, k0 + 8)
                                srcap = euT[k0 * 128:k1 * 128, sbi * 512:(sbi + 1) * 512].rearrange("(kt p) c -> p kt c", p=128)
                                S.dma("pool", wt.t[:, k0:k1, :], srcap, writes=[wt], part=True)

                        def issue_AT(sbi):
                            wt = wbufs[sbi % 2]
                            g_ = gT[sbi % 2]
                            for j in range(4):
                                pb = ps[j]
                                for kt in range(KT):
                                    S.op("pe", lambda e: e.matmul(pb.t[:, :NP], wt.t[:, kt, j * 128:(j + 1) * 128], pan.t[:, kt, 0:NP],
                                                                  start=(kt == 0), stop=(kt == KT - 1)), reads=[wt, pan], writes=[pb])
                                    if kt % 4 == 3 and kt != KT - 1:
                                        yield
                                S.op("act", lambda e: e.activation(g_.t[:, j, :], pb.t[:, :NP], AF.Gelu), reads=[pb], writes=[g_])
                                yield

                        load_w(0)
                        if NSB > 1:
                            load_w(1)
                        for _ in issue_AT(0):
                            pass
                        for sbi in range(NSB):
                            if sbi + 2 < NSB:
                                load_w(sbi + 2)
                            nxt = issue_AT(sbi + 1) if sbi + 1 < NSB else None
                            g_ = gT[sbi % 2]
                            col0 = sbi * 512
                            i1a = col0 // 128
                            pairs = [(tt, h) for tt in range(NTT) for h in range(8)]
                            held = []
                            for i in range(len(pairs) + 1):
                                if i < len(pairs):
                                    tt, h = pairs[i]
                                    M_ = Mb[k["m"] % 3]; Mh_ = Mh[k["m"] % 3]; k["m"] += 1
                                    P3 = M_.t[:, :].rearrange("p (a b) -> p a b", b=128)
                                    S.op("dve", lambda e: e.tensor_tensor(P3, et[tt].t[:, 2 * h, i1a:i1a + 4].unsqueeze(2).to_broadcast([128, 4, 128]),
                                                                          et[tt].t[:, 2 * h + 1, :].unsqueeze(1).to_broadcast([128, 4, 128]), ALU.mult),
                                         reads=[et[tt]], writes=[M_])
                                    held.append((tt, h, M_, Mh_))
                                if i >= 1:
                                    tt, h, M_, Mh_ = held.pop(0)
                                    S.op("dve", lambda e: e.scalar_tensor_tensor(Mh_.t[:, :], M_.t[:, :], thr.t[:, tt, h:h + 1], M_.t[:, :],
                                                                                 ALU.is_ge, ALU.mult), reads=[M_, thr], writes=[Mh_])
                                    for b in range(4):
                                        S.op("pe", lambda e: e.matmul(WT[b].t[:, tt * 128:(tt + 1) * 128], Mh_.t[:, b * 128:(b + 1) * 128],
                                                                      id_bf.t[:, :], start=(h == 0), stop=(h == 7)),
                                             reads=[Mh_, id_bf], writes=[WT[b]])
                                    if nxt is not None:
                                        try:
                                            next(nxt)
                                        except StopIteration:
                                            nxt = None
                            if nxt is not None:
                                for _ in nxt:
                                    pass
                            ws = wst[sbi % 2]
                            for b in range(4):
                                S.op("dve", lambda e: e.tensor_tensor(ws.t[:, b, :], WT[b].t[:, :NP], g_.t[:, b, :], ALU.mult),
                                     reads=[WT[b], g_], writes=[ws])
                                S.dma("act", WGTd.t[col0 + b * 128:col0 + (b + 1) * 128, p0:p0 + NP], ws.t[:, b, :], reads=[ws], writes=[WGTd], part=True)

            _barrier(S)
            with ExitStack() as st:
                NY = min(1024, NO)
                NYT = NY // 128
                EC = 16
                wv = [sb(st, f"yv{i}", [128, EC, 512], BF16) for i in range(2)]
                wa = [sb(st, f"ya{i}", [128, EC, NY], BF16) for i in range(2)]
                h1t = [sb(st, f"yh{i}", [128, 512], F32) for i in range(2)]
                yo = [sb(st, f"yo{i}", [128, 512], F32) for i in range(2)]
                NEC = c.PN // (128 * EC)
                gi = 0
                ci = 0
                oi = 0
                for cb in range(0, D, 512):
                    for p0 in range(0, NO, NY):
                        bks = [ps[i] for i in range(NYT)]
                        gi += 1
                        for ec in range(NEC):
                            e0 = ec * EC * 128
                            v_, a_ = wv[ci % 2], wa[ci % 2]
                            ci += 1
                            for k0 in range(0, EC, 8):
                                S.dma("pool", v_.t[:, k0:k0 + 8, :], ev[e0 + k0 * 128:e0 + (k0 + 8) * 128, cb:cb + 512].rearrange("(k p) c -> p k c", p=128),
                                      writes=[v_], part=True)
                                S.dma("sp", a_.t[:, k0:k0 + 8, :], WGTd.t[e0 + k0 * 128:e0 + (k0 + 8) * 128, p0:p0 + NY].rearrange("(k p) c -> p k c", p=128),
                                      reads=[WGTd], writes=[a_], part=True)
                            for tt in range(NYT):
                                for kt in range(EC):
                                    S.op("pe", lambda e: e.matmul(bks[tt].t[:, :512], a_.t[:, kt, tt * 128:(tt + 1) * 128], v_.t[:, kt, :],
                                                                  start=(ec == 0 and kt == 0), stop=(ec == NEC - 1 and kt == EC - 1)),
                                         reads=[a_, v_], writes=[bks[tt]])
                        for tt in range(NYT):
                            h_, o_ = h1t[oi % 2], yo[oi % 2]
                            oi += 1
                            r0 = p0 + tt * 128
                            S.dma("sp", h_.t[:], H1d.t[r0:r0 + 128, cb:cb + 512], reads=[H1d], writes=[h_])
                            S.op("dve", lambda e: e.tensor_tensor(o_.t[:], h_.t[:], bks[tt].t[:, :512], ALU.add), reads=[h_, bks[tt]], writes=[o_])
                            S.dma("act", out_own[r0:r0 + 128, cb:cb + 512], o_.t[:], reads=[o_], writes=[OUTb], part=True)

        _barrier(S)
        peer_phase()

        for i in range(NDSEM):
            if S.dval[i]:
                nc.sync.wait_ge(S.dsem[i], S.dval[i])
        print("instructions:", S.ninst, {k: v for k, v in S.cnt.items()})
    return nc


def _prep(c, inp, core):
    f32 = np.float32
    b, hh = core // 2, core % 2
    D, SEQ, NO, H, G, NJ, NCT = c.D, c.SEQ, c.NO, c.H, c.G, c.NJ, c.NCT
    A = lambda v: np.ascontiguousarray(np.asarray(v), dtype=f32)
    x = np.asarray(inp["x"][b], dtype=f32)
    m = {}
    m["x_ctx"] = np.concatenate([np.asarray(inp["meta_tokens"], dtype=f32), x], 0)
    m["x_own"] = A(x.reshape(SEQ // 128, 128, D)[hh::2].reshape(NO, D))
    m["g1rep"] = A(np.broadcast_to(np.asarray(inp["norm1_g"][0])[None, :], (128, D)))
    m["g2rep"] = A(np.broadcast_to(np.asarray(inp["norm2_g"][0])[None, :], (128, D)))
    m["w_in"] = A(inp["w_in"][0])
    m["bfg"] = A(np.asarray(inp["b_forget"][0]).reshape(H, 1))
    m["qg"] = A(np.asarray(inp["q_norm_g"][0]).reshape(128, 1))
    m["kg"] = A(np.asarray(inp["k_norm_g"][0]).reshape(128, 1))
    lr = np.asarray(inp["lam_re"][0], dtype=f32); li = np.asarray(inp["lam_im"][0], dtype=f32)
    ld = np.asarray(inp["log_dt"][0], dtype=f32)
    m["lrA"] = A(lr.reshape(NJ, 128).T); m["liA"] = A(li.reshape(NJ, 128).T)
    m["ldA"] = A(np.repeat(ld.reshape(NJ, 2), 64, axis=1).T)
    m["lrB"] = A(np.broadcast_to(lr.reshape(1, -1), (128, G * 64)))
    m["liB"] = A(np.broadcast_to(li.reshape(1, -1), (128, G * 64)))
    m["ldB"] = A(np.broadcast_to(np.repeat(ld, 64)[None, :], (128, G * 64)))
    bre = np.asarray(inp["b_re"][0], dtype=f32); bim = np.asarray(inp["b_im"][0], dtype=f32)
    cre = np.asarray(inp["c_re"][0], dtype=f32); cim = np.asarray(inp["c_im"][0], dtype=f32)
    brB = np.zeros((128, G * 64), f32); biB = np.zeros((128, G * 64), f32)
    crB = np.zeros((128, NJ * 128), f32); ciB = np.zeros((128, NJ * 128), f32)
    for g in range(G):
        r0 = (g % 8) * 16
        brB[r0:r0 + 16, g * 64:(g + 1) * 64] = bre[g].T
        biB[r0:r0 + 16, g * 64:(g + 1) * 64] = bim[g].T
        j, g2 = g // 2, g % 2
        crB[g2 * 64:(g2 + 1) * 64, j * 128 + r0:j * 128 + r0 + 16] = cre[g].T
        ciB[g2 * 64:(g2 + 1) * 64, j * 128 + r0:j * 128 + r0 + 16] = cim[g].T
    m["brB"], m["biB"], m["crB"], m["ciB"] = brB, biB, crB, ciB
    m["dsk"] = A(np.asarray(inp["d_skip"][0]).reshape(NCT, 128).T)
    m["w_glu"] = A(inp["w_glu"][0]); m["w_a"] = A(inp["w_branch_attn"][0]); m["w_b"] = A(inp["w_branch_ssm"][0])
    m["w_out"] = A(inp["w_out"][0]); m["w_query"] = A(inp["w_query"][0])
    m["skT"] = A(np.asarray(inp["sub_keys"][0]).transpose(2, 0, 1).reshape(128, 256))
    m["euT"] = A(np.asarray(inp["expert_u"][0]).T); m["ev"] = A(inp["expert_v"][0])
    tri = (np.arange(128)[None, :] >= np.arange(128)[:, None]).astype(f32)
    m["maskA"] = tri if hh == 0 else np.ones((128, 128), f32)
    m["maskB"] = np.zeros((128, 128), f32) if hh == 0 else tri
    m["w01"] = A(np.broadcast_to(np.array([[1.0, 0.0]] if hh == 0 else [[0.0, 1.0]], f32), (128, 2)))
    LSM = 256 * c.SEGB + 16
    m["t_loc"] = A(np.broadcast_to(np.arange(LSM, dtype=f32)[None, :], (128, LSM)))
    io = np.arange(NO)
    pos = 16 + 128 * (2 * (io // 128) + hh) + (io % 128)
    m["t_own"] = A(np.broadcast_to(pos.astype(f32)[None, :], (128, NO)))
    m["ident"] = np.eye(128, dtype=f32)
    return m


_NC_CACHE = {}


def run_cfg(c, inputs):
    key = (c.D, c.SEQ, c.B)
    if key not in _NC_CACHE:
        _NC_CACHE[key] = build(c)
    nc = _NC_CACHE[key]
    ncores = 2 * c.B
    shared = None
    in_maps = []
    for core in range(ncores):
        in_maps.append(_prep(c, inputs, core))
    res = run_bass_kernel_spmd(nc, in_maps, core_ids=list(range(ncores)))
    if getattr(c, "debug", False):
        c.dbg = res.results
    out = np.zeros((c.B, c.SEQ, c.D), np.float32)
    for core in range(ncores):
        b, hh = core // 2, core % 2
        o = np.asarray(res.results[core]["out_own"], dtype=np.float32).reshape(c.NB, 128, c.D)
        out[b].reshape(c.SEQ // 128, 128, c.D)[hh::2] = o
    return out


def kernel(**inputs):
    return run_cfg(Cfg(), inputs)
```

```python
import math
from contextlib import ExitStack
import numpy as np
import ml_dtypes
import concourse.bass as bass
import concourse.mybir as mybir
from concourse.bass_utils import run_bass_kernel_spmd

F32 = mybir.dt.float32
BF16 = mybir.dt.bfloat16
I32 = mybir.dt.int32
AF = mybir.ActivationFunctionType
ALU = mybir.AluOpType
AX = mybir.AxisListType

EPOCH = 16000
NDSEM = 40
TWO_PI = 2.0 * math.pi
STAB = 30.0


class Buf:
    __slots__ = ("w", "r")

    def __init__(self):
        self.w = {}
        self.r = {}


class TT:
    def __init__(self, t):
        self.t = t
        self.b = Buf()


def _b(x):
    return x.b if isinstance(x, TT) else x


class Sched:
    def __init__(self, nc, stack):
        self.nc = nc
        self.engs = {"pe": nc.tensor, "dve": nc.vector, "act": nc.scalar,
                     "pool": nc.gpsimd, "sp": nc.sync}
        self.stack = stack
        self.esem = {}
        self.cnt = {e: 0 for e in self.engs}
        self.seen = {e: {} for e in self.engs}
        self.dsem = [stack.enter_context(nc.semaphore(f"d{i}")) for i in range(NDSEM)]
        self.dval = [0] * NDSEM
        self.dnext = 0
        self.ninst = 0

    def _esem(self, eng, epoch):
        k = (eng, epoch)
        if k not in self.esem:
            self.esem[k] = self.stack.enter_context(self.nc.semaphore(f"e_{eng}_{epoch}"))
        return self.esem[k]

    def _sem_of(self, key):
        if key[0] == "d":
            return self.dsem[key[1]]
        return self._esem(key[0], key[1])

    def _wait(self, eng, deps):
        s = self.seen[eng]
        for k, v in deps.items():
            if eng == "pe" and k[0] == "pe":
                continue
            if s.get(k, 0) < v:
                self.engs[eng].wait_ge(self._sem_of(k), v)
                s[k] = v

    @staticmethod
    def _acc(deps, d):
        for k, v in d.items():
            if deps.get(k, 0) < v:
                deps[k] = v

    def _commit(self, tok, reads, writes, part=False):
        k, v = tok
        for b in writes:
            if not part:
                b.w = {}
            b.w[k] = max(b.w.get(k, 0), v)
            b.r = {}
        for b in reads:
            if b.r.get(k, 0) < v:
                b.r[k] = v

    def op(self, eng, fn, reads=(), writes=()):
        reads = [_b(x) for x in reads]
        writes = [_b(x) for x in writes]
        deps = {}
        for b in reads:
            self._acc(deps, b.w)
        for b in writes:
            self._acc(deps, b.w)
            self._acc(deps, b.r)
        self._wait(eng, deps)
        ins = fn(self.engs[eng])
        c = self.cnt[eng]
        epoch, val = divmod(c, EPOCH)
        ins.then_inc(self._esem(eng, epoch), 1)
        self.cnt[eng] = c + 1
        self._commit(((eng, epoch), val + 1), reads, writes)
        self.ninst += 1
        return ins

    def dma(self, q, out, in_, reads=(), writes=(), part=False):
        reads = [_b(x) for x in reads]
        writes = [_b(x) for x in writes]
        deps = {}
        for b in reads:
            self._acc(deps, b.w)
        for b in writes:
            if not part:
                self._acc(deps, b.w)
            self._acc(deps, b.r)
        i = self.dnext
        self.dnext = (i + 1) % NDSEM
        if self.dval[i]:
            deps[("d", i)] = max(deps.get(("d", i), 0), self.dval[i])
        self._wait(q, deps)
        ins = self.engs[q].dma_start(out=out, in_=in_)
        ins.then_inc(self.dsem[i], 16)
        self.dval[i] += 16
        assert self.dval[i] < 60000
        self._commit((("d", i), self.dval[i]), reads, writes, part=part)
        self.ninst += 1
        return ins


def _barrier(S):
    deps = {}
    for e, cnt in S.cnt.items():
        if cnt:
            epoch, val = divmod(cnt - 1, EPOCH)
            deps[(e, epoch)] = val + 1
    for i in range(NDSEM):
        if S.dval[i]:
            deps[("d", i)] = S.dval[i]
    for e in S.engs:
        s = S.seen[e]
        for k, v in deps.items():
            if k[0] == e:
                continue
            if s.get(k, 0) < v:
                S.engs[e].wait_ge(S._sem_of(k), v)
                s[k] = v


class Cfg:
    def __init__(s, D=4096, SEQ=4096, B=4, NPAN=1024, PPAN=512, SEGB=2, WB6=256):
        s.D, s.SEQ, s.B = D, SEQ, B
        s.NM = 16
        s.H = D // 256
        s.AW = s.H * 128
        s.G = D // 32
        s.SW = s.G * 16
        s.NJ = s.G // 2
        s.NCT = s.SW // 128
        s.PH, s.PK, s.TOPK = 8, 128, 16
        s.PN = s.PK * s.PK
        s.QW = s.PH * 256
        s.L = SEQ + s.NM
        s.NO = SEQ // 2
        s.NB = s.NO // 128
        s.KT = D // 128
        s.NPAN = min(NPAN, s.NO)
        s.PPAN = min(PPAN, s.NO)
        s.SEGB = min(SEGB, s.NB)
        s.WB6 = WB6
        s.COL_Q = 0
        s.COL_K = s.AW
        s.COL_V = 2 * s.AW
        s.COL_F = 3 * s.AW
        s.COL_U = s.COL_F + s.H
        s.COL_GA = s.COL_U + s.SW
        s.COL_GB = s.COL_GA + D
        s.NCOLS = s.COL_GB + D


def build(cfg):
    c = cfg
    D, L, NO, KT, H, NB = c.D, c.L, c.NO, c.KT, c.H, c.NB
    nc = bass.Bass("TRN2", target_bir_lowering=False)

    def din(name, shape, dt=F32):
        return nc.dram_tensor(name, list(shape), dt, kind="ExternalInput").ap()

    def dscr(name, shape, dt):
        return TT(nc.dram_tensor(name, list(shape), dt, kind="ExternalOutput" if getattr(c, "debug", False) else "Internal").ap())

    x_ctx = din("x_ctx", [L, D]); x_own = din("x_own", [NO, D])
    g1rep = din("g1rep", [128, D]); g2rep = din("g2rep", [128, D])
    w_in = din("w_in", [D, c.NCOLS])
    bfg = din("bfg", [H, 1]); qg = din("qg", [128, 1]); kg = din("kg", [128, 1])
    lrA = din("lrA", [128, c.NJ]); liA = din("liA", [128, c.NJ]); ldA = din("ldA", [128, c.NJ])
    lrB = din("lrB", [128, c.NJ * 128]); liB = din("liB", [128, c.NJ * 128]); ldB = din("ldB", [128, c.NJ * 128])
    brB = din("brB", [128, c.NJ * 128]); biB = din("biB", [128, c.NJ * 128])
    crB = din("crB", [128, c.NJ * 128]); ciB = din("ciB", [128, c.NJ * 128])
    dsk = din("dsk", [128, c.NCT])
    w_glu = din("w_glu", [c.SW, c.SW]); w_a = din("w_a", [c.AW, D]); w_b = din("w_b", [c.SW, D])
    w_out = din("w_out", [D, D]); w_query = din("w_query", [D, c.QW])
    skT = din("skT", [128, 2 * 128])
    euT = din("euT", [D, c.PN]); ev = din("ev", [c.PN, D])
    maskA = din("maskA", [128, 128]); maskB = din("maskB", [128, 128])
    w01 = din("w01", [128, 2])
    t_loc = din("t_loc", [128, 256 * c.SEGB + 16]); t_own = din("t_own", [128, NO])
    ident = din("ident", [128, 128])
    out_own = nc.dram_tensor("out_own", [NO, D], F32, kind="ExternalOutput").ap()

    KTd = dscr("KTd", [H, 128, L], BF16)
    Vd = dscr("Vd", [L, c.AW], BF16)
    UTd = dscr("UTd", [c.SW, L], BF16)
    QTd = dscr("QTd", [H, 128, NO], BF16)
    GAd = dscr("GAd", [D, NO], F32)
    GBd = dscr("GBd", [D, NO], F32)
    UOd = dscr("UOd", [c.SW, NO], F32)
    CQd = dscr("CQd", [H, NO], F32)
    CQ3d = dscr("CQ3d", [3, H, NO], BF16)
    BBRd = dscr("BBRd", [128, c.NJ * 128], BF16)
    BBId = dscr("BBId", [128, c.NJ * 128], BF16)
    ZFd = dscr("ZFd", [c.SW, NO], F32)
    ZTd = dscr("ZTd", [c.SW, NO], BF16)
    SSTd = dscr("SSTd", [c.SW, NO], BF16)
    ATd = dscr("ATd", [c.AW, NO], BF16)
    H1d = dscr("H1d", [NO, D], F32)
    WGTd = dscr("WGTd", [c.PN, NO], BF16)
    OUTb = Buf()

    with ExitStack() as gst:
        S = Sched(nc, gst)

        uid = {"n": 0}

        def sb(st, name, shape, dt):
            uid["n"] += 1
            return TT(st.enter_context(nc.sbuf_tensor(f"{name}_{uid['n']}", list(shape), dt)))

        ps = [TT(gst.enter_context(nc.psum_tensor(f"ps{i}", [128, 512], F32))) for i in range(8)]
        psb = []
        for i in range(2):
            v = TT(ps[6 + i].t.bitcast(BF16))
            v.b = ps[6 + i].b
            psb.append(v)
        id_bf = sb(gst, "id_bf", [128, 128], BF16)
        id_f = sb(gst, "id_f", [128, 128], F32)
        ones_bf = sb(gst, "ones_bf", [128, 128], BF16)
        ones_f = sb(gst, "ones_f", [1, 128], F32)
        cst = sb(gst, "cst", [128, 8], F32)
        w01t = sb(gst, "w01t", [128, 2], F32)
        qgt = sb(gst, "qgt", [128, 1], F32); kgt = sb(gst, "kgt", [128, 1], F32)
        cacc = sb(gst, "cacc", [16, L], F32)
        rhoA = sb(gst, "rhoA", [128, c.NJ], F32)
        phiA = sb(gst, "phiA", [128, c.NJ], F32)
        S.dma("pool", id_bf.t[:], ident, writes=[id_bf])
        S.dma("sp", id_f.t[:], ident, writes=[id_f])
        S.dma("sp", w01t.t[:], w01, writes=[w01t])
        S.dma("sp", qgt.t[:], qg, writes=[qgt])
        S.dma("sp", kgt.t[:], kg, writes=[kgt])
        S.op("dve", lambda e: e.memset(ones_bf.t[:], 1.0), writes=[ones_bf])
        S.op("dve", lambda e: e.memset(ones_f.t[:], 1.0), writes=[ones_f])
        S.op("dve", lambda e: e.memset(cst.t[:, 0:1], 1e-6), writes=[cst])
        S.op("dve", lambda e: e.memset(cst.t[:, 1:2], 1.0), writes=[cst])
        S.op("dve", lambda e: e.memset(cst.t[:, 2:3], 0.0), writes=[cst])
        EPS = cst.t[:, 0:1]
        ONE = cst.t[:, 1:2]

        rr = {"e": 0}

        def alt():
            rr["e"] ^= 1
            return "act" if rr["e"] else "dve"

        def copy(eng, out, in_, reads, writes):
            if eng == "act":
                S.op("act", lambda e: e.activation(out, in_, AF.Copy), reads=reads, writes=writes)
            else:
                S.op(eng, lambda e: e.tensor_copy(out, in_), reads=reads, writes=writes)

        def make_nt(st, tagp):
          xt = sb(st, tagp + "xt", [128, D], F32)
          xn = sb(st, tagp + "xn", [128, D], BF16)
          ss = sb(st, tagp + "ss", [128, 2], F32)

          def norm_transpose(src, n_rows, grep, panel, srcbuf=None):
            for t0 in range(0, n_rows, 128):
                r = min(128, n_rows - t0)
                S.dma("sp", xt.t[:r, :], src[t0:t0 + r, :], reads=[srcbuf] if srcbuf else [], writes=[xt])
                S.op("dve", lambda e: e.memset(ss.t[:r, 0:1], 0.0), writes=[ss])
                S.op("act", lambda e: e.activation(xn.t[:r, :], xt.t[:r, :], AF.Square, accum_out=ss.t[:r, 0:1]),
                     reads=[xt], writes=[xn, ss])
                S.op("dve", lambda e: e.tensor_scalar(ss.t[:r, 1:2], ss.t[:r, 0:1], 1.0 / D, 1e-6, ALU.mult, ALU.add),
                     reads=[ss], writes=[ss])
                S.op("act", lambda e: e.activation(ss.t[:r, 1:2], ss.t[:r, 1:2], AF.Sqrt), reads=[ss], writes=[ss])
                S.op("dve", lambda e: e.reciprocal(ss.t[:r, 1:2], ss.t[:r, 1:2]), reads=[ss], writes=[ss])
                S.op("dve", lambda e: e.scalar_tensor_tensor(xn.t[:r, :], xt.t[:r, :], ss.t[:r, 1:2], grep.t[:r, :],
                                                             ALU.mult, ALU.mult),
                     reads=[xt, ss, grep], writes=[xn])
                for k0 in range(0, KT, 8):
                    k1 = min(KT, k0 + 8)
                    pb = psb[(k0 // 8) % 2]
                    for kt in range(k0, k1):
                        S.op("pe", lambda e: e.transpose(pb.t[:, (kt - k0) * 128:(kt - k0) * 128 + r],
                                                         xn.t[:r, kt * 128:(kt + 1) * 128], id_bf.t[:r, :r]),
                             reads=[xn, id_bf], writes=[pb])
                    src_ap = pb.t[:, 0:(k1 - k0) * 128].rearrange("p (k r) -> p k r", r=128)[:, :, :r]
                    copy(alt(), panel.t[:, k0:k1, t0:t0 + r], src_ap, [pb], [panel])
          return norm_transpose

        wstate = {"i": 0}

        def gemm(mode, wbufs, wsrc, K, c0, ncols, act, actbufs, N, epi, kgroups=None, WB=None, banks=(0, 1, 2, 3),
                 wq="pool", wsrcbuf=None):
            KTl = K // 128
            WB = WB or wbufs[0].t.shape[2]
            kgroups = kgroups or [(0, KTl)]
            bi = 0
            for sb0 in range(0, ncols, WB):
                cw = min(WB, ncols - sb0)
                wt = wbufs[wstate["i"] % len(wbufs)]
                wstate["i"] += 1
                srcs = wsrc if isinstance(wsrc, list) else [(wsrc, K)]
                kbase = 0
                for (wap, Ks) in srcs:
                    for k0 in range(0, Ks // 128, 8):
                        k1 = min(Ks // 128, k0 + 8)
                        srcap = wap[k0 * 128:k1 * 128, c0 + sb0:c0 + sb0 + cw].rearrange("(kt p) c -> p kt c", p=128)
                        S.dma(wq, wt.t[:, kbase + k0:kbase + k1, :cw], srcap, reads=[wsrcbuf] if wsrcbuf else [],
                              writes=[wt], part=True)
                    kbase += Ks // 128
                if mode == "fm":
                    for j0 in range(0, cw, 128):
                        m = min(128, cw - j0)
                        for n0 in range(0, N, 512):
                            n1 = min(N, n0 + 512)
                            pss = []
                            for (ka, kb) in kgroups:
                                pb = ps[banks[bi % len(banks)]]
                                bi += 1
                                for kt in range(ka, kb):
                                    S.op("pe", lambda e: e.matmul(pb.t[:m, :n1 - n0], wt.t[:, kt, j0:j0 + m],
                                                                  act(kt, n0, n1), start=(kt == ka), stop=(kt == kb - 1)),
                                         reads=[wt] + actbufs, writes=[pb])
                                pss.append(pb)
                            epi(c0 + sb0 + j0, m, n0, n1, pss)
                else:
                    for t0 in range(0, N, 128):
                        t1 = min(N, t0 + 128)
                        pss = []
                        for (ka, kb) in kgroups:
                            pb = ps[banks[bi % len(banks)]]
                            bi += 1
                            for kt in range(ka, kb):
                                S.op("pe", lambda e: e.matmul(pb.t[:t1 - t0, :cw], act(kt, t0, t1), wt.t[:, kt, :cw],
                                                              start=(kt == ka), stop=(kt == kb - 1)),
                                     reads=[wt] + actbufs, writes=[pb])
                            pss.append(pb)
                        epi(t0, t1, c0 + sb0, cw, pss)

        def coeffs(st, tag, lr_ap, li_ap, ld_ap, F):
            lr = sb(st, tag + "lr", [128, F], F32); li = sb(st, tag + "li", [128, F], F32)
            ld = sb(st, tag + "ld", [128, F], F32)
            t1 = sb(st, tag + "t1", [128, F], F32); t2 = sb(st, tag + "t2", [128, F], F32)
            ti = sb(st, tag + "ti", [128, F], I32)
            rho = sb(st, tag + "rho", [128, F], F32); phi = sb(st, tag + "phi", [128, F], F32)
            sn = sb(st, tag + "sn", [128, F], F32); cs = sb(st, tag + "cs", [128, F], F32)
            fr = sb(st, tag + "fr", [128, F], F32); fi = sb(st, tag + "fi", [128, F], F32)

            def run(lr_src, li_src, ld_src):
                S.dma("sp", lr.t[:], lr_src, writes=[lr]); S.dma("sp", li.t[:], li_src, writes=[li])
                S.dma("sp", ld.t[:], ld_src, writes=[ld])
                S.op("act", lambda e: e.activation(ld.t[:], ld.t[:], AF.Exp), reads=[ld], writes=[ld])
                S.op("dve", lambda e: e.tensor_tensor(t1.t[:], lr.t[:], ld.t[:], ALU.mult), reads=[lr, ld], writes=[t1])
                S.op("act", lambda e: e.activation(rho.t[:], t1.t[:], AF.Exp), reads=[t1], writes=[rho])
                S.op("dve", lambda e: e.scalar_tensor_tensor(t1.t[:], li.t[:], 1.0 / TWO_PI, ld.t[:], ALU.mult, ALU.mult),
                     reads=[li, ld], writes=[t1])
                S.op("dve", lambda e: e.tensor_copy(ti.t[:], t1.t[:]), reads=[t1], writes=[ti])
                S.op("dve", lambda e: e.tensor_copy(t2.t[:], ti.t[:]), reads=[ti], writes=[t2])
                S.op("dve", lambda e: e.tensor_sub(phi.t[:], t1.t[:], t2.t[:]), reads=[t1, t2], writes=[phi])
                S.op("act", lambda e: e.activation(sn.t[:], phi.t[:], AF.Sin, scale=TWO_PI), reads=[phi], writes=[sn])
                S.op("dve", lambda e: e.tensor_scalar(t1.t[:], phi.t[:], 0.25, None, ALU.add), reads=[phi], writes=[t1])
                S.op("dve", lambda e: e.tensor_copy(ti.t[:], t1.t[:]), reads=[t1], writes=[ti])
                S.op("dve", lambda e: e.tensor_copy(t2.t[:], ti.t[:]), reads=[ti], writes=[t2])
                S.op("dve", lambda e: e.tensor_sub(t1.t[:], t1.t[:], t2.t[:]), reads=[t1, t2], writes=[t1])
                S.op("act", lambda e: e.activation(cs.t[:], t1.t[:], AF.Sin, scale=TWO_PI), reads=[t1], writes=[cs])
                S.op("dve", lambda e: e.tensor_tensor(cs.t[:], cs.t[:], rho.t[:], ALU.mult), reads=[cs, rho], writes=[cs])
                S.op("dve", lambda e: e.tensor_tensor(sn.t[:], sn.t[:], rho.t[:], ALU.mult), reads=[sn, rho], writes=[sn])
                S.op("dve", lambda e: e.tensor_scalar(t1.t[:], cs.t[:], -1.0, None, ALU.add), reads=[cs], writes=[t1])
                S.op("dve", lambda e: e.tensor_tensor(t2.t[:], lr.t[:], lr.t[:], ALU.mult), reads=[lr], writes=[t2])
                S.op("dve", lambda e: e.tensor_tensor(fr.t[:], li.t[:], li.t[:], ALU.mult), reads=[li], writes=[fr])
                S.op("dve", lambda e: e.tensor_add(t2.t[:], t2.t[:], fr.t[:]), reads=[t2, fr], writes=[t2])
                S.op("dve", lambda e: e.reciprocal(t2.t[:], t2.t[:]), reads=[t2], writes=[t2])
                S.op("dve", lambda e: e.tensor_tensor(fr.t[:], t1.t[:], lr.t[:], ALU.mult), reads=[t1, lr], writes=[fr])
                S.op("dve", lambda e: e.tensor_tensor(fi.t[:], sn.t[:], li.t[:], ALU.mult), reads=[sn, li], writes=[fi])
                S.op("dve", lambda e: e.tensor_add(fr.t[:], fr.t[:], fi.t[:]), reads=[fr, fi], writes=[fr])
                S.op("dve", lambda e: e.tensor_tensor(fr.t[:], fr.t[:], t2.t[:], ALU.mult), reads=[fr, t2], writes=[fr])
                S.op("dve", lambda e: e.tensor_tensor(fi.t[:], sn.t[:], lr.t[:], ALU.mult), reads=[sn, lr], writes=[fi])
                S.op("dve", lambda e: e.tensor_tensor(t1.t[:], t1.t[:], li.t[:], ALU.mult), reads=[t1, li], writes=[t1])
                S.op("dve", lambda e: e.tensor_sub(fi.t[:], fi.t[:], t1.t[:]), reads=[fi, t1], writes=[fi])
                S.op("dve", lambda e: e.tensor_tensor(fi.t[:], fi.t[:], t2.t[:], ALU.mult), reads=[fi, t2], writes=[fi])
            return run, rho, phi, fr, fi

        with ExitStack() as st:
            run, rho, phi, fr, fi = coeffs(st, "ca", None, None, None, c.NJ)
            run(lrA, liA, ldA)
            S.op("dve", lambda e: e.tensor_copy(rhoA.t[:], rho.t[:]), reads=[rho], writes=[rhoA])
            S.op("dve", lambda e: e.tensor_copy(phiA.t[:], phi.t[:]), reads=[phi], writes=[phiA])
        _barrier(S)
        with ExitStack() as st:
            FC = min(1024, c.NJ * 128)
            run, rho, phi, fr, fi = coeffs(st, "cb", None, None, None, FC)
            br = sb(st, "br", [128, FC], F32); bi_ = sb(st, "bi", [128, FC], F32)
            o1 = sb(st, "o1", [128, FC], F32); o2 = sb(st, "o2", [128, FC], F32)
            ob1 = sb(st, "ob1", [128, FC], BF16); ob2 = sb(st, "ob2", [128, FC], BF16)
            for f0 in range(0, c.NJ * 128, FC):
                run(lrB[:, f0:f0 + FC], liB[:, f0:f0 + FC], ldB[:, f0:f0 + FC])
                S.dma("sp", br.t[:], brB[:, f0:f0 + FC], writes=[br])
                S.dma("sp", bi_.t[:], biB[:, f0:f0 + FC], writes=[bi_])
                S.op("dve", lambda e: e.tensor_tensor(o1.t[:], fr.t[:], br.t[:], ALU.mult), reads=[fr, br], writes=[o1])
                S.op("dve", lambda e: e.tensor_tensor(o2.t[:], fi.t[:], bi_.t[:], ALU.mult), reads=[fi, bi_], writes=[o2])
                S.op("dve", lambda e: e.tensor_sub(ob1.t[:], o1.t[:], o2.t[:]), reads=[o1, o2], writes=[ob1])
                S.op("dve", lambda e: e.tensor_tensor(o1.t[:], fr.t[:], bi_.t[:], ALU.mult), reads=[fr, bi_], writes=[o1])
                S.op("dve", lambda e: e.tensor_tensor(o2.t[:], fi.t[:], br.t[:], ALU.mult), reads=[fi, br], writes=[o2])
                S.op("dve", lambda e: e.tensor_add(ob2.t[:], o1.t[:], o2.t[:]), reads=[o1, o2], writes=[ob2])
                S.dma("sp", BBRd.t[:, f0:f0 + FC], ob1.t[:], reads=[ob1], writes=[BBRd], part=True)
                S.dma("sp", BBId.t[:, f0:f0 + FC], ob2.t[:], reads=[ob2], writes=[BBId], part=True)

        def proj_phase(is_ctx):
            with ExitStack() as st:
                tag = "c" if is_ctx else "o"
                NPmax = c.NPAN + (c.NM if is_ctx else 0)
                panel = sb(st, tag + "pan", [128, KT, NPmax], BF16)
                grep = sb(st, tag + "g1", [128, D], F32)
                S.dma("sp", grep.t[:], g1rep, writes=[grep])
                wbufs = [sb(st, tag + f"w{i}", [128, KT, 256], BF16) for i in range(2)]
                stg = [sb(st, tag + f"stg{i}", [128, 512], BF16) for i in range(3)]
                stf = [sb(st, tag + f"stf{i}", [128, 512], F32) for i in range(3)]
                sq = sb(st, tag + "sq", [128, 512], BF16)
                rinv = sb(st, tag + "rinv", [128, 512], F32)
                bft = sb(st, tag + "bft", [16, 1], F32)
                lf = sb(st, tag + "lf", [16, 512], F32)
                ones_t = sb(st, tag + "ones", [16, 512], F32)
                S.dma("sp", bft.t[:H, :], bfg, writes=[bft])
                S.op("dve", lambda e: e.tensor_scalar(bft.t[:H, :], bft.t[:H, :], -1.0, None, ALU.mult), reads=[bft], writes=[bft])
                S.op("dve", lambda e: e.memset(ones_t.t[:], 1.0), writes=[ones_t])
                si = {"g": 0, "f": 0}
                src = x_ctx if is_ctx else x_own
                ntot = L if is_ctx else NO
                p0 = 0
                nt = make_nt(st, tag)
                while p0 < ntot:
                    pn = min(c.NPAN + (c.NM if (is_ctx and p0 == 0) else 0), ntot - p0)
                    nt(src[p0:p0 + pn, :], pn, grep, panel)

                    def act(kt, n0, n1):
                        return panel.t[:, kt, n0:n1]

                    def epi_qk(dst, gcol, colbase):
                        def epi(col, m, n0, n1, pss):
                            n = n1 - n0
                            h = (col - colbase) // 128
                            pb = pss[0]
                            S.op("act", lambda e: e.activation(sq.t[:, :n], pb.t[:, :n], AF.Square), reads=[pb], writes=[sq])
                            p2 = ps[4]
                            S.op("pe", lambda e: e.matmul(p2.t[:, :n], ones_bf.t[:], sq.t[:, :n], start=True, stop=True),
                                 reads=[ones_bf, sq], writes=[p2])
                            S.op("act", lambda e: e.activation(rinv.t[:, :n], p2.t[:, :n], AF.Sqrt, bias=EPS, scale=1.0 / 128),
                                 reads=[p2, cst], writes=[rinv])
                            S.op("dve", lambda e: e.reciprocal(rinv.t[:, :n], rinv.t[:, :n]), reads=[rinv], writes=[rinv])
                            sg = stg[si["g"] % 3]; si["g"] += 1
                            S.op("dve", lambda e: e.scalar_tensor_tensor(sg.t[:, :n], pb.t[:, :n], gcol.t[:, 0:1], rinv.t[:, :n],
                                                                         ALU.mult, ALU.mult),
                                 reads=[pb, gcol, rinv], writes=[sg])
                            S.dma("sp", dst.t[h, :, p0 + n0:p0 + n1], sg.t[:, :n], reads=[sg], writes=[dst], part=True)
                        return epi

                    def epi_store_bf(dst):
                        def epi(col, m, n0, n1, pss):
                            n = n1 - n0
                            sg = stg[si["g"] % 3]; si["g"] += 1
                            copy(alt(), sg.t[:m, :n], pss[0].t[:m, :n], [pss[0]], [sg])
                            S.dma("sp", dst.t[col:col + m, p0 + n0:p0 + n1], sg.t[:m, :n], reads=[sg], writes=[dst], part=True)
                        return epi

                    if is_ctx:
                        gemm("fm", wbufs, w_in, D, c.COL_K, c.AW, act, [panel], pn, epi_qk(KTd, kgt, c.COL_K))

                        def epi_v(t0, t1, col, cw, pss):
                            r = t1 - t0
                            sg = stg[si["g"] % 3]; si["g"] += 1
                            copy(alt(), sg.t[:r, :cw], pss[0].t[:r, :cw], [pss[0]], [sg])
                            S.dma("sp", Vd.t[p0 + t0:p0 + t1, col - c.COL_V:col - c.COL_V + cw], sg.t[:r, :cw],
                                  reads=[sg], writes=[Vd], part=True)
                        gemm("tm", wbufs, w_in, D, c.COL_V, c.AW, act, [panel], pn, epi_v)

                        def epi_u(col, m, n0, n1, pss):
                            n = n1 - n0
                            sg = stg[si["g"] % 3]; si["g"] += 1
                            copy(alt(), sg.t[:m, :n], pss[0].t[:m, :n], [pss[0]], [sg])
                            S.dma("sp", UTd.t[col - c.COL_U:col - c.COL_U + m, p0 + n0:p0 + n1], sg.t[:m, :n],
                                  reads=[sg], writes=[UTd], part=True)
                        gemm("fm", wbufs, w_in, D, c.COL_U, c.SW, act, [panel], pn, epi_u)

                        def epi_f(col, m, n0, n1, pss):
                            n = n1 - n0
                            pb = pss[0]
                            S.op("act", lambda e: e.activation(lf.t[:H, :n], pb.t[:H, :n], AF.Exp, bias=bft.t[:H, 0:1], scale=-1.0),
                                 reads=[pb, bft], writes=[lf])
                            S.op("act", lambda e: e.activation(lf.t[:H, :n], lf.t[:H, :n], AF.Ln, bias=ONE[:H, :], scale=1.0),
                                 reads=[lf, cst], writes=[lf])
                            S.op("dve", lambda e: e.tensor_scalar(lf.t[:H, :n], lf.t[:H, :n], -1.0, None, ALU.mult), reads=[lf], writes=[lf])
                            a0 = p0 + n0
                            init = 0.0 if a0 == 0 else cacc.t[:H, a0 - 1:a0]
                            S.op("dve", lambda e: e.tensor_tensor_scan(cacc.t[:H, a0:a0 + n], ones_t.t[:H, :n], lf.t[:H, :n], init,
                                                                       ALU.mult, ALU.add),
                                 reads=[ones_t, lf, cacc], writes=[cacc])
                        gemm("fm", wbufs, w_in, D, c.COL_F, H, act, [panel], pn, epi_f)
                    else:
                        gemm("fm", wbufs, w_in, D, c.COL_Q, c.AW, act, [panel], pn, epi_qk(QTd, qgt, c.COL_Q))

                        def epi_f32(dst, colbase, func):
                            def epi(col, m, n0, n1, pss):
                                n = n1 - n0
                                sf = stf[si["f"] % 3]; si["f"] += 1
                                S.op("act", lambda e: e.activation(sf.t[:m, :n], pss[0].t[:m, :n], func), reads=[pss[0]], writes=[sf])
                                S.dma("sp", dst.t[col - colbase:col - colbase + m, p0 + n0:p0 + n1], sf.t[:m, :n],
                                      reads=[sf], writes=[dst], part=True)
                            return epi
                        gemm("fm", wbufs, w_in, D, c.COL_U, c.SW, act, [panel], pn, epi_f32(UOd, c.COL_U, AF.Copy))
                        gemm("fm", wbufs, w_in, D, c.COL_GA, D, act, [panel], pn, epi_f32(GAd, c.COL_GA, AF.Sigmoid))
                        gemm("fm", wbufs, w_in, D, c.COL_GB, D, act, [panel], pn, epi_f32(GBd, c.COL_GB, AF.Sigmoid))
                    p0 += pn

        _barrier(S)
        proj_phase(True)
        _barrier(S)
        proj_phase(False)

        NKT = 2 * NB + 1

        def ktile(kt):
            return (0, 16) if kt == 0 else (16 + 128 * (kt - 1), 128)

        MAGIC = 12582912.0

        def ssm_gen(st):
            if True:
                LSM = 256 * c.SEGB + 16
                NSEG = NB // c.SEGB
                PC = min(2, c.SEGB)
                tl = sb(st, "tl", [128, LSM], F32); S.dma("sp", tl.t[:], t_loc, writes=[tl])
                onesL = sb(st, "onesL", [128, LSM], F32)
                S.op("dve", lambda e: e.memset(onesL.t[:], 1.0), writes=[onesL])
                hpi = sb(st, "hpi", [128, 1], F32)
                S.op("dve", lambda e: e.memset(hpi.t[:], math.pi / 2), writes=[hpi])
                dskt = sb(st, "dskt", [128, c.NCT], F32); S.dma("sp", dskt.t[:], dsk, writes=[dskt])
                uT = sb(st, "uT", [128, L], BF16)
                uo = sb(st, "uo", [128, NO], F32)
                wbr = sb(st, "wbr", [128, 512], BF16); wbi = sb(st, "wbi", [128, 512], BF16)
                wcr = sb(st, "wcr", [128, 512], BF16); wci = sb(st, "wci", [128, 512], BF16)
                zr = sb(st, "zr", [128, 4, LSM], BF16); nzi = sb(st, "nzi", [128, 4, LSM], BF16)
                z2 = sb(st, "z2", [128, 4, LSM], BF16); z4 = sb(st, "z4", [128, 4, LSM], BF16)
                nwcr = sb(st, "nwcr", [128, 512], BF16); nwci = sb(st, "nwci", [128, 512], BF16)
                rho_t = sb(st, "rho_t", [128, 4, LSM], F32)
                LA = 2
                sets = [[sb(st, f"s{k}_{i}", [128, LSM], F32) for i in range(9)] + [sb(st, f"ab{k}", [128, 1], F32)] for k in range(LA + 1)]
                carry = sb(st, "carry", [128, 4, 2], F32)
                ybufs = [(sb(st, f"sel{i}", [128, 256], F32), sb(st, f"yf{i}", [128, 256], F32), sb(st, f"zf{i}", [128, 256], F32),
                          sb(st, f"zb{i}", [128, 256], BF16)) for i in range(3)]
                yk = {"i": 0}
                for ct in range(c.NCT):
                    S.dma("sp", uT.t[:], UTd.t[ct * 128:(ct + 1) * 128, :], reads=[UTd], writes=[uT])
                    S.dma("sp", uo.t[:], UOd.t[ct * 128:(ct + 1) * 128, :], reads=[UOd], writes=[uo])
                    cols = slice(ct * 512, (ct + 1) * 512)
                    S.dma("sp", wbr.t[:], BBRd.t[:, cols], reads=[BBRd], writes=[wbr])
                    S.dma("sp", wbi.t[:], BBId.t[:, cols], reads=[BBId], writes=[wbi])
                    S.dma("pool", wcr.t[:], crB[:, cols], writes=[wcr])
                    S.dma("pool", wci.t[:], ciB[:, cols], writes=[wci])
                    S.op("pool", lambda e: e.tensor_scalar(nwcr.t[:], wcr.t[:], -1.0, None, ALU.mult), reads=[wcr], writes=[nwcr])
                    S.op("pool", lambda e: e.tensor_scalar(nwci.t[:], wci.t[:], -1.0, None, ALU.mult), reads=[wci], writes=[nwci])
                    for jj in range(4):
                        j = 4 * ct + jj
                        S.op("act", lambda e: e.activation(rho_t.t[:, jj, :], onesL.t[:], AF.Copy, scale=rhoA.t[:, j:j + 1]),
                             reads=[onesL, rhoA], writes=[rho_t])
                    units = [(s_, jj_) for s_ in range(NSEG) for jj_ in range(4)]

                    def seginfo(s):
                        a0 = 0 if s == 0 else 16 + 256 * s * c.SEGB
                        a1 = 16 + 256 * (s + 1) * c.SEGB
                        return a0, a1 - a0, (16 if s == 0 else 0)

                    def stage1(u, s, jj):
                        a0, Ls, off = seginfo(s)
                        V = lambda t: t.t[:, :Ls]
                        j = 4 * ct + jj
                        xr, xi, ang, rr, sn, cs, bA, bB, bC, ab = sets[u % (LA + 1)]
                        for n0 in range(0, Ls, 512):
                            n1 = min(Ls, n0 + 512)
                            pr, pi = ps[5], ps[6]
                            S.op("pe", lambda e: e.matmul(pr.t[:, :n1 - n0], wbr.t[:, jj * 128:(jj + 1) * 128], uT.t[:, a0 + n0:a0 + n1],
                                                          start=True, stop=True), reads=[wbr, uT], writes=[pr])
                            S.op("pe", lambda e: e.matmul(pi.t[:, :n1 - n0], wbi.t[:, jj * 128:(jj + 1) * 128], uT.t[:, a0 + n0:a0 + n1],
                                                          start=True, stop=True), reads=[wbi, uT], writes=[pi])
                            copy("act", xr.t[:, n0:n1], pr.t[:, :n1 - n0], [pr], [xr])
                            copy("act", xi.t[:, n0:n1], pi.t[:, :n1 - n0], [pi], [xi])
                        S.op("act", lambda e: e.activation(ab.t[:], phiA.t[:, j:j + 1], AF.Copy, scale=float(a0)), reads=[phiA], writes=[ab])
                        S.op("act", lambda e: e.activation(V(ang), V(tl), AF.Identity, bias=ab.t[:, 0:1], scale=phiA.t[:, j:j + 1]),
                             reads=[tl, ab, phiA], writes=[ang])
                        S.op("dve", lambda e: e.tensor_scalar(V(rr), V(ang), MAGIC, MAGIC, ALU.add, ALU.subtract), reads=[ang], writes=[rr])
                        S.op("dve", lambda e: e.tensor_tensor(V(sn), V(ang), V(rr), ALU.subtract), reads=[ang, rr], writes=[sn])
                        S.op("act", lambda e: e.activation(V(cs), V(sn), AF.Abs), reads=[sn], writes=[cs])
                        S.op("act", lambda e: e.activation(V(cs), V(cs), AF.Sin, bias=hpi.t[:, 0:1], scale=-TWO_PI), reads=[cs, hpi], writes=[cs])
                        S.op("act", lambda e: e.activation(V(sn), V(sn), AF.Sin, scale=TWO_PI), reads=[sn], writes=[sn])

                    def stage2(u, s, jj):
                        a0, Ls, off = seginfo(s)
                        V = lambda t: t.t[:, :Ls]
                        xr, xi, ang, rr, sn, cs, bA, bB, bC, ab = sets[u % (LA + 1)]
                        S.op("dve", lambda e: e.tensor_tensor(V(bA), V(cs), V(xr), ALU.mult), reads=[cs, xr], writes=[bA])
                        S.op("dve", lambda e: e.tensor_tensor(V(bB), V(cs), V(xi), ALU.mult), reads=[cs, xi], writes=[bB])
                        S.op("dve", lambda e: e.tensor_tensor(V(bC), V(sn), V(xi), ALU.mult), reads=[sn, xi], writes=[bC])
                        S.op("dve", lambda e: e.tensor_tensor(V(rr), V(sn), V(xr), ALU.mult), reads=[sn, xr], writes=[rr])
                        S.op("dve", lambda e: e.tensor_add(V(bA), V(bA), V(bC)), reads=[bA, bC], writes=[bA])
                        S.op("dve", lambda e: e.tensor_sub(V(bB), V(bB), V(rr)), reads=[bB, rr], writes=[bB])
                        ir = 0.0 if s == 0 else carry.t[:, jj, 0:1]
                        ii = 0.0 if s == 0 else carry.t[:, jj, 1:2]
                        S.op("dve", lambda e: e.tensor_tensor_scan(V(xr), rho_t.t[:, jj, :Ls], V(bA), ir, ALU.mult, ALU.add),
                             reads=[rho_t, bA, carry], writes=[xr])
                        S.op("dve", lambda e: e.tensor_tensor_scan(V(xi), rho_t.t[:, jj, :Ls], V(bB), ii, ALU.mult, ALU.add),
                             reads=[rho_t, bB, carry], writes=[xi])
                        S.op("dve", lambda e: e.tensor_tensor(zr.t[:, jj, :Ls], V(cs), V(xr), ALU.mult), reads=[cs, xr], writes=[zr])
                        S.op("dve", lambda e: e.tensor_tensor(z2.t[:, jj, :Ls], V(sn), V(xi), ALU.mult), reads=[sn, xi], writes=[z2])
                        S.op("dve", lambda e: e.tensor_tensor(nzi.t[:, jj, :Ls], V(sn), V(xr), ALU.mult), reads=[sn, xr], writes=[nzi])
                        S.op("dve", lambda e: e.tensor_tensor(z4.t[:, jj, :Ls], V(cs), V(xi), ALU.mult), reads=[cs, xi], writes=[z4])
                        S.op("dve", lambda e: e.tensor_copy(carry.t[:, jj, 0:1], xr.t[:, Ls - 1:Ls]), reads=[xr], writes=[carry])
                        S.op("dve", lambda e: e.tensor_copy(carry.t[:, jj, 1:2], xi.t[:, Ls - 1:Ls]), reads=[xi], writes=[carry])
                        if jj == 3:
                            ypart(s)

                    def ypart(s):
                        a0, Ls, off = seginfo(s)
                        for q0 in range(0, c.SEGB, PC):
                            n = 256 * PC
                            l0 = off + 256 * q0
                            pb = ps[7]
                            for jj in range(4):
                                wsl = slice(jj * 128, (jj + 1) * 128)
                                S.op("pe", lambda e: e.matmul(pb.t[:, :n], wcr.t[:, wsl], zr.t[:, jj, l0:l0 + n],
                                                              start=(jj == 0), stop=False), reads=[wcr, zr], writes=[pb])
                                S.op("pe", lambda e: e.matmul(pb.t[:, :n], nwcr.t[:, wsl], z2.t[:, jj, l0:l0 + n],
                                                              start=False, stop=False), reads=[nwcr, z2], writes=[pb])
                                S.op("pe", lambda e: e.matmul(pb.t[:, :n], nwci.t[:, wsl], nzi.t[:, jj, l0:l0 + n],
                                                              start=False, stop=False), reads=[nwci, nzi], writes=[pb])
                                S.op("pe", lambda e: e.matmul(pb.t[:, :n], nwci.t[:, wsl], z4.t[:, jj, l0:l0 + n],
                                                              start=False, stop=(jj == 3)), reads=[nwci, z4], writes=[pb])
                            no = 128 * PC
                            o0 = (s * c.SEGB + q0) * 128
                            sel, yf, zf, zb = ybufs[yk["i"] % 3]; yk["i"] += 1
                            p4 = pb.t[:, :n].rearrange("p (j two r) -> p j two r", two=2, r=128)
                            s3 = sel.t[:, :no].rearrange("p (j r) -> p j r", r=128)
                            S.op("dve", lambda e: e.tensor_scalar(s3, p4[:, :, 0, :], w01t.t[:, 0:1], None, ALU.mult), reads=[pb, w01t], writes=[sel])
                            S.op("dve", lambda e: e.scalar_tensor_tensor(s3, p4[:, :, 1, :], w01t.t[:, 1:2], s3, ALU.mult, ALU.add),
                                 reads=[pb, w01t, sel], writes=[sel])
                            S.op("dve", lambda e: e.scalar_tensor_tensor(yf.t[:, :no], uo.t[:, o0:o0 + no], dskt.t[:, ct:ct + 1], sel.t[:, :no],
                                                                         ALU.mult, ALU.add), reads=[uo, dskt, sel], writes=[yf])
                            S.op("act", lambda e: e.activation(zf.t[:, :no], yf.t[:, :no], AF.Gelu), reads=[yf], writes=[zf])
                            S.op("dve", lambda e: e.tensor_copy(zb.t[:, :no], zf.t[:, :no]), reads=[zf], writes=[zb])
                            S.dma("sp", ZFd.t[ct * 128:(ct + 1) * 128, o0:o0 + no], zf.t[:, :no], reads=[zf], writes=[ZFd], part=True)
                            S.dma("sp", ZTd.t[ct * 128:(ct + 1) * 128, o0:o0 + no], zb.t[:, :no], reads=[zb], writes=[ZTd], part=True)

                    for i in range(len(units) + LA):
                        if i < len(units):
                            stage1(i, *units[i])
                        if i >= LA:
                            stage2(i - LA, *units[i - LA])
                        yield

        def glu_phase():
            with ExitStack() as st:
                NP = c.NPAN
                zp = sb(st, "zp", [128, c.NCT, NP], BF16)
                wbufs = [sb(st, f"gw{i}", [128, c.NCT, 512], BF16) for i in range(2)]
                gbufs = [(sb(st, f"gsg{i}", [128, 512], F32), sb(st, f"gzt{i}", [128, 512], F32), sb(st, f"gob{i}", [128, 512], BF16)) for i in range(3)]
                gk = {"i": 0}
                for p0 in range(0, NO, NP):
                    S.dma("sp", zp.t[:], ZTd.t[:, p0:p0 + NP].rearrange("(k p) n -> p k n", p=128), reads=[ZTd], writes=[zp])

                    def epi(col, m, n0, n1, pss):
                        n = n1 - n0
                        sg, zt, ob = gbufs[gk["i"] % 3]; gk["i"] += 1
                        S.op("act", lambda e: e.activation(sg.t[:m, :n], pss[0].t[:m, :n], AF.Sigmoid), reads=[pss[0]], writes=[sg])
                        S.dma("sp", zt.t[:m, :n], ZFd.t[col:col + m, p0 + n0:p0 + n1], reads=[ZFd], writes=[zt])
                        S.op("dve", lambda e: e.tensor_tensor(ob.t[:m, :n], zt.t[:m, :n], sg.t[:m, :n], ALU.mult), reads=[zt, sg], writes=[ob])
                        S.dma("sp", SSTd.t[col:col + m, p0 + n0:p0 + n1], ob.t[:m, :n], reads=[ob], writes=[SSTd], part=True)
                    gemm("fm", wbufs, w_glu, c.SW, 0, c.SW, lambda kt, n0, n1: zp.t[:, kt, n0:n1], [zp], NP, epi)

        HG = min(2, H)
        QG = min(4, NB)

        def attn_prep(stp, st):
            if True:
                cball = sb(stp, "cball", [128, NKT, H], F32)
                mA = sb(stp, "mA", [128, 128], BF16); mB = sb(stp, "mB", [128, 128], BF16)
                cq = sb(st, "cq", [16, NO], F32)
                c3 = [sb(st, f"c3_{i}", [16, NO], BF16) for i in range(3)]
                r1 = sb(st, "cr1", [16, NO], F32)
                S.dma("pool", mA.t[:], maskA, writes=[mA]); S.dma("pool", mB.t[:], maskB, writes=[mB])
                for kt in range(NKT):
                    a, nk = ktile(kt)
                    pb = ps[kt % 2]
                    S.op("pe", lambda e: e.transpose(pb.t[:nk, :H], cacc.t[:H, a:a + nk], id_f.t[:H, :H]), reads=[cacc, id_f], writes=[pb])
                    S.op("dve", lambda e: e.tensor_scalar(cball.t[:nk, kt, :], pb.t[:nk, :H], -1.0, -STAB, ALU.mult, ALU.add),
                         reads=[pb], writes=[cball])
                v4 = cacc.t[:H, 16:16 + 256 * NB].rearrange("h (j two r) -> h j two r", two=2, r=128)
                d3 = cq.t[:H, :].rearrange("h (j r) -> h j r", r=128)
                S.op("dve", lambda e: e.tensor_scalar(d3, v4[:, :, 0, :], w01t.t[:H, 0:1], None, ALU.mult), reads=[cacc, w01t], writes=[cq])
                S.op("dve", lambda e: e.scalar_tensor_tensor(d3, v4[:, :, 1, :], w01t.t[:H, 1:2], d3, ALU.mult, ALU.add),
                     reads=[cacc, w01t, cq], writes=[cq])
                S.op("dve", lambda e: e.tensor_scalar(cq.t[:H, :], cq.t[:H, :], math.sqrt(128.0), None, ALU.mult), reads=[cq], writes=[cq])
                S.op("dve", lambda e: e.tensor_copy(c3[0].t[:H, :], cq.t[:H, :]), reads=[cq], writes=[c3[0]])
                S.op("dve", lambda e: e.tensor_sub(r1.t[:H, :], cq.t[:H, :], c3[0].t[:H, :]), reads=[cq, c3[0]], writes=[r1])
                S.op("dve", lambda e: e.tensor_copy(c3[1].t[:H, :], r1.t[:H, :]), reads=[r1], writes=[c3[1]])
                S.op("dve", lambda e: e.tensor_sub(r1.t[:H, :], r1.t[:H, :], c3[1].t[:H, :]), reads=[r1, c3[1]], writes=[r1])
                S.op("dve", lambda e: e.tensor_copy(c3[2].t[:H, :], r1.t[:H, :]), reads=[r1], writes=[c3[2]])
                for i in range(3):
                    S.dma("sp", CQ3d.t[i], c3[i].t[:H, :], reads=[c3[i]], writes=[CQ3d], part=True)
                S.dma("sp", CQd.t[:, :], cq.t[:H, :], reads=[cq], writes=[CQd])
            return cball, mA, mB

        def attn_gen(st, cball, mA, mB):
            if True:
                vg = sb(st, "vg", [128, NKT, HG * 128], BF16)
                kh = [sb(st, f"kh{i}", [128, L], BF16) for i in range(2)]
                qh = [sb(st, f"qh{i}", [128, NO], BF16) for i in range(2)]
                cqh = [sb(st, f"cqh{i}", [3, NO], BF16) for i in range(2)]
                pt = [sb(st, f"pt{i}", [128, 512], BF16) for i in range(3)]
                rs = [sb(st, f"rs{i}", [128, 512], F32) for i in range(2)]
                ao = [sb(st, f"ao{i}", [128, NO], BF16) for i in range(2)]
                pti = 0
                gi = 0
                scale = 1.0 / math.sqrt(128.0)
                for hg in range(0, H, HG):
                    S.dma("sp", vg.t[:16, 0, :], Vd.t[0:16, hg * 128:(hg + HG) * 128], reads=[Vd], writes=[vg], part=True)
                    for t0 in range(0, 2 * NB, 8):
                        t1 = min(2 * NB, t0 + 8)
                        S.dma("sp", vg.t[:, 1 + t0:1 + t1, :],
                              Vd.t[16 + 128 * t0:16 + 128 * t1, hg * 128:(hg + HG) * 128].rearrange("(t p) c -> p t c", p=128),
                              reads=[Vd], writes=[vg], part=True)
                    for h in range(hg, hg + HG):
                        khb, qhb, cqb, aob = kh[h % 2], qh[h % 2], cqh[h % 2], ao[h % 2]
                        S.dma("sp", khb.t[:], KTd.t[h], reads=[KTd], writes=[khb])
                        S.dma("sp", qhb.t[:], QTd.t[h], reads=[QTd], writes=[qhb])
                        S.dma("sp", cqb.t[:], CQ3d.t[:, h, :], reads=[CQ3d], writes=[cqb])
                        for g in range(0, NB, QG):
                            OT = ps[2 + gi % 2]; SM = ps[4]
                            gi += 1
                            W = QG * 128
                            ktmax = 2 * (g + QG - 1) + 2
                            pend = None
                            for kt in range(0, ktmax + 2):
                                if kt <= ktmax:
                                    a, nk = ktile(kt)
                                    jmin = max(g, (kt - 1) // 2) if kt > 0 else g
                                    q0 = (jmin - g) * 128
                                    qa, qb = g * 128 + q0, (g + QG) * 128
                                    STb = ps[kt % 2]
                                    S.op("pe", lambda e: e.matmul(STb.t[:nk, q0:W], khb.t[:, a:a + nk], qhb.t[:, qa:qb], start=True, stop=False),
                                         reads=[khb, qhb], writes=[STb])
                                    S.op("pe", lambda e: e.matmul(STb.t[:nk, q0:W], ones_bf.t[0:3, :nk], cqb.t[0:3, qa:qb], start=False, stop=True),
                                         reads=[ones_bf, cqb], writes=[STb])
                                    p = pt[pti % 3]; pti += 1
                                    S.op("act", lambda e: e.activation(p.t[:nk, q0:W], STb.t[:nk, q0:W], AF.Exp, bias=cball.t[:nk, kt, h:h + 1], scale=scale),
                                         reads=[STb, cball], writes=[p])
                                    if kt >= 1 and kt % 2 == 1:
                                        j = (kt - 1) // 2
                                        if g <= j < g + QG:
                                            cs_ = slice((j - g) * 128, (j - g + 1) * 128)
                                            S.op("pool", lambda e: e.tensor_tensor(p.t[:, cs_], p.t[:, cs_], mA.t[:, :], ALU.mult), reads=[p, mA], writes=[p])
                                    if kt >= 2 and kt % 2 == 0:
                                        j = (kt - 2) // 2
                                        if g <= j < g + QG:
                                            cs_ = slice((j - g) * 128, (j - g + 1) * 128)
                                            S.op("pool", lambda e: e.tensor_tensor(p.t[:, cs_], p.t[:, cs_], mB.t[:, :], ALU.mult), reads=[p, mB], writes=[p])
                                    cur = (kt, nk, q0, p)
                                else:
                                    cur = None
                                if pend is not None:
                                    kt_, nk_, q0_, p_ = pend
                                    last = kt_ == ktmax
                                    S.op("pe", lambda e: e.matmul(OT.t[:, q0_:W], vg.t[:nk_, kt_, (h - hg) * 128:(h - hg + 1) * 128], p_.t[:nk_, q0_:W],
                                                                  start=(kt_ == 0), stop=last), reads=[vg, p_], writes=[OT])
                                    S.op("pe", lambda e: e.matmul(SM.t[:, q0_:W], ones_bf.t[:nk_, :], p_.t[:nk_, q0_:W], start=(kt_ == 0), stop=last),
                                         reads=[ones_bf, p_], writes=[SM])
                                pend = cur
                                if kt % 2 == 1:
                                    yield
                            r = rs[gi % 2]
                            S.op("dve", lambda e: e.reciprocal(r.t[:, :W], SM.t[:, :W]), reads=[SM], writes=[r])
                            S.op("dve", lambda e: e.tensor_tensor(aob.t[:, g * 128:g * 128 + W], OT.t[:, :W], r.t[:, :W], ALU.mult), reads=[OT, r], writes=[aob])
                        S.dma("sp", ATd.t[h * 128:(h + 1) * 128, :], aob.t[:], reads=[aob], writes=[ATd], part=True)

        _barrier(S)
        with ExitStack() as stp:
            with ExitStack() as st0:
                cball_, mA_, mB_ = attn_prep(stp, st0)
            _barrier(S)
            with ExitStack() as stj:
                g1_ = ssm_gen(stj)
                g2_ = attn_gen(stj, cball_, mA_, mB_)
                live = [g1_, g2_]
                while live:
                    for g_ in list(live):
                        try:
                            next(g_)
                        except StopIteration:
                            live.remove(g_)
        _barrier(S)
        glu_phase()

        def mix_phase():
            with ExitStack() as st:
                NP = min(1024, NO)
                KA, KS = c.AW // 128, c.SW // 128
                cat = sb(st, "cat", [128, KA + KS, NP], BF16)
                mixT = sb(st, "mixT", [128, KT, NP], BF16)
                wbufs = [sb(st, f"mw{i}", [128, max(KT, KA + KS), 256], BF16) for i in range(2)]
                mbufs = [(sb(st, f"ga{i}", [128, 512], F32), sb(st, f"gb{i}", [128, 512], F32),
                          sb(st, f"m1{i}", [128, 512], F32), sb(st, f"m2{i}", [128, 512], F32)) for i in range(2)]
                mk = {"i": 0}
                xo = [sb(st, f"xo{i}", [128, 512], F32) for i in range(2)]
                ho = [sb(st, f"ho{i}", [128, 512], F32) for i in range(2)]
                k = {"i": 0}
                for p0 in range(0, NO, NP):
                    S.dma("sp", cat.t[:, 0:KA, :], ATd.t[:, p0:p0 + NP].rearrange("(k p) n -> p k n", p=128), reads=[ATd], writes=[cat], part=True)
                    S.dma("sp", cat.t[:, KA:KA + KS, :], SSTd.t[:, p0:p0 + NP].rearrange("(k p) n -> p k n", p=128), reads=[SSTd], writes=[cat], part=True)

                    def epi(col, m, n0, n1, pss):
                        n = n1 - n0
                        ga, gb, m1, m2 = mbufs[mk["i"] % 2]; mk["i"] += 1
                        S.dma("sp", ga.t[:m, :n], GAd.t[col:col + m, p0 + n0:p0 + n1], reads=[GAd], writes=[ga])
                        S.dma("sp", gb.t[:m, :n], GBd.t[col:col + m, p0 + n0:p0 + n1], reads=[GBd], writes=[gb])
                        S.op("dve", lambda e: e.tensor_tensor(m1.t[:m, :n], ga.t[:m, :n], pss[0].t[:m, :n], ALU.mult), reads=[ga, pss[0]], writes=[m1])
                        S.op("dve", lambda e: e.tensor_tensor(m2.t[:m, :n], gb.t[:m, :n], pss[1].t[:m, :n], ALU.mult), reads=[gb, pss[1]], writes=[m2])
                        S.op("dve", lambda e: e.tensor_add(mixT.t[:m, col // 128, n0:n1], m1.t[:m, :n], m2.t[:m, :n]), reads=[m1, m2], writes=[mixT])
                    gemm("fm", wbufs, [(w_a, c.AW), (w_b, c.SW)], c.AW + c.SW, 0, D, lambda kt, n0, n1: cat.t[:, kt, n0:n1], [cat], NP, epi,
                         kgroups=[(0, KA), (KA, KA + KS)])

                    def epi2(t0, t1, col, cw, pss):
                        r = t1 - t0
                        x_ = xo[k["i"] % 2]; h_ = ho[k["i"] % 2]; k["i"] += 1
                        S.dma("sp", x_.t[:r, :cw], x_own[p0 + t0:p0 + t1, col:col + cw], writes=[x_])
                        S.op("dve", lambda e: e.tensor_tensor(h_.t[:r, :cw], x_.t[:r, :cw], pss[0].t[:r, :cw], ALU.add), reads=[x_, pss[0]], writes=[h_])
                        S.dma("act", H1d.t[p0 + t0:p0 + t1, col:col + cw], h_.t[:r, :cw], reads=[h_], writes=[H1d], part=True)
                    gemm("tm", wbufs, w_out, D, 0, D, lambda kt, t0, t1: mixT.t[:, kt, t0:t1], [mixT], NP, epi2)

        _barrier(S)
        mix_phase()

        def peer_phase():
            with ExitStack() as st:
                NP = c.PPAN
                NTT = NP // 128
                pan = sb(st, "ppan", [128, KT, NP], BF16)
                et = [sb(st, f"et{i}", [128, 16, 128], F32) for i in range(NTT)]
                thr = sb(st, "thr", [128, NTT, 8], F32); rz = sb(st, "rz", [128, NTT, 8], F32)
                for p0 in range(0, NO, NP):
                    _barrier(S)
                    with ExitStack() as s1:
                        grep = sb(s1, "g2", [128, D], F32); S.dma("sp", grep.t[:], g2rep, writes=[grep])
                        nt = make_nt(s1, "p")
                        nt(H1d.t[p0:p0 + NP, :], NP, grep, pan, srcbuf=H1d)
                    _barrier(S)
                    with ExitStack() as s2:
                        qpT = sb(s2, "qpT", [128, 16, NP], BF16)
                        skb = sb(s2, "skb", [128, 256], BF16); S.dma("pool", skb.t[:], skT, writes=[skb])
                        wbufs = [sb(s2, f"pw{i}", [128, KT, 256], BF16) for i in range(2)]
                        sc = sb(s2, "sc", [128, 16, 128], F32)
                        wk = sb(s2, "wk", [128, 256], F32)
                        m16 = sb(s2, "m16", [128, 16, 16], F32); e16 = sb(s2, "e16", [128, 16, 16], F32)
                        nm = sb(s2, "nm", [128, 16], F32)
                        cand = sb(s2, "cand", [128, 256], F32); c16 = sb(s2, "c16", [128, 16], F32)

                        def epi_q(col, m, n0, n1, pss):
                            copy(alt(), qpT.t[:, col // 128, n0:n1], pss[0].t[:, :n1 - n0], [pss[0]], [qpT])
                        gemm("fm", wbufs, w_query, D, 0, c.QW, lambda kt, n0, n1: pan.t[:, kt, n0:n1], [pan], NP, epi_q)
                        for tt in range(NTT):
                            ts_ = slice(tt * 128, (tt + 1) * 128)
                            for hc in range(16):
                                pb = ps[hc // 4]
                                S.op("pe", lambda e: e.matmul(pb.t[:, (hc % 4) * 128:(hc % 4 + 1) * 128], qpT.t[:, hc, ts_],
                                                              skb.t[:, (hc % 2) * 128:(hc % 2 + 1) * 128], start=True, stop=True),
                                     reads=[qpT, skb], writes=[pb])
                                if hc % 4 == 3:
                                    copy(alt(), sc.t[:, hc - 3:hc + 1, :], pb.t[:, :].rearrange("p (a b) -> p a b", b=128), [pb], [sc])
                            for hc in range(16):
                                S.op("dve", lambda e: e.max(m16.t[:, hc, 0:8], sc.t[:, hc, :]), reads=[sc], writes=[m16])
                                S.op("dve", lambda e: e.match_replace(wk.t[:, :128], m16.t[:, hc, 0:8], sc.t[:, hc, :], -1e30),
                                     reads=[sc, m16], writes=[wk])
                                S.op("dve", lambda e: e.max(m16.t[:, hc, 8:16], wk.t[:, :128]), reads=[wk], writes=[m16])
                            S.op("dve", lambda e: e.tensor_scalar(nm.t[:, :], m16.t[:, :, 0], -1.0, None, ALU.mult), reads=[m16], writes=[nm])
                            for hc in range(16):
                                S.op("act", lambda e: e.activation(et[tt].t[:, hc, :], sc.t[:, hc, :], AF.Exp, bias=nm.t[:, hc:hc + 1], scale=1.0),
                                     reads=[sc, nm], writes=[et[tt]])
                                S.op("act", lambda e: e.activation(e16.t[:, hc, :], m16.t[:, hc, :], AF.Exp, bias=nm.t[:, hc:hc + 1], scale=1.0),
                                     reads=[m16, nm], writes=[e16])
                            for h in range(8):
                                c3 = cand.t[:, :].rearrange("p (a b) -> p a b", b=16)
                                S.op("dve", lambda e: e.tensor_tensor(c3, e16.t[:, 2 * h, :].unsqueeze(2).to_broadcast([128, 16, 16]),
                                                                      e16.t[:, 2 * h + 1, :].unsqueeze(1).to_broadcast([128, 16, 16]), ALU.mult),
                                     reads=[e16], writes=[cand])
                                S.op("dve", lambda e: e.max(c16.t[:, 0:8], cand.t[:, :]), reads=[cand], writes=[c16])
                                S.op("dve", lambda e: e.match_replace(wk.t[:, :], c16.t[:, 0:8], cand.t[:, :], -1.0), reads=[cand, c16], writes=[wk])
                                S.op("dve", lambda e: e.max(c16.t[:, 8:16], wk.t[:, :]), reads=[wk], writes=[c16])
                                S.op("dve", lambda e: e.tensor_scalar(thr.t[:, tt, h:h + 1], c16.t[:, 15:16], 1.0 - 1e-5, None, ALU.mult),
                                     reads=[c16], writes=[thr])
                                S.op("dve", lambda e: e.tensor_reduce(rz.t[:, tt, h:h + 1], c16.t[:, :], AX.X, ALU.add), reads=[c16], writes=[rz])
                            S.op("dve", lambda e: e.reciprocal(rz.t[:, tt, :], rz.t[:, tt, :]), reads=[rz], writes=[rz])
                            for h in range(8):
                                S.op("dve", lambda e: e.tensor_scalar(e16.t[:, 2 * h, :], e16.t[:, 2 * h, :], rz.t[:, tt, h:h + 1], None, ALU.mult),
                                     reads=[e16, rz], writes=[e16])
                                S.op("dve", lambda e: e.tensor_scalar(et[tt].t[:, 2 * h, :], et[tt].t[:, 2 * h, :], rz.t[:, tt, h:h + 1], None, ALU.mult),
                                     reads=[et[tt], rz], writes=[et[tt]])
                                c3 = cand.t[:, :].rearrange("p (a b) -> p a b", b=16)
                                S.op("dve", lambda e: e.tensor_tensor(c3, e16.t[:, 2 * h, :].unsqueeze(2).to_broadcast([128, 16, 16]),
                                                                      e16.t[:, 2 * h + 1, :].unsqueeze(1).to_broadcast([128, 16, 16]), ALU.mult),
                                     reads=[e16], writes=[cand])
                                S.op("dve", lambda e: e.max(c16.t[:, 0:8], cand.t[:, :]), reads=[cand], writes=[c16])
                                S.op("dve", lambda e: e.match_replace(wk.t[:, :], c16.t[:, 0:8], cand.t[:, :], -1.0), reads=[cand, c16], writes=[wk])
                                S.op("dve", lambda e: e.max(c16.t[:, 8:16], wk.t[:, :]), reads=[wk], writes=[c16])
                                S.op("dve", lambda e: e.tensor_scalar(thr.t[:, tt, h:h + 1], c16.t[:, 15:16], 1.0 - 1e-5, None, ALU.mult),
                                     reads=[c16], writes=[thr])
                    _barrier(S)
                    with ExitStack() as s3:
                        wbufs = [sb(s3, f"aw{i}", [128, KT, 512], BF16) for i in range(2)]
                        gT = [sb(s3, f"gT{i}", [128, 4, NP], F32) for i in range(2)]
                        Mb = [sb(s3, f"Mb{i}", [128, 512], F32) for i in range(3)]
                        Mh = [sb(s3, f"Mh{i}", [128, 512], BF16) for i in range(3)]
                        wst = [sb(s3, f"wst{i}", [128, 4, NP], BF16) for i in range(2)]
                        k = {"m": 0}
                        NSB = c.PN // 512
                        WT = [ps[4 + b] for b in range(4)]

                        def load_w(sbi):
                            wt = wbufs[sbi % 2]
                            for k0 in range(0, KT, 8):
                                k1 = min(KT, k0 + 8)
                                srcap = euT[k0 * 128:k1 * 128, sbi * 512:(sbi + 1) * 512].rearrange("(kt p) c -> p kt c", p=128)
                                S.dma("pool", wt.t[:, k0:k1, :], srcap, writes=[wt], part=True)

                        def issue_AT(sbi):
                            wt = wbufs[sbi % 2]
                            g_ = gT[sbi % 2]
                            for j in range(4):
                                pb = ps[j]
                                for kt in range(KT):
                                    S.op("pe", lambda e: e.matmul(pb.t[:, :NP], wt.t[:, kt, j * 128:(j + 1) * 128], pan.t[:, kt, 0:NP],
                                                                  start=(kt == 0), stop=(kt == KT - 1)), reads=[wt, pan], writes=[pb])
                                    if kt % 4 == 3 and kt != KT - 1:
                                        yield
                                S.op("act", lambda e: e.activation(g_.t[:, j, :], pb.t[:, :NP], AF.Gelu), reads=[pb], writes=[g_])
                                yield

                        load_w(0)
                        if NSB > 1:
                            load_w(1)
                        for _ in issue_AT(0):
                            pass
                        for sbi in range(NSB):
                            if sbi + 2 < NSB:
                                load_w(sbi + 2)
                            nxt = issue_AT(sbi + 1) if sbi + 1 < NSB else None
                            g_ = gT[sbi % 2]
                            col0 = sbi * 512
                            i1a = col0 // 128
                            pairs = [(tt, h) for tt in range(NTT) for h in range(8)]
                            held = []
                            for i in range(len(pairs) + 1):
                                if i < len(pairs):
                                    tt, h = pairs[i]
                                    M_ = Mb[k["m"] % 3]; Mh_ = Mh[k["m"] % 3]; k["m"] += 1
                                    P3 = M_.t[:, :].rearrange("p (a b) -> p a b", b=128)
                                    S.op("dve", lambda e: e.tensor_tensor(P3, et[tt].t[:, 2 * h, i1a:i1a + 4].unsqueeze(2).to_broadcast([128, 4, 128]),
                                                                          et[tt].t[:, 2 * h + 1, :].unsqueeze(1).to_broadcast([128, 4, 128]), ALU.mult),
                                         reads=[et[tt]], writes=[M_])
                                    held.append((tt, h, M_, Mh_))
                                if i >= 1:
                                    tt, h, M_, Mh_ = held.pop(0)
                                    S.op("dve", lambda e: e.scalar_tensor_tensor(Mh_.t[:, :], M_.t[:, :], thr.t[:, tt, h:h + 1], M_.t[:, :],
                                                                                 ALU.is_ge, ALU.mult), reads=[M_, thr], writes=[Mh_])
                                    for b in range(4):
                                        S.op("pe", lambda e: e.matmul(WT[b].t[:, tt * 128:(tt + 1) * 128], Mh_.t[:, b * 128:(b + 1) * 128],
                                                                      id_bf.t[:, :], start=(h == 0), stop=(h == 7)),
                                             reads=[Mh_, id_bf], writes=[WT[b]])
                                    if nxt is not None:
                                        try:
                                            next(nxt)
                                        except StopIteration:
                                            nxt = None
                            if nxt is not None:
                                for _ in nxt:
                                    pass
                            ws = wst[sbi % 2]
                            for b in range(4):
                                S.op("dve", lambda e: e.tensor_tensor(ws.t[:, b, :], WT[b].t[:, :NP], g_.t[:, b, :], ALU.mult),
                                     reads=[WT[b], g_], writes=[ws])
                            for b in range(4):
                                S.dma("act", WGTd.t[col0 + b * 128:col0 + (b + 1) * 128, p0:p0 + NP], ws.t[:, b, :], reads=[ws], writes=[WGTd], part=True)

            _barrier(S)
            with ExitStack() as st:
                NY = min(1024, NO)
                NYT = NY // 128
                EC = 16
                wv = [sb(st, f"yv{i}", [128, EC, 512], BF16) for i in range(2)]
                wa = [sb(st, f"ya{i}", [128, EC, NY], BF16) for i in range(2)]
                h1t = [sb(st, f"yh{i}", [128, 512], F32) for i in range(2)]
                yo = [sb(st, f"yo{i}", [128, 512], F32) for i in range(2)]
                NEC = c.PN // (128 * EC)
                gi = 0
                ci = 0
                oi = 0
                for cb in range(0, D, 512):
                    for p0 in range(0, NO, NY):
                        bks = [ps[i] for i in range(NYT)]
                        gi += 1
                        for ec in range(NEC):
                            e0 = ec * EC * 128
                            v_, a_ = wv[ci % 2], wa[ci % 2]
                            ci += 1
                            for k0 in range(0, EC, 8):
                                S.dma("pool", v_.t[:, k0:k0 + 8, :], ev[e0 + k0 * 128:e0 + (k0 + 8) * 128, cb:cb + 512].rearrange("(k p) c -> p k c", p=128),
                                      writes=[v_], part=True)
                                S.dma("sp", a_.t[:, k0:k0 + 8, :], WGTd.t[e0 + k0 * 128:e0 + (k0 + 8) * 128, p0:p0 + NY].rearrange("(k p) c -> p k c", p=128),
                                      reads=[WGTd], writes=[a_], part=True)
                            for tt in range(NYT):
                                for kt in range(EC):
                                    S.op("pe", lambda e: e.matmul(bks[tt].t[:, :512], a_.t[:, kt, tt * 128:(tt + 1) * 128], v_.t[:, kt, :],
                                                                  start=(ec == 0 and kt == 0), stop=(ec == NEC - 1 and kt == EC - 1)),
                                         reads=[a_, v_], writes=[bks[tt]])
                        for tt in range(NYT):
                            h_, o_ = h1t[oi % 2], yo[oi % 2]
                            oi += 1
                            r0 = p0 + tt * 128
                            S.dma("sp", h_.t[:], H1d.t[r0:r0 + 128, cb:cb + 512], reads=[H1d], writes=[h_])
                            S.op("dve", lambda e: e.tensor_tensor(o_.t[:], h_.t[:], bks[tt].t[:, :512], ALU.add), reads=[h_, bks[tt]], writes=[o_])
                            S.dma("act", out_own[r0:r0 + 128, cb:cb + 512], o_.t[:], reads=[o_], writes=[OUTb], part=True)

        _barrier(S)
        peer_phase()

        for i in range(NDSEM):
            if S.dval[i]:
                nc.sync.wait_ge(S.dsem[i], S.dval[i])
        print("instructions:", S.ninst, {k: v for k, v in S.cnt.items()})
    return nc


def _prep(c, inp, core):
    f32 = np.float32
    b, hh = core // 2, core % 2
    D, SEQ, NO, H, G, NJ, NCT = c.D, c.SEQ, c.NO, c.H, c.G, c.NJ, c.NCT
    A = lambda v: np.ascontiguousarray(np.asarray(v), dtype=f32)
    x = np.asarray(inp["x"][b], dtype=f32)
    m = {}
    m["x_ctx"] = np.concatenate([np.asarray(inp["meta_tokens"], dtype=f32), x], 0)
    m["x_own"] = A(x.reshape(SEQ // 128, 128, D)[hh::2].reshape(NO, D))
    m["g1rep"] = A(np.broadcast_to(np.asarray(inp["norm1_g"][0])[None, :], (128, D)))
    m["g2rep"] = A(np.broadcast_to(np.asarray(inp["norm2_g"][0])[None, :], (128, D)))
    m["w_in"] = A(inp["w_in"][0])
    m["bfg"] = A(np.asarray(inp["b_forget"][0]).reshape(H, 1))
    m["qg"] = A(np.asarray(inp["q_norm_g"][0]).reshape(128, 1))
    m["kg"] = A(np.asarray(inp["k_norm_g"][0]).reshape(128, 1))
    lr = np.asarray(inp["lam_re"][0], dtype=f32); li = np.asarray(inp["lam_im"][0], dtype=f32)
    ld = np.asarray(inp["log_dt"][0], dtype=f32)
    m["lrA"] = A(lr.reshape(NJ, 128).T); m["liA"] = A(li.reshape(NJ, 128).T)
    m["ldA"] = A(np.repeat(ld.reshape(NJ, 2), 64, axis=1).T)
    m["lrB"] = A(np.broadcast_to(lr.reshape(1, -1), (128, G * 64)))
    m["liB"] = A(np.broadcast_to(li.reshape(1, -1), (128, G * 64)))
    m["ldB"] = A(np.broadcast_to(np.repeat(ld, 64)[None, :], (128, G * 64)))
    bre = np.asarray(inp["b_re"][0], dtype=f32); bim = np.asarray(inp["b_im"][0], dtype=f32)
    cre = np.asarray(inp["c_re"][0], dtype=f32); cim = np.asarray(inp["c_im"][0], dtype=f32)
    brB = np.zeros((128, G * 64), f32); biB = np.zeros((128, G * 64), f32)
    crB = np.zeros((128, NJ * 128), f32); ciB = np.zeros((128, NJ * 128), f32)
    for g in range(G):
        r0 = (g % 8) * 16
        brB[r0:r0 + 16, g * 64:(g + 1) * 64] = bre[g].T
        biB[r0:r0 + 16, g * 64:(g + 1) * 64] = bim[g].T
        j, g2 = g // 2, g % 2
        crB[g2 * 64:(g2 + 1) * 64, j * 128 + r0:j * 128 + r0 + 16] = cre[g].T
        ciB[g2 * 64:(g2 + 1) * 64, j * 128 + r0:j * 128 + r0 + 16] = cim[g].T
    m["brB"], m["biB"], m["crB"], m["ciB"] = brB, biB, crB, ciB
    m["dsk"] = A(np.asarray(inp["d_skip"][0]).reshape(NCT, 128).T)
    m["w_glu"] = A(inp["w_glu"][0]); m["w_a"] = A(inp["w_branch_attn"][0]); m["w_b"] = A(inp["w_branch_ssm"][0])
    m["w_out"] = A(inp["w_out"][0]); m["w_query"] = A(inp["w_query"][0])
    m["skT"] = A(np.asarray(inp["sub_keys"][0]).transpose(2, 0, 1).reshape(128, 256))
    m["euT"] = A(np.asarray(inp["expert_u"][0]).T); m["ev"] = A(inp["expert_v"][0])
    tri = (np.arange(128)[None, :] >= np.arange(128)[:, None]).astype(f32)
    m["maskA"] = tri if hh == 0 else np.ones((128, 128), f32)
    m["maskB"] = np.zeros((128, 128), f32) if hh == 0 else tri
    m["w01"] = A(np.broadcast_to(np.array([[1.0, 0.0]] if hh == 0 else [[0.0, 1.0]], f32), (128, 2)))
    LSM = 256 * c.SEGB + 16
    m["t_loc"] = A(np.broadcast_to(np.arange(LSM, dtype=f32)[None, :], (128, LSM)))
    io = np.arange(NO)
    pos = 16 + 128 * (2 * (io // 128) + hh) + (io % 128)
    m["t_own"] = A(np.broadcast_to(pos.astype(f32)[None, :], (128, NO)))
    m["ident"] = np.eye(128, dtype=f32)
    return m


_NC_CACHE = {}


def run_cfg(c, inputs):
    key = (c.D, c.SEQ, c.B)
    if key not in _NC_CACHE:
        _NC_CACHE[key] = build(c)
    nc = _NC_CACHE[key]
    ncores = 2 * c.B
    shared = None
    in_maps = []
    for core in range(ncores):
        in_maps.append(_prep(c, inputs, core))
    res = run_bass_kernel_spmd(nc, in_maps, core_ids=list(range(ncores)))
    if getattr(c, "debug", False):
        c.dbg = res.results
    out = np.zeros((c.B, c.SEQ, c.D), np.float32)
    for core in range(ncores):
        b, hh = core // 2, core % 2
        o = np.asarray(res.results[core]["out_own"], dtype=np.float32).reshape(c.NB, 128, c.D)
        out[b].reshape(c.SEQ // 128, 128, c.D)[hh::2] = o
    return out


def kernel(**inputs):
    return run_cfg(Cfg(), inputs)
```

```python
import math
from contextlib import ExitStack
import numpy as np
import ml_dtypes
import concourse.bass as bass
import concourse.mybir as mybir
from concourse.bass_utils import run_bass_kernel_spmd

F32 = mybir.dt.float32
BF16 = mybir.dt.bfloat16
I32 = mybir.dt.int32
AF = mybir.ActivationFunctionType
ALU = mybir.AluOpType
AX = mybir.AxisListType

EPOCH = 16000
NDSEM = 40
TWO_PI = 2.0 * math.pi
STAB = 30.0


class Buf:
    __slots__ = ("w", "r")

    def __init__(self):
        self.w = {}
        self.r = {}


class TT:
    def __init__(self, t):
        self.t = t
        self.b = Buf()


def _b(x):
    return x.b if isinstance(x, TT) else x


class Sched:
    def __init__(self, nc, stack):
        self.nc = nc
        self.engs = {"pe": nc.tensor, "dve": nc.vector, "act": nc.scalar,
                     "pool": nc.gpsimd, "sp": nc.sync}
        self.stack = stack
        self.esem = {}
        self.cnt = {e: 0 for e in self.engs}
        self.seen = {e: {} for e in self.engs}
        self.dsem = [stack.enter_context(nc.semaphore(f"d{i}")) for i in range(NDSEM)]
        self.dval = [0] * NDSEM
        self.dnext = 0
        self.ninst = 0

    def _esem(self, eng, epoch):
        k = (eng, epoch)
        if k not in self.esem:
            self.esem[k] = self.stack.enter_context(self.nc.semaphore(f"e_{eng}_{epoch}"))
        return self.esem[k]

    def _sem_of(self, key):
        if key[0] == "d":
            return self.dsem[key[1]]
        return self._esem(key[0], key[1])

    def _wait(self, eng, deps):
        s = self.seen[eng]
        for k, v in deps.items():
            if eng == "pe" and k[0] == "pe":
                continue
            if s.get(k, 0) < v:
                self.engs[eng].wait_ge(self._sem_of(k), v)
                s[k] = v

    @staticmethod
    def _acc(deps, d):
        for k, v in d.items():
            if deps.get(k, 0) < v:
                deps[k] = v

    def _commit(self, tok, reads, writes, part=False):
        k, v = tok
        for b in writes:
            if not part:
                b.w = {}
            b.w[k] = max(b.w.get(k, 0), v)
            b.r = {}
        for b in reads:
            if b.r.get(k, 0) < v:
                b.r[k] = v

    def op(self, eng, fn, reads=(), writes=()):
        reads = [_b(x) for x in reads]
        writes = [_b(x) for x in writes]
        deps = {}
        for b in reads:
            self._acc(deps, b.w)
        for b in writes:
            self._acc(deps, b.w)
            self._acc(deps, b.r)
        self._wait(eng, deps)
        ins = fn(self.engs[eng])
        c = self.cnt[eng]
        epoch, val = divmod(c, EPOCH)
        ins.then_inc(self._esem(eng, epoch), 1)
        self.cnt[eng] = c + 1
        self._commit(((eng, epoch), val + 1), reads, writes)
        self.ninst += 1
        return ins

    def dma(self, q, out, in_, reads=(), writes=(), part=False):
        reads = [_b(x) for x in reads]
        writes = [_b(x) for x in writes]
        deps = {}
        for b in reads:
            self._acc(deps, b.w)
        for b in writes:
            if not part:
                self._acc(deps, b.w)
            self._acc(deps, b.r)
        i = self.dnext
        self.dnext = (i + 1) % NDSEM
        if self.dval[i]:
            deps[("d", i)] = max(deps.get(("d", i), 0), self.dval[i])
        self._wait(q, deps)
        ins = self.engs[q].dma_start(out=out, in_=in_)
        ins.then_inc(self.dsem[i], 16)
        self.dval[i] += 16
        assert self.dval[i] < 60000
        self._commit((("d", i), self.dval[i]), reads, writes, part=part)
        self.ninst += 1
        return ins


def _barrier(S):
    deps = {}
    for e, cnt in S.cnt.items():
        if cnt:
            epoch, val = divmod(cnt - 1, EPOCH)
            deps[(e, epoch)] = val + 1
    for i in range(NDSEM):
        if S.dval[i]:
            deps[("d", i)] = S.dval[i]
    for e in S.engs:
        s = S.seen[e]
        for k, v in deps.items():
            if k[0] == e:
                continue
            if s.get(k, 0) < v:
                S.engs[e].wait_ge(S._sem_of(k), v)
                s[k] = v


class Cfg:
    def __init__(s, D=4096, SEQ=4096, B=4, NPAN=1024, PPAN=512, SEGB=2, WB6=256):
        s.D, s.SEQ, s.B = D, SEQ, B
        s.NM = 16
        s.H = D // 256
        s.AW = s.H * 128
        s.G = D // 32
        s.SW = s.G * 16
        s.NJ = s.G // 2
        s.NCT = s.SW // 128
        s.PH, s.PK, s.TOPK = 8, 128, 16
        s.PN = s.PK * s.PK
        s.QW = s.PH * 256
        s.L = SEQ + s.NM
        s.NO = SEQ // 2
        s.NB = s.NO // 128
        s.KT = D // 128
        s.NPAN = min(NPAN, s.NO)
        s.PPAN = min(PPAN, s.NO)
        s.SEGB = min(SEGB, s.NB)
        s.WB6 = WB6
        s.COL_Q = 0
        s.COL_K = s.AW
        s.COL_V = 2 * s.AW
        s.COL_F = 3 * s.AW
        s.COL_U = s.COL_F + s.H
        s.COL_GA = s.COL_U + s.SW
        s.COL_GB = s.COL_GA + D
        s.NCOLS = s.COL_GB + D


def build(cfg):
    c = cfg
    D, L, NO, KT, H, NB = c.D, c.L, c.NO, c.KT, c.H, c.NB
    nc = bass.Bass("TRN2", target_bir_lowering=False)

    def din(name, shape, dt=F32):
        return nc.dram_tensor(name, list(shape), dt, kind="ExternalInput").ap()

    def dscr(name, shape, dt):
        return TT(nc.dram_tensor(name, list(shape), dt, kind="ExternalOutput" if getattr(c, "debug", False) else "Internal").ap())

    x_ctx = din("x_ctx", [L, D]); x_own = din("x_own", [NO, D])
    g1rep = din("g1rep", [128, D]); g2rep = din("g2rep", [128, D])
    w_in = din("w_in", [D, c.NCOLS])
    bfg = din("bfg", [H, 1]); qg = din("qg", [128, 1]); kg = din("kg", [128, 1])
    lrA = din("lrA", [128, c.NJ]); liA = din("liA", [128, c.NJ]); ldA = din("ldA", [128, c.NJ])
    lrB = din("lrB", [128, c.NJ * 128]); liB = din("liB", [128, c.NJ * 128]); ldB = din("ldB", [128, c.NJ * 128])
    brB = din("brB", [128, c.NJ * 128]); biB = din("biB", [128, c.NJ * 128])
    crB = din("crB", [128, c.NJ * 128]); ciB = din("ciB", [128, c.NJ * 128])
    dsk = din("dsk", [128, c.NCT])
    w_glu = din("w_glu", [c.SW, c.SW]); w_a = din("w_a", [c.AW, D]); w_b = din("w_b", [c.SW, D])
    w_out = din("w_out", [D, D]); w_query = din("w_query", [D, c.QW])
    skT = din("skT", [128, 2 * 128])
    euT = din("euT", [D, c.PN]); ev = din("ev", [c.PN, D])
    maskA = din("maskA", [128, 128]); maskB = din("maskB", [128, 128])
    w01 = din("w01", [128, 2])
    t_loc = din("t_loc", [128, 256 * c.SEGB + 16]); t_own = din("t_own", [128, NO])
    ident = din("ident", [128, 128])
    out_own = nc.dram_tensor("out_own", [NO, D], F32, kind="ExternalOutput").ap()

    KTd = dscr("KTd", [H, 128, L], BF16)
    Vd = dscr("Vd", [L, c.AW], BF16)
    UTd = dscr("UTd", [c.SW, L], BF16)
    QTd = dscr("QTd", [H, 128, NO], BF16)
    GAd = dscr("GAd", [D, NO], F32)
    GBd = dscr("GBd", [D, NO], F32)
    UOd = dscr("UOd", [c.SW, NO], F32)
    CQd = dscr("CQd", [H, NO], F32)
    CQ3d = dscr("CQ3d", [3, H, NO], BF16)
    BBRd = dscr("BBRd", [128, c.NJ * 128], BF16)
    BBId = dscr("BBId", [128, c.NJ * 128], BF16)
    ZFd = dscr("ZFd", [c.SW, NO], F32)
    ZTd = dscr("ZTd", [c.SW, NO], BF16)
    SSTd = dscr("SSTd", [c.SW, NO], BF16)
    ATd = dscr("ATd", [c.AW, NO], BF16)
    H1d = dscr("H1d", [NO, D], F32)
    WGTd = dscr("WGTd", [c.PN, NO], BF16)
    OUTb = Buf()

    with ExitStack() as gst:
        S = Sched(nc, gst)

        uid = {"n": 0}

        def sb(st, name, shape, dt):
            uid["n"] += 1
            return TT(st.enter_context(nc.sbuf_tensor(f"{name}_{uid['n']}", list(shape), dt)))

        ps = [TT(gst.enter_context(nc.psum_tensor(f"ps{i}", [128, 512], F32))) for i in range(8)]
        psb = []
        for i in range(2):
            v = TT(ps[6 + i].t.bitcast(BF16))
            v.b = ps[6 + i].b
            psb.append(v)
        id_bf = sb(gst, "id_bf", [128, 128], BF16)
        id_f = sb(gst, "id_f", [128, 128], F32)
        ones_bf = sb(gst, "ones_bf", [128, 128], BF16)
        ones_f = sb(gst, "ones_f", [1, 128], F32)
        cst = sb(gst, "cst", [128, 8], F32)
        w01t = sb(gst, "w01t", [128, 2], F32)
        qgt = sb(gst, "qgt", [128, 1], F32); kgt = sb(gst, "kgt", [128, 1], F32)
        cacc = sb(gst, "cacc", [16, L], F32)
        rhoA = sb(gst, "rhoA", [128, c.NJ], F32)
        phiA = sb(gst, "phiA", [128, c.NJ], F32)
        S.dma("pool", id_bf.t[:], ident, writes=[id_bf])
        S.dma("sp", id_f.t[:], ident, writes=[id_f])
        S.dma("sp", w01t.t[:], w01, writes=[w01t])
        S.dma("sp", qgt.t[:], qg, writes=[qgt])
        S.dma("sp", kgt.t[:], kg, writes=[kgt])
        S.op("dve", lambda e: e.memset(ones_bf.t[:], 1.0), writes=[ones_bf])
        S.op("dve", lambda e: e.memset(ones_f.t[:], 1.0), writes=[ones_f])
        S.op("dve", lambda e: e.memset(cst.t[:, 0:1], 1e-6), writes=[cst])
        S.op("dve", lambda e: e.memset(cst.t[:, 1:2], 1.0), writes=[cst])
        S.op("dve", lambda e: e.memset(cst.t[:, 2:3], 0.0), writes=[cst])
        EPS = cst.t[:, 0:1]
        ONE = cst.t[:, 1:2]

        rr = {"e": 0}

        def alt():
            rr["e"] ^= 1
            return "act" if rr["e"] else "dve"

        def copy(eng, out, in_, reads, writes):
            if eng == "act":
                S.op("act", lambda e: e.activation(out, in_, AF.Copy), reads=reads, writes=writes)
            else:
                S.op(eng, lambda e: e.tensor_copy(out, in_), reads=reads, writes=writes)

        def make_nt(st, tagp):
          xt = sb(st, tagp + "xt", [128, D], F32)
          xn = sb(st, tagp + "xn", [128, D], BF16)
          ss = sb(st, tagp + "ss", [128, 2], F32)

          def norm_transpose(src, n_rows, grep, panel, srcbuf=None):
            for t0 in range(0, n_rows, 128):
                r = min(128, n_rows - t0)
                S.dma("sp", xt.t[:r, :], src[t0:t0 + r, :], reads=[srcbuf] if srcbuf else [], writes=[xt])
                S.op("dve", lambda e: e.memset(ss.t[:r, 0:1], 0.0), writes=[ss])
                S.op("act", lambda e: e.activation(xn.t[:r, :], xt.t[:r, :], AF.Square, accum_out=ss.t[:r, 0:1]),
                     reads=[xt], writes=[xn, ss])
                S.op("dve", lambda e: e.tensor_scalar(ss.t[:r, 1:2], ss.t[:r, 0:1], 1.0 / D, 1e-6, ALU.mult, ALU.add),
                     reads=[ss], writes=[ss])
                S.op("act", lambda e: e.activation(ss.t[:r, 1:2], ss.t[:r, 1:2], AF.Sqrt), reads=[ss], writes=[ss])
                S.op("dve", lambda e: e.reciprocal(ss.t[:r, 1:2], ss.t[:r, 1:2]), reads=[ss], writes=[ss])
                S.op("dve", lambda e: e.scalar_tensor_tensor(xn.t[:r, :], xt.t[:r, :], ss.t[:r, 1:2], grep.t[:r, :],
                                                             ALU.mult, ALU.mult),
                     reads=[xt, ss, grep], writes=[xn])
                for k0 in range(0, KT, 8):
                    k1 = min(KT, k0 + 8)
                    pb = psb[(k0 // 8) % 2]
                    for kt in range(k0, k1):
                        S.op("pe", lambda e: e.transpose(pb.t[:, (kt - k0) * 128:(kt - k0) * 128 + r],
                                                         xn.t[:r, kt * 128:(kt + 1) * 128], id_bf.t[:r, :r]),
                             reads=[xn, id_bf], writes=[pb])
                    src_ap = pb.t[:, 0:(k1 - k0) * 128].rearrange("p (k r) -> p k r", r=128)[:, :, :r]
                    copy(alt(), panel.t[:, k0:k1, t0:t0 + r], src_ap, [pb], [panel])
          return norm_transpose

        wstate = {"i": 0}

        def gemm(mode, wbufs, wsrc, K, c0, ncols, act, actbufs, N, epi, kgroups=None, WB=None, banks=(0, 1, 2, 3),
                 wq="pool", wsrcbuf=None):
            KTl = K // 128
            WB = WB or wbufs[0].t.shape[2]
            kgroups = kgroups or [(0, KTl)]
            bi = 0
            for sb0 in range(0, ncols, WB):
                cw = min(WB, ncols - sb0)
                wt = wbufs[wstate["i"] % len(wbufs)]
                wstate["i"] += 1
                srcs = wsrc if isinstance(wsrc, list) else [(wsrc, K)]
                kbase = 0
                for (wap, Ks) in srcs:
                    for k0 in range(0, Ks // 128, 8):
                        k1 = min(Ks // 128, k0 + 8)
                        srcap = wap[k0 * 128:k1 * 128, c0 + sb0:c0 + sb0 + cw].rearrange("(kt p) c -> p kt c", p=128)
                        S.dma(wq, wt.t[:, kbase + k0:kbase + k1, :cw], srcap, reads=[wsrcbuf] if wsrcbuf else [],
                              writes=[wt], part=True)
                    kbase += Ks // 128
                if mode == "fm":
                    for j0 in range(0, cw, 128):
                        m = min(128, cw - j0)
                        for n0 in range(0, N, 512):
                            n1 = min(N, n0 + 512)
                            pss = []
                            for (ka, kb) in kgroups:
                                pb = ps[banks[bi % len(banks)]]
                                bi += 1
                                for kt in range(ka, kb):
                                    S.op("pe", lambda e: e.matmul(pb.t[:m, :n1 - n0], wt.t[:, kt, j0:j0 + m],
                                                                  act(kt, n0, n1), start=(kt == ka), stop=(kt == kb - 1)),
                                         reads=[wt] + actbufs, writes=[pb])
                                pss.append(pb)
                            epi(c0 + sb0 + j0, m, n0, n1, pss)
                else:
                    for t0 in range(0, N, 128):
                        t1 = min(N, t0 + 128)
                        pss = []
                        for (ka, kb) in kgroups:
                            pb = ps[banks[bi % len(banks)]]
                            bi += 1
                            for kt in range(ka, kb):
                                S.op("pe", lambda e: e.matmul(pb.t[:t1 - t0, :cw], act(kt, t0, t1), wt.t[:, kt, :cw],
                                                              start=(kt == ka), stop=(kt == kb - 1)),
                                     reads=[wt] + actbufs, writes=[pb])
                            pss.append(pb)
                        epi(t0, t1, c0 + sb0, cw, pss)

        def coeffs(st, tag, lr_ap, li_ap, ld_ap, F):
            lr = sb(st, tag + "lr", [128, F], F32); li = sb(st, tag + "li", [128, F], F32)
            ld = sb(st, tag + "ld", [128, F], F32)
            t1 = sb(st, tag + "t1", [128, F], F32); t2 = sb(st, tag + "t2", [128, F], F32)
            ti = sb(st, tag + "ti", [128, F], I32)
            rho = sb(st, tag + "rho", [128, F], F32); phi = sb(st, tag + "phi", [128, F], F32)
            sn = sb(st, tag + "sn", [128, F], F32); cs = sb(st, tag + "cs", [128, F], F32)
            fr = sb(st, tag + "fr", [128, F], F32); fi = sb(st, tag + "fi", [128, F], F32)

            def run(lr_src, li_src, ld_src):
                S.dma("sp", lr.t[:], lr_src, writes=[lr]); S.dma("sp", li.t[:], li_src, writes=[li])
                S.dma("sp", ld.t[:], ld_src, writes=[ld])
                S.op("act", lambda e: e.activation(ld.t[:], ld.t[:], AF.Exp), reads=[ld], writes=[ld])
                S.op("dve", lambda e: e.tensor_tensor(t1.t[:], lr.t[:], ld.t[:], ALU.mult), reads=[lr, ld], writes=[t1])
                S.op("act", lambda e: e.activation(rho.t[:], t1.t[:], AF.Exp), reads=[t1], writes=[rho])
                S.op("dve", lambda e: e.scalar_tensor_tensor(t1.t[:], li.t[:], 1.0 / TWO_PI, ld.t[:], ALU.mult, ALU.mult),
                     reads=[li, ld], writes=[t1])
                S.op("dve", lambda e: e.tensor_copy(ti.t[:], t1.t[:]), reads=[t1], writes=[ti])
                S.op("dve", lambda e: e.tensor_copy(t2.t[:], ti.t[:]), reads=[ti], writes=[t2])
                S.op("dve", lambda e: e.tensor_sub(phi.t[:], t1.t[:], t2.t[:]), reads=[t1, t2], writes=[phi])
                S.op("act", lambda e: e.activation(sn.t[:], phi.t[:], AF.Sin, scale=TWO_PI), reads=[phi], writes=[sn])
                S.op("dve", lambda e: e.tensor_scalar(t1.t[:], phi.t[:], 0.25, None, ALU.add), reads=[phi], writes=[t1])
                S.op("dve", lambda e: e.tensor_copy(ti.t[:], t1.t[:]), reads=[t1], writes=[ti])
                S.op("dve", lambda e: e.tensor_copy(t2.t[:], ti.t[:]), reads=[ti], writes=[t2])
                S.op("dve", lambda e: e.tensor_sub(t1.t[:], t1.t[:], t2.t[:]), reads=[t1, t2], writes=[t1])
                S.op("act", lambda e: e.activation(cs.t[:], t1.t[:], AF.Sin, scale=TWO_PI), reads=[t1], writes=[cs])
                S.op("dve", lambda e: e.tensor_tensor(cs.t[:], cs.t[:], rho.t[:], ALU.mult), reads=[cs, rho], writes=[cs])
                S.op("dve", lambda e: e.tensor_tensor(sn.t[:], sn.t[:], rho.t[:], ALU.mult), reads=[sn, rho], writes=[sn])
                S.op("dve", lambda e: e.tensor_scalar(t1.t[:], cs.t[:], -1.0, None, ALU.add), reads=[cs], writes=[t1])
                S.op("dve", lambda e: e.tensor_tensor(t2.t[:], lr.t[:], lr.t[:], ALU.mult), reads=[lr], writes=[t2])
                S.op("dve", lambda e: e.tensor_tensor(fr.t[:], li.t[:], li.t[:], ALU.mult), reads=[li], writes=[fr])
                S.op("dve", lambda e: e.tensor_add(t2.t[:], t2.t[:], fr.t[:]), reads=[t2, fr], writes=[t2])
                S.op("dve", lambda e: e.reciprocal(t2.t[:], t2.t[:]), reads=[t2], writes=[t2])
                S.op("dve", lambda e: e.tensor_tensor(fr.t[:], t1.t[:], lr.t[:], ALU.mult), reads=[t1, lr], writes=[fr])
                S.op("dve", lambda e: e.tensor_tensor(fi.t[:], sn.t[:], li.t[:], ALU.mult), reads=[sn, li], writes=[fi])
                S.op("dve", lambda e: e.tensor_add(fr.t[:], fr.t[:], fi.t[:]), reads=[fr, fi], writes=[fr])
                S.op("dve", lambda e: e.tensor_tensor(fr.t[:], fr.t[:], t2.t[:], ALU.mult), reads=[fr, t2], writes=[fr])
                S.op("dve", lambda e: e.tensor_tensor(fi.t[:], sn.t[:], lr.t[:], ALU.mult), reads=[sn, lr], writes=[fi])
                S.op("dve", lambda e: e.tensor_tensor(t1.t[:], t1.t[:], li.t[:], ALU.mult), reads=[t1, li], writes=[t1])
                S.op("dve", lambda e: e.tensor_sub(fi.t[:], fi.t[:], t1.t[:]), reads=[fi, t1], writes=[fi])
                S.op("dve", lambda e: e.tensor_tensor(fi.t[:], fi.t[:], t2.t[:], ALU.mult), reads=[fi, t2], writes=[fi])
            return run, rho, phi, fr, fi

        with ExitStack() as st:
            run, rho, phi, fr, fi = coeffs(st, "ca", None, None, None, c.NJ)
            run(lrA, liA, ldA)
            S.op("dve", lambda e: e.tensor_copy(rhoA.t[:], rho.t[:]), reads=[rho], writes=[rhoA])
            S.op("dve", lambda e: e.tensor_copy(phiA.t[:], phi.t[:]), reads=[phi], writes=[phiA])
        _barrier(S)
        with ExitStack() as st:
            FC = min(1024, c.NJ * 128)
            run, rho, phi, fr, fi = coeffs(st, "cb", None, None, None, FC)
            br = sb(st, "br", [128, FC], F32); bi_ = sb(st, "bi", [128, FC], F32)
            o1 = sb(st, "o1", [128, FC], F32); o2 = sb(st, "o2", [128, FC], F32)
            ob1 = sb(st, "ob1", [128, FC], BF16); ob2 = sb(st, "ob2", [128, FC], BF16)
            for f0 in range(0, c.NJ * 128, FC):
                run(lrB[:, f0:f0 + FC], liB[:, f0:f0 + FC], ldB[:, f0:f0 + FC])
                S.dma("sp", br.t[:], brB[:, f0:f0 + FC], writes=[br])
                S.dma("sp", bi_.t[:], biB[:, f0:f0 + FC], writes=[bi_])
                S.op("dve", lambda e: e.tensor_tensor(o1.t[:], fr.t[:], br.t[:], ALU.mult), reads=[fr, br], writes=[o1])
                S.op("dve", lambda e: e.tensor_tensor(o2.t[:], fi.t[:], bi_.t[:], ALU.mult), reads=[fi, bi_], writes=[o2])
                S.op("dve", lambda e: e.tensor_sub(ob1.t[:], o1.t[:], o2.t[:]), reads=[o1, o2], writes=[ob1])
                S.op("dve", lambda e: e.tensor_tensor(o1.t[:], fr.t[:], bi_.t[:], ALU.mult), reads=[fr, bi_], writes=[o1])
                S.op("dve", lambda e: e.tensor_tensor(o2.t[:], fi.t[:], br.t[:], ALU.mult), reads=[fi, br], writes=[o2])
                S.op("dve", lambda e: e.tensor_add(ob2.t[:], o1.t[:], o2.t[:]), reads=[o1, o2], writes=[ob2])
                S.dma("sp", BBRd.t[:, f0:f0 + FC], ob1.t[:], reads=[ob1], writes=[BBRd], part=True)
                S.dma("sp", BBId.t[:, f0:f0 + FC], ob2.t[:], reads=[ob2], writes=[BBId], part=True)

        def proj_phase(is_ctx):
            with ExitStack() as st:
                tag = "c" if is_ctx else "o"
                NPmax = c.NPAN + (c.NM if is_ctx else 0)
                panel = sb(st, tag + "pan", [128, KT, NPmax], BF16)
                grep = sb(st, tag + "g1", [128, D], F32)
                S.dma("sp", grep.t[:], g1rep, writes=[grep])
                wbufs = [sb(st, tag + f"w{i}", [128, KT, 256], BF16) for i in range(2)]
                stg = [sb(st, tag + f"stg{i}", [128, 512], BF16) for i in range(3)]
                stf = [sb(st, tag + f"stf{i}", [128, 512], F32) for i in range(3)]
                sq = sb(st, tag + "sq", [128, 512], BF16)
                rinv = sb(st, tag + "rinv", [128, 512], F32)
                bft = sb(st, tag + "bft", [16, 1], F32)
                lf = sb(st, tag + "lf", [16, 512], F32)
                ones_t = sb(st, tag + "ones", [16, 512], F32)
                S.dma("sp", bft.t[:H, :], bfg, writes=[bft])
                S.op("dve", lambda e: e.tensor_scalar(bft.t[:H, :], bft.t[:H, :], -1.0, None, ALU.mult), reads=[bft], writes=[bft])
                S.op("dve", lambda e: e.memset(ones_t.t[:], 1.0), writes=[ones_t])
                si = {"g": 0, "f": 0}
                src = x_ctx if is_ctx else x_own
                ntot = L if is_ctx else NO
                p0 = 0
                nt = make_nt(st, tag)
                while p0 < ntot:
                    pn = min(c.NPAN + (c.NM if (is_ctx and p0 == 0) else 0), ntot - p0)
                    nt(src[p0:p0 + pn, :], pn, grep, panel)

                    def act(kt, n0, n1):
                        return panel.t[:, kt, n0:n1]

                    def epi_qk(dst, gcol, colbase):
                        def epi(col, m, n0, n1, pss):
                            n = n1 - n0
                            h = (col - colbase) // 128
                            pb = pss[0]
                            S.op("act", lambda e: e.activation(sq.t[:, :n], pb.t[:, :n], AF.Square), reads=[pb], writes=[sq])
                            p2 = ps[4]
                            S.op("pe", lambda e: e.matmul(p2.t[:, :n], ones_bf.t[:], sq.t[:, :n], start=True, stop=True),
                                 reads=[ones_bf, sq], writes=[p2])
                            S.op("act", lambda e: e.activation(rinv.t[:, :n], p2.t[:, :n], AF.Sqrt, bias=EPS, scale=1.0 / 128),
                                 reads=[p2, cst], writes=[rinv])
                            S.op("dve", lambda e: e.reciprocal(rinv.t[:, :n], rinv.t[:, :n]), reads=[rinv], writes=[rinv])
                            sg = stg[si["g"] % 3]; si["g"] += 1
                            S.op("dve", lambda e: e.scalar_tensor_tensor(sg.t[:, :n], pb.t[:, :n], gcol.t[:, 0:1], rinv.t[:, :n],
                                                                         ALU.mult, ALU.mult),
                                 reads=[pb, gcol, rinv], writes=[sg])
                            S.dma("sp", dst.t[h, :, p0 + n0:p0 + n1], sg.t[:, :n], reads=[sg], writes=[dst], part=True)
                        return epi

                    def epi_store_bf(dst):
                        def epi(col, m, n0, n1, pss):
                            n = n1 - n0
                            sg = stg[si["g"] % 3]; si["g"] += 1
                            copy(alt(), sg.t[:m, :n], pss[0].t[:m, :n], [pss[0]], [sg])
                            S.dma("sp", dst.t[col:col + m, p0 + n0:p0 + n1], sg.t[:m, :n], reads=[sg], writes=[dst], part=True)
                        return epi

                    if is_ctx:
                        gemm("fm", wbufs, w_in, D, c.COL_K, c.AW, act, [panel], pn, epi_qk(KTd, kgt, c.COL_K))

                        def epi_v(t0, t1, col, cw, pss):
                            r = t1 - t0
                            sg = stg[si["g"] % 3]; si["g"] += 1
                            copy(alt(), sg.t[:r, :cw], pss[0].t[:r, :cw], [pss[0]], [sg])
                            S.dma("sp", Vd.t[p0 + t0:p0 + t1, col - c.COL_V:col - c.COL_V + cw], sg.t[:r, :cw],
                                  reads=[sg], writes=[Vd], part=True)
                        gemm("tm", wbufs, w_in, D, c.COL_V, c.AW, act, [panel], pn, epi_v)

                        def epi_u(col, m, n0, n1, pss):
                            n = n1 - n0
                            sg = stg[si["g"] % 3]; si["g"] += 1
                            copy(alt(), sg.t[:m, :n], pss[0].t[:m, :n], [pss[0]], [sg])
                            S.dma("sp", UTd.t[col - c.COL_U:col - c.COL_U + m, p0 + n0:p0 + n1], sg.t[:m, :n],
                                  reads=[sg], writes=[UTd], part=True)
                        gemm("fm", wbufs, w_in, D, c.COL_U, c.SW, act, [panel], pn, epi_u)

                        def epi_f(col, m, n0, n1, pss):
                            n = n1 - n0
                            pb = pss[0]
                            S.op("act", lambda e: e.activation(lf.t[:H, :n], pb.t[:H, :n], AF.Exp, bias=bft.t[:H, 0:1], scale=-1.0),
                                 reads=[pb, bft], writes=[lf])
                            S.op("act", lambda e: e.activation(lf.t[:H, :n], lf.t[:H, :n], AF.Ln, bias=ONE[:H, :], scale=1.0),
                                 reads=[lf, cst], writes=[lf])
                            S.op("dve", lambda e: e.tensor_scalar(lf.t[:H, :n], lf.t[:H, :n], -1.0, None, ALU.mult), reads=[lf], writes=[lf])
                            a0 = p0 + n0
                            init = 0.0 if a0 == 0 else cacc.t[:H, a0 - 1:a0]
                            S.op("dve", lambda e: e.tensor_tensor_scan(cacc.t[:H, a0:a0 + n], ones_t.t[:H, :n], lf.t[:H, :n], init,
                                                                       ALU.mult, ALU.add),
                                 reads=[ones_t, lf, cacc], writes=[cacc])
                        gemm("fm", wbufs, w_in, D, c.COL_F, H, act, [panel], pn, epi_f)
                    else:
                        gemm("fm", wbufs, w_in, D, c.COL_Q, c.AW, act, [panel], pn, epi_qk(QTd, qgt, c.COL_Q))

                        def epi_f32(dst, colbase, func):
                            def epi(col, m, n0, n1, pss):
                                n = n1 - n0
                                sf = stf[si["f"] % 3]; si["f"] += 1
                                S.op("act", lambda e: e.activation(sf.t[:m, :n], pss[0].t[:m, :n], func), reads=[pss[0]], writes=[sf])
                                S.dma("sp", dst.t[col - colbase:col - colbase + m, p0 + n0:p0 + n1], sf.t[:m, :n],
                                      reads=[sf], writes=[dst], part=True)
                            return epi
                        gemm("fm", wbufs, w_in, D, c.COL_U, c.SW, act, [panel], pn, epi_f32(UOd, c.COL_U, AF.Copy))
                        gemm("fm", wbufs, w_in, D, c.COL_GA, D, act, [panel], pn, epi_f32(GAd, c.COL_GA, AF.Sigmoid))
                        gemm("fm", wbufs, w_in, D, c.COL_GB, D, act, [panel], pn, epi_f32(GBd, c.COL_GB, AF.Sigmoid))
                    p0 += pn

        _barrier(S)
        proj_phase(True)
        _barrier(S)
        proj_phase(False)

        NKT = 2 * NB + 1

        def ktile(kt):
            return (0, 16) if kt == 0 else (16 + 128 * (kt - 1), 128)

        MAGIC = 12582912.0

        def ssm_gen(st):
            if True:
                LSM = 256 * c.SEGB + 16
                NSEG = NB // c.SEGB
                PC = min(2, c.SEGB)
                tl = sb(st, "tl", [128, LSM], F32); S.dma("sp", tl.t[:], t_loc, writes=[tl])
                onesL = sb(st, "onesL", [128, LSM], F32)
                S.op("dve", lambda e: e.memset(onesL.t[:], 1.0), writes=[onesL])
                hpi = sb(st, "hpi", [128, 1], F32)
                S.op("dve", lambda e: e.memset(hpi.t[:], math.pi / 2), writes=[hpi])
                dskt = sb(st, "dskt", [128, c.NCT], F32); S.dma("sp", dskt.t[:], dsk, writes=[dskt])
                uT = sb(st, "uT", [128, L], BF16)
                uo = sb(st, "uo", [128, NO], F32)
                wbr = sb(st, "wbr", [128, 512], BF16); wbi = sb(st, "wbi", [128, 512], BF16)
                wcr = sb(st, "wcr", [128, 512], BF16); wci = sb(st, "wci", [128, 512], BF16)
                zr = sb(st, "zr", [128, 4, LSM], BF16); nzi = sb(st, "nzi", [128, 4, LSM], BF16)
                z2 = sb(st, "z2", [128, 4, LSM], BF16); z4 = sb(st, "z4", [128, 4, LSM], BF16)
                nwcr = sb(st, "nwcr", [128, 512], BF16); nwci = sb(st, "nwci", [128, 512], BF16)
                rho_t = sb(st, "rho_t", [128, 4, LSM], F32)
                LA = 2
                sets = [[sb(st, f"s{k}_{i}", [128, LSM], F32) for i in range(9)] + [sb(st, f"ab{k}", [128, 1], F32)] for k in range(LA + 1)]
                carry = sb(st, "carry", [128, 4, 2], F32)
                ybufs = [(sb(st, f"sel{i}", [128, 256], F32), sb(st, f"yf{i}", [128, 256], F32), sb(st, f"zf{i}", [128, 256], F32),
                          sb(st, f"zb{i}", [128, 256], BF16)) for i in range(3)]
                yk = {"i": 0}
                for ct in range(c.NCT):
                    S.dma("sp", uT.t[:], UTd.t[ct * 128:(ct + 1) * 128, :], reads=[UTd], writes=[uT])
                    S.dma("sp", uo.t[:], UOd.t[ct * 128:(ct + 1) * 128, :], reads=[UOd], writes=[uo])
                    cols = slice(ct * 512, (ct + 1) * 512)
                    S.dma("sp", wbr.t[:], BBRd.t[:, cols], reads=[BBRd], writes=[wbr])
                    S.dma("sp", wbi.t[:], BBId.t[:, cols], reads=[BBId], writes=[wbi])
                    S.dma("pool", wcr.t[:], crB[:, cols], writes=[wcr])
                    S.dma("pool", wci.t[:], ciB[:, cols], writes=[wci])
                    S.op("pool", lambda e: e.tensor_scalar(nwcr.t[:], wcr.t[:], -1.0, None, ALU.mult), reads=[wcr], writes=[nwcr])
                    S.op("pool", lambda e: e.tensor_scalar(nwci.t[:], wci.t[:], -1.0, None, ALU.mult), reads=[wci], writes=[nwci])
                    for jj in range(4):
                        j = 4 * ct + jj
                        S.op("act", lambda e: e.activation(rho_t.t[:, jj, :], onesL.t[:], AF.Copy, scale=rhoA.t[:, j:j + 1]),
                             reads=[onesL, rhoA], writes=[rho_t])
                    units = [(s_, jj_) for s_ in range(NSEG) for jj_ in range(4)]

                    def seginfo(s):
                        a0 = 0 if s == 0 else 16 + 256 * s * c.SEGB
                        a1 = 16 + 256 * (s + 1) * c.SEGB
                        return a0, a1 - a0, (16 if s == 0 else 0)

                    def stage1(u, s, jj):
                        a0, Ls, off = seginfo(s)
                        V = lambda t: t.t[:, :Ls]
                        j = 4 * ct + jj
                        xr, xi, ang, rr, sn, cs, bA, bB, bC, ab = sets[u % (LA + 1)]
                        for n0 in range(0, Ls, 512):
                            n1 = min(Ls, n0 + 512)
                            pr, pi = ps[5], ps[6]
                            S.op("pe", lambda e: e.matmul(pr.t[:, :n1 - n0], wbr.t[:, jj * 128:(jj + 1) * 128], uT.t[:, a0 + n0:a0 + n1],
                                                          start=True, stop=True), reads=[wbr, uT], writes=[pr])
                            S.op("pe", lambda e: e.matmul(pi.t[:, :n1 - n0], wbi.t[:, jj * 128:(jj + 1) * 128], uT.t[:, a0 + n0:a0 + n1],
                                                          start=True, stop=True), reads=[wbi, uT], writes=[pi])
                            copy("act", xr.t[:, n0:n1], pr.t[:, :n1 - n0], [pr], [xr])
                            copy("act", xi.t[:, n0:n1], pi.t[:, :n1 - n0], [pi], [xi])
                        S.op("act", lambda e: e.activation(ab.t[:], phiA.t[:, j:j + 1], AF.Copy, scale=float(a0)), reads=[phiA], writes=[ab])
                        S.op("act", lambda e: e.activation(V(ang), V(tl), AF.Identity, bias=ab.t[:, 0:1], scale=phiA.t[:, j:j + 1]),
                             reads=[tl, ab, phiA], writes=[ang])

                    def stage1b(u, s, jj):
                        a0, Ls, off = seginfo(s)
                        V = lambda t: t.t[:, :Ls]
                        xr, xi, ang, rr, sn, cs, bA, bB, bC, ab = sets[u % (LA + 1)]
                        S.op("dve", lambda e: e.tensor_scalar(V(rr), V(ang), MAGIC, MAGIC, ALU.add, ALU.subtract), reads=[ang], writes=[rr])
                        S.op("dve", lambda e: e.tensor_tensor(V(sn), V(ang), V(rr), ALU.subtract), reads=[ang, rr], writes=[sn])
                        S.op("act", lambda e: e.activation(V(cs), V(sn), AF.Abs), reads=[sn], writes=[cs])
                        S.op("act", lambda e: e.activation(V(cs), V(cs), AF.Sin, bias=hpi.t[:, 0:1], scale=-TWO_PI), reads=[cs, hpi], writes=[cs])
                        S.op("act", lambda e: e.activation(V(sn), V(sn), AF.Sin, scale=TWO_PI), reads=[sn], writes=[sn])

                    def stage2(u, s, jj):
                        a0, Ls, off = seginfo(s)
                        V = lambda t: t.t[:, :Ls]
                        xr, xi, ang, rr, sn, cs, bA, bB, bC, ab = sets[u % (LA + 1)]
                        S.op("dve", lambda e: e.tensor_tensor(V(bA), V(cs), V(xr), ALU.mult), reads=[cs, xr], writes=[bA])
                        S.op("dve", lambda e: e.tensor_tensor(V(bB), V(cs), V(xi), ALU.mult), reads=[cs, xi], writes=[bB])
                        S.op("dve", lambda e: e.tensor_tensor(V(bC), V(sn), V(xi), ALU.mult), reads=[sn, xi], writes=[bC])
                        S.op("dve", lambda e: e.tensor_tensor(V(rr), V(sn), V(xr), ALU.mult), reads=[sn, xr], writes=[rr])
                        S.op("dve", lambda e: e.tensor_add(V(bA), V(bA), V(bC)), reads=[bA, bC], writes=[bA])
                        S.op("dve", lambda e: e.tensor_sub(V(bB), V(bB), V(rr)), reads=[bB, rr], writes=[bB])
                        ir = 0.0 if s == 0 else carry.t[:, jj, 0:1]
                        ii = 0.0 if s == 0 else carry.t[:, jj, 1:2]
                        S.op("dve", lambda e: e.tensor_tensor_scan(V(xr), rho_t.t[:, jj, :Ls], V(bA), ir, ALU.mult, ALU.add),
                             reads=[rho_t, bA, carry], writes=[xr])
                        S.op("dve", lambda e: e.tensor_tensor_scan(V(xi), rho_t.t[:, jj, :Ls], V(bB), ii, ALU.mult, ALU.add),
                             reads=[rho_t, bB, carry], writes=[xi])
                        S.op("dve", lambda e: e.tensor_tensor(zr.t[:, jj, :Ls], V(cs), V(xr), ALU.mult), reads=[cs, xr], writes=[zr])
                        S.op("dve", lambda e: e.tensor_tensor(z2.t[:, jj, :Ls], V(sn), V(xi), ALU.mult), reads=[sn, xi], writes=[z2])
                        S.op("dve", lambda e: e.tensor_tensor(nzi.t[:, jj, :Ls], V(sn), V(xr), ALU.mult), reads=[sn, xr], writes=[nzi])
                        S.op("dve", lambda e: e.tensor_tensor(z4.t[:, jj, :Ls], V(cs), V(xi), ALU.mult), reads=[cs, xi], writes=[z4])
                        S.op("dve", lambda e: e.tensor_copy(carry.t[:, jj, 0:1], xr.t[:, Ls - 1:Ls]), reads=[xr], writes=[carry])
                        S.op("dve", lambda e: e.tensor_copy(carry.t[:, jj, 1:2], xi.t[:, Ls - 1:Ls]), reads=[xi], writes=[carry])
                        if jj == 3:
                            ypart(s)

                    def ypart(s):
                        a0, Ls, off = seginfo(s)
                        for q0 in range(0, c.SEGB, PC):
                            n = 256 * PC
                            l0 = off + 256 * q0
                            pb = ps[7]
                            for jj in range(4):
                                wsl = slice(jj * 128, (jj + 1) * 128)
                                S.op("pe", lambda e: e.matmul(pb.t[:, :n], wcr.t[:, wsl], zr.t[:, jj, l0:l0 + n],
                                                              start=(jj == 0), stop=False), reads=[wcr, zr], writes=[pb])
                                S.op("pe", lambda e: e.matmul(pb.t[:, :n], nwcr.t[:, wsl], z2.t[:, jj, l0:l0 + n],
                                                              start=False, stop=False), reads=[nwcr, z2], writes=[pb])
                                S.op("pe", lambda e: e.matmul(pb.t[:, :n], nwci.t[:, wsl], nzi.t[:, jj, l0:l0 + n],
                                                              start=False, stop=False), reads=[nwci, nzi], writes=[pb])
                                S.op("pe", lambda e: e.matmul(pb.t[:, :n], nwci.t[:, wsl], z4.t[:, jj, l0:l0 + n],
                                                              start=False, stop=(jj == 3)), reads=[nwci, z4], writes=[pb])
                            no = 128 * PC
                            o0 = (s * c.SEGB + q0) * 128
                            sel, yf, zf, zb = ybufs[yk["i"] % 3]; yk["i"] += 1
                            p4 = pb.t[:, :n].rearrange("p (j two r) -> p j two r", two=2, r=128)
                            s3 = sel.t[:, :no].rearrange("p (j r) -> p j r", r=128)
                            S.op("dve", lambda e: e.tensor_scalar(s3, p4[:, :, 0, :], w01t.t[:, 0:1], None, ALU.mult), reads=[pb, w01t], writes=[sel])
                            S.op("dve", lambda e: e.scalar_tensor_tensor(s3, p4[:, :, 1, :], w01t.t[:, 1:2], s3, ALU.mult, ALU.add),
                                 reads=[pb, w01t, sel], writes=[sel])
                            S.op("dve", lambda e: e.scalar_tensor_tensor(yf.t[:, :no], uo.t[:, o0:o0 + no], dskt.t[:, ct:ct + 1], sel.t[:, :no],
                                                                         ALU.mult, ALU.add), reads=[uo, dskt, sel], writes=[yf])
                            S.op("act", lambda e: e.activation(zf.t[:, :no], yf.t[:, :no], AF.Gelu), reads=[yf], writes=[zf])
                            S.op("dve", lambda e: e.tensor_copy(zb.t[:, :no], zf.t[:, :no]), reads=[zf], writes=[zb])
                            S.dma("sp", ZFd.t[ct * 128:(ct + 1) * 128, o0:o0 + no], zf.t[:, :no], reads=[zf], writes=[ZFd], part=True)
                            S.dma("sp", ZTd.t[ct * 128:(ct + 1) * 128, o0:o0 + no], zb.t[:, :no], reads=[zb], writes=[ZTd], part=True)

                    for i in range(len(units) + LA):
                        if i < len(units):
                            stage1(i, *units[i])
                        if 1 <= i <= len(units):
                            stage1b(i - 1, *units[i - 1])
                        if i >= LA:
                            stage2(i - LA, *units[i - LA])
                        yield

        def glu_phase():
            with ExitStack() as st:
                NP = c.NPAN
                zp = sb(st, "zp", [128, c.NCT, NP], BF16)
                wbufs = [sb(st, f"gw{i}", [128, c.NCT, 512], BF16) for i in range(2)]
                gbufs = [(sb(st, f"gsg{i}", [128, 512], F32), sb(st, f"gzt{i}", [128, 512], F32), sb(st, f"gob{i}", [128, 512], BF16)) for i in range(3)]
                gk = {"i": 0}
                for p0 in range(0, NO, NP):
                    S.dma("sp", zp.t[:], ZTd.t[:, p0:p0 + NP].rearrange("(k p) n -> p k n", p=128), reads=[ZTd], writes=[zp])

                    def epi(col, m, n0, n1, pss):
                        n = n1 - n0
                        sg, zt, ob = gbufs[gk["i"] % 3]; gk["i"] += 1
                        S.op("act", lambda e: e.activation(sg.t[:m, :n], pss[0].t[:m, :n], AF.Sigmoid), reads=[pss[0]], writes=[sg])
                        S.dma("sp", zt.t[:m, :n], ZFd.t[col:col + m, p0 + n0:p0 + n1], reads=[ZFd], writes=[zt])
                        S.op("dve", lambda e: e.tensor_tensor(ob.t[:m, :n], zt.t[:m, :n], sg.t[:m, :n], ALU.mult), reads=[zt, sg], writes=[ob])
                        S.dma("sp", SSTd.t[col:col + m, p0 + n0:p0 + n1], ob.t[:m, :n], reads=[ob], writes=[SSTd], part=True)
                    gemm("fm", wbufs, w_glu, c.SW, 0, c.SW, lambda kt, n0, n1: zp.t[:, kt, n0:n1], [zp], NP, epi)

        HG = min(2, H)
        QG = min(4, NB)

        def attn_prep(stp, st):
            if True:
                cball = sb(stp, "cball", [128, NKT, H], F32)
                mA = sb(stp, "mA", [128, 128], BF16); mB = sb(stp, "mB", [128, 128], BF16)
                cq = sb(st, "cq", [16, NO], F32)
                c3 = [sb(st, f"c3_{i}", [16, NO], BF16) for i in range(3)]
                r1 = sb(st, "cr1", [16, NO], F32)
                S.dma("pool", mA.t[:], maskA, writes=[mA]); S.dma("pool", mB.t[:], maskB, writes=[mB])
                for kt in range(NKT):
                    a, nk = ktile(kt)
                    pb = ps[kt % 2]
                    S.op("pe", lambda e: e.transpose(pb.t[:nk, :H], cacc.t[:H, a:a + nk], id_f.t[:H, :H]), reads=[cacc, id_f], writes=[pb])
                    S.op("dve", lambda e: e.tensor_scalar(cball.t[:nk, kt, :], pb.t[:nk, :H], -1.0, -STAB, ALU.mult, ALU.add),
                         reads=[pb], writes=[cball])
                v4 = cacc.t[:H, 16:16 + 256 * NB].rearrange("h (j two r) -> h j two r", two=2, r=128)
                d3 = cq.t[:H, :].rearrange("h (j r) -> h j r", r=128)
                S.op("dve", lambda e: e.tensor_scalar(d3, v4[:, :, 0, :], w01t.t[:H, 0:1], None, ALU.mult), reads=[cacc, w01t], writes=[cq])
                S.op("dve", lambda e: e.scalar_tensor_tensor(d3, v4[:, :, 1, :], w01t.t[:H, 1:2], d3, ALU.mult, ALU.add),
                     reads=[cacc, w01t, cq], writes=[cq])
                S.op("dve", lambda e: e.tensor_scalar(cq.t[:H, :], cq.t[:H, :], math.sqrt(128.0), None, ALU.mult), reads=[cq], writes=[cq])
                S.op("dve", lambda e: e.tensor_copy(c3[0].t[:H, :], cq.t[:H, :]), reads=[cq], writes=[c3[0]])
                S.op("dve", lambda e: e.tensor_sub(r1.t[:H, :], cq.t[:H, :], c3[0].t[:H, :]), reads=[cq, c3[0]], writes=[r1])
                S.op("dve", lambda e: e.tensor_copy(c3[1].t[:H, :], r1.t[:H, :]), reads=[r1], writes=[c3[1]])
                S.op("dve", lambda e: e.tensor_sub(r1.t[:H, :], r1.t[:H, :], c3[1].t[:H, :]), reads=[r1, c3[1]], writes=[r1])
                S.op("dve", lambda e: e.tensor_copy(c3[2].t[:H, :], r1.t[:H, :]), reads=[r1], writes=[c3[2]])
                for i in range(3):
                    S.dma("sp", CQ3d.t[i], c3[i].t[:H, :], reads=[c3[i]], writes=[CQ3d], part=True)
                S.dma("sp", CQd.t[:, :], cq.t[:H, :], reads=[cq], writes=[CQd])
            return cball, mA, mB

        def attn_gen(st, cball, mA, mB):
            if True:
                vg = sb(st, "vg", [128, NKT, HG * 128], BF16)
                kh = [sb(st, f"kh{i}", [128, L], BF16) for i in range(2)]
                qh = [sb(st, f"qh{i}", [128, NO], BF16) for i in range(2)]
                cqh = [sb(st, f"cqh{i}", [3, NO], BF16) for i in range(2)]
                pt = [sb(st, f"pt{i}", [128, 512], BF16) for i in range(3)]
                rs = [sb(st, f"rs{i}", [128, 512], F32) for i in range(2)]
                ao = [sb(st, f"ao{i}", [128, NO], BF16) for i in range(2)]
                pti = 0
                gi = 0
                scale = 1.0 / math.sqrt(128.0)
                for hg in range(0, H, HG):
                    S.dma("sp", vg.t[:16, 0, :], Vd.t[0:16, hg * 128:(hg + HG) * 128], reads=[Vd], writes=[vg], part=True)
                    for t0 in range(0, 2 * NB, 8):
                        t1 = min(2 * NB, t0 + 8)
                        S.dma("sp", vg.t[:, 1 + t0:1 + t1, :],
                              Vd.t[16 + 128 * t0:16 + 128 * t1, hg * 128:(hg + HG) * 128].rearrange("(t p) c -> p t c", p=128),
                              reads=[Vd], writes=[vg], part=True)
                    for h in range(hg, hg + HG):
                        khb, qhb, cqb, aob = kh[h % 2], qh[h % 2], cqh[h % 2], ao[h % 2]
                        S.dma("sp", khb.t[:], KTd.t[h], reads=[KTd], writes=[khb])
                        S.dma("sp", qhb.t[:], QTd.t[h], reads=[QTd], writes=[qhb])
                        S.dma("sp", cqb.t[:], CQ3d.t[:, h, :], reads=[CQ3d], writes=[cqb])
                        for g in range(0, NB, QG):
                            OT = ps[2 + gi % 2]; SM = ps[4]
                            gi += 1
                            W = QG * 128
                            ktmax = 2 * (g + QG - 1) + 2
                            pend = None
                            for kt in range(0, ktmax + 2):
                                if kt <= ktmax:
                                    a, nk = ktile(kt)
                                    jmin = max(g, (kt - 1) // 2) if kt > 0 else g
                                    q0 = (jmin - g) * 128
                                    qa, qb = g * 128 + q0, (g + QG) * 128
                                    STb = ps[kt % 2]
                                    S.op("pe", lambda e: e.matmul(STb.t[:nk, q0:W], khb.t[:, a:a + nk], qhb.t[:, qa:qb], start=True, stop=False),
                                         reads=[khb, qhb], writes=[STb])
                                    S.op("pe", lambda e: e.matmul(STb.t[:nk, q0:W], ones_bf.t[0:3, :nk], cqb.t[0:3, qa:qb], start=False, stop=True),
                                         reads=[ones_bf, cqb], writes=[STb])
                                    p = pt[pti % 3]; pti += 1
                                    S.op("act", lambda e: e.activation(p.t[:nk, q0:W], STb.t[:nk, q0:W], AF.Exp, bias=cball.t[:nk, kt, h:h + 1], scale=scale),
                                         reads=[STb, cball], writes=[p])
                                    if kt >= 1 and kt % 2 == 1:
                                        j = (kt - 1) // 2
                                        if g <= j < g + QG:
                                            cs_ = slice((j - g) * 128, (j - g + 1) * 128)
                                            S.op("pool", lambda e: e.tensor_tensor(p.t[:, cs_], p.t[:, cs_], mA.t[:, :], ALU.mult), reads=[p, mA], writes=[p])
                                    if kt >= 2 and kt % 2 == 0:
                                        j = (kt - 2) // 2
                                        if g <= j < g + QG:
                                            cs_ = slice((j - g) * 128, (j - g + 1) * 128)
                                            S.op("pool", lambda e: e.tensor_tensor(p.t[:, cs_], p.t[:, cs_], mB.t[:, :], ALU.mult), reads=[p, mB], writes=[p])
                                    cur = (kt, nk, q0, p)
                                else:
                                    cur = None
                                if pend is not None:
                                    kt_, nk_, q0_, p_ = pend
                                    last = kt_ == ktmax
                                    S.op("pe", lambda e: e.matmul(OT.t[:, q0_:W], vg.t[:nk_, kt_, (h - hg) * 128:(h - hg + 1) * 128], p_.t[:nk_, q0_:W],
                                                                  start=(kt_ == 0), stop=last), reads=[vg, p_], writes=[OT])
                                    S.op("pe", lambda e: e.matmul(SM.t[:, q0_:W], ones_bf.t[:nk_, :], p_.t[:nk_, q0_:W], start=(kt_ == 0), stop=last),
                                         reads=[ones_bf, p_], writes=[SM])
                                pend = cur
                                if kt % 2 == 1:
                                    yield
                            r = rs[gi % 2]
                            S.op("dve", lambda e: e.reciprocal(r.t[:, :W], SM.t[:, :W]), reads=[SM], writes=[r])
                            S.op("dve", lambda e: e.tensor_tensor(aob.t[:, g * 128:g * 128 + W], OT.t[:, :W], r.t[:, :W], ALU.mult), reads=[OT, r], writes=[aob])
                        S.dma("sp", ATd.t[h * 128:(h + 1) * 128, :], aob.t[:], reads=[aob], writes=[ATd], part=True)

        _barrier(S)
        with ExitStack() as stp:
            with ExitStack() as st0:
                cball_, mA_, mB_ = attn_prep(stp, st0)
            _barrier(S)
            with ExitStack() as stj:
                g1_ = ssm_gen(stj)
                g2_ = attn_gen(stj, cball_, mA_, mB_)
                live = [g1_, g2_]
                while live:
                    for g_ in list(live):
                        try:
                            next(g_)
                        except StopIteration:
                            live.remove(g_)
        _barrier(S)
        glu_phase()

        def mix_phase():
            with ExitStack() as st:
                NP = min(1024, NO)
                KA, KS = c.AW // 128, c.SW // 128
                cat = sb(st, "cat", [128, KA + KS, NP], BF16)
                mixT = sb(st, "mixT", [128, KT, NP], BF16)
                wbufs = [sb(st, f"mw{i}", [128, max(KT, KA + KS), 256], BF16) for i in range(2)]
                mbufs = [(sb(st, f"ga{i}", [128, 512], F32), sb(st, f"gb{i}", [128, 512], F32),
                          sb(st, f"m1{i}", [128, 512], F32), sb(st, f"m2{i}", [128, 512], F32)) for i in range(2)]
                mk = {"i": 0}
                xo = [sb(st, f"xo{i}", [128, 512], F32) for i in range(2)]
                ho = [sb(st, f"ho{i}", [128, 512], F32) for i in range(2)]
                k = {"i": 0}
                for p0 in range(0, NO, NP):
                    S.dma("sp", cat.t[:, 0:KA, :], ATd.t[:, p0:p0 + NP].rearrange("(k p) n -> p k n", p=128), reads=[ATd], writes=[cat], part=True)
                    S.dma("sp", cat.t[:, KA:KA + KS, :], SSTd.t[:, p0:p0 + NP].rearrange("(k p) n -> p k n", p=128), reads=[SSTd], writes=[cat], part=True)

                    def epi(col, m, n0, n1, pss):
                        n = n1 - n0
                        ga, gb, m1, m2 = mbufs[mk["i"] % 2]; mk["i"] += 1
                        S.dma("sp", ga.t[:m, :n], GAd.t[col:col + m, p0 + n0:p0 + n1], reads=[GAd], writes=[ga])
                        S.dma("sp", gb.t[:m, :n], GBd.t[col:col + m, p0 + n0:p0 + n1], reads=[GBd], writes=[gb])
                        S.op("dve", lambda e: e.tensor_tensor(m1.t[:m, :n], ga.t[:m, :n], pss[0].t[:m, :n], ALU.mult), reads=[ga, pss[0]], writes=[m1])
                        S.op("dve", lambda e: e.tensor_tensor(m2.t[:m, :n], gb.t[:m, :n], pss[1].t[:m, :n], ALU.mult), reads=[gb, pss[1]], writes=[m2])
                        S.op("dve", lambda e: e.tensor_add(mixT.t[:m, col // 128, n0:n1], m1.t[:m, :n], m2.t[:m, :n]), reads=[m1, m2], writes=[mixT])
                    gemm("fm", wbufs, [(w_a, c.AW), (w_b, c.SW)], c.AW + c.SW, 0, D, lambda kt, n0, n1: cat.t[:, kt, n0:n1], [cat], NP, epi,
                         kgroups=[(0, KA), (KA, KA + KS)])

                    def epi2(t0, t1, col, cw, pss):
                        r = t1 - t0
                        x_ = xo[k["i"] % 2]; h_ = ho[k["i"] % 2]; k["i"] += 1
                        S.dma("sp", x_.t[:r, :cw], x_own[p0 + t0:p0 + t1, col:col + cw], writes=[x_])
                        S.op("dve", lambda e: e.tensor_tensor(h_.t[:r, :cw], x_.t[:r, :cw], pss[0].t[:r, :cw], ALU.add), reads=[x_, pss[0]], writes=[h_])
                        S.dma("act", H1d.t[p0 + t0:p0 + t1, col:col + cw], h_.t[:r, :cw], reads=[h_], writes=[H1d], part=True)
                    gemm("tm", wbufs, w_out, D, 0, D, lambda kt, t0, t1: mixT.t[:, kt, t0:t1], [mixT], NP, epi2)

        _barrier(S)
        mix_phase()

        def peer_phase():
            with ExitStack() as st:
                NP = c.PPAN
                NTT = NP // 128
                pan = sb(st, "ppan", [128, KT, NP], BF16)
                et = [sb(st, f"et{i}", [128, 16, 128], F32) for i in range(NTT)]
                thr = sb(st, "thr", [128, NTT, 8], F32); rz = sb(st, "rz", [128, NTT, 8], F32)
                for p0 in range(0, NO, NP):
                    _barrier(S)
                    with ExitStack() as s1:
                        grep = sb(s1, "g2", [128, D], F32); S.dma("sp", grep.t[:], g2rep, writes=[grep])
                        nt = make_nt(s1, "p")
                        nt(H1d.t[p0:p0 + NP, :], NP, grep, pan, srcbuf=H1d)
                    _barrier(S)
                    with ExitStack() as s2:
                        qpT = sb(s2, "qpT", [128, 16, NP], BF16)
                        skb = sb(s2, "skb", [128, 256], BF16); S.dma("pool", skb.t[:], skT, writes=[skb])
                        wbufs = [sb(s2, f"pw{i}", [128, KT, 256], BF16) for i in range(2)]
                        sc = sb(s2, "sc", [128, 16, 128], F32)
                        wk = sb(s2, "wk", [128, 256], F32)
                        m16 = sb(s2, "m16", [128, 16, 16], F32); e16 = sb(s2, "e16", [128, 16, 16], F32)
                        nm = sb(s2, "nm", [128, 16], F32)
                        cand = sb(s2, "cand", [128, 256], F32); c16 = sb(s2, "c16", [128, 16], F32)

                        def epi_q(col, m, n0, n1, pss):
                            copy(alt(), qpT.t[:, col // 128, n0:n1], pss[0].t[:, :n1 - n0], [pss[0]], [qpT])
                        gemm("fm", wbufs, w_query, D, 0, c.QW, lambda kt, n0, n1: pan.t[:, kt, n0:n1], [pan], NP, epi_q)
                        for tt in range(NTT):
                            ts_ = slice(tt * 128, (tt + 1) * 128)
                            for hc in range(16):
                                pb = ps[hc // 4]
                                S.op("pe", lambda e: e.matmul(pb.t[:, (hc % 4) * 128:(hc % 4 + 1) * 128], qpT.t[:, hc, ts_],
                                                              skb.t[:, (hc % 2) * 128:(hc % 2 + 1) * 128], start=True, stop=True),
                                     reads=[qpT, skb], writes=[pb])
                                if hc % 4 == 3:
                                    copy(alt(), sc.t[:, hc - 3:hc + 1, :], pb.t[:, :].rearrange("p (a b) -> p a b", b=128), [pb], [sc])
                            for hc in range(16):
                                S.op("dve", lambda e: e.max(m16.t[:, hc, 0:8], sc.t[:, hc, :]), reads=[sc], writes=[m16])
                                S.op("dve", lambda e: e.match_replace(wk.t[:, :128], m16.t[:, hc, 0:8], sc.t[:, hc, :], -1e30),
                                     reads=[sc, m16], writes=[wk])
                                S.op("dve", lambda e: e.max(m16.t[:, hc, 8:16], wk.t[:, :128]), reads=[wk], writes=[m16])
                            S.op("dve", lambda e: e.tensor_scalar(nm.t[:, :], m16.t[:, :, 0], -1.0, None, ALU.mult), reads=[m16], writes=[nm])
                            for hc in range(16):
                                S.op("act", lambda e: e.activation(et[tt].t[:, hc, :], sc.t[:, hc, :], AF.Exp, bias=nm.t[:, hc:hc + 1], scale=1.0),
                                     reads=[sc, nm], writes=[et[tt]])
                                S.op("act", lambda e: e.activation(e16.t[:, hc, :], m16.t[:, hc, :], AF.Exp, bias=nm.t[:, hc:hc + 1], scale=1.0),
                                     reads=[m16, nm], writes=[e16])
                            for h in range(8):
                                c3 = cand.t[:, :].rearrange("p (a b) -> p a b", b=16)
                                S.op("dve", lambda e: e.tensor_tensor(c3, e16.t[:, 2 * h, :].unsqueeze(2).to_broadcast([128, 16, 16]),
                                                                      e16.t[:, 2 * h + 1, :].unsqueeze(1).to_broadcast([128, 16, 16]), ALU.mult),
                                     reads=[e16], writes=[cand])
                                S.op("dve", lambda e: e.max(c16.t[:, 0:8], cand.t[:, :]), reads=[cand], writes=[c16])
                                S.op("dve", lambda e: e.match_replace(wk.t[:, :], c16.t[:, 0:8], cand.t[:, :], -1.0), reads=[cand, c16], writes=[wk])
                                S.op("dve", lambda e: e.max(c16.t[:, 8:16], wk.t[:, :]), reads=[wk], writes=[c16])
                                S.op("dve", lambda e: e.tensor_scalar(thr.t[:, tt, h:h + 1], c16.t[:, 15:16], 1.0 - 1e-5, None, ALU.mult),
                                     reads=[c16], writes=[thr])
                                S.op("dve", lambda e: e.tensor_reduce(rz.t[:, tt, h:h + 1], c16.t[:, :], AX.X, ALU.add), reads=[c16], writes=[rz])
                            S.op("dve", lambda e: e.reciprocal(rz.t[:, tt, :], rz.t[:, tt, :]), reads=[rz], writes=[rz])
                            for h in range(8):
                                S.op("dve", lambda e: e.tensor_scalar(e16.t[:, 2 * h, :], e16.t[:, 2 * h, :], rz.t[:, tt, h:h + 1], None, ALU.mult),
                                     reads=[e16, rz], writes=[e16])
                                S.op("dve", lambda e: e.tensor_scalar(et[tt].t[:, 2 * h, :], et[tt].t[:, 2 * h, :], rz.t[:, tt, h:h + 1], None, ALU.mult),
                                     reads=[et[tt], rz], writes=[et[tt]])
                                c3 = cand.t[:, :].rearrange("p (a b) -> p a b", b=16)
                                S.op("dve", lambda e: e.tensor_tensor(c3, e16.t[:, 2 * h, :].unsqueeze(2).to_broadcast([128, 16, 16]),
                                                                      e16.t[:, 2 * h + 1, :].unsqueeze(1).to_broadcast([128, 16, 16]), ALU.mult),
                                     reads=[e16], writes=[cand])
                                S.op("dve", lambda e: e.max(c16.t[:, 0:8], cand.t[:, :]), reads=[cand], writes=[c16])
                                S.op("dve", lambda e: e.match_replace(wk.t[:, :], c16.t[:, 0:8], cand.t[:, :], -1.0), reads=[cand, c16], writes=[wk])
                                S.op("dve", lambda e: e.max(c16.t[:, 8:16], wk.t[:, :]), reads=[wk], writes=[c16])
                                S.op("dve", lambda e: e.tensor_scalar(thr.t[:, tt, h:h + 1], c16.t[:, 15:16], 1.0 - 1e-5, None, ALU.mult),
                                     reads=[c16], writes=[thr])
                    _barrier(S)
                    with ExitStack() as s3:
                        wbufs = [sb(s3, f"aw{i}", [128, KT, 512], BF16) for i in range(2)]
                        gT = [sb(s3, f"gT{i}", [128, 4, NP], F32) for i in range(2)]
                        Mb = [sb(s3, f"Mb{i}", [128, 512], F32) for i in range(3)]
                        Mh = [sb(s3, f"Mh{i}", [128, 512], BF16) for i in range(3)]
                        wst = [sb(s3, f"wst{i}", [128, 4, NP], BF16) for i in range(2)]
                        k = {"m": 0}
                        NSB = c.PN // 512
                        WT = [ps[4 + b] for b in range(4)]

                        def load_w(sbi):
                            wt = wbufs[sbi % 2]
                            for k0 in range(0, KT, 8):
                                k1 = min(KT, k0 + 8)
                                srcap = euT[k0 * 128:k1 * 128, sbi * 512:(sbi + 1) * 512].rearrange("(kt p) c -> p kt c", p=128)
                                S.dma("pool", wt.t[:, k0:k1, :], srcap, writes=[wt], part=True)

                        def issue_AT(sbi):
                            wt = wbufs[sbi % 2]
                            g_ = gT[sbi % 2]
                            for j in range(4):
                                pb = ps[j]
                                for kt in range(KT):
                                    S.op("pe", lambda e: e.matmul(pb.t[:, :NP], wt.t[:, kt, j * 128:(j + 1) * 128], pan.t[:, kt, 0:NP],
                                                                  start=(kt == 0), stop=(kt == KT - 1)), reads=[wt, pan], writes=[pb])
                                    if kt % 4 == 3 and kt != KT - 1:
                                        yield
                                S.op("act", lambda e: e.activation(g_.t[:, j, :], pb.t[:, :NP], AF.Gelu), reads=[pb], writes=[g_])
                                yield

                        load_w(0)
                        if NSB > 1:
                            load_w(1)
                        for _ in issue_AT(0):
                            pass
                        for sbi in range(NSB):
                            if sbi + 2 < NSB:
                                load_w(sbi + 2)
                            nxt = issue_AT(sbi + 1) if sbi + 1 < NSB else None
                            g_ = gT[sbi % 2]
                            col0 = sbi * 512
                            i1a = col0 // 128
                            pairs = [(tt, h) for tt in range(NTT) for h in range(8)]
                            held = []
                            for i in range(len(pairs) + 1):
                                if i < len(pairs):
                                    tt, h = pairs[i]
                                    M_ = Mb[k["m"] % 3]; Mh_ = Mh[k["m"] % 3]; k["m"] += 1
                                    P3 = M_.t[:, :].rearrange("p (a b) -> p a b", b=128)
                                    S.op("dve", lambda e: e.tensor_tensor(P3, et[tt].t[:, 2 * h, i1a:i1a + 4].unsqueeze(2).to_broadcast([128, 4, 128]),
                                                                          et[tt].t[:, 2 * h + 1, :].unsqueeze(1).to_broadcast([128, 4, 128]), ALU.mult),
                                         reads=[et[tt]], writes=[M_])
                                    held.append((tt, h, M_, Mh_))
                                if i >= 1:
                                    tt, h, M_, Mh_ = held.pop(0)
                                    S.op("dve", lambda e: e.scalar_tensor_tensor(Mh_.t[:, :], M_.t[:, :], thr.t[:, tt, h:h + 1], M_.t[:, :],
                                                                                 ALU.is_ge, ALU.mult), reads=[M_, thr], writes=[Mh_])
                                    for b in range(4):
                                        S.op("pe", lambda e: e.matmul(WT[b].t[:, tt * 128:(tt + 1) * 128], Mh_.t[:, b * 128:(b + 1) * 128],
                                                                      id_bf.t[:, :], start=(h == 0), stop=(h == 7)),
                                             reads=[Mh_, id_bf], writes=[WT[b]])
                                    if nxt is not None:
                                        try:
                                            next(nxt)
                                        except StopIteration:
                                            nxt = None
                            if nxt is not None:
                                for _ in nxt:
                                    pass
                            ws = wst[sbi % 2]
                            for b in range(4):
                                S.op("dve", lambda e: e.tensor_tensor(ws.t[:, b, :], WT[b].t[:, :NP], g_.t[:, b, :], ALU.mult),
                                     reads=[WT[b], g_], writes=[ws])
                            for b in range(4):
                                S.dma("act", WGTd.t[col0 + b * 128:col0 + (b + 1) * 128, p0:p0 + NP], ws.t[:, b, :], reads=[ws], writes=[WGTd], part=True)

            _barrier(S)
            with ExitStack() as st:
                NY = min(1024, NO)
                NYT = NY // 128
                EC = 16
                wv = [sb(st, f"yv{i}", [128, EC, 512], BF16) for i in range(2)]
                wa = [sb(st, f"ya{i}", [128, EC, NY], BF16) for i in range(2)]
                h1t = [sb(st, f"yh{i}", [128, 512], F32) for i in range(2)]
                yo = [sb(st, f"yo{i}", [128, 512], F32) for i in range(2)]
                NEC = c.PN // (128 * EC)
                gi = 0
                ci = 0
                oi = 0
                for cb in range(0, D, 512):
                    for p0 in range(0, NO, NY):
                        bks = [ps[i] for i in range(NYT)]
                        gi += 1
                        for ec in range(NEC):
                            e0 = ec * EC * 128
                            v_, a_ = wv[ci % 2], wa[ci % 2]
                            ci += 1
                            for k0 in range(0, EC, 8):
                                S.dma("pool", v_.t[:, k0:k0 + 8, :], ev[e0 + k0 * 128:e0 + (k0 + 8) * 128, cb:cb + 512].rearrange("(k p) c -> p k c", p=128),
                                      writes=[v_], part=True)
                                S.dma("sp", a_.t[:, k0:k0 + 8, :], WGTd.t[e0 + k0 * 128:e0 + (k0 + 8) * 128, p0:p0 + NY].rearrange("(k p) c -> p k c", p=128),
                                      reads=[WGTd], writes=[a_], part=True)
                            for tt in range(NYT):
                                for kt in range(EC):
                                    S.op("pe", lambda e: e.matmul(bks[tt].t[:, :512], a_.t[:, kt, tt * 128:(tt + 1) * 128], v_.t[:, kt, :],
                                                                  start=(ec == 0 and kt == 0), stop=(ec == NEC - 1 and kt == EC - 1)),
                                         reads=[a_, v_], writes=[bks[tt]])
                        for tt in range(NYT):
                            h_, o_ = h1t[oi % 2], yo[oi % 2]
                            oi += 1
                            r0 = p0 + tt * 128
                            S.dma("sp", h_.t[:], H1d.t[r0:r0 + 128, cb:cb + 512], reads=[H1d], writes=[h_])
                            S.op("dve", lambda e: e.tensor_tensor(o_.t[:], h_.t[:], bks[tt].t[:, :512], ALU.add), reads=[h_, bks[tt]], writes=[o_])
                            S.dma("act", out_own[r0:r0 + 128, cb:cb + 512], o_.t[:], reads=[o_], writes=[OUTb], part=True)

        _barrier(S)
        peer_phase()

        for i in range(NDSEM):
            if S.dval[i]:
                nc.sync.wait_ge(S.dsem[i], S.dval[i])
        print("instructions:", S.ninst, {k: v for k, v in S.cnt.items()})
    return nc


def _prep(c, inp, core):
    f32 = np.float32
    b, hh = core // 2, core % 2
    D, SEQ, NO, H, G, NJ, NCT = c.D, c.SEQ, c.NO, c.H, c.G, c.NJ, c.NCT
    A = lambda v: np.ascontiguousarray(np.asarray(v), dtype=f32)
    x = np.asarray(inp["x"][b], dtype=f32)
    m = {}
    m["x_ctx"] = np.concatenate([np.asarray(inp["meta_tokens"], dtype=f32), x], 0)
    m["x_own"] = A(x.reshape(SEQ // 128, 128, D)[hh::2].reshape(NO, D))
    m["g1rep"] = A(np.broadcast_to(np.asarray(inp["norm1_g"][0])[None, :], (128, D)))
    m["g2rep"] = A(np.broadcast_to(np.asarray(inp["norm2_g"][0])[None, :], (128, D)))
    m["w_in"] = A(inp["w_in"][0])
    m["bfg"] = A(np.asarray(inp["b_forget"][0]).reshape(H, 1))
    m["qg"] = A(np.asarray(inp["q_norm_g"][0]).reshape(128, 1))
    m["kg"] = A(np.asarray(inp["k_norm_g"][0]).reshape(128, 1))
    lr = np.asarray(inp["lam_re"][0], dtype=f32); li = np.asarray(inp["lam_im"][0], dtype=f32)
    ld = np.asarray(inp["log_dt"][0], dtype=f32)
    m["lrA"] = A(lr.reshape(NJ, 128).T); m["liA"] = A(li.reshape(NJ, 128).T)
    m["ldA"] = A(np.repeat(ld.reshape(NJ, 2), 64, axis=1).T)
    m["lrB"] = A(np.broadcast_to(lr.reshape(1, -1), (128, G * 64)))
    m["liB"] = A(np.broadcast_to(li.reshape(1, -1), (128, G * 64)))
    m["ldB"] = A(np.broadcast_to(np.repeat(ld, 64)[None, :], (128, G * 64)))
    bre = np.asarray(inp["b_re"][0], dtype=f32); bim = np.asarray(inp["b_im"][0], dtype=f32)
    cre = np.asarray(inp["c_re"][0], dtype=f32); cim = np.asarray(inp["c_im"][0], dtype=f32)
    brB = np.zeros((128, G * 64), f32); biB = np.zeros((128, G * 64), f32)
    crB = np.zeros((128, NJ * 128), f32); ciB = np.zeros((128, NJ * 128), f32)
    for g in range(G):
        r0 = (g % 8) * 16
        brB[r0:r0 + 16, g * 64:(g + 1) * 64] = bre[g].T
        biB[r0:r0 + 16, g * 64:(g + 1) * 64] = bim[g].T
        j, g2 = g // 2, g % 2
        crB[g2 * 64:(g2 + 1) * 64, j * 128 + r0:j * 128 + r0 + 16] = cre[g].T
        ciB[g2 * 64:(g2 + 1) * 64, j * 128 + r0:j * 128 + r0 + 16] = cim[g].T
    m["brB"], m["biB"], m["crB"], m["ciB"] = brB, biB, crB, ciB
    m["dsk"] = A(np.asarray(inp["d_skip"][0]).reshape(NCT, 128).T)
    m["w_glu"] = A(inp["w_glu"][0]); m["w_a"] = A(inp["w_branch_attn"][0]); m["w_b"] = A(inp["w_branch_ssm"][0])
    m["w_out"] = A(inp["w_out"][0]); m["w_query"] = A(inp["w_query"][0])
    m["skT"] = A(np.asarray(inp["sub_keys"][0]).transpose(2, 0, 1).reshape(128, 256))
    m["euT"] = A(np.asarray(inp["expert_u"][0]).T); m["ev"] = A(inp["expert_v"][0])
    tri = (np.arange(128)[None, :] >= np.arange(128)[:, None]).astype(f32)
    m["maskA"] = tri if hh == 0 else np.ones((128, 128), f32)
    m["maskB"] = np.zeros((128, 128), f32) if hh == 0 else tri
    m["w01"] = A(np.broadcast_to(np.array([[1.0, 0.0]] if hh == 0 else [[0.0, 1.0]], f32), (128, 2)))
    LSM = 256 * c.SEGB + 16
    m["t_loc"] = A(np.broadcast_to(np.arange(LSM, dtype=f32)[None, :], (128, LSM)))
    io = np.arange(NO)
    pos = 16 + 128 * (2 * (io // 128) + hh) + (io % 128)
    m["t_own"] = A(np.broadcast_to(pos.astype(f32)[None, :], (128, NO)))
    m["ident"] = np.eye(128, dtype=f32)
    return m


_NC_CACHE = {}


def run_cfg(c, inputs):
    key = (c.D, c.SEQ, c.B)
    if key not in _NC_CACHE:
        _NC_CACHE[key] = build(c)
    nc = _NC_CACHE[key]
    ncores = 2 * c.B
    shared = None
    in_maps = []
    for core in range(ncores):
        in_maps.append(_prep(c, inputs, core))
    res = run_bass_kernel_spmd(nc, in_maps, core_ids=list(range(ncores)))
    if getattr(c, "debug", False):
        c.dbg = res.results
    out = np.zeros((c.B, c.SEQ, c.D), np.float32)
    for core in range(ncores):
        b, hh = core // 2, core % 2
        o = np.asarray(res.results[core]["out_own"], dtype=np.float32).reshape(c.NB, 128, c.D)
        out[b].reshape(c.SEQ // 128, 128, c.D)[hh::2] = o
    return out


def kernel(**inputs):
    return run_cfg(Cfg(), inputs)
```

```python
import math
from contextlib import ExitStack
import numpy as np
import ml_dtypes
import concourse.bass as bass
import concourse.mybir as mybir
from concourse.bass_utils import run_bass_kernel_spmd

F32 = mybir.dt.float32
BF16 = mybir.dt.bfloat16
I32 = mybir.dt.int32
AF = mybir.ActivationFunctionType
ALU = mybir.AluOpType
AX = mybir.AxisListType

EPOCH = 16000
NDSEM = 40
TWO_PI = 2.0 * math.pi
STAB = 30.0


class Buf:
    __slots__ = ("w", "r")

    def __init__(self):
        self.w = {}
        self.r = {}


class TT:
    def __init__(self, t):
        self.t = t
        self.b = Buf()


def _b(x):
    return x.b if isinstance(x, TT) else x


class Sched:
    def __init__(self, nc, stack):
        self.nc = nc
        self.engs = {"pe": nc.tensor, "dve": nc.vector, "act": nc.scalar,
                     "pool": nc.gpsimd, "sp": nc.sync}
        self.stack = stack
        self.esem = {}
        self.cnt = {e: 0 for e in self.engs}
        self.seen = {e: {} for e in self.engs}
        self.dsem = [stack.enter_context(nc.semaphore(f"d{i}")) for i in range(NDSEM)]
        self.dval = [0] * NDSEM
        self.dnext = 0
        self.ninst = 0

    def _esem(self, eng, epoch):
        k = (eng, epoch)
        if k not in self.esem:
            self.esem[k] = self.stack.enter_context(self.nc.semaphore(f"e_{eng}_{epoch}"))
        return self.esem[k]

    def _sem_of(self, key):
        if key[0] == "d":
            return self.dsem[key[1]]
        return self._esem(key[0], key[1])

    def _wait(self, eng, deps):
        s = self.seen[eng]
        for k, v in deps.items():
            if eng == "pe" and k[0] == "pe":
                continue
            if s.get(k, 0) < v:
                self.engs[eng].wait_ge(self._sem_of(k), v)
                s[k] = v

    @staticmethod
    def _acc(deps, d):
        for k, v in d.items():
            if deps.get(k, 0) < v:
                deps[k] = v

    def _commit(self, tok, reads, writes, part=False):
        k, v = tok
        for b in writes:
            if not part:
                b.w = {}
            b.w[k] = max(b.w.get(k, 0), v)
            b.r = {}
        for b in reads:
            if b.r.get(k, 0) < v:
                b.r[k] = v

    def op(self, eng, fn, reads=(), writes=()):
        reads = [_b(x) for x in reads]
        writes = [_b(x) for x in writes]
        deps = {}
        for b in reads:
            self._acc(deps, b.w)
        for b in writes:
            self._acc(deps, b.w)
            self._acc(deps, b.r)
        self._wait(eng, deps)
        ins = fn(self.engs[eng])
        c = self.cnt[eng]
        epoch, val = divmod(c, EPOCH)
        ins.then_inc(self._esem(eng, epoch), 1)
        self.cnt[eng] = c + 1
        self._commit(((eng, epoch), val + 1), reads, writes)
        self.ninst += 1
        return ins

    def dma(self, q, out, in_, reads=(), writes=(), part=False):
        reads = [_b(x) for x in reads]
        writes = [_b(x) for x in writes]
        deps = {}
        for b in reads:
            self._acc(deps, b.w)
        for b in writes:
            if not part:
                self._acc(deps, b.w)
            self._acc(deps, b.r)
        i = self.dnext
        self.dnext = (i + 1) % NDSEM
        if self.dval[i]:
            deps[("d", i)] = max(deps.get(("d", i), 0), self.dval[i])
        self._wait(q, deps)
        ins = self.engs[q].dma_start(out=out, in_=in_)
        ins.then_inc(self.dsem[i], 16)
        self.dval[i] += 16
        assert self.dval[i] < 60000
        self._commit((("d", i), self.dval[i]), reads, writes, part=part)
        self.ninst += 1
        return ins


def _barrier(S):
    deps = {}
    for e, cnt in S.cnt.items():
        if cnt:
            epoch, val = divmod(cnt - 1, EPOCH)
            deps[(e, epoch)] = val + 1
    for i in range(NDSEM):
        if S.dval[i]:
            deps[("d", i)] = S.dval[i]
    for e in S.engs:
        s = S.seen[e]
        for k, v in deps.items():
            if k[0] == e:
                continue
            if s.get(k, 0) < v:
                S.engs[e].wait_ge(S._sem_of(k), v)
                s[k] = v


class Cfg:
    def __init__(s, D=4096, SEQ=4096, B=4, NPAN=1024, PPAN=512, SEGB=2, WB6=256):
        s.D, s.SEQ, s.B = D, SEQ, B
        s.NM = 16
        s.H = D // 256
        s.AW = s.H * 128
        s.G = D // 32
        s.SW = s.G * 16
        s.NJ = s.G // 2
        s.NCT = s.SW // 128
        s.PH, s.PK, s.TOPK = 8, 128, 16
        s.PN = s.PK * s.PK
        s.QW = s.PH * 256
        s.L = SEQ + s.NM
        s.NO = SEQ // 2
        s.NB = s.NO // 128
        s.KT = D // 128
        s.NPAN = min(NPAN, s.NO)
        s.PPAN = min(PPAN, s.NO)
        s.SEGB = min(SEGB, s.NB)
        s.WB6 = WB6
        s.COL_Q = 0
        s.COL_K = s.AW
        s.COL_V = 2 * s.AW
        s.COL_F = 3 * s.AW
        s.COL_U = s.COL_F + s.H
        s.COL_GA = s.COL_U + s.SW
        s.COL_GB = s.COL_GA + D
        s.NCOLS = s.COL_GB + D


def build(cfg):
    c = cfg
    D, L, NO, KT, H, NB = c.D, c.L, c.NO, c.KT, c.H, c.NB
    nc = bass.Bass("TRN2", target_bir_lowering=False)

    def din(name, shape, dt=F32):
        return nc.dram_tensor(name, list(shape), dt, kind="ExternalInput").ap()

    def dscr(name, shape, dt):
        return TT(nc.dram_tensor(name, list(shape), dt, kind="ExternalOutput" if getattr(c, "debug", False) else "Internal").ap())

    x_ctx = din("x_ctx", [L, D]); x_own = din("x_own", [NO, D])
    g1rep = din("g1rep", [128, D]); g2rep = din("g2rep", [128, D])
    w_in = din("w_in", [D, c.NCOLS])
    bfg = din("bfg", [H, 1]); qg = din("qg", [128, 1]); kg = din("kg", [128, 1])
    lrA = din("lrA", [128, c.NJ]); liA = din("liA", [128, c.NJ]); ldA = din("ldA", [128, c.NJ])
    lrB = din("lrB", [128, c.NJ * 128]); liB = din("liB", [128, c.NJ * 128]); ldB = din("ldB", [128, c.NJ * 128])
    brB = din("brB", [128, c.NJ * 128]); biB = din("biB", [128, c.NJ * 128])
    crB = din("crB", [128, c.NJ * 128]); ciB = din("ciB", [128, c.NJ * 128])
    dsk = din("dsk", [128, c.NCT])
    w_glu = din("w_glu", [c.SW, c.SW]); w_a = din("w_a", [c.AW, D]); w_b = din("w_b", [c.SW, D])
    w_out = din("w_out", [D, D]); w_query = din("w_query", [D, c.QW])
    skT = din("skT", [128, 2 * 128])
    euT = din("euT", [D, c.PN]); ev = din("ev", [c.PN, D])
    maskA = din("maskA", [128, 128]); maskB = din("maskB", [128, 128])
    w01 = din("w01", [128, 2])
    t_loc = din("t_loc", [128, 256 * c.SEGB + 16]); t_own = din("t_own", [128, NO])
    ident = din("ident", [128, 128])
    out_own = nc.dram_tensor("out_own", [NO, D], F32, kind="ExternalOutput").ap()

    KTd = dscr("KTd", [H, 128, L], BF16)
    Vd = dscr("Vd", [L, c.AW], BF16)
    UTd = dscr("UTd", [c.SW, L], BF16)
    QTd = dscr("QTd", [H, 128, NO], BF16)
    GAd = dscr("GAd", [D, NO], F32)
    GBd = dscr("GBd", [D, NO], F32)
    UOd = dscr("UOd", [c.SW, NO], F32)
    CQd = dscr("CQd", [H, NO], F32)
    CQ3d = dscr("CQ3d", [3, H, NO], BF16)
    BBRd = dscr("BBRd", [128, c.NJ * 128], BF16)
    BBId = dscr("BBId", [128, c.NJ * 128], BF16)
    ZFd = dscr("ZFd", [c.SW, NO], F32)
    ZTd = dscr("ZTd", [c.SW, NO], BF16)
    SSTd = dscr("SSTd", [c.SW, NO], BF16)
    ATd = dscr("ATd", [c.AW, NO], BF16)
    H1d = dscr("H1d", [NO, D], F32)
    WGTd = dscr("WGTd", [c.PN, NO], BF16)
    OUTb = Buf()

    with ExitStack() as gst:
        S = Sched(nc, gst)

        uid = {"n": 0}

        def sb(st, name, shape, dt):
            uid["n"] += 1
            return TT(st.enter_context(nc.sbuf_tensor(f"{name}_{uid['n']}", list(shape), dt)))

        ps = [TT(gst.enter_context(nc.psum_tensor(f"ps{i}", [128, 512], F32))) for i in range(8)]
        psb = []
        for i in range(2):
            v = TT(ps[6 + i].t.bitcast(BF16))
            v.b = ps[6 + i].b
            psb.append(v)
        id_bf = sb(gst, "id_bf", [128, 128], BF16)
        id_f = sb(gst, "id_f", [128, 128], F32)
        ones_bf = sb(gst, "ones_bf", [128, 128], BF16)
        ones_f = sb(gst, "ones_f", [1, 128], F32)
        cst = sb(gst, "cst", [128, 8], F32)
        w01t = sb(gst, "w01t", [128, 2], F32)
        qgt = sb(gst, "qgt", [128, 1], F32); kgt = sb(gst, "kgt", [128, 1], F32)
        cacc = sb(gst, "cacc", [16, L], F32)
        rhoA = sb(gst, "rhoA", [128, c.NJ], F32)
        phiA = sb(gst, "phiA", [128, c.NJ], F32)
        S.dma("pool", id_bf.t[:], ident, writes=[id_bf])
        S.dma("sp", id_f.t[:], ident, writes=[id_f])
        S.dma("sp", w01t.t[:], w01, writes=[w01t])
        S.dma("sp", qgt.t[:], qg, writes=[qgt])
        S.dma("sp", kgt.t[:], kg, writes=[kgt])
        S.op("dve", lambda e: e.memset(ones_bf.t[:], 1.0), writes=[ones_bf])
        S.op("dve", lambda e: e.memset(ones_f.t[:], 1.0), writes=[ones_f])
        S.op("dve", lambda e: e.memset(cst.t[:, 0:1], 1e-6), writes=[cst])
        S.op("dve", lambda e: e.memset(cst.t[:, 1:2], 1.0), writes=[cst])
        S.op("dve", lambda e: e.memset(cst.t[:, 2:3], 0.0), writes=[cst])
        EPS = cst.t[:, 0:1]
        ONE = cst.t[:, 1:2]

        rr = {"e": 0}

        def alt():
            rr["e"] ^= 1
            return "act" if rr["e"] else "dve"

        def copy(eng, out, in_, reads, writes):
            if eng == "act":
                S.op("act", lambda e: e.activation(out, in_, AF.Copy), reads=reads, writes=writes)
            else:
                S.op(eng, lambda e: e.tensor_copy(out, in_), reads=reads, writes=writes)

        def make_nt(st, tagp):
          xt = sb(st, tagp + "xt", [128, D], F32)
          xn = sb(st, tagp + "xn", [128, D], BF16)
          ss = sb(st, tagp + "ss", [128, 2], F32)

          def norm_transpose(src, n_rows, grep, panel, srcbuf=None):
            for t0 in range(0, n_rows, 128):
                r = min(128, n_rows - t0)
                S.dma("sp", xt.t[:r, :], src[t0:t0 + r, :], reads=[srcbuf] if srcbuf else [], writes=[xt])
                S.op("dve", lambda e: e.memset(ss.t[:r, 0:1], 0.0), writes=[ss])
                S.op("act", lambda e: e.activation(xn.t[:r, :], xt.t[:r, :], AF.Square, accum_out=ss.t[:r, 0:1]),
                     reads=[xt], writes=[xn, ss])
                S.op("dve", lambda e: e.tensor_scalar(ss.t[:r, 1:2], ss.t[:r, 0:1], 1.0 / D, 1e-6, ALU.mult, ALU.add),
                     reads=[ss], writes=[ss])
                S.op("act", lambda e: e.activation(ss.t[:r, 1:2], ss.t[:r, 1:2], AF.Sqrt), reads=[ss], writes=[ss])
                S.op("dve", lambda e: e.reciprocal(ss.t[:r, 1:2], ss.t[:r, 1:2]), reads=[ss], writes=[ss])
                S.op("dve", lambda e: e.scalar_tensor_tensor(xn.t[:r, :], xt.t[:r, :], ss.t[:r, 1:2], grep.t[:r, :],
                                                             ALU.mult, ALU.mult),
                     reads=[xt, ss, grep], writes=[xn])
                for k0 in range(0, KT, 8):
                    k1 = min(KT, k0 + 8)
                    pb = psb[(k0 // 8) % 2]
                    for kt in range(k0, k1):
                        S.op("pe", lambda e: e.transpose(pb.t[:, (kt - k0) * 128:(kt - k0) * 128 + r],
                                                         xn.t[:r, kt * 128:(kt + 1) * 128], id_bf.t[:r, :r]),
                             reads=[xn, id_bf], writes=[pb])
                    src_ap = pb.t[:, 0:(k1 - k0) * 128].rearrange("p (k r) -> p k r", r=128)[:, :, :r]
                    copy(alt(), panel.t[:, k0:k1, t0:t0 + r], src_ap, [pb], [panel])
          return norm_transpose

        wstate = {"i": 0}

        def gemm(mode, wbufs, wsrc, K, c0, ncols, act, actbufs, N, epi, kgroups=None, WB=None, banks=(0, 1, 2, 3),
                 wq="pool", wsrcbuf=None):
            KTl = K // 128
            WB = WB or wbufs[0].t.shape[2]
            kgroups = kgroups or [(0, KTl)]
            bi = 0
            for sb0 in range(0, ncols, WB):
                cw = min(WB, ncols - sb0)
                wt = wbufs[wstate["i"] % len(wbufs)]
                wstate["i"] += 1
                srcs = wsrc if isinstance(wsrc, list) else [(wsrc, K)]
                kbase = 0
                for (wap, Ks) in srcs:
                    for k0 in range(0, Ks // 128, 8):
                        k1 = min(Ks // 128, k0 + 8)
                        srcap = wap[k0 * 128:k1 * 128, c0 + sb0:c0 + sb0 + cw].rearrange("(kt p) c -> p kt c", p=128)
                        S.dma(wq, wt.t[:, kbase + k0:kbase + k1, :cw], srcap, reads=[wsrcbuf] if wsrcbuf else [],
                              writes=[wt], part=True)
                    kbase += Ks // 128
                if mode == "fm":
                    for j0 in range(0, cw, 128):
                        m = min(128, cw - j0)
                        for n0 in range(0, N, 512):
                            n1 = min(N, n0 + 512)
                            pss = []
                            for (ka, kb) in kgroups:
                                pb = ps[banks[bi % len(banks)]]
                                bi += 1
                                for kt in range(ka, kb):
                                    S.op("pe", lambda e: e.matmul(pb.t[:m, :n1 - n0], wt.t[:, kt, j0:j0 + m],
                                                                  act(kt, n0, n1), start=(kt == ka), stop=(kt == kb - 1)),
                                         reads=[wt] + actbufs, writes=[pb])
                                pss.append(pb)
                            epi(c0 + sb0 + j0, m, n0, n1, pss)
                else:
                    for t0 in range(0, N, 128):
                        t1 = min(N, t0 + 128)
                        pss = []
                        for (ka, kb) in kgroups:
                            pb = ps[banks[bi % len(banks)]]
                            bi += 1
                            for kt in range(ka, kb):
                                S.op("pe", lambda e: e.matmul(pb.t[:t1 - t0, :cw], act(kt, t0, t1), wt.t[:, kt, :cw],
                                                              start=(kt == ka), stop=(kt == kb - 1)),
                                     reads=[wt] + actbufs, writes=[pb])
                            pss.append(pb)
                        epi(t0, t1, c0 + sb0, cw, pss)

        def coeffs(st, tag, lr_ap, li_ap, ld_ap, F):
            lr = sb(st, tag + "lr", [128, F], F32); li = sb(st, tag + "li", [128, F], F32)
            ld = sb(st, tag + "ld", [128, F], F32)
            t1 = sb(st, tag + "t1", [128, F], F32); t2 = sb(st, tag + "t2", [128, F], F32)
            ti = sb(st, tag + "ti", [128, F], I32)
            rho = sb(st, tag + "rho", [128, F], F32); phi = sb(st, tag + "phi", [128, F], F32)
            sn = sb(st, tag + "sn", [128, F], F32); cs = sb(st, tag + "cs", [128, F], F32)
            fr = sb(st, tag + "fr", [128, F], F32); fi = sb(st, tag + "fi", [128, F], F32)

            def run(lr_src, li_src, ld_src):
                S.dma("sp", lr.t[:], lr_src, writes=[lr]); S.dma("sp", li.t[:], li_src, writes=[li])
                S.dma("sp", ld.t[:], ld_src, writes=[ld])
                S.op("act", lambda e: e.activation(ld.t[:], ld.t[:], AF.Exp), reads=[ld], writes=[ld])
                S.op("dve", lambda e: e.tensor_tensor(t1.t[:], lr.t[:], ld.t[:], ALU.mult), reads=[lr, ld], writes=[t1])
                S.op("act", lambda e: e.activation(rho.t[:], t1.t[:], AF.Exp), reads=[t1], writes=[rho])
                S.op("dve", lambda e: e.scalar_tensor_tensor(t1.t[:], li.t[:], 1.0 / TWO_PI, ld.t[:], ALU.mult, ALU.mult),
                     reads=[li, ld], writes=[t1])
                S.op("dve", lambda e: e.tensor_copy(ti.t[:], t1.t[:]), reads=[t1], writes=[ti])
                S.op("dve", lambda e: e.tensor_copy(t2.t[:], ti.t[:]), reads=[ti], writes=[t2])
                S.op("dve", lambda e: e.tensor_sub(phi.t[:], t1.t[:], t2.t[:]), reads=[t1, t2], writes=[phi])
                S.op("act", lambda e: e.activation(sn.t[:], phi.t[:], AF.Sin, scale=TWO_PI), reads=[phi], writes=[sn])
                S.op("dve", lambda e: e.tensor_scalar(t1.t[:], phi.t[:], 0.25, None, ALU.add), reads=[phi], writes=[t1])
                S.op("dve", lambda e: e.tensor_copy(ti.t[:], t1.t[:]), reads=[t1], writes=[ti])
                S.op("dve", lambda e: e.tensor_copy(t2.t[:], ti.t[:]), reads=[ti], writes=[t2])
                S.op("dve", lambda e: e.tensor_sub(t1.t[:], t1.t[:], t2.t[:]), reads=[t1, t2], writes=[t1])
                S.op("act", lambda e: e.activation(cs.t[:], t1.t[:], AF.Sin, scale=TWO_PI), reads=[t1], writes=[cs])
                S.op("dve", lambda e: e.tensor_tensor(cs.t[:], cs.t[:], rho.t[:], ALU.mult), reads=[cs, rho], writes=[cs])
                S.op("dve", lambda e: e.tensor_tensor(sn.t[:], sn.t[:], rho.t[:], ALU.mult), reads=[sn, rho], writes=[sn])
                S.op("dve", lambda e: e.tensor_scalar(t1.t[:], cs.t[:], -1.0, None, ALU.add), reads=[cs], writes=[t1])
                S.op("dve", lambda e: e.tensor_tensor(t2.t[:], lr.t[:], lr.t[:], ALU.mult), reads=[lr], writes=[t2])
                S.op("dve", lambda e: e.tensor_tensor(fr.t[:], li.t[:], li.t[:], ALU.mult), reads=[li], writes=[fr])
                S.op("dve", lambda e: e.tensor_add(t2.t[:], t2.t[:], fr.t[:]), reads=[t2, fr], writes=[t2])
                S.op("dve", lambda e: e.reciprocal(t2.t[:], t2.t[:]), reads=[t2], writes=[t2])
                S.op("dve", lambda e: e.tensor_tensor(fr.t[:], t1.t[:], lr.t[:], ALU.mult), reads=[t1, lr], writes=[fr])
                S.op("dve", lambda e: e.tensor_tensor(fi.t[:], sn.t[:], li.t[:], ALU.mult), reads=[sn, li], writes=[fi])
                S.op("dve", lambda e: e.tensor_add(fr.t[:], fr.t[:], fi.t[:]), reads=[fr, fi], writes=[fr])
                S.op("dve", lambda e: e.tensor_tensor(fr.t[:], fr.t[:], t2.t[:], ALU.mult), reads=[fr, t2], writes=[fr])
                S.op("dve", lambda e: e.tensor_tensor(fi.t[:], sn.t[:], lr.t[:], ALU.mult), reads=[sn, lr], writes=[fi])
                S.op("dve", lambda e: e.tensor_tensor(t1.t[:], t1.t[:], li.t[:], ALU.mult), reads=[t1, li], writes=[t1])
                S.op("dve", lambda e: e.tensor_sub(fi.t[:], fi.t[:], t1.t[:]), reads=[fi, t1], writes=[fi])
                S.op("dve", lambda e: e.tensor_tensor(fi.t[:], fi.t[:], t2.t[:], ALU.mult), reads=[fi, t2], writes=[fi])
            return run, rho, phi, fr, fi

        with ExitStack() as st:
            run, rho, phi, fr, fi = coeffs(st, "ca", None, None, None, c.NJ)
            run(lrA, liA, ldA)
            S.op("dve", lambda e: e.tensor_copy(rhoA.t[:], rho.t[:]), reads=[rho], writes=[rhoA])
            S.op("dve", lambda e: e.tensor_copy(phiA.t[:], phi.t[:]), reads=[phi], writes=[phiA])
        _barrier(S)
        with ExitStack() as st:
            FC = min(1024, c.NJ * 128)
            run, rho, phi, fr, fi = coeffs(st, "cb", None, None, None, FC)
            br = sb(st, "br", [128, FC], F32); bi_ = sb(st, "bi", [128, FC], F32)
            o1 = sb(st, "o1", [128, FC], F32); o2 = sb(st, "o2", [128, FC], F32)
            ob1 = sb(st, "ob1", [128, FC], BF16); ob2 = sb(st, "ob2", [128, FC], BF16)
            for f0 in range(0, c.NJ * 128, FC):
                run(lrB[:, f0:f0 + FC], liB[:, f0:f0 + FC], ldB[:, f0:f0 + FC])
                S.dma("sp", br.t[:], brB[:, f0:f0 + FC], writes=[br])
                S.dma("sp", bi_.t[:], biB[:, f0:f0 + FC], writes=[bi_])
                S.op("dve", lambda e: e.tensor_tensor(o1.t[:], fr.t[:], br.t[:], ALU.mult), reads=[fr, br], writes=[o1])
                S.op("dve", lambda e: e.tensor_tensor(o2.t[:], fi.t[:], bi_.t[:], ALU.mult), reads=[fi, bi_], writes=[o2])
                S.op("dve", lambda e: e.tensor_sub(ob1.t[:], o1.t[:], o2.t[:]), reads=[o1, o2], writes=[ob1])
                S.op("dve", lambda e: e.tensor_tensor(o1.t[:], fr.t[:], bi_.t[:], ALU.mult), reads=[fr, bi_], writes=[o1])
                S.op("dve", lambda e: e.tensor_tensor(o2.t[:], fi.t[:], br.t[:], ALU.mult), reads=[fi, br], writes=[o2])
                S.op("dve", lambda e: e.tensor_add(ob2.t[:], o1.t[:], o2.t[:]), reads=[o1, o2], writes=[ob2])
                S.dma("sp", BBRd.t[:, f0:f0 + FC], ob1.t[:], reads=[ob1], writes=[BBRd], part=True)
                S.dma("sp", BBId.t[:, f0:f0 + FC], ob2.t[:], reads=[ob2], writes=[BBId], part=True)

        def proj_phase(is_ctx):
            with ExitStack() as st:
                tag = "c" if is_ctx else "o"
                NPmax = c.NPAN + (c.NM if is_ctx else 0)
                panel = sb(st, tag + "pan", [128, KT, NPmax], BF16)
                grep = sb(st, tag + "g1", [128, D], F32)
                S.dma("sp", grep.t[:], g1rep, writes=[grep])
                wbufs = [sb(st, tag + f"w{i}", [128, KT, 256], BF16) for i in range(2)]
                stg = [sb(st, tag + f"stg{i}", [128, 512], BF16) for i in range(3)]
                stf = [sb(st, tag + f"stf{i}", [128, 512], F32) for i in range(3)]
                sq = sb(st, tag + "sq", [128, 512], BF16)
                rinv = sb(st, tag + "rinv", [128, 512], F32)
                bft = sb(st, tag + "bft", [16, 1], F32)
                lf = sb(st, tag + "lf", [16, 512], F32)
                ones_t = sb(st, tag + "ones", [16, 512], F32)
                S.dma("sp", bft.t[:H, :], bfg, writes=[bft])
                S.op("dve", lambda e: e.tensor_scalar(bft.t[:H, :], bft.t[:H, :], -1.0, None, ALU.mult), reads=[bft], writes=[bft])
                S.op("dve", lambda e: e.memset(ones_t.t[:], 1.0), writes=[ones_t])
                si = {"g": 0, "f": 0}
                src = x_ctx if is_ctx else x_own
                ntot = L if is_ctx else NO
                p0 = 0
                nt = make_nt(st, tag)
                while p0 < ntot:
                    pn = min(c.NPAN + (c.NM if (is_ctx and p0 == 0) else 0), ntot - p0)
                    nt(src[p0:p0 + pn, :], pn, grep, panel)

                    def act(kt, n0, n1):
                        return panel.t[:, kt, n0:n1]

                    def epi_qk(dst, gcol, colbase):
                        def epi(col, m, n0, n1, pss):
                            n = n1 - n0
                            h = (col - colbase) // 128
                            pb = pss[0]
                            S.op("act", lambda e: e.activation(sq.t[:, :n], pb.t[:, :n], AF.Square), reads=[pb], writes=[sq])
                            p2 = ps[4]
                            S.op("pe", lambda e: e.matmul(p2.t[:, :n], ones_bf.t[:], sq.t[:, :n], start=True, stop=True),
                                 reads=[ones_bf, sq], writes=[p2])
                            S.op("act", lambda e: e.activation(rinv.t[:, :n], p2.t[:, :n], AF.Sqrt, bias=EPS, scale=1.0 / 128),
                                 reads=[p2, cst], writes=[rinv])
                            S.op("dve", lambda e: e.reciprocal(rinv.t[:, :n], rinv.t[:, :n]), reads=[rinv], writes=[rinv])
                            sg = stg[si["g"] % 3]; si["g"] += 1
                            S.op("dve", lambda e: e.scalar_tensor_tensor(sg.t[:, :n], pb.t[:, :n], gcol.t[:, 0:1], rinv.t[:, :n],
                                                                         ALU.mult, ALU.mult),
                                 reads=[pb, gcol, rinv], writes=[sg])
                            S.dma("sp", dst.t[h, :, p0 + n0:p0 + n1], sg.t[:, :n], reads=[sg], writes=[dst], part=True)
                        return epi

                    def epi_store_bf(dst):
                        def epi(col, m, n0, n1, pss):
                            n = n1 - n0
                            sg = stg[si["g"] % 3]; si["g"] += 1
                            copy(alt(), sg.t[:m, :n], pss[0].t[:m, :n], [pss[0]], [sg])
                            S.dma("sp", dst.t[col:col + m, p0 + n0:p0 + n1], sg.t[:m, :n], reads=[sg], writes=[dst], part=True)
                        return epi

                    if is_ctx:
                        gemm("fm", wbufs, w_in, D, c.COL_K, c.AW, act, [panel], pn, epi_qk(KTd, kgt, c.COL_K))

                        def epi_v(t0, t1, col, cw, pss):
                            r = t1 - t0
                            sg = stg[si["g"] % 3]; si["g"] += 1
                            copy(alt(), sg.t[:r, :cw], pss[0].t[:r, :cw], [pss[0]], [sg])
                            S.dma("sp", Vd.t[p0 + t0:p0 + t1, col - c.COL_V:col - c.COL_V + cw], sg.t[:r, :cw],
                                  reads=[sg], writes=[Vd], part=True)
                        gemm("tm", wbufs, w_in, D, c.COL_V, c.AW, act, [panel], pn, epi_v)

                        def epi_u(col, m, n0, n1, pss):
                            n = n1 - n0
                            sg = stg[si["g"] % 3]; si["g"] += 1
                            copy(alt(), sg.t[:m, :n], pss[0].t[:m, :n], [pss[0]], [sg])
                            S.dma("sp", UTd.t[col - c.COL_U:col - c.COL_U + m, p0 + n0:p0 + n1], sg.t[:m, :n],
                                  reads=[sg], writes=[UTd], part=True)
                        gemm("fm", wbufs, w_in, D, c.COL_U, c.SW, act, [panel], pn, epi_u)

                        def epi_f(col, m, n0, n1, pss):
                            n = n1 - n0
                            pb = pss[0]
                            S.op("act", lambda e: e.activation(lf.t[:H, :n], pb.t[:H, :n], AF.Exp, bias=bft.t[:H, 0:1], scale=-1.0),
                                 reads=[pb, bft], writes=[lf])
                            S.op("act", lambda e: e.activation(lf.t[:H, :n], lf.t[:H, :n], AF.Ln, bias=ONE[:H, :], scale=1.0),
                                 reads=[lf, cst], writes=[lf])
                            S.op("dve", lambda e: e.tensor_scalar(lf.t[:H, :n], lf.t[:H, :n], -1.0, None, ALU.mult), reads=[lf], writes=[lf])
                            a0 = p0 + n0
                            init = 0.0 if a0 == 0 else cacc.t[:H, a0 - 1:a0]
                            S.op("dve", lambda e: e.tensor_tensor_scan(cacc.t[:H, a0:a0 + n], ones_t.t[:H, :n], lf.t[:H, :n], init,
                                                                       ALU.mult, ALU.add),
                                 reads=[ones_t, lf, cacc], writes=[cacc])
                        gemm("fm", wbufs, w_in, D, c.COL_F, H, act, [panel], pn, epi_f)
                    else:
                        gemm("fm", wbufs, w_in, D, c.COL_Q, c.AW, act, [panel], pn, epi_qk(QTd, qgt, c.COL_Q))

                        def epi_f32(dst, colbase, func):
                            def epi(col, m, n0, n1, pss):
                                n = n1 - n0
                                sf = stf[si["f"] % 3]; si["f"] += 1
                                S.op("act", lambda e: e.activation(sf.t[:m, :n], pss[0].t[:m, :n], func), reads=[pss[0]], writes=[sf])
                                S.dma("sp", dst.t[col - colbase:col - colbase + m, p0 + n0:p0 + n1], sf.t[:m, :n],
                                      reads=[sf], writes=[dst], part=True)
                            return epi
                        gemm("fm", wbufs, w_in, D, c.COL_U, c.SW, act, [panel], pn, epi_f32(UOd, c.COL_U, AF.Copy))
                        gemm("fm", wbufs, w_in, D, c.COL_GA, D, act, [panel], pn, epi_f32(GAd, c.COL_GA, AF.Sigmoid))
                        gemm("fm", wbufs, w_in, D, c.COL_GB, D, act, [panel], pn, epi_f32(GBd, c.COL_GB, AF.Sigmoid))
                    p0 += pn

        _barrier(S)
        proj_phase(True)
        _barrier(S)
        proj_phase(False)

        NKT = 2 * NB + 1

        def ktile(kt):
            return (0, 16) if kt == 0 else (16 + 128 * (kt - 1), 128)

        MAGIC = 12582912.0

        def ssm_gen(st):
            if True:
                LSM = 256 * c.SEGB + 16
                NSEG = NB // c.SEGB
                PC = min(2, c.SEGB)
                tl = sb(st, "tl", [128, LSM], F32); S.dma("sp", tl.t[:], t_loc, writes=[tl])
                onesL = sb(st, "onesL", [128, LSM], F32)
                S.op("dve", lambda e: e.memset(onesL.t[:], 1.0), writes=[onesL])
                hpi = sb(st, "hpi", [128, 1], F32)
                S.op("dve", lambda e: e.memset(hpi.t[:], math.pi / 2), writes=[hpi])
                dskt = sb(st, "dskt", [128, c.NCT], F32); S.dma("sp", dskt.t[:], dsk, writes=[dskt])
                uT = sb(st, "uT", [128, L], BF16)
                uo = sb(st, "uo", [128, NO], F32)
                wbr = sb(st, "wbr", [128, 512], BF16); wbi = sb(st, "wbi", [128, 512], BF16)
                wcr = sb(st, "wcr", [128, 512], BF16); wci = sb(st, "wci", [128, 512], BF16)
                zr = sb(st, "zr", [128, 4, LSM], BF16); nzi = sb(st, "nzi", [128, 4, LSM], BF16)
                z2 = sb(st, "z2", [128, 4, LSM], BF16); z4 = sb(st, "z4", [128, 4, LSM], BF16)
                nwcr = sb(st, "nwcr", [128, 512], BF16); nwci = sb(st, "nwci", [128, 512], BF16)
                rho_t = sb(st, "rho_t", [128, 4, LSM], F32)
                LA = 2
                sets = [[sb(st, f"s{k}_{i}", [128, LSM], F32) for i in range(9)] + [sb(st, f"ab{k}", [128, 1], F32)] for k in range(LA + 1)]
                carry = sb(st, "carry", [128, 4, 2], F32)
                ybufs = [(sb(st, f"sel{i}", [128, 256], F32), sb(st, f"yf{i}", [128, 256], F32), sb(st, f"zf{i}", [128, 256], F32),
                          sb(st, f"zb{i}", [128, 256], BF16)) for i in range(3)]
                yk = {"i": 0}
                for ct in range(c.NCT):
                    S.dma("sp", uT.t[:], UTd.t[ct * 128:(ct + 1) * 128, :], reads=[UTd], writes=[uT])
                    S.dma("sp", uo.t[:], UOd.t[ct * 128:(ct + 1) * 128, :], reads=[UOd], writes=[uo])
                    cols = slice(ct * 512, (ct + 1) * 512)
                    S.dma("sp", wbr.t[:], BBRd.t[:, cols], reads=[BBRd], writes=[wbr])
                    S.dma("sp", wbi.t[:], BBId.t[:, cols], reads=[BBId], writes=[wbi])
                    S.dma("pool", wcr.t[:], crB[:, cols], writes=[wcr])
                    S.dma("pool", wci.t[:], ciB[:, cols], writes=[wci])
                    S.op("pool", lambda e: e.tensor_scalar(nwcr.t[:], wcr.t[:], -1.0, None, ALU.mult), reads=[wcr], writes=[nwcr])
                    S.op("pool", lambda e: e.tensor_scalar(nwci.t[:], wci.t[:], -1.0, None, ALU.mult), reads=[wci], writes=[nwci])
                    for jj in range(4):
                        j = 4 * ct + jj
                        S.op("act", lambda e: e.activation(rho_t.t[:, jj, :], onesL.t[:], AF.Copy, scale=rhoA.t[:, j:j + 1]),
                             reads=[onesL, rhoA], writes=[rho_t])
                    units = [(s_, jj_) for s_ in range(NSEG) for jj_ in range(4)]

                    def seginfo(s):
                        a0 = 0 if s == 0 else 16 + 256 * s * c.SEGB
                        a1 = 16 + 256 * (s + 1) * c.SEGB
                        return a0, a1 - a0, (16 if s == 0 else 0)

                    def stage1(u, s, jj):
                        a0, Ls, off = seginfo(s)
                        V = lambda t: t.t[:, :Ls]
                        j = 4 * ct + jj
                        xr, xi, ang, rr, sn, cs, bA, bB, bC, ab = sets[u % (LA + 1)]
                        for n0 in range(0, Ls, 512):
                            n1 = min(Ls, n0 + 512)
                            pr, pi = ps[5], ps[6]
                            S.op("pe", lambda e: e.matmul(pr.t[:, :n1 - n0], wbr.t[:, jj * 128:(jj + 1) * 128], uT.t[:, a0 + n0:a0 + n1],
                                                          start=True, stop=True), reads=[wbr, uT], writes=[pr])
                            S.op("pe", lambda e: e.matmul(pi.t[:, :n1 - n0], wbi.t[:, jj * 128:(jj + 1) * 128], uT.t[:, a0 + n0:a0 + n1],
                                                          start=True, stop=True), reads=[wbi, uT], writes=[pi])
                            copy("act", xr.t[:, n0:n1], pr.t[:, :n1 - n0], [pr], [xr])
                            copy("act", xi.t[:, n0:n1], pi.t[:, :n1 - n0], [pi], [xi])
                        S.op("act", lambda e: e.activation(ab.t[:], phiA.t[:, j:j + 1], AF.Copy, scale=float(a0)), reads=[phiA], writes=[ab])
                        S.op("act", lambda e: e.activation(V(ang), V(tl), AF.Identity, bias=ab.t[:, 0:1], scale=phiA.t[:, j:j + 1]),
                             reads=[tl, ab, phiA], writes=[ang])

                    def stage1b(u, s, jj):
                        a0, Ls, off = seginfo(s)
                        V = lambda t: t.t[:, :Ls]
                        xr, xi, ang, rr, sn, cs, bA, bB, bC, ab = sets[u % (LA + 1)]
                        S.op("dve", lambda e: e.tensor_scalar(V(rr), V(ang), MAGIC, MAGIC, ALU.add, ALU.subtract), reads=[ang], writes=[rr])
                        S.op("dve", lambda e: e.tensor_tensor(V(sn), V(ang), V(rr), ALU.subtract), reads=[ang, rr], writes=[sn])
                        S.op("act", lambda e: e.activation(V(cs), V(sn), AF.Abs), reads=[sn], writes=[cs])
                        S.op("act", lambda e: e.activation(V(cs), V(cs), AF.Sin, bias=hpi.t[:, 0:1], scale=-TWO_PI), reads=[cs, hpi], writes=[cs])
                        S.op("act", lambda e: e.activation(V(sn), V(sn), AF.Sin, scale=TWO_PI), reads=[sn], writes=[sn])

                    def stage2(u, s, jj):
                        a0, Ls, off = seginfo(s)
                        V = lambda t: t.t[:, :Ls]
                        xr, xi, ang, rr, sn, cs, bA, bB, bC, ab = sets[u % (LA + 1)]
                        S.op("dve", lambda e: e.tensor_tensor(V(bA), V(cs), V(xr), ALU.mult), reads=[cs, xr], writes=[bA])
                        S.op("dve", lambda e: e.tensor_tensor(V(bB), V(cs), V(xi), ALU.mult), reads=[cs, xi], writes=[bB])
                        S.op("dve", lambda e: e.tensor_tensor(V(bC), V(sn), V(xi), ALU.mult), reads=[sn, xi], writes=[bC])
                        S.op("dve", lambda e: e.tensor_tensor(V(rr), V(sn), V(xr), ALU.mult), reads=[sn, xr], writes=[rr])
                        S.op("dve", lambda e: e.tensor_add(V(bA), V(bA), V(bC)), reads=[bA, bC], writes=[bA])
                        S.op("dve", lambda e: e.tensor_sub(V(bB), V(bB), V(rr)), reads=[bB, rr], writes=[bB])
                        ir = 0.0 if s == 0 else carry.t[:, jj, 0:1]
                        ii = 0.0 if s == 0 else carry.t[:, jj, 1:2]
                        S.op("dve", lambda e: e.tensor_tensor_scan(V(xr), rho_t.t[:, jj, :Ls], V(bA), ir, ALU.mult, ALU.add),
                             reads=[rho_t, bA, carry], writes=[xr])
                        S.op("dve", lambda e: e.tensor_tensor_scan(V(xi), rho_t.t[:, jj, :Ls], V(bB), ii, ALU.mult, ALU.add),
                             reads=[rho_t, bB, carry], writes=[xi])
                        S.op("dve", lambda e: e.tensor_tensor(zr.t[:, jj, :Ls], V(cs), V(xr), ALU.mult), reads=[cs, xr], writes=[zr])
                        S.op("dve", lambda e: e.tensor_tensor(z2.t[:, jj, :Ls], V(sn), V(xi), ALU.mult), reads=[sn, xi], writes=[z2])
                        S.op("dve", lambda e: e.tensor_tensor(nzi.t[:, jj, :Ls], V(sn), V(xr), ALU.mult), reads=[sn, xr], writes=[nzi])
                        S.op("dve", lambda e: e.tensor_tensor(z4.t[:, jj, :Ls], V(cs), V(xi), ALU.mult), reads=[cs, xi], writes=[z4])
                        S.op("dve", lambda e: e.tensor_copy(carry.t[:, jj, 0:1], xr.t[:, Ls - 1:Ls]), reads=[xr], writes=[carry])
                        S.op("dve", lambda e: e.tensor_copy(carry.t[:, jj, 1:2], xi.t[:, Ls - 1:Ls]), reads=[xi], writes=[carry])
                        if jj == 3:
                            ypart(s)

                    def ypart(s):
                        a0, Ls, off = seginfo(s)
                        for q0 in range(0, c.SEGB, PC):
                            n = 256 * PC
                            l0 = off + 256 * q0
                            pb = ps[7]
                            for jj in range(4):
                                wsl = slice(jj * 128, (jj + 1) * 128)
                                S.op("pe", lambda e: e.matmul(pb.t[:, :n], wcr.t[:, wsl], zr.t[:, jj, l0:l0 + n],
                                                              start=(jj == 0), stop=False), reads=[wcr, zr], writes=[pb])
                                S.op("pe", lambda e: e.matmul(pb.t[:, :n], nwcr.t[:, wsl], z2.t[:, jj, l0:l0 + n],
                                                              start=False, stop=False), reads=[nwcr, z2], writes=[pb])
                                S.op("pe", lambda e: e.matmul(pb.t[:, :n], nwci.t[:, wsl], nzi.t[:, jj, l0:l0 + n],
                                                              start=False, stop=False), reads=[nwci, nzi], writes=[pb])
                                S.op("pe", lambda e: e.matmul(pb.t[:, :n], nwci.t[:, wsl], z4.t[:, jj, l0:l0 + n],
                                                              start=False, stop=(jj == 3)), reads=[nwci, z4], writes=[pb])
                            no = 128 * PC
                            o0 = (s * c.SEGB + q0) * 128
                            sel, yf, zf, zb = ybufs[yk["i"] % 3]; yk["i"] += 1
                            p4 = pb.t[:, :n].rearrange("p (j two r) -> p j two r", two=2, r=128)
                            s3 = sel.t[:, :no].rearrange("p (j r) -> p j r", r=128)
                            S.op("dve", lambda e: e.tensor_scalar(s3, p4[:, :, 0, :], w01t.t[:, 0:1], None, ALU.mult), reads=[pb, w01t], writes=[sel])
                            S.op("dve", lambda e: e.scalar_tensor_tensor(s3, p4[:, :, 1, :], w01t.t[:, 1:2], s3, ALU.mult, ALU.add),
                                 reads=[pb, w01t, sel], writes=[sel])
                            S.op("dve", lambda e: e.scalar_tensor_tensor(yf.t[:, :no], uo.t[:, o0:o0 + no], dskt.t[:, ct:ct + 1], sel.t[:, :no],
                                                                         ALU.mult, ALU.add), reads=[uo, dskt, sel], writes=[yf])
                            S.op("act", lambda e: e.activation(zf.t[:, :no], yf.t[:, :no], AF.Gelu), reads=[yf], writes=[zf])
                            S.op("dve", lambda e: e.tensor_copy(zb.t[:, :no], zf.t[:, :no]), reads=[zf], writes=[zb])
                            S.dma("sp", ZFd.t[ct * 128:(ct + 1) * 128, o0:o0 + no], zf.t[:, :no], reads=[zf], writes=[ZFd], part=True)
                            S.dma("sp", ZTd.t[ct * 128:(ct + 1) * 128, o0:o0 + no], zb.t[:, :no], reads=[zb], writes=[ZTd], part=True)

                    for i in range(len(units) + LA):
                        if i < len(units):
                            stage1(i, *units[i])
                        if 1 <= i <= len(units):
                            stage1b(i - 1, *units[i - 1])
                        if i >= LA:
                            stage2(i - LA, *units[i - LA])
                        yield

        def glu_phase():
            with ExitStack() as st:
                NP = c.NPAN
                zp = sb(st, "zp", [128, c.NCT, NP], BF16)
                wbufs = [sb(st, f"gw{i}", [128, c.NCT, 512], BF16) for i in range(2)]
                gbufs = [(sb(st, f"gsg{i}", [128, 512], F32), sb(st, f"gzt{i}", [128, 512], F32), sb(st, f"gob{i}", [128, 512], BF16)) for i in range(3)]
                gk = {"i": 0}
                for p0 in range(0, NO, NP):
                    S.dma("sp", zp.t[:], ZTd.t[:, p0:p0 + NP].rearrange("(k p) n -> p k n", p=128), reads=[ZTd], writes=[zp])

                    def epi(col, m, n0, n1, pss):
                        n = n1 - n0
                        sg, zt, ob = gbufs[gk["i"] % 3]; gk["i"] += 1
                        S.op("act", lambda e: e.activation(sg.t[:m, :n], pss[0].t[:m, :n], AF.Sigmoid), reads=[pss[0]], writes=[sg])
                        S.dma("sp", zt.t[:m, :n], ZFd.t[col:col + m, p0 + n0:p0 + n1], reads=[ZFd], writes=[zt])
                        S.op("dve", lambda e: e.tensor_tensor(ob.t[:m, :n], zt.t[:m, :n], sg.t[:m, :n], ALU.mult), reads=[zt, sg], writes=[ob])
                        S.dma("sp", SSTd.t[col:col + m, p0 + n0:p0 + n1], ob.t[:m, :n], reads=[ob], writes=[SSTd], part=True)
                    gemm("fm", wbufs, w_glu, c.SW, 0, c.SW, lambda kt, n0, n1: zp.t[:, kt, n0:n1], [zp], NP, epi)

        HG = min(2, H)
        QG = min(4, NB)

        def attn_prep(stp, st):
            if True:
                cball = sb(stp, "cball", [128, NKT, H], F32)
                mA = sb(stp, "mA", [128, 128], BF16); mB = sb(stp, "mB", [128, 128], BF16)
                cq = sb(st, "cq", [16, NO], F32)
                c3 = [sb(st, f"c3_{i}", [16, NO], BF16) for i in range(3)]
                r1 = sb(st, "cr1", [16, NO], F32)
                S.dma("pool", mA.t[:], maskA, writes=[mA]); S.dma("pool", mB.t[:], maskB, writes=[mB])
                for kt in range(NKT):
                    a, nk = ktile(kt)
                    pb = ps[kt % 2]
                    S.op("pe", lambda e: e.transpose(pb.t[:nk, :H], cacc.t[:H, a:a + nk], id_f.t[:H, :H]), reads=[cacc, id_f], writes=[pb])
                    S.op("dve", lambda e: e.tensor_scalar(cball.t[:nk, kt, :], pb.t[:nk, :H], -1.0, -STAB, ALU.mult, ALU.add),
                         reads=[pb], writes=[cball])
                v4 = cacc.t[:H, 16:16 + 256 * NB].rearrange("h (j two r) -> h j two r", two=2, r=128)
                d3 = cq.t[:H, :].rearrange("h (j r) -> h j r", r=128)
                S.op("dve", lambda e: e.tensor_scalar(d3, v4[:, :, 0, :], w01t.t[:H, 0:1], None, ALU.mult), reads=[cacc, w01t], writes=[cq])
                S.op("dve", lambda e: e.scalar_tensor_tensor(d3, v4[:, :, 1, :], w01t.t[:H, 1:2], d3, ALU.mult, ALU.add),
                     reads=[cacc, w01t, cq], writes=[cq])
                S.op("dve", lambda e: e.tensor_scalar(cq.t[:H, :], cq.t[:H, :], math.sqrt(128.0), None, ALU.mult), reads=[cq], writes=[cq])
                S.op("dve", lambda e: e.tensor_copy(c3[0].t[:H, :], cq.t[:H, :]), reads=[cq], writes=[c3[0]])
                S.op("dve", lambda e: e.tensor_sub(r1.t[:H, :], cq.t[:H, :], c3[0].t[:H, :]), reads=[cq, c3[0]], writes=[r1])
                S.op("dve", lambda e: e.tensor_copy(c3[1].t[:H, :], r1.t[:H, :]), reads=[r1], writes=[c3[1]])
                S.op("dve", lambda e: e.tensor_sub(r1.t[:H, :], r1.t[:H, :], c3[1].t[:H, :]), reads=[r1, c3[1]], writes=[r1])
                S.op("dve", lambda e: e.tensor_copy(c3[2].t[:H, :], r1.t[:H, :]), reads=[r1], writes=[c3[2]])
                for i in range(3):
                    S.dma("sp", CQ3d.t[i], c3[i].t[:H, :], reads=[c3[i]], writes=[CQ3d], part=True)
                S.dma("sp", CQd.t[:, :], cq.t[:H, :], reads=[cq], writes=[CQd])
            return cball, mA, mB

        def attn_gen(st, cball, mA, mB):
            if True:
                vg = sb(st, "vg", [128, NKT, HG * 128], BF16)
                kh = [sb(st, f"kh{i}", [128, L], BF16) for i in range(2)]
                qh = [sb(st, f"qh{i}", [128, NO], BF16) for i in range(2)]
                cqh = [sb(st, f"cqh{i}", [3, NO], BF16) for i in range(2)]
                pt = [sb(st, f"pt{i}", [128, 512], BF16) for i in range(3)]
                rs = [sb(st, f"rs{i}", [128, 512], F32) for i in range(2)]
                ao = [sb(st, f"ao{i}", [128, NO], BF16) for i in range(2)]
                pti = 0
                gi = 0
                scale = 1.0 / math.sqrt(128.0)
                for hg in range(0, H, HG):
                    S.dma("sp", vg.t[:16, 0, :], Vd.t[0:16, hg * 128:(hg + HG) * 128], reads=[Vd], writes=[vg], part=True)
                    for t0 in range(0, 2 * NB, 8):
                        t1 = min(2 * NB, t0 + 8)
                        S.dma("sp", vg.t[:, 1 + t0:1 + t1, :],
                              Vd.t[16 + 128 * t0:16 + 128 * t1, hg * 128:(hg + HG) * 128].rearrange("(t p) c -> p t c", p=128),
                              reads=[Vd], writes=[vg], part=True)
                    for h in range(hg, hg + HG):
                        khb, qhb, cqb, aob = kh[h % 2], qh[h % 2], cqh[h % 2], ao[h % 2]
                        S.dma("sp", khb.t[:], KTd.t[h], reads=[KTd], writes=[khb])
                        S.dma("sp", qhb.t[:], QTd.t[h], reads=[QTd], writes=[qhb])
                        S.dma("sp", cqb.t[:], CQ3d.t[:, h, :], reads=[CQ3d], writes=[cqb])
                        for g in range(0, NB, QG):
                            OT = ps[2 + gi % 2]; SM = ps[4]
                            gi += 1
                            W = QG * 128
                            ktmax = 2 * (g + QG - 1) + 2
                            pend = None
                            for kt in range(0, ktmax + 2):
                                if kt <= ktmax:
                                    a, nk = ktile(kt)
                                    jmin = max(g, (kt - 1) // 2) if kt > 0 else g
                                    q0 = (jmin - g) * 128
                                    qa, qb = g * 128 + q0, (g + QG) * 128
                                    STb = ps[kt % 2]
                                    S.op("pe", lambda e: e.matmul(STb.t[:nk, q0:W], khb.t[:, a:a + nk], qhb.t[:, qa:qb], start=True, stop=False),
                                         reads=[khb, qhb], writes=[STb])
                                    S.op("pe", lambda e: e.matmul(STb.t[:nk, q0:W], ones_bf.t[0:3, :nk], cqb.t[0:3, qa:qb], start=False, stop=True),
                                         reads=[ones_bf, cqb], writes=[STb])
                                    p = pt[pti % 3]; pti += 1
                                    S.op("act", lambda e: e.activation(p.t[:nk, q0:W], STb.t[:nk, q0:W], AF.Exp, bias=cball.t[:nk, kt, h:h + 1], scale=scale),
                                         reads=[STb, cball], writes=[p])
                                    if kt >= 1 and kt % 2 == 1:
                                        j = (kt - 1) // 2
                                        if g <= j < g + QG:
                                            cs_ = slice((j - g) * 128, (j - g + 1) * 128)
                                            S.op("pool", lambda e: e.tensor_tensor(p.t[:, cs_], p.t[:, cs_], mA.t[:, :], ALU.mult), reads=[p, mA], writes=[p])
                                    if kt >= 2 and kt % 2 == 0:
                                        j = (kt - 2) // 2
                                        if g <= j < g + QG:
                                            cs_ = slice((j - g) * 128, (j - g + 1) * 128)
                                            S.op("pool", lambda e: e.tensor_tensor(p.t[:, cs_], p.t[:, cs_], mB.t[:, :], ALU.mult), reads=[p, mB], writes=[p])
                                    cur = (kt, nk, q0, p)
                                else:
                                    cur = None
                                if pend is not None:
                                    kt_, nk_, q0_, p_ = pend
                                    last = kt_ == ktmax
                                    S.op("pe", lambda e: e.matmul(OT.t[:, q0_:W], vg.t[:nk_, kt_, (h - hg) * 128:(h - hg + 1) * 128], p_.t[:nk_, q0_:W],
                                                                  start=(kt_ == 0), stop=last), reads=[vg, p_], writes=[OT])
                                    S.op("pe", lambda e: e.matmul(SM.t[:, q0_:W], ones_bf.t[:nk_, :], p_.t[:nk_, q0_:W], start=(kt_ == 0), stop=last),
                                         reads=[ones_bf, p_], writes=[SM])
                                pend = cur
                                if kt % 3 == 2:
                                    yield
                            r = rs[gi % 2]
                            S.op("dve", lambda e: e.reciprocal(r.t[:, :W], SM.t[:, :W]), reads=[SM], writes=[r])
                            S.op("dve", lambda e: e.tensor_tensor(aob.t[:, g * 128:g * 128 + W], OT.t[:, :W], r.t[:, :W], ALU.mult), reads=[OT, r], writes=[aob])
                        S.dma("sp", ATd.t[h * 128:(h + 1) * 128, :], aob.t[:], reads=[aob], writes=[ATd], part=True)

        _barrier(S)
        with ExitStack() as stp:
            with ExitStack() as st0:
                cball_, mA_, mB_ = attn_prep(stp, st0)
            _barrier(S)
            with ExitStack() as stj:
                g1_ = ssm_gen(stj)
                g2_ = attn_gen(stj, cball_, mA_, mB_)
                live = [g1_, g2_]
                while live:
                    for g_ in list(live):
                        try:
                            next(g_)
                        except StopIteration:
                            live.remove(g_)
        _barrier(S)
        glu_phase()

        def mix_phase():
            with ExitStack() as st:
                NP = min(1024, NO)
                KA, KS = c.AW // 128, c.SW // 128
                cat = sb(st, "cat", [128, KA + KS, NP], BF16)
                mixT = sb(st, "mixT", [128, KT, NP], BF16)
                wbufs = [sb(st, f"mw{i}", [128, max(KT, KA + KS), 256], BF16) for i in range(2)]
                mbufs = [(sb(st, f"ga{i}", [128, 512], F32), sb(st, f"gb{i}", [128, 512], F32),
                          sb(st, f"m1{i}", [128, 512], F32), sb(st, f"m2{i}", [128, 512], F32)) for i in range(2)]
                mk = {"i": 0}
                xo = [sb(st, f"xo{i}", [128, 512], F32) for i in range(2)]
                ho = [sb(st, f"ho{i}", [128, 512], F32) for i in range(2)]
                k = {"i": 0}
                for p0 in range(0, NO, NP):
                    S.dma("sp", cat.t[:, 0:KA, :], ATd.t[:, p0:p0 + NP].rearrange("(k p) n -> p k n", p=128), reads=[ATd], writes=[cat], part=True)
                    S.dma("sp", cat.t[:, KA:KA + KS, :], SSTd.t[:, p0:p0 + NP].rearrange("(k p) n -> p k n", p=128), reads=[SSTd], writes=[cat], part=True)

                    def epi(col, m, n0, n1, pss):
                        n = n1 - n0
                        ga, gb, m1, m2 = mbufs[mk["i"] % 2]; mk["i"] += 1
                        S.dma("sp", ga.t[:m, :n], GAd.t[col:col + m, p0 + n0:p0 + n1], reads=[GAd], writes=[ga])
                        S.dma("sp", gb.t[:m, :n], GBd.t[col:col + m, p0 + n0:p0 + n1], reads=[GBd], writes=[gb])
                        S.op("dve", lambda e: e.tensor_tensor(m1.t[:m, :n], ga.t[:m, :n], pss[0].t[:m, :n], ALU.mult), reads=[ga, pss[0]], writes=[m1])
                        S.op("dve", lambda e: e.tensor_tensor(m2.t[:m, :n], gb.t[:m, :n], pss[1].t[:m, :n], ALU.mult), reads=[gb, pss[1]], writes=[m2])
                        S.op("dve", lambda e: e.tensor_add(mixT.t[:m, col // 128, n0:n1], m1.t[:m, :n], m2.t[:m, :n]), reads=[m1, m2], writes=[mixT])
                    gemm("fm", wbufs, [(w_a, c.AW), (w_b, c.SW)], c.AW + c.SW, 0, D, lambda kt, n0, n1: cat.t[:, kt, n0:n1], [cat], NP, epi,
                         kgroups=[(0, KA), (KA, KA + KS)])

                    def epi2(t0, t1, col, cw, pss):
                        r = t1 - t0
                        x_ = xo[k["i"] % 2]; h_ = ho[k["i"] % 2]; k["i"] += 1
                        S.dma("sp", x_.t[:r, :cw], x_own[p0 + t0:p0 + t1, col:col + cw], writes=[x_])
                        S.op("dve", lambda e: e.tensor_tensor(h_.t[:r, :cw], x_.t[:r, :cw], pss[0].t[:r, :cw], ALU.add), reads=[x_, pss[0]], writes=[h_])
                        S.dma("act", H1d.t[p0 + t0:p0 + t1, col:col + cw], h_.t[:r, :cw], reads=[h_], writes=[H1d], part=True)
                    gemm("tm", wbufs, w_out, D, 0, D, lambda kt, t0, t1: mixT.t[:, kt, t0:t1], [mixT], NP, epi2)

        _barrier(S)
        mix_phase()

        def peer_phase():
            with ExitStack() as st:
                NP = c.PPAN
                NTT = NP // 128
                pan = sb(st, "ppan", [128, KT, NP], BF16)
                et = [sb(st, f"et{i}", [128, 16, 128], F32) for i in range(NTT)]
                thr = sb(st, "thr", [128, NTT, 8], F32); rz = sb(st, "rz", [128, NTT, 8], F32)
                for p0 in range(0, NO, NP):
                    _barrier(S)
                    with ExitStack() as s1:
                        grep = sb(s1, "g2", [128, D], F32); S.dma("sp", grep.t[:], g2rep, writes=[grep])
                        nt = make_nt(s1, "p")
                        nt(H1d.t[p0:p0 + NP, :], NP, grep, pan, srcbuf=H1d)
                    _barrier(S)
                    with ExitStack() as s2:
                        qpT = sb(s2, "qpT", [128, 16, NP], BF16)
                        skb = sb(s2, "skb", [128, 256], BF16); S.dma("pool", skb.t[:], skT, writes=[skb])
                        wbufs = [sb(s2, f"pw{i}", [128, KT, 256], BF16) for i in range(2)]
                        sc = sb(s2, "sc", [128, 16, 128], F32)
                        wk = sb(s2, "wk", [128, 256], F32)
                        m16 = sb(s2, "m16", [128, 16, 16], F32); e16 = sb(s2, "e16", [128, 16, 16], F32)
                        nm = sb(s2, "nm", [128, 16], F32)
                        cand = sb(s2, "cand", [128, 256], F32); c16 = sb(s2, "c16", [128, 16], F32)

                        def epi_q(col, m, n0, n1, pss):
                            copy(alt(), qpT.t[:, col // 128, n0:n1], pss[0].t[:, :n1 - n0], [pss[0]], [qpT])
                        gemm("fm", wbufs, w_query, D, 0, c.QW, lambda kt, n0, n1: pan.t[:, kt, n0:n1], [pan], NP, epi_q)
                        for tt in range(NTT):
                            ts_ = slice(tt * 128, (tt + 1) * 128)
                            for hc in range(16):
                                pb = ps[hc // 4]
                                S.op("pe", lambda e: e.matmul(pb.t[:, (hc % 4) * 128:(hc % 4 + 1) * 128], qpT.t[:, hc, ts_],
                                                              skb.t[:, (hc % 2) * 128:(hc % 2 + 1) * 128], start=True, stop=True),
                                     reads=[qpT, skb], writes=[pb])
                                if hc % 4 == 3:
                                    copy(alt(), sc.t[:, hc - 3:hc + 1, :], pb.t[:, :].rearrange("p (a b) -> p a b", b=128), [pb], [sc])
                            for hc in range(16):
                                S.op("dve", lambda e: e.max(m16.t[:, hc, 0:8], sc.t[:, hc, :]), reads=[sc], writes=[m16])
                                S.op("dve", lambda e: e.match_replace(wk.t[:, :128], m16.t[:, hc, 0:8], sc.t[:, hc, :], -1e30),
                                     reads=[sc, m16], writes=[wk])
                                S.op("dve", lambda e: e.max(m16.t[:, hc, 8:16], wk.t[:, :128]), reads=[wk], writes=[m16])
                            S.op("dve", lambda e: e.tensor_scalar(nm.t[:, :], m16.t[:, :, 0], -1.0, None, ALU.mult), reads=[m16], writes=[nm])
                            for hc in range(16):
                                S.op("act", lambda e: e.activation(et[tt].t[:, hc, :], sc.t[:, hc, :], AF.Exp, bias=nm.t[:, hc:hc + 1], scale=1.0),
                                     reads=[sc, nm], writes=[et[tt]])
                                S.op("act", lambda e: e.activation(e16.t[:, hc, :], m16.t[:, hc, :], AF.Exp, bias=nm.t[:, hc:hc + 1], scale=1.0),
                                     reads=[m16, nm], writes=[e16])
                            for h in range(8):
                                c3 = cand.t[:, :].rearrange("p (a b) -> p a b", b=16)
                                S.op("dve", lambda e: e.tensor_tensor(c3, e16.t[:, 2 * h, :].unsqueeze(2).to_broadcast([128, 16, 16]),
                                                                      e16.t[:, 2 * h + 1, :].unsqueeze(1).to_broadcast([128, 16, 16]), ALU.mult),
                                     reads=[e16], writes=[cand])
                                S.op("dve", lambda e: e.max(c16.t[:, 0:8], cand.t[:, :]), reads=[cand], writes=[c16])
                                S.op("dve", lambda e: e.match_replace(wk.t[:, :], c16.t[:, 0:8], cand.t[:, :], -1.0), reads=[cand, c16], writes=[wk])
                                S.op("dve", lambda e: e.max(c16.t[:, 8:16], wk.t[:, :]), reads=[wk], writes=[c16])
                                S.op("dve", lambda e: e.tensor_scalar(thr.t[:, tt, h:h + 1], c16.t[:, 15:16], 1.0 - 1e-5, None, ALU.mult),
                                     reads=[c16], writes=[thr])
                                S.op("dve", lambda e: e.tensor_reduce(rz.t[:, tt, h:h + 1], c16.t[:, :], AX.X, ALU.add), reads=[c16], writes=[rz])
                            S.op("dve", lambda e: e.reciprocal(rz.t[:, tt, :], rz.t[:, tt, :]), reads=[rz], writes=[rz])
                            for h in range(8):
                                S.op("dve", lambda e: e.tensor_scalar(e16.t[:, 2 * h, :], e16.t[:, 2 * h, :], rz.t[:, tt, h:h + 1], None, ALU.mult),
                                     reads=[e16, rz], writes=[e16])
                                S.op("dve", lambda e: e.tensor_scalar(et[tt].t[:, 2 * h, :], et[tt].t[:, 2 * h, :], rz.t[:, tt, h:h + 1], None, ALU.mult),
                                     reads=[et[tt], rz], writes=[et[tt]])
                                c3 = cand.t[:, :].rearrange("p (a b) -> p a b", b=16)
                                S.op("dve", lambda e: e.tensor_tensor(c3, e16.t[:, 2 * h, :].unsqueeze(2).to_broadcast([128, 16, 16]),
                                                                      e16.t[:, 2 * h + 1, :].unsqueeze(1).to_broadcast([128, 16, 16]), ALU.mult),
                                     reads=[e16], writes=[cand])
                                S.op("dve", lambda e: e.max(c16.t[:, 0:8], cand.t[:, :]), reads=[cand], writes=[c16])
                                S.op("dve", lambda e: e.match_replace(wk.t[:, :], c16.t[:, 0:8], cand.t[:, :], -1.0), reads=[cand, c16], writes=[wk])
                                S.op("dve", lambda e: e.max(c16.t[:, 8:16], wk.t[:, :]), reads=[wk], writes=[c16])
                                S.op("dve", lambda e: e.tensor_scalar(thr.t[:, tt, h:h + 1], c16.t[:, 15:16], 1.0 - 1e-5, None, ALU.mult),
                                     reads=[c16], writes=[thr])
                    _barrier(S)
                    with ExitStack() as s3:
                        wbufs = [sb(s3, f"aw{i}", [128, KT, 512], BF16) for i in range(2)]
                        gT = [sb(s3, f"gT{i}", [128, 4, NP], F32) for i in range(2)]
                        Mb = [sb(s3, f"Mb{i}", [128, 512], F32) for i in range(3)]
                        Mh = [sb(s3, f"Mh{i}", [128, 512], BF16) for i in range(3)]
                        wst = [sb(s3, f"wst{i}", [128, 4, NP], BF16) for i in range(2)]
                        k = {"m": 0}
                        NSB = c.PN // 512
                        WT = [ps[4 + b] for b in range(4)]

                        def load_w(sbi):
                            wt = wbufs[sbi % 2]
                            for k0 in range(0, KT, 8):
                                k1 = min(KT, k0 + 8)
                                srcap = euT[k0 * 128:k1 * 128, sbi * 512:(sbi + 1) * 512].rearrange("(kt p) c -> p kt c", p=128)
                                S.dma("pool", wt.t[:, k0:k1, :], srcap, writes=[wt], part=True)

                        def issue_AT(sbi):
                            wt = wbufs[sbi % 2]
                            g_ = gT[sbi % 2]
                            for j in range(4):
                                pb = ps[j]
                                for kt in range(KT):
                                    S.op("pe", lambda e: e.matmul(pb.t[:, :NP], wt.t[:, kt, j * 128:(j + 1) * 128], pan.t[:, kt, 0:NP],
                                                                  start=(kt == 0), stop=(kt == KT - 1)), reads=[wt, pan], writes=[pb])
                                    if kt % 4 == 3 and kt != KT - 1:
                                        yield
                                S.op("act", lambda e: e.activation(g_.t[:, j, :], pb.t[:, :NP], AF.Gelu), reads=[pb], writes=[g_])
                                yield

                        load_w(0)
                        if NSB > 1:
                            load_w(1)
                        for _ in issue_AT(0):
                            pass
                        for sbi in range(NSB):
                            if sbi + 2 < NSB:
                                load_w(sbi + 2)
                            nxt = issue_AT(sbi + 1) if sbi + 1 < NSB else None
                            g_ = gT[sbi % 2]
                            col0 = sbi * 512
                            i1a = col0 // 128
                            pairs = [(tt, h) for tt in range(NTT) for h in range(8)]
                            held = []
                            for i in range(len(pairs) + 1):
                                if i < len(pairs):
                                    tt, h = pairs[i]
                                    M_ = Mb[k["m"] % 3]; Mh_ = Mh[k["m"] % 3]; k["m"] += 1
                                    P3 = M_.t[:, :].rearrange("p (a b) -> p a b", b=128)
                                    S.op("dve", lambda e: e.tensor_tensor(P3, et[tt].t[:, 2 * h, i1a:i1a + 4].unsqueeze(2).to_broadcast([128, 4, 128]),
                                                                          et[tt].t[:, 2 * h + 1, :].unsqueeze(1).to_broadcast([128, 4, 128]), ALU.mult),
                                         reads=[et[tt]], writes=[M_])
                                    held.append((tt, h, M_, Mh_))
                                if i >= 1:
                                    tt, h, M_, Mh_ = held.pop(0)
                                    S.op("dve", lambda e: e.scalar_tensor_tensor(Mh_.t[:, :], M_.t[:, :], thr.t[:, tt, h:h + 1], M_.t[:, :],
                                                                                 ALU.is_ge, ALU.mult), reads=[M_, thr], writes=[Mh_])
                                    for b in range(4):
                                        S.op("pe", lambda e: e.matmul(WT[b].t[:, tt * 128:(tt + 1) * 128], Mh_.t[:, b * 128:(b + 1) * 128],
                                                                      id_bf.t[:, :], start=(h == 0), stop=(h == 7)),
                                             reads=[Mh_, id_bf], writes=[WT[b]])
                                    if nxt is not None:
                                        try:
                                            next(nxt)
                                        except StopIteration:
                                            nxt = None
                            if nxt is not None:
                                for _ in nxt:
                                    pass
                            ws = wst[sbi % 2]
                            for b in range(4):
                                S.op("dve", lambda e: e.tensor_tensor(ws.t[:, b, :], WT[b].t[:, :NP], g_.t[:, b, :], ALU.mult),
                                     reads=[WT[b], g_], writes=[ws])
                            for b in range(4):
                                S.dma("act", WGTd.t[col0 + b * 128:col0 + (b + 1) * 128, p0:p0 + NP], ws.t[:, b, :], reads=[ws], writes=[WGTd], part=True)

            _barrier(S)
            with ExitStack() as st:
                NY = min(1024, NO)
                NYT = NY // 128
                EC = 16
                wv = [sb(st, f"yv{i}", [128, EC, 512], BF16) for i in range(2)]
                wa = [sb(st, f"ya{i}", [128, EC, NY], BF16) for i in range(2)]
                h1t = [sb(st, f"yh{i}", [128, 512], F32) for i in range(2)]
                yo = [sb(st, f"yo{i}", [128, 512], F32) for i in range(2)]
                NEC = c.PN // (128 * EC)
                gi = 0
                ci = 0
                oi = 0
                for cb in range(0, D, 512):
                    for p0 in range(0, NO, NY):
                        bks = [ps[i] for i in range(NYT)]
                        gi += 1
                        for ec in range(NEC):
                            e0 = ec * EC * 128
                            v_, a_ = wv[ci % 2], wa[ci % 2]
                            ci += 1
                            for k0 in range(0, EC, 8):
                                S.dma("pool", v_.t[:, k0:k0 + 8, :], ev[e0 + k0 * 128:e0 + (k0 + 8) * 128, cb:cb + 512].rearrange("(k p) c -> p k c", p=128),
                                      writes=[v_], part=True)
                                S.dma("sp", a_.t[:, k0:k0 + 8, :], WGTd.t[e0 + k0 * 128:e0 + (k0 + 8) * 128, p0:p0 + NY].rearrange("(k p) c -> p k c", p=128),
                                      reads=[WGTd], writes=[a_], part=True)
                            for tt in range(NYT):
                                for kt in range(EC):
                                    S.op("pe", lambda e: e.matmul(bks[tt].t[:, :512], a_.t[:, kt, tt * 128:(tt + 1) * 128], v_.t[:, kt, :],
                                                                  start=(ec == 0 and kt == 0), stop=(ec == NEC - 1 and kt == EC - 1)),
                                         reads=[a_, v_], writes=[bks[tt]])
                        for tt in range(NYT):
                            h_, o_ = h1t[oi % 2], yo[oi % 2]
                            oi += 1
                            r0 = p0 + tt * 128
                            S.dma("sp", h_.t[:], H1d.t[r0:r0 + 128, cb:cb + 512], reads=[H1d], writes=[h_])
                            S.op("dve", lambda e: e.tensor_tensor(o_.t[:], h_.t[:], bks[tt].t[:, :512], ALU.add), reads=[h_, bks[tt]], writes=[o_])
                            S.dma("act", out_own[r0:r0 + 128, cb:cb + 512], o_.t[:], reads=[o_], writes=[OUTb], part=True)

        _barrier(S)
        peer_phase()

        for i in range(NDSEM):
            if S.dval[i]:
                nc.sync.wait_ge(S.dsem[i], S.dval[i])
        print("instructions:", S.ninst, {k: v for k, v in S.cnt.items()})
    return nc


def _prep(c, inp, core):
    f32 = np.float32
    b, hh = core // 2, core % 2
    D, SEQ, NO, H, G, NJ, NCT = c.D, c.SEQ, c.NO, c.H, c.G, c.NJ, c.NCT
    A = lambda v: np.ascontiguousarray(np.asarray(v), dtype=f32)
    x = np.asarray(inp["x"][b], dtype=f32)
    m = {}
    m["x_ctx"] = np.concatenate([np.asarray(inp["meta_tokens"], dtype=f32), x], 0)
    m["x_own"] = A(x.reshape(SEQ // 128, 128, D)[hh::2].reshape(NO, D))
    m["g1rep"] = A(np.broadcast_to(np.asarray(inp["norm1_g"][0])[None, :], (128, D)))
    m["g2rep"] = A(np.broadcast_to(np.asarray(inp["norm2_g"][0])[None, :], (128, D)))
    m["w_in"] = A(inp["w_in"][0])
    m["bfg"] = A(np.asarray(inp["b_forget"][0]).reshape(H, 1))
    m["qg"] = A(np.asarray(inp["q_norm_g"][0]).reshape(128, 1))
    m["kg"] = A(np.asarray(inp["k_norm_g"][0]).reshape(128, 1))
    lr = np.asarray(inp["lam_re"][0], dtype=f32); li = np.asarray(inp["lam_im"][0], dtype=f32)
    ld = np.asarray(inp["log_dt"][0], dtype=f32)
    m["lrA"] = A(lr.reshape(NJ, 128).T); m["liA"] = A(li.reshape(NJ, 128).T)
    m["ldA"] = A(np.repeat(ld.reshape(NJ, 2), 64, axis=1).T)
    m["lrB"] = A(np.broadcast_to(lr.reshape(1, -1), (128, G * 64)))
    m["liB"] = A(np.broadcast_to(li.reshape(1, -1), (128, G * 64)))
    m["ldB"] = A(np.broadcast_to(np.repeat(ld, 64)[None, :], (128, G * 64)))
    bre = np.asarray(inp["b_re"][0], dtype=f32); bim = np.asarray(inp["b_im"][0], dtype=f32)
    cre = np.asarray(inp["c_re"][0], dtype=f32); cim = np.asarray(inp["c_im"][0], dtype=f32)
    brB = np.zeros((128, G * 64), f32); biB = np.zeros((128, G * 64), f32)
    crB = np.zeros((128, NJ * 128), f32); ciB = np.zeros((128, NJ * 128), f32)
    for g in range(G):
        r0 = (g % 8) * 16
        brB[r0:r0 + 16, g * 64:(g + 1) * 64] = bre[g].T
        biB[r0:r0 + 16, g * 64:(g + 1) * 64] = bim[g].T
        j, g2 = g // 2, g % 2
        crB[g2 * 64:(g2 + 1) * 64, j * 128 + r0:j * 128 + r0 + 16] = cre[g].T
        ciB[g2 * 64:(g2 + 1) * 64, j * 128 + r0:j * 128 + r0 + 16] = cim[g].T
    m["brB"], m["biB"], m["crB"], m["ciB"] = brB, biB, crB, ciB
    m["dsk"] = A(np.asarray(inp["d_skip"][0]).reshape(NCT, 128).T)
    m["w_glu"] = A(inp["w_glu"][0]); m["w_a"] = A(inp["w_branch_attn"][0]); m["w_b"] = A(inp["w_branch_ssm"][0])
    m["w_out"] = A(inp["w_out"][0]); m["w_query"] = A(inp["w_query"][0])
    m["skT"] = A(np.asarray(inp["sub_keys"][0]).transpose(2, 0, 1).reshape(128, 256))
    m["euT"] = A(np.asarray(inp["expert_u"][0]).T); m["ev"] = A(inp["expert_v"][0])
    tri = (np.arange(128)[None, :] >= np.arange(128)[:, None]).astype(f32)
    m["maskA"] = tri if hh == 0 else np.ones((128, 128), f32)
    m["maskB"] = np.zeros((128, 128), f32) if hh == 0 else tri
    m["w01"] = A(np.broadcast_to(np.array([[1.0, 0.0]] if hh == 0 else [[0.0, 1.0]], f32), (128, 2)))
    LSM = 256 * c.SEGB + 16
    m["t_loc"] = A(np.broadcast_to(np.arange(LSM, dtype=f32)[None, :], (128, LSM)))
    io = np.arange(NO)
    pos = 16 + 128 * (2 * (io // 128) + hh) + (io % 128)
    m["t_own"] = A(np.broadcast_to(pos.astype(f32)[None, :], (128, NO)))
    m["ident"] = np.eye(128, dtype=f32)
    return m


_NC_CACHE = {}


def run_cfg(c, inputs):
    key = (c.D, c.SEQ, c.B)
    if key not in _NC_CACHE:
        _NC_CACHE[key] = build(c)
    nc = _NC_CACHE[key]
    ncores = 2 * c.B
    shared = None
    in_maps = []
    for core in range(ncores):
        in_maps.append(_prep(c, inputs, core))
    res = run_bass_kernel_spmd(nc, in_maps, core_ids=list(range(ncores)))
    if getattr(c, "debug", False):
        c.dbg = res.results
    out = np.zeros((c.B, c.SEQ, c.D), np.float32)
    for core in range(ncores):
        b, hh = core // 2, core % 2
        o = np.asarray(res.results[core]["out_own"], dtype=np.float32).reshape(c.NB, 128, c.D)
        out[b].reshape(c.SEQ // 128, 128, c.D)[hh::2] = o
    return out


def kernel(**inputs):
    return run_cfg(Cfg(), inputs)
```
